# Optimizing a Trainium2 kernel written in Bass

```python
import math
import jax, jax.numpy as jnp
from jax import lax
import numpy as np

D_MODEL = 1024
BATCH = 2
SEQ = 8192
DEPTH = 2

GRID_W = 64
CTX_LEN = 256
N_BRANCH = 3
D_BR = D_MODEL // 2
HY_ORDER = 2
HY_BANDS = 16
HY_POS_DIM = 1 + 2 * HY_BANDS
HY_FILTER_HID = 64
HY_DECAY_SLOW = 3.07
HY_DECAY_FAST = 15.35
FN_GROUPS = 4
FN_GROUP_W = D_BR // FN_GROUPS
ML_HEADS = 4
ML_HD = D_BR // ML_HEADS
ML_CHUNK = 128
ML_FGATE_LO = 3.0
ML_FGATE_HI = 6.0
P_HY = 3 * D_BR
P_FN = D_BR
P_ML = 4 * D_BR
P_MLG = 4 * ML_HEADS
P_GATE = N_BRANCH * D_MODEL
OFF_FN = P_HY
OFF_ML = OFF_FN + P_FN
OFF_MLG = OFF_ML + P_ML
OFF_GATE = OFF_MLG + P_MLG
P_TOT = OFF_GATE + P_GATE
MOE_GROUPS = 4
MOE_PER_GROUP = 4
MOE_EXPERTS = MOE_GROUPS * MOE_PER_GROUP
MOE_TOPK = 2
EXPERT_HID = D_MODEL // 4
EPS = 1e-6

kernel_name = "hybrid_hyena_fnet_mlstm_hmoe_dit"


def rmsnorm(x, w):
    xf = x.astype(jnp.float32)
    y = xf * lax.rsqrt(jnp.mean(xf * xf, axis=-1, keepdims=True) + EPS)
    return (y * w.astype(jnp.float32)).astype(x.dtype)


def modulate(h, shift, scale):
    return h * (1 + scale) + shift


def dwconv_grid(u, w, b, grid):
    B, L, C = u.shape
    rows, width = grid
    img = u.reshape(B, rows, width, C)
    out = lax.conv_general_dilated(img, w[:, :, None, :].astype(u.dtype), (1, 1), "SAME",
                                   dimension_numbers=("NHWC", "HWIO", "NHWC"),
                                   feature_group_count=C)
    return out.reshape(B, L, C) + b.astype(u.dtype)


def hyena_filter(L, lp):
    f32 = jnp.float32
    t = jnp.arange(L, dtype=f32) / L
    bands = jnp.linspace(1e-4, HY_BANDS - 1, HY_BANDS, dtype=f32)
    ang = (2 * math.pi) * t[:, None] * bands[None, :]
    feats = jnp.concatenate([t[:, None], jnp.cos(ang), jnp.sin(ang)], axis=-1)
    freq = lp["hy_f_freq"].astype(f32)
    h = jnp.sin(freq * (feats @ lp["hy_f_w1"].astype(f32) + lp["hy_f_b1"].astype(f32)))
    h = jnp.sin(freq * (h @ lp["hy_f_w2"].astype(f32) + lp["hy_f_b2"].astype(f32)))
    h = (h @ lp["hy_f_w3"].astype(f32)).reshape(L, HY_ORDER, 2, D_BR)
    h = h * jnp.exp(-t[:, None, None, None] * jnp.abs(lp["hy_decay"].astype(f32)))
    fwd, bwd = h[:, :, 0], h[:, :, 1]
    k = jnp.concatenate([fwd, jnp.zeros_like(fwd[:1]), jnp.flip(bwd[1:], axis=0)], axis=0)
    k = k * lax.rsqrt(jnp.sum(k * k, axis=0, keepdims=True) + EPS)
    return jnp.fft.rfft(k, axis=0)


def fft_longconv(u, kf, skip):
    L = u.shape[1]
    uf = jnp.fft.rfft(u, n=2 * L, axis=1)
    y = jnp.fft.irfft(uf * kf[None], n=2 * L, axis=1)[:, :L]
    return y + u * skip


def hyena_branch(zh, grid, lp):
    L = zh.shape[1]
    u = dwconv_grid(zh, lp["hy_conv_w"], lp["hy_conv_b"], grid).astype(jnp.float32)
    v, x1, x2 = jnp.split(u, 3, axis=-1)
    kf = hyena_filter(L, lp)
    skip = lp["hy_skip"].astype(jnp.float32)
    z = x1 * fft_longconv(v, kf[:, 0], skip[0])
    return x2 * fft_longconv(z, kf[:, 1], skip[1])


def fnet_branch(zf):
    B, L, _ = zf.shape
    u = zf.astype(jnp.float32).reshape(B, L, FN_GROUPS, FN_GROUP_W)
    y = jnp.fft.fft2(u, axes=(1, 3), norm="ortho").real
    return y.reshape(B, L, D_BR)


def zero_state(batch):
    f32 = jnp.float32
    return (jnp.zeros((batch, ML_HEADS, ML_HD, ML_HD), f32),
            jnp.zeros((batch, ML_HEADS, ML_HD), f32),
            jnp.zeros((batch, ML_HEADS), f32))


def mlstm_scan(q, k, v, i_pre, f_pre, state):
    B, H, L, d = q.shape
    T = ML_CHUNK
    nc = L // T

    def chunks(a):
        return jnp.moveaxis(a.reshape((B, H, nc, T) + a.shape[3:]), 2, 0)

    q = q * (d ** -0.5)
    logf = jax.nn.log_sigmoid(f_pre)
    mask = jnp.tril(jnp.ones((T, T), dtype=bool))

    def step(carry, inp):
        C, n, m = carry
        qc, kc, vc, ic, lfc = inp
        b = jnp.cumsum(lfc, axis=-1)
        Dm = jnp.where(mask, b[..., :, None] - b[..., None, :] + ic[..., None, :], -jnp.inf)
        inter = b + m[..., None]
        m_row = jnp.maximum(inter, jnp.max(Dm, axis=-1))
        w_intra = jnp.exp(Dm - m_row[..., None])
        w_inter = jnp.exp(inter - m_row)
        s = jnp.einsum("bhtk,bhsk->bhts", qc, kc) * w_intra
        num = jnp.einsum("bhts,bhsv->bhtv", s, vc) + w_inter[..., None] * jnp.einsum("bhvk,bhtk->bhtv", C, qc)
        den = jnp.sum(s, axis=-1) + w_inter * jnp.einsum("bhk,bhtk->bht", n, qc)
        den = jnp.maximum(jnp.abs(den), jnp.exp(-m_row))
        h = num / den[..., None]
        bT = b[..., -1]
        a = bT[..., None] - b + ic
        m_new = jnp.maximum(bT + m, jnp.max(a, axis=-1))
        sc = jnp.exp(a - m_new[..., None])
        decay = jnp.exp(bT + m - m_new)
        C_new = decay[..., None, None] * C + jnp.einsum("bhs,bhsv,bhsk->bhvk", sc, vc, kc)
        n_new = decay[..., None] * n + jnp.einsum("bhs,bhsk->bhk", sc, kc)
        return (C_new, n_new, m_new), h

    state, h = lax.scan(step, state, (chunks(q), chunks(k), chunks(v), chunks(i_pre), chunks(logf)))
    return jnp.moveaxis(h, 0, 2).reshape(B, H, L, d), state


def mlstm_branch(zm, zg, grid, lp, state_f, state_b):
    B, L, _ = zm.shape
    qk = jax.nn.silu(dwconv_grid(zm[..., :2 * D_BR], lp["ml_conv_w"], lp["ml_conv_b"], grid))
    v = zm[..., 2 * D_BR:3 * D_BR]
    o = zm[..., 3 * D_BR:]

    def heads(a):
        return a.reshape(B, L, ML_HEADS, ML_HD).transpose(0, 2, 1, 3).astype(jnp.float32)

    q, k, v = heads(qk[..., :D_BR]), heads(qk[..., D_BR:]), heads(v)
    g = zg.astype(jnp.float32).reshape(B, L, 4, ML_HEADS).transpose(2, 0, 3, 1)
    h_f, st_f = mlstm_scan(q, k, v, g[0], g[1], state_f)
    rev = lambda a: jnp.flip(a, axis=2)
    h_b, st_b = mlstm_scan(rev(q), rev(k), rev(v), rev(g[2]), rev(g[3]), state_b)
    h = h_f + rev(h_b)
    h = h * lax.rsqrt(jnp.mean(h * h, axis=-1, keepdims=True) + EPS)
    h = h.transpose(0, 2, 1, 3).reshape(B, L, D_BR) * lp["ml_norm_w"].astype(jnp.float32)
    return jax.nn.sigmoid(o.astype(jnp.float32)) * h, st_f, st_b


def merge_branches(y_hy, y_fn, y_ml, gate_pre, lp):
    B, L, _ = gate_pre.shape
    dt = gate_pre.dtype
    ys = jnp.stack([y_hy, y_fn, y_ml], axis=2).astype(dt)
    proj = jnp.einsum("blkc,kcd->blkd", ys, lp["w_branch"])
    g = jax.nn.sigmoid(gate_pre.astype(jnp.float32)).reshape(B, L, N_BRANCH, D_MODEL).astype(dt)
    return jnp.sum(g * proj, axis=2) @ lp["w_out"]


def token_mixer_out(z, grid, lp, ml_out):
    y_hy = hyena_branch(z[..., :OFF_FN], grid, lp)
    y_fn = fnet_branch(z[..., OFF_FN:OFF_ML])
    return merge_branches(y_hy, y_fn, ml_out, z[..., OFF_GATE:], lp)


def hier_moe(h, lp):
    B, L, D = h.shape
    t = h.reshape(B * L, D)
    g_logits = (t @ lp["moe_rg_w"] + lp["moe_rg_b"]).astype(jnp.float32)
    p_top, g_sel = lax.top_k(jax.nn.softmax(g_logits, axis=-1), 1)
    e_logits = (t @ lp["moe_re_w"] + lp["moe_re_b"]).astype(jnp.float32).reshape(B * L, MOE_GROUPS, MOE_PER_GROUP)
    g_onehot = jax.nn.one_hot(g_sel[:, 0], MOE_GROUPS, dtype=jnp.float32)
    e_in_group = jnp.sum(e_logits * g_onehot[:, :, None], axis=1)
    e_top, e_sel = lax.top_k(e_in_group, MOE_TOPK)
    w_sel = p_top * jax.nn.softmax(e_top, axis=-1)
    e_idx = g_sel * MOE_PER_GROUP + e_sel
    combine = jnp.sum(jax.nn.one_hot(e_idx, MOE_EXPERTS, dtype=jnp.float32) * w_sel[..., None], axis=1)
    hg = jnp.einsum("nd,edh->neh", t, lp["moe_w_gate"])
    hu = jnp.einsum("nd,edh->neh", t, lp["moe_w_up"])
    a = jax.nn.silu(hg) * hu * combine[..., None].astype(t.dtype)
    return jnp.einsum("neh,ehd->nd", a, lp["moe_w_down"]).reshape(B, L, D)


def setup_inputs(seed: int = 0) -> dict:
    key = jax.random.key(seed)
    ks = iter(jax.random.split(key, 40))
    f32 = jnp.float32

    def nrm(shape, scale):
        return jax.random.normal(next(ks), shape, f32) * scale

    x = nrm((BATCH, SEQ, D_MODEL), 1.0)
    c = nrm((BATCH, D_MODEL), 1.0)
    ctx = nrm((BATCH, CTX_LEN, D_MODEL), 1.0)
    c_ctx = nrm((D_MODEL,), 1.0)
    ada_w = nrm((DEPTH, D_MODEL, 6 * D_MODEL), 0.5 * D_MODEL ** -0.5)
    ada_b = nrm((DEPTH, 6 * D_MODEL), 0.02)
    norm1_w = 1.0 + nrm((DEPTH, D_MODEL), 0.02)
    norm2_w = 1.0 + nrm((DEPTH, D_MODEL), 0.02)
    w_in = nrm((DEPTH, D_MODEL, P_TOT), D_MODEL ** -0.5)
    lin = jnp.linspace(ML_FGATE_LO, ML_FGATE_HI, ML_HEADS, dtype=f32)
    zh = jnp.zeros((ML_HEADS,), f32)
    fbias = jnp.concatenate([zh, lin, zh, lin])
    b_in = nrm((DEPTH, P_TOT), 0.02).at[:, OFF_MLG:OFF_GATE].add(fbias)
    hy_conv_w = nrm((DEPTH, 3, 3, P_HY), 1.0 / 3.0)
    hy_conv_b = nrm((DEPTH, P_HY), 0.02)
    hy_f_w1 = nrm((DEPTH, HY_POS_DIM, HY_FILTER_HID), HY_POS_DIM ** -0.5)
    hy_f_b1 = nrm((DEPTH, HY_FILTER_HID), 0.02)
    hy_f_w2 = nrm((DEPTH, HY_FILTER_HID, HY_FILTER_HID), HY_FILTER_HID ** -0.5)
    hy_f_b2 = nrm((DEPTH, HY_FILTER_HID), 0.02)
    hy_f_w3 = nrm((DEPTH, HY_FILTER_HID, HY_ORDER * 2 * D_BR), HY_FILTER_HID ** -0.5)
    hy_f_freq = 1.0 + nrm((DEPTH, HY_FILTER_HID), 0.02)
    decay_base = jnp.linspace(HY_DECAY_SLOW, HY_DECAY_FAST, D_BR, dtype=f32)
    hy_decay = decay_base[None, None, None, :] + nrm((DEPTH, HY_ORDER, 2, D_BR), 0.1)
    hy_skip = nrm((DEPTH, HY_ORDER, D_BR), 0.5)
    ml_conv_w = nrm((DEPTH, 3, 3, 2 * D_BR), 1.0 / 3.0)
    ml_conv_b = nrm((DEPTH, 2 * D_BR), 0.02)
    ml_norm_w = 1.0 + nrm((DEPTH, D_BR), 0.02)
    w_branch = nrm((DEPTH, N_BRANCH, D_BR, D_MODEL), D_BR ** -0.5)
    w_out = nrm((DEPTH, D_MODEL, D_MODEL), D_MODEL ** -0.5)
    moe_rg_w = nrm((DEPTH, D_MODEL, MOE_GROUPS), D_MODEL ** -0.5)
    moe_rg_b = nrm((DEPTH, MOE_GROUPS), 0.01)
    moe_re_w = nrm((DEPTH, D_MODEL, MOE_EXPERTS), D_MODEL ** -0.5)
    moe_re_b = nrm((DEPTH, MOE_EXPERTS), 0.01)
    moe_w_gate = nrm((DEPTH, MOE_EXPERTS, D_MODEL, EXPERT_HID), D_MODEL ** -0.5)
    moe_w_up = nrm((DEPTH, MOE_EXPERTS, D_MODEL, EXPERT_HID), D_MODEL ** -0.5)
    moe_w_down = nrm((DEPTH, MOE_EXPERTS, EXPERT_HID, D_MODEL), EXPERT_HID ** -0.5)
    norm_f_w = 1.0 + nrm((D_MODEL,), 0.02)
    return {"x": x, "c": c, "ctx": ctx, "c_ctx": c_ctx, "ada_w": ada_w, "ada_b": ada_b,
            "norm1_w": norm1_w, "norm2_w": norm2_w, "w_in": w_in, "b_in": b_in,
            "hy_conv_w": hy_conv_w, "hy_conv_b": hy_conv_b, "hy_f_w1": hy_f_w1, "hy_f_b1": hy_f_b1,
            "hy_f_w2": hy_f_w2, "hy_f_b2": hy_f_b2, "hy_f_w3": hy_f_w3, "hy_f_freq": hy_f_freq,
            "hy_decay": hy_decay, "hy_skip": hy_skip, "ml_conv_w": ml_conv_w, "ml_conv_b": ml_conv_b,
            "ml_norm_w": ml_norm_w, "w_branch": w_branch, "w_out": w_out,
            "moe_rg_w": moe_rg_w, "moe_rg_b": moe_rg_b, "moe_re_w": moe_re_w, "moe_re_b": moe_re_b,
            "moe_w_gate": moe_w_gate, "moe_w_up": moe_w_up, "moe_w_down": moe_w_down,
            "norm_f_w": norm_f_w}


def reference(x, c, ctx, c_ctx, ada_w, ada_b, norm1_w, norm2_w, w_in, b_in,
              hy_conv_w, hy_conv_b, hy_f_w1, hy_f_b1, hy_f_w2, hy_f_b2, hy_f_w3, hy_f_freq,
              hy_decay, hy_skip, ml_conv_w, ml_conv_b, ml_norm_w, w_branch, w_out,
              moe_rg_w, moe_rg_b, moe_re_w, moe_re_b, moe_w_gate, moe_w_up, moe_w_down,
              norm_f_w):
    B = x.shape[0]
    rows = x.shape[1] // GRID_W
    lat_grid = (rows, GRID_W)
    ctx_grid = (1, ctx.shape[1])
    for l in range(DEPTH):
        lp = {"hy_conv_w": hy_conv_w[l], "hy_conv_b": hy_conv_b[l], "hy_f_w1": hy_f_w1[l],
              "hy_f_b1": hy_f_b1[l], "hy_f_w2": hy_f_w2[l], "hy_f_b2": hy_f_b2[l],
              "hy_f_w3": hy_f_w3[l], "hy_f_freq": hy_f_freq[l], "hy_decay": hy_decay[l],
              "hy_skip": hy_skip[l], "ml_conv_w": ml_conv_w[l], "ml_conv_b": ml_conv_b[l],
              "ml_norm_w": ml_norm_w[l], "w_branch": w_branch[l], "w_out": w_out[l],
              "moe_rg_w": moe_rg_w[l], "moe_rg_b": moe_rg_b[l], "moe_re_w": moe_re_w[l],
              "moe_re_b": moe_re_b[l], "moe_w_gate": moe_w_gate[l], "moe_w_up": moe_w_up[l],
              "moe_w_down": moe_w_down[l]}
        last = l == DEPTH - 1
        mod_l = jnp.split((jax.nn.silu(c) @ ada_w[l] + ada_b[l])[:, None, :], 6, axis=-1)
        mod_c = jnp.split((jax.nn.silu(c_ctx) @ ada_w[l] + ada_b[l])[None, None, :], 6, axis=-1)
        zc = modulate(rmsnorm(ctx, norm1_w[l]), mod_c[0], mod_c[1]) @ w_in[l] + b_in[l]
        zl = modulate(rmsnorm(x, norm1_w[l]), mod_l[0], mod_l[1]) @ w_in[l] + b_in[l]
        st0 = zero_state(B)
        ml_c, st_f, st_b = mlstm_branch(zc[..., OFF_ML:OFF_MLG], zc[..., OFF_MLG:OFF_GATE], ctx_grid, lp, st0, st0)
        ml_l, _, _ = mlstm_branch(zl[..., OFF_ML:OFF_MLG], zl[..., OFF_MLG:OFF_GATE], lat_grid, lp, st_f, st_b)
        x = x + mod_l[2] * token_mixer_out(zl, lat_grid, lp, ml_l)
        x = x + mod_l[5] * hier_moe(modulate(rmsnorm(x, norm2_w[l]), mod_l[3], mod_l[4]), lp)
        if not last:
            ctx = ctx + mod_c[2] * token_mixer_out(zc, ctx_grid, lp, ml_c)
            ctx = ctx + mod_c[5] * hier_moe(modulate(rmsnorm(ctx, norm2_w[l]), mod_c[3], mod_c[4]), lp)
    return rmsnorm(x, norm_f_w)
```

```python
import os
import numpy as np
import ml_dtypes

from contextlib import ExitStack, contextmanager
import concourse.bass as bass
import concourse.mybir as mybir
from concourse.bass_utils import run_bass_kernel_spmd

F32 = mybir.dt.float32
BF16 = mybir.dt.bfloat16
ALU = mybir.AluOpType
AF = mybir.ActivationFunctionType
AX = mybir.AxisListType


class SemCtr:
    __slots__ = ("sem", "count")

    def __init__(self, sem):
        self.sem = sem
        self.count = 0


class Res:
    __slots__ = ("name", "last_w", "readers", "dsem")

    def __init__(self, name):
        self.name = name
        self.last_w = None
        self.readers = {}
        self.dsem = None


class Prog:
    ENG = ("pe", "dve", "act", "pool", "sp")

    def __init__(self, same_engine_sync=True):
        self.nc = bass.Bass("TRN2", target_bir_lowering=False)
        self.st = ExitStack()
        self.stk = [self.st]
        self.q = {e: [] for e in self.ENG}
        self.cnt = {e: 0 for e in self.ENG}
        self.seen = {e: {} for e in self.ENG}
        self.esem = {e: self.st.enter_context(self.nc.semaphore("es_" + e)) for e in self.ENG}
        self.esem_ids = {id(s) for s in self.esem.values()}
        self.own_ids = {e: {id(self.esem[e])} for e in self.ENG}
        self.same = same_engine_sync
        self.dma_events = {}
        self.sem_pool = []
        self.block_log = []
        self.label = ''
        self.scope_res = [[]]
        self.uid = 0
        self.ninst = 0
        self.flush_every = int(os.environ.get("FLUSH_EVERY", "1200"))

    def dram(self, name, shape, dt, kind=None):
        if kind is None:
            t = self.nc.dram_tensor(name, list(shape), dt)
        else:
            t = self.nc.dram_tensor(name, list(shape), dt, kind=kind)
        return t.ap()

    def sbuf(self, name, shape, dt):
        self.uid += 1
        return self.stk[-1].enter_context(self.nc.sbuf_tensor("%s_%d" % (name, self.uid), list(shape), dt))

    def psum(self, name, shape, dt=F32):
        self.uid += 1
        return self.stk[-1].enter_context(self.nc.psum_tensor("%s_%d" % (name, self.uid), list(shape), dt))

    def res(self, name=None):
        self.uid += 1
        r = Res("%s_%d" % (name or "r", self.uid))
        self.scope_res[-1].append(r)
        return r

    def semctr(self):
        while self.sem_pool:
            sc = self.sem_pool.pop()
            if sc.count < 20000:
                return sc
        return SemCtr(self.newsem("ds"))

    def newsem(self, name):
        self.uid += 1
        return self.st.enter_context(self.nc.semaphore("%s_%d" % (name, self.uid)))

    def _waits(self, eng, r, w, dma=False):
        waits = {}

        def need(ev):
            if ev is None:
                return
            s, v = ev
            if (not self.same or eng == "pe") and id(s) in self.own_ids[eng]:
                return
            if self.seen[eng].get(id(s), (None, 0))[1] < v:
                if waits.get(id(s), (None, 0))[1] < v:
                    waits[id(s)] = (s, v)

        for x in r:
            need(x.last_w)
            if dma:
                for ev in x.readers.values():
                    if id(ev[0]) not in self.esem_ids:
                        need(ev)
        for x in w:
            need(x.last_w)
            for ev in x.readers.values():
                need(ev)
        for k, sv in waits.items():
            self.seen[eng][k] = sv
        return list(waits.values())

    def op(self, eng, fn, r=(), w=()):
        waits = self._waits(eng, r, w)
        if self.cnt[eng] >= 20000:
            self.esem[eng] = self.newsem("es_" + eng)
            self.esem_ids.add(id(self.esem[eng]))
            self.own_ids[eng].add(id(self.esem[eng]))
            self.cnt[eng] = 0
        self.cnt[eng] += 1
        ev = (self.esem[eng], self.cnt[eng])
        self.q[eng].append((waits, fn, ev, 1))
        for x in r:
            x.readers[id(ev[0])] = ev
        for x in w:
            x.last_w = ev
            x.readers = {}
        self.ninst += 1
        self._autoflush()
        return ev

    def _autoflush(self):
        if sum(len(v) for v in self.q.values()) >= self.flush_every:
            self.flush()

    def _async(self, eng, fn, inc, r, w):
        dst = w[0]
        waits = self._waits(eng, r, w, dma=True)
        if dst.dsem is None:
            dst.dsem = self.semctr()
        dst.dsem.count += inc
        ev = (dst.dsem.sem, dst.dsem.count)
        self.q[eng].append((waits, fn, ev, inc))
        for x in r:
            x.readers[id(ev[0])] = ev
        dst.last_w = ev
        dst.readers = {}
        self.dma_events[id(ev[0])] = ev
        self.ninst += 1
        self._autoflush()
        return ev

    def dma(self, eng, out, in_, r=(), w=(), slow=False):
        if slow:
            return self._async(eng, lambda e: e.dma_start(out=out, in_=in_, allow_slow_non_contiguous=True), 16, r, w)
        return self._async(eng, lambda e: e.dma_start(out=out, in_=in_), 16, r, w)

    def coll(self, kind, out, in_, groups, r=(), w=()):
        return self._async("pool", lambda e: e.collective_compute(
            kind, ALU.bypass, replica_groups=groups, ins=[in_.opt()], outs=[out.opt()]), 1, r, w)

    def barrier(self):
        for eng in self.ENG:
            waits = []
            evs = [(self.esem[x], self.cnt[x]) for x in self.ENG if x != eng and self.cnt[x] > 0]
            evs += list(self.dma_events.values())
            for s, v in evs:
                if self.seen[eng].get(id(s), (None, 0))[1] < v:
                    waits.append((s, v))
                    self.seen[eng][id(s)] = (s, v)
            if waits:
                self.q[eng].append((waits, None, None, 0))
        self.dma_events = {}

    def flush(self):
        nc = self.nc
        with nc.Block() as block:
            def mk(name):
                def run(e):
                    for waits, fn, ev, inc in self.q[name]:
                        for s, v in waits:
                            e.wait_ge(s, v)
                        if fn is not None:
                            fn(e).then_inc(ev[0], inc)
                return run
            block.sync(mk("sp"))
            block.tensor(mk("pe"))
            block.vector(mk("dve"))
            block.scalar(mk("act"))
            block.gpsimd(mk("pool"))
        self.block_log.append((self.label, sum(len(v) for v in self.q.values())))
        self.q = {e: [] for e in self.ENG}

    @contextmanager
    def scope(self):
        st = ExitStack()
        self.stk.append(st)
        self.scope_res.append([])
        try:
            yield
            self.barrier()
            self.flush()
        finally:
            self.stk.pop()
            st.close()
            for r in self.scope_res.pop():
                if r.dsem is not None:
                    self.sem_pool.append(r.dsem)
                    r.dsem = None

    def finish(self):
        self.barrier()
        self.flush()
        self.st.close()
        return self.nc

    def mm(self, out, lhsT, rhs, start, stop, r, w):
        return self.op("pe", lambda e: e.matmul(out, lhsT=lhsT, rhs=rhs, start=start, stop=stop), r, w)

    def tr(self, out, in_, ident, r, w):
        return self.op("pe", lambda e: e.transpose(out, in_, ident), r, w)

    def act(self, out, in_, func, r, w, bias=0.0, scale=1.0, accum_out=None):
        if accum_out is None:
            return self.op("act", lambda e: e.activation(out=out, in_=in_, func=func, bias=bias, scale=scale), r, w)
        return self.op("act", lambda e: e.activation(out=out, in_=in_, func=func, bias=bias, scale=scale, accum_out=accum_out), r, w)

    def tt(self, eng, out, a, b, op, r, w):
        return self.op(eng, lambda e: e.tensor_tensor(out=out, in0=a, in1=b, op=op), r, w)

    def ts(self, eng, out, a, s1, op0, r, w, s2=None, op1=None):
        if op1 is None:
            return self.op(eng, lambda e: e.tensor_scalar(out=out, in0=a, scalar1=s1, scalar2=None, op0=op0), r, w)
        return self.op(eng, lambda e: e.tensor_scalar(out=out, in0=a, scalar1=s1, scalar2=s2, op0=op0, op1=op1), r, w)

    def stt(self, out, in0, scalar, in1, op0, op1, r, w):
        return self.op("dve", lambda e: e.scalar_tensor_tensor(out=out, in0=in0, scalar=scalar, in1=in1, op0=op0, op1=op1), r, w)

    def cp(self, eng, out, in_, r, w):
        if eng == "act":
            return self.op("act", lambda e: e.copy(out=out, in_=in_), r, w)
        return self.op(eng, lambda e: e.tensor_copy(out=out, in_=in_), r, w)

    def memset(self, eng, out, val, w):
        return self.op(eng, lambda e: e.memset(out, val), (), w)


def make_ident(p, dt=BF16, name="ident"):
    idf = p.sbuf(name + "f", [128, 128], F32); r = p.res(name)
    p.memset("dve", idf[:], 0.0, w=[r])
    p.op("pool", lambda e: e.affine_select(out=idf[:], in_=idf[:], pattern=[[-1, 128]], compare_op=ALU.not_equal,
                                           fill=1.0, base=0, channel_multiplier=1), r=[r], w=[r])
    if dt == F32:
        return idf, r
    idb = p.sbuf(name + "b", [128, 128], dt)
    p.cp("dve", idb[:], idf[:], r=[r], w=[r])
    return idb, r


def conv_stage(p, jobs, R, W):
    L = R * W
    PAD = W + 1
    CW = min(512, L)
    with p.scope():
        ident, idr = make_ident(p)
        stg = p.sbuf("stg", [128, L], F32); stgr = p.res("stg")
        outt = p.sbuf("outt", [128, L], F32); outr = p.res("outt")
        nver = 3 if R > 1 else 1
        P = [p.sbuf("P%d" % i, [128, L + 2 * PAD], BF16) for i in range(nver)]
        Pr = [p.res("P%d" % i) for i in range(nver)]
        for i in range(nver):
            p.memset("pool", P[i][:, 0:PAD], 0.0, w=[Pr[i]])
            p.memset("pool", P[i][:, PAD + L:], 0.0, w=[Pr[i]])
        wsb = p.sbuf("wsb", [128, 9], F32); bsb = p.sbuf("bsb", [128, 1], F32); wr = p.res("wsb")
        D = p.sbuf("D", [128, 9, 128], BF16); Dr = p.res("D")
        ps = [p.psum("ps%d" % i, [128, 512]) for i in range(4)]; psr = [p.res("ps%d" % i) for i in range(4)]
        k = 0
        for (src, srcr, w_ap, b_ap, c0, silu, dst, dstr) in jobs:
            p.dma("sp", stg[:], src, r=[srcr], w=[stgr])
            p.dma("pool", wsb[:], w_ap.rearrange("a b c -> c (a b)")[c0:c0 + 128, :], w=[wr], slow=True)
            p.dma("pool", bsb[:], b_ap.rearrange("(a b) -> a b", b=1)[c0:c0 + 128, :], w=[wr])
            p.cp("act", P[0][:, PAD:PAD + L], stg[:], r=[stgr], w=[Pr[0]])
            if nver == 3:
                p.cp("pool", P[1][:, PAD:PAD + L], P[0][:, PAD:PAD + L], r=[Pr[0]], w=[Pr[1]])
                p.cp("dve", P[2][:, PAD:PAD + L], P[0][:, PAD:PAD + L], r=[Pr[0]], w=[Pr[2]])
                v1 = P[1][:, PAD:PAD + L].rearrange("p (r w) -> p r w", w=W)
                v2 = P[2][:, PAD:PAD + L].rearrange("p (r w) -> p r w", w=W)
                p.memset("pool", v1[:, :, W - 1:W], 0.0, w=[Pr[1]])
                p.memset("dve", v2[:, :, 0:1], 0.0, w=[Pr[2]])
            for tap in range(9):
                p.ts("dve", D[:, tap, :], ident[:], wsb[:, tap:tap + 1], ALU.mult, r=[idr, wr], w=[Dr])
            taps = [(dy, dx) for dy in (-1, 0, 1) for dx in (-1, 0, 1) if (R > 1 or dy == 0)]
            for c in range(L // CW):
                h = k % 4; k += 1
                for ti, (dy, dx) in enumerate(taps):
                    ver = 0 if nver == 1 else (1 if dx == -1 else (2 if dx == 1 else 0))
                    o = PAD + c * CW + W * dy + dx
                    p.mm(ps[h][:, 0:CW], D[:, (dy + 1) * 3 + dx + 1, :], P[ver][:, o:o + CW], ti == 0, ti == len(taps) - 1,
                         r=[Dr, Pr[ver]], w=[psr[h]])
                p.act(outt[:, c * CW:(c + 1) * CW], ps[h][:, 0:CW], AF.Silu if silu else AF.Identity,
                      r=[psr[h], wr], w=[outr], bias=bsb[:, 0:1])
            p.dma("sp", dst, outt[:], r=[outr], w=[dstr])


HY_BANDS = 16


def hfilt_consts(L):
    t = np.arange(L, dtype=np.float64) / L
    bands = np.linspace(1e-4, HY_BANDS - 1, HY_BANDS)
    ang = 2 * np.pi * t[:, None] * bands[None, :]
    feats = np.concatenate([t[:, None], np.cos(ang), np.sin(ang)], axis=-1)
    return np.ascontiguousarray(feats.T).astype(np.float32), t.astype(np.float32)


def hfilt_stage(p, featsT, trow, w, j, L, dst, dst_res):
    CW = min(512, L)
    NCH = L // CW
    TWO_PI = 2 * np.pi
    with p.scope():
        wr = p.res("w")
        ft = p.sbuf("ft", [33, L], F32); tb = p.sbuf("tb", [128, L], F32)
        w1 = p.sbuf("w1", [33, 64], F32); w2 = p.sbuf("w2", [64, 64], F32); w3 = p.sbuf("w3", [64, 4, 128], F32)
        sc = p.sbuf("sc", [64, 4], F32)
        dc = p.sbuf("dc", [128, 4], F32)
        p.dma("sp", ft[:], featsT, w=[wr]); p.dma("sp", tb[:], trow.partition_broadcast(128), w=[wr])
        p.dma("sp", w1[:], w["f_w1"], w=[wr]); p.dma("sp", w2[:], w["f_w2"], w=[wr])
        p.dma("sp", w3[:], w["f_w3"], w=[wr])
        p.dma("sp", sc[:, 0:1], w["f_b1"].rearrange("(a b) -> a b", b=1), w=[wr])
        p.dma("sp", sc[:, 1:2], w["f_freq"].rearrange("(a b) -> a b", b=1), w=[wr])
        p.dma("sp", sc[:, 2:3], w["f_b2"].rearrange("(a b) -> a b", b=1), w=[wr])
        p.dma("sp", dc[:], w["decay"], w=[wr])
        p.act(dc[:], dc[:], AF.Abs, r=[wr], w=[wr])
        p.ts("dve", dc[:], dc[:], -1.0, ALU.mult, r=[wr], w=[wr])
        h1f = p.sbuf("h1", [128, L], F32); h1 = h1f[0:64]; h2 = p.sbuf("h2", [64, L], F32)
        h1r, h2r = p.res("h1"), p.res("h2")
        ps = [p.psum("ps%d" % i, [128, 512]) for i in range(2)]; psr = [p.res("ps%d" % i) for i in range(2)]
        tmp = p.sbuf("tmp", [64, 512], F32); tmpr = p.res("tmp")
        tm1 = p.sbuf("tm1", [64, 512], F32); tm1r = p.res("tm1"); tm2 = p.sbuf("tm2", [64, 512], F32); tm2r = p.res("tm2")
        k = 0
        for (src, srcr, lw, bcol, dsth, dstr) in ((ft[:], wr, w1, 0, h1, h1r), (h1, h1r, w2, 2, h2[:], h2r)):
            for c in range(NCH):
                cs = slice(c * CW, (c + 1) * CW)
                h = k % 2; k += 1
                p.mm(ps[h][0:64, 0:CW], lw[:], src[:, cs], True, True, r=[wr, srcr], w=[psr[h]])
                p.ts("dve", tmp[:, 0:CW], ps[h][0:64, 0:CW], sc[:, bcol:bcol + 1], ALU.add, r=[psr[h], wr], w=[tmpr],
                     s2=sc[:, 1:2], op1=ALU.mult)
                tv = tmp[:, 0:CW]
                p.ts("dve", tm1[:, 0:CW], tv, float(np.pi), ALU.is_gt, r=[tmpr], w=[tm1r], s2=-TWO_PI, op1=ALU.mult)
                p.ts("pool", tm2[:, 0:CW], tv, float(-np.pi), ALU.is_lt, r=[tmpr], w=[tm2r], s2=TWO_PI, op1=ALU.mult)
                p.tt("dve", tv, tv, tm1[:, 0:CW], ALU.add, r=[tmpr, tm1r], w=[tmpr])
                p.tt("dve", tv, tv, tm2[:, 0:CW], ALU.add, r=[tmpr, tm2r], w=[tmpr])
                p.ts("dve", tv, tv, float(np.pi), ALU.min, r=[tmpr], w=[tmpr], s2=float(-np.pi), op1=ALU.max)
                p.act(dsth[:, cs], tmp[:, 0:CW], AF.Sin, r=[tmpr], w=[dstr])
        F = [p.sbuf("F%d" % i, [128, L], F32) for i in range(2)]; Fr = [p.res("F%d" % i) for i in range(2)]
        wnd = p.sbuf("wnd", [128, 512], F32); wndr = p.res("wnd")
        ss = p.sbuf("ss", [128, 4], F32); ssr = p.res("ss")
        junk = h1f; junkr = h1r
        for o in range(2):
            for d in range(2):
                od = o * 2 + d
                for c in range(NCH):
                    cs = slice(c * CW, (c + 1) * CW)
                    h = k % 2; k += 1
                    p.mm(ps[h][:, 0:CW], w3[:, od, :], h2[:, cs], True, True, r=[wr, h2r], w=[psr[h]])
                    p.act(wnd[:, 0:CW], tb[:, cs], AF.Exp, r=[wr], w=[wndr], scale=dc[:, od:od + 1])
                    p.tt("dve", F[d][:, cs], ps[h][:, 0:CW], wnd[:, 0:CW], ALU.mult, r=[psr[h], wndr], w=[Fr[d]])
                if d == 1:
                    p.memset("dve", F[d][:, 0:1], 0.0, w=[Fr[d]])
                p.act(junk[:], F[d][:], AF.Square, r=[Fr[d]], w=[junkr], accum_out=ss[:, od:od + 1])
                ssr.last_w = junkr.last_w
            tot = ss[:, 2 * o:2 * o + 1]
            p.tt("dve", tot, tot, ss[:, 2 * o + 1:2 * o + 2], ALU.add, r=[ssr], w=[ssr])
            p.act(tot, tot, AF.Sqrt, r=[ssr], w=[ssr], bias=1e-6)
            p.op("dve", lambda e, t_=tot: e.reciprocal(out=t_, in_=t_), r=[ssr], w=[ssr])
            for d in range(2):
                p.ts("dve", F[d][:], F[d][:], tot, ALU.mult, r=[Fr[d], ssr], w=[Fr[d]])
                p.dma("sp", dst[o * 2 + d], F[d][:], r=[Fr[d]], w=[dst_res])


NFFT = 16384


def hconv_tables():
    a = np.arange(128, dtype=np.float64)
    th128 = 2 * np.pi * np.outer(a, a) / 128.0
    C = np.cos(th128); S = np.sin(th128)
    thN = 2 * np.pi * np.outer(a, a) / NFFT
    bf = lambda x: np.ascontiguousarray(x.astype(np.float32)).astype(ml_dtypes.bfloat16)
    t = {}
    t["hc_w1"] = bf(np.concatenate([C[:64], -S[:64]], axis=1))
    t["hc_sq"] = bf(np.stack([C, -C, S, -S], axis=1))
    t["hc_wab"] = bf(np.stack([np.concatenate([C, S], 1), np.concatenate([-C, -S], 1),
                                np.concatenate([-S, C], 1)], axis=1))
    t["hc_i2"] = bf(np.stack([C[:, :64], S[:, :64], -S[:, :64]], axis=1))
    t["hc_tw"] = np.stack([np.cos(thN), -np.sin(thN)], axis=1).astype(np.float32)
    return t


def hconv_stage(p, tabs, srcs, dst, dst_res, CH, L_in, first, G=32, u_out=None, u_res=None):
    NP = L_in // 128
    with p.scope():
        w1 = p.sbuf("w1", [64, 256], BF16); sq = p.sbuf("sq", [128, 4, 128], BF16)
        wab = p.sbuf("wab", [128, 3, 256], BF16); i2 = p.sbuf("i2", [128, 3, 64], BF16)
        tw = p.sbuf("tw", [128, 2, 128], F32)
        tr = p.res("tabs")
        for sb_, nm in ((w1, "hc_w1"), (sq, "hc_sq"), (wab, "hc_wab"), (i2, "hc_i2"), (tw, "hc_tw")):
            p.dma("sp", sb_[:], tabs[nm], w=[tr])
        PS = p.psum("ps", [128, 4096])
        PH = [PS[:, 0:2048], PS[:, 2048:4096]]
        phr = [p.res("psA"), p.res("psB")]
        stg = [p.sbuf("stg%d" % i, [64, G, 128], F32) for i in range(3)]
        stgr = [p.res("stg%d" % i) for i in range(3)]
        skb = p.sbuf("skb", [64, G], F32); skr = p.res("skb")
        X = p.sbuf("X", [64, G, 128], BF16); xr = p.res("X")
        NS4 = G // 4
        T = [p.sbuf("T%d" % i, [128, G, 128], BF16) for i in range(4)]
        TrS = [[p.res("T%d_%d" % (i, s)) for s in range(NS4)] for i in range(4)]
        Kr = p.sbuf("Kr", [128, G, 128], F32); Ki = p.sbuf("Ki", [128, G, 128], F32)
        kresS = [p.res("K%d" % s) for s in range(NS4)]
        Q = [p.sbuf("Q%d" % i, [128, G, 128], BF16) for i in range(4)]
        QrS = [[p.res("Q%d_%d" % (i, s)) for s in range(NS4)] for i in range(4)]
        OUT2 = p.sbuf("OUT2", [64, G, 128], F32); outr = p.res("OUT2")
        Trb = tw[:, 0:1, :].to_broadcast([128, 4, 128])
        Tib = tw[:, 1:2, :].to_broadcast([128, 4, 128])
        PF = [PS[:, 0:1024], PS[:, 1024:2048]]; pfr = [p.res("pfA"), p.res("pfB")]
        PX = [PS[:, 2048:3072], PS[:, 3072:4096]]; pxr = [p.res("pxA"), p.res("pxB")]
        cnt = {"f": 0, "x": 0, "i": 0}

        def t2d(ap, c0):
            return ap[c0:c0 + G, :].rearrange("g (a b) -> a g b", b=128)

        def load_u(c0, which):
            if NP < 64:
                p.memset("pool", X[:], 0.0, w=[xr])
            if which != "u" or first:
                ap, rs = srcs["v" if which == "u" else which]
                p.dma("sp", stg[0][0:NP], t2d(ap, c0), r=[rs], w=[stgr[0]])
                p.cp("pool", X[0:NP], stg[0][0:NP], r=[stgr[0]], w=[xr])
            else:
                for i, nm in enumerate(("v", "c1", "x1")):
                    ap, rs = srcs[nm]
                    p.dma("sp", stg[i][0:NP], t2d(ap, c0), r=[rs], w=[stgr[i]])
                ap, rs = srcs["skip"]
                p.dma("sp", skb[0:NP], ap[c0:c0 + G].partition_broadcast(NP), r=[rs], w=[skr])
                p.tt("pool", stg[0][0:NP], stg[0][0:NP], skb[0:NP, :, None].to_broadcast([NP, G, 128]), ALU.mult,
                     r=[stgr[0], skr], w=[stgr[0]])
                p.tt("pool", stg[0][0:NP], stg[0][0:NP], stg[1][0:NP], ALU.add, r=[stgr[0], stgr[1]], w=[stgr[0]])
                p.tt("pool", stg[0][0:NP], stg[0][0:NP], stg[2][0:NP], ALU.mult, r=[stgr[0], stgr[2]], w=[stgr[0]])
                p.cp("act", X[0:NP], stg[0][0:NP], r=[stgr[0]], w=[xr])
                if u_out is not None:
                    p.dma("pool", t2d(u_out, c0), stg[0][0:NP], r=[stgr[0]], w=[u_res])

        Cm, Cn, Sm, Sn = (sq[:, k, :] for k in range(4))

        def f1_tw(s4):
            h = cnt["f"] % 2; cnt["f"] += 1
            pv = PF[h].rearrange("p (c k) -> p c k", k=256)
            for c in range(4):
                p.mm(pv[:, c, :], X[:, s4 * 4 + c, :], w1[:], True, True, r=[xr, tr], w=[pfr[h]])
            Ar = pv[:, :, 0:128]; Ai = pv[:, :, 128:256]
            gs = slice(s4 * 4, s4 * 4 + 4)
            p.tt("dve", T[0][:, gs, :], Ar, Trb, ALU.mult, r=[pfr[h], tr], w=[TrS[0][s4]])
            p.tt("dve", T[1][:, gs, :], Ai, Tib, ALU.mult, r=[pfr[h], tr], w=[TrS[1][s4]])
            p.tt("dve", T[2][:, gs, :], Ar, Tib, ALU.mult, r=[pfr[h], tr], w=[TrS[2][s4]])
            p.tt("dve", T[3][:, gs, :], Ai, Trb, ALU.mult, r=[pfr[h], tr], w=[TrS[3][s4]])

        def f2_post(s4, which):
            h = cnt["x"] % 2; cnt["x"] += 1
            gs = slice(s4 * 4, s4 * 4 + 4)
            xr_ps = PX[h][:, 0:512]; xi_ps = PX[h][:, 512:1024]
            rr = [tr] + [TrS[k][s4] for k in range(4)]
            p.mm(xr_ps, Cm, T[0][:, gs, :], True, False, r=rr, w=[pxr[h]])
            p.mm(xr_ps, Cn, T[1][:, gs, :], False, False, r=rr, w=[pxr[h]])
            p.mm(xr_ps, Sm, T[2][:, gs, :], False, False, r=rr, w=[pxr[h]])
            p.mm(xr_ps, Sm, T[3][:, gs, :], False, True, r=rr, w=[pxr[h]])
            p.mm(xi_ps, Cm, T[2][:, gs, :], True, False, r=rr, w=[pxr[h]])
            p.mm(xi_ps, Cm, T[3][:, gs, :], False, False, r=rr, w=[pxr[h]])
            p.mm(xi_ps, Sn, T[0][:, gs, :], False, False, r=rr, w=[pxr[h]])
            p.mm(xi_ps, Sm, T[1][:, gs, :], False, True, r=rr, w=[pxr[h]])
            xr3 = xr_ps.rearrange("p (c k) -> p c k", k=128); xi3 = xi_ps.rearrange("p (c k) -> p c k", k=128)
            kres = kresS[s4]
            if which == "kf":
                p.cp("act", Kr[:, gs, :], xr3, r=[pxr[h]], w=[kres])
                p.cp("act", Ki[:, gs, :], xi3, r=[pxr[h]], w=[kres])
            elif which == "kg":
                p.tt("dve", Kr[:, gs, :], xr3, Kr[:, gs, :], ALU.add, r=[pxr[h], kres], w=[kres])
                p.stt(Ki[:, gs, :], xi3, -1.0, Ki[:, gs, :], ALU.mult, ALU.add, r=[pxr[h], kres], w=[kres])
            else:
                p.tt("dve", Q[0][:, gs, :], xr3, Kr[:, gs, :], ALU.mult, r=[pxr[h], kres], w=[QrS[0][s4]])
                p.tt("dve", Q[1][:, gs, :], xi3, Ki[:, gs, :], ALU.mult, r=[pxr[h], kres], w=[QrS[1][s4]])
                p.tt("dve", Q[2][:, gs, :], xr3, Ki[:, gs, :], ALU.mult, r=[pxr[h], kres], w=[QrS[2][s4]])
                p.tt("dve", Q[3][:, gs, :], xi3, Kr[:, gs, :], ALU.mult, r=[pxr[h], kres], w=[QrS[3][s4]])

        def fwd(which):
            for s4 in range(NS4 + 1):
                if s4 < NS4:
                    f1_tw(s4)
                if s4 >= 1:
                    f2_post(s4 - 1, which)

        def inv(c0):
            Wa, Wan, Wb = (wab[:, k, :] for k in range(3))
            for s4 in range(NS4 + 1):
                if s4 >= 1:
                    i2_step(s4 - 1, c0)
                if s4 == NS4:
                    break
                h = cnt["f"] % 2; cnt["f"] += 1
                pv = PF[h].rearrange("p (c k) -> p c k", k=256)
                rr = [tr] + [QrS[k][s4] for k in range(4)]
                for c in range(4):
                    g = s4 * 4 + c
                    p.mm(pv[:, c, :], Q[0][:, g, :], Wa, True, False, r=rr, w=[pfr[h]])
                    p.mm(pv[:, c, :], Q[1][:, g, :], Wan, False, False, r=rr, w=[pfr[h]])
                    p.mm(pv[:, c, :], Q[2][:, g, :], Wb, False, False, r=rr, w=[pfr[h]])
                    p.mm(pv[:, c, :], Q[3][:, g, :], Wb, False, True, r=rr, w=[pfr[h]])
                Br = pv[:, :, 0:128]; Bi = pv[:, :, 128:256]
                gs = slice(s4 * 4, s4 * 4 + 4)
                p.tt("dve", T[0][:, gs, :], Br, Trb, ALU.mult, r=[pfr[h], tr], w=[TrS[0][s4]])
                p.tt("dve", T[1][:, gs, :], Bi, Tib, ALU.mult, r=[pfr[h], tr], w=[TrS[1][s4]])
                p.tt("dve", T[2][:, gs, :], Br, Tib, ALU.mult, r=[pfr[h], tr], w=[TrS[2][s4]])
                p.tt("dve", T[3][:, gs, :], Bi, Trb, ALU.mult, r=[pfr[h], tr], w=[TrS[3][s4]])
            p.dma("sp", t2d(dst, c0), OUT2[0:NP], r=[outr], w=[dst_res])

        CI, SI, SIn = (i2[:, k, :] for k in range(3))

        def i2_step(s4, c0):
            h = cnt["x"] % 2; cnt["x"] += 1
            gs = slice(s4 * 4, s4 * 4 + 4)
            o_ps = PX[h][0:NP, 0:512]
            rr = [tr] + [TrS[k][s4] for k in range(4)]
            p.mm(o_ps, CI[:, 0:NP], T[0][:, gs, :], True, False, r=rr, w=[pxr[h]])
            p.mm(o_ps, CI[:, 0:NP], T[1][:, gs, :], False, False, r=rr, w=[pxr[h]])
            p.mm(o_ps, SI[:, 0:NP], T[2][:, gs, :], False, False, r=rr, w=[pxr[h]])
            p.mm(o_ps, SIn[:, 0:NP], T[3][:, gs, :], False, True, r=rr, w=[pxr[h]])
            p.act(OUT2[0:NP, gs, :], o_ps.rearrange("p (c k) -> p c k", k=128), AF.Copy, r=[pxr[h]], w=[outr], scale=1.0 / NFFT)

        for c0 in range(0, CH, G):
            for which in ("kf", "kg", "u"):
                load_u(c0, which)
                fwd(which)
            inv(c0)


LF = 8192


def fnet_tables():
    a = np.arange(128, dtype=np.float64)
    th = 2 * np.pi * np.outer(a, a) / 128.0
    C = np.cos(th); S = np.sin(th)
    b = np.arange(64, dtype=np.float64)
    th64 = 2 * np.pi * np.outer(b, b) / 64.0
    C64 = np.cos(th64); S64 = np.sin(th64)
    thL = 2 * np.pi * np.outer(b, a) / LF
    bf = lambda x: np.ascontiguousarray(x.astype(np.float32)).astype(ml_dtypes.bfloat16)
    return {"fn_w": bf(np.stack([np.concatenate([C, -S], 1), np.concatenate([S, C], 1)], axis=1)),
            "fn_64": bf(np.stack([C64, -C64, S64], axis=1)),
            "fn_tw": np.stack([np.cos(thL), -np.sin(thL)], axis=1).astype(np.float32)}


def fnet_stage(p, tabs, src, src_res, dst, dst_res, L_in):
    scale = 1.0 / np.sqrt(L_in * 128.0)
    f1_list = list(range(128)) if L_in == LF else [0, 32, 64, 96]
    with p.scope():
        wt = p.sbuf("wt", [128, 2, 256], BF16); t64 = p.sbuf("t64", [64, 3, 64], BF16); tw = p.sbuf("tw", [64, 2, 128], F32)
        tr = p.res("tabs")
        p.dma("sp", wt[:], tabs["fn_w"], w=[tr]); p.dma("sp", t64[:], tabs["fn_64"], w=[tr]); p.dma("sp", tw[:], tabs["fn_tw"], w=[tr])
        stg = p.sbuf("stg", [128, LF], F32); stgr = p.res("stg")
        ub = p.sbuf("ub", [128, LF], BF16); ubr = p.res("ub")
        V = p.sbuf("V", [128, 64, 256], BF16); Vr = p.res("V")
        T = [p.sbuf("T%d" % i, [64, 64, 128], BF16) for i in range(4)]; Tr_ = [p.res("T%d" % i) for i in range(4)]
        PS = p.psum("ps", [128, 4096]); PH = [PS[:, 0:2048], PS[:, 2048:4096]]; phr = [p.res("psA"), p.res("psB")]
        if L_in < LF:
            p.memset("pool", stg[:], 0.0, w=[stgr])
        p.dma("sp", stg[:, 0:L_in], src, r=[src_res], w=[stgr])
        p.cp("act", ub[:], stg[:], r=[stgr], w=[ubr])
        tog = 0
        uv = ub[:].rearrange("p (a b) -> p b a", b=64)
        for s8 in range(8):
            h = tog; tog ^= 1
            pv = PH[h].rearrange("p (c k) -> p c k", k=256)
            for c in range(8):
                n2 = s8 * 8 + c
                p.mm(pv[:, c, :], uv[:, n2, :], wt[:, 0, :], True, True, r=[ubr, tr], w=[phr[h]])
            p.cp("act" if s8 % 2 else "dve", V[:, s8 * 8:(s8 + 1) * 8, :], pv, r=[phr[h]], w=[Vr])
        Trb = tw[:, 0:1, :].to_broadcast([64, 8, 128]); Tib = tw[:, 1:2, :].to_broadcast([64, 8, 128])
        OUT = stg; outr = stgr
        for half in range(2):
            for s8 in range(8):
                h = tog; tog ^= 1
                pv = PH[h][0:64, :].rearrange("p (c k) -> p c k", k=256)
                for c in range(8):
                    ch = half * 64 + s8 * 8 + c
                    p.mm(pv[:, c, :], V[:, :, ch], wt[:, 0, :], True, False, r=[Vr, tr], w=[phr[h]])
                    p.mm(pv[:, c, :], V[:, :, 128 + ch], wt[:, 1, :], False, True, r=[Vr, tr], w=[phr[h]])
                Ar = pv[:, :, 0:128]; Ai = pv[:, :, 128:256]
                gs = slice(s8 * 8, s8 * 8 + 8)
                p.tt("dve", T[0][:, gs, :], Ar, Trb, ALU.mult, r=[phr[h], tr], w=[Tr_[0]])
                p.tt("dve", T[1][:, gs, :], Ai, Tib, ALU.mult, r=[phr[h], tr], w=[Tr_[1]])
                p.tt("dve", T[2][:, gs, :], Ar, Tib, ALU.mult, r=[phr[h], tr], w=[Tr_[2]])
                p.tt("dve", T[3][:, gs, :], Ai, Trb, ALU.mult, r=[phr[h], tr], w=[Tr_[3]])
            C64, C64n, S64 = (t64[:, k, :] for k in range(3))
            rr = [tr] + Tr_
            for s32 in range(0, len(f1_list), 32):
                fl = f1_list[s32:s32 + 32]
                h = tog; tog ^= 1
                pv = PH[h][0:64, :].rearrange("p (m k) -> p m k", k=64)
                for i, f1 in enumerate(fl):
                    p.mm(pv[:, i, :], T[0][:, :, f1], C64, True, False, r=rr, w=[phr[h]])
                    p.mm(pv[:, i, :], T[1][:, :, f1], C64n, False, False, r=rr, w=[phr[h]])
                    p.mm(pv[:, i, :], T[2][:, :, f1], S64, False, False, r=rr, w=[phr[h]])
                    p.mm(pv[:, i, :], T[3][:, :, f1], S64, False, True, r=rr, w=[phr[h]])
                ov = OUT[half * 64:half * 64 + 64, :].rearrange("p (a b) -> p a b", b=128)
                step = fl[1] - fl[0] if len(fl) > 1 else 1
                osl = ov[:, :, fl[0]:fl[-1] + 1:step].rearrange("p a b -> p b a")
                p.act(osl, pv[:, 0:len(fl), :], AF.Copy, r=[phr[h]], w=[outr], scale=float(scale))
        if L_in == LF:
            p.dma("sp", dst, OUT[:], r=[outr], w=[dst_res])
        else:
            ov = OUT[:].rearrange("p (a b) -> p a b", b=128)[:, :, 0:128:32]
            p.dma("sp", dst.rearrange("p (a b) -> p a b", b=4), ov, r=[outr], w=[dst_res], slow=True)


STOP = 99


NCK = 66
NEG = -30000.0


def mlstm_stage(p, src, dst_lat, dst_ctx, dst_res, normw_ap, j):
    TOK = NCK * 128
    with p.scope():
        ident, idr = make_ident(p, F32, "idf")
        identb = p.sbuf("identb", [128, 128], BF16)
        p.cp("dve", identb[:], ident[:], r=[idr], w=[idr])
        cr = p.res("consts")
        ones = p.sbuf("ones", [128, 128], F32); onesb = p.sbuf("onesb", [128, 128], BF16)
        p.memset("dve", ones[:], 1.0, w=[cr]); p.memset("dve", onesb[:], 1.0, w=[cr])
        tri = p.sbuf("tri", [128, 2, 128], F32); mneg = p.sbuf("mneg", [128, 2, 128], F32)
        p.memset("dve", tri[:], 1.0, w=[cr]); p.memset("dve", mneg[:], 0.0, w=[cr])
        p.op("pool", lambda e: e.affine_select(out=tri[:, 0, :], in_=tri[:, 0, :], pattern=[[1, 128]], compare_op=ALU.is_ge, fill=0.0, base=0, channel_multiplier=-1), r=[cr], w=[cr])
        p.op("pool", lambda e: e.affine_select(out=tri[:, 1, :], in_=tri[:, 1, :], pattern=[[-1, 128]], compare_op=ALU.is_ge, fill=0.0, base=0, channel_multiplier=1), r=[cr], w=[cr])
        p.op("pool", lambda e: e.affine_select(out=mneg[:, 0, :], in_=mneg[:, 0, :], pattern=[[1, 128]], compare_op=ALU.is_ge, fill=NEG, base=0, channel_multiplier=-1), r=[cr], w=[cr])
        p.op("pool", lambda e: e.affine_select(out=mneg[:, 1, :], in_=mneg[:, 1, :], pattern=[[-1, 128]], compare_op=ALU.is_ge, fill=NEG, base=0, channel_multiplier=1), r=[cr], w=[cr])

        stg = p.sbuf("stg", [128, TOK], F32); stgr = p.res("stg")
        qT = p.sbuf("qT", [128, TOK], BF16); kT = p.sbuf("kT", [128, TOK], BF16); vT = p.sbuf("vT", [128, TOK], BF16)
        qr, kr, vr = p.res("qT"), p.res("kT"), p.res("vT")
        for nm, t_, r_, sc in (("qT", qT, qr, 128.0 ** -0.5), ("kT", kT, kr, 1.0), ("vT", vT, vr, 1.0)):
            lat, ctx, rs = src[nm]
            p.dma("sp", stg[:, 0:256], ctx, r=[rs], w=[stgr]); p.dma("sp", stg[:, 256:], lat, r=[rs], w=[stgr])
            p.act(t_[:], stg[:], AF.Copy, r=[stgr], w=[r_], scale=sc)
        ktok = p.sbuf("ktok", [128, NCK, 128], BF16); vtok = p.sbuf("vtok", [128, NCK, 128], BF16)
        ktr, vtr = p.res("ktok"), p.res("vtok")
        ptb = [p.psum("ptb%d" % i, [128, 8, 128], BF16) for i in range(2)]; ptbr = [p.res("ptb%d" % i) for i in range(2)]
        kk = 0
        for srcT, sr, dstt, dr in ((kT, kr, ktok, ktr), (vT, vr, vtok, vtr)):
            for c8 in range(0, NCK, 8):
                n = min(8, NCK - c8)
                h = kk % 2; kk += 1
                for c in range(n):
                    p.tr(ptb[h][:, c, :], srcT[:, (c8 + c) * 128:(c8 + c + 1) * 128], identb[:], r=[sr, idr], w=[ptbr[h]])
                p.cp("act" if h else "dve", dstt[:, c8:c8 + n, :], ptb[h][:, 0:n, :], r=[ptbr[h]], w=[dr])
        if STOP == 1:
            p.dma('sp', dst_ctx, stg[:, 0:256], r=[stgr], w=[dst_res]); return
        gst = p.sbuf("gst", [NCK, 4, 128], F32); gr = p.res("gst")
        lat, ctx, rs = src["g"]
        p.dma("sp", gst[0:2], ctx.rearrange("g (c s) -> c g s", s=128), r=[rs], w=[gr])
        p.dma("sp", gst[2:NCK], lat.rearrange("g (c s) -> c g s", s=128), r=[rs], w=[gr])
        G = p.sbuf("G", [128, 4, NCK], F32); Gr = p.res("G")
        pg = p.psum("pg", [128, 4, 128]); pgr = p.res("pg")
        for g in range(4):
            p.tr(pg[:, g, 0:NCK], gst[:, g, :], ident[0:NCK, 0:NCK], r=[gr, idr], w=[pgr])
        p.cp("dve", G[:], pg[:, :, 0:NCK], r=[pgr], w=[Gr])
        LFt = p.sbuf("LF", [128, 2, NCK], F32); lfr = p.res("LF")
        for d in range(2):
            p.act(LFt[:, d, :], G[:, 2 * d + 1, :], AF.Exp, r=[Gr], w=[lfr], scale=-1.0)
        p.act(LFt[:], LFt[:], AF.Ln, r=[lfr], w=[lfr], bias=1.0)
        p.ts("dve", LFt[:], LFt[:], -1.0, ALU.mult, r=[lfr], w=[lfr])
        CUM = p.sbuf("CUM", [128, 2, NCK], F32); IB = p.sbuf("IB", [128, 2, NCK], F32)
        SC = p.sbuf("SC", [128, 2, NCK], F32); ET = p.sbuf("ET", [128, 2, NCK], F32)
        pr = p.res("pre")
        pc = p.psum("pc", [128, 4, 128]); pcr = p.res("pc")
        for d in range(2):
            p.mm(pc[:, d, 0:NCK], tri[:, d, :], LFt[:, d, :], True, True, r=[cr, lfr], w=[pcr])
            p.mm(pc[:, 2 + d, 0:NCK], ones[:], LFt[:, d, :], True, True, r=[cr, lfr], w=[pcr])
        p.cp("dve", CUM[:], pc[:, 0:2, 0:NCK], r=[pcr], w=[pr])
        for d in range(2):
            p.tt("dve", IB[:, d, :], G[:, 2 * d, :], CUM[:, d, :], ALU.subtract, r=[Gr, pr], w=[pr])
            p.tt("dve", SC[:, d, :], pc[:, 2 + d, 0:NCK], IB[:, d, :], ALU.add, r=[pcr, pr], w=[pr])
        p.act(SC[:], SC[:], AF.Exp, r=[pr], w=[pr])
        p.act(ET[:], pc[:, 2:4, 0:NCK], AF.Exp, r=[pcr], w=[pr])

        if STOP == 2:
            p.dma('sp', dst_ctx[:, 0:66], SC[:, 0, :], r=[pr], w=[dst_res]); return
        H = p.sbuf("H", [128, TOK], F32); Hr = p.res("H")
        p.memset("pool", H[:], 0.0, w=[Hr])
        PA = [p.psum("pa%d" % d, [128, 4, 128]) for d in range(2)]
        PB = [p.psum("pb%d" % d, [128, 512]) for d in range(2)]
        rP1 = [p.res() for _ in range(2)]; rP2 = [p.res() for _ in range(2)]; rKQ = [p.res() for _ in range(2)]
        rNUM = [p.res() for _ in range(2)]; rDEN = [p.res() for _ in range(2)]; rST = [p.res() for _ in range(2)]
        def T(nm, shape, dt):
            return [p.sbuf("%s%d" % (nm, d), shape, dt) for d in range(2)], [p.res("%s%d" % (nm, d)) for d in range(2)]
        DG, DGr = T("DG", [128, 128], F32); WT, WTr = T("WT", [128, 128], F32); EC, ECr = T("EC", [128, 128], F32)
        QE, QEr = T("QE", [128, 128], BF16); STt, STr = T("ST", [128, 128], BF16)
        DD, DDr = T("DD", [128, 128], F32); HT, HTr = T("HT", [128, 128], F32)
        VS, VSr = T("VS", [128, 132], BF16); CF, CFr = T("CF", [128, 132], F32)
        CB, CBr = T("CB", [128, 128], BF16); NB, NBr = T("NB", [128, 128], BF16)
        for d in range(2):
            p.memset("pool", CF[d][:], 0.0, w=[CFr[d]]); p.memset("pool", CB[d][:], 0.0, w=[CBr[d]]); p.memset("pool", NB[d][:], 0.0, w=[NBr[d]])
        LE = 'dve'
        order = [list(range(NCK)), [1, 0] + list(range(NCK - 1, 1, -1))]
        for step in range(NCK):
            for d in range(2):
                c = order[d][step]
                cs = slice(c * 128, (c + 1) * 128)
                P1 = PA[d][:, 0, :]; P2 = PA[d][:, 1, :]; KQ = PA[d][:, 2, :]
                NUM = PB[d][:, 0:128]; DEN = PB[d][:, 128:256]; STP = PB[d][:, 256:256 + 129]
                p.ts(LE, DG[d][:], ident[:], CUM[:, d, c:c + 1], ALU.mult, r=[idr, pr], w=[DGr[d]])
                p.mm(P1, ones[:], DG[d][:], True, True, r=[cr, DGr[d]], w=[rP1[d]])
                p.mm(P2, ones[:], DG[d][:], True, False, r=[cr, DGr[d]], w=[rP2[d]])
                p.mm(P2, ident[:], mneg[:, d, :], False, True, r=[cr, idr], w=[rP2[d]])
                p.mm(KQ, kT[:, cs], qT[:, cs], True, True, r=[kr, qr], w=[rKQ[d]])
                p.act(WT[d][:], P2, AF.Exp, r=[rP2[d], pr], w=[WTr[d]], bias=IB[:, d, c:c + 1])
                p.act(EC[d][:], P1, AF.Exp, r=[rP1[d]], w=[ECr[d]])
                p.tt("dve", STt[d][:], KQ, WT[d][:], ALU.mult, r=[rKQ[d], WTr[d]], w=[STr[d]])
                p.tt(LE, QE[d][:], qT[:, cs], EC[d][:], ALU.mult, r=[qr, ECr[d]], w=[QEr[d]])
                p.mm(NUM, vtok[:, c, :], STt[d][:], True, False, r=[vtr, STr[d]], w=[rNUM[d]])
                p.mm(NUM, CB[d][:], QE[d][:], False, True, r=[CBr[d], QEr[d]], w=[rNUM[d]])
                p.mm(DEN, onesb[:], STt[d][:], True, False, r=[cr, STr[d]], w=[rDEN[d]])
                p.mm(DEN, NB[d][:], QE[d][:], False, True, r=[NBr[d], QEr[d]], w=[rDEN[d]])
                p.act(DD[d][:], DEN, AF.Abs, r=[rDEN[d]], w=[DDr[d]])
                p.ts("dve", DD[d][:], DD[d][:], 1.0, ALU.max, r=[DDr[d]], w=[DDr[d]])
                p.op("dve", lambda e, t_=DD[d]: e.reciprocal(out=t_[:], in_=t_[:]), r=[DDr[d]], w=[DDr[d]])
                p.tt("dve", HT[d][:], NUM, DD[d][:], ALU.mult, r=[rNUM[d], DDr[d]], w=[HTr[d]])
                p.tt(LE, H[:, cs], H[:, cs], HT[d][:], ALU.add, r=[Hr, HTr[d]], w=[Hr])
                p.ts(LE, VS[d][:, 0:128], vtok[:, c, :], SC[:, d, c:c + 1], ALU.mult, r=[vtr, pr], w=[VSr[d]])
                p.cp(LE, VS[d][:, 128:129], SC[:, d, c:c + 1], r=[pr], w=[VSr[d]])
                p.mm(STP, ktok[:, c, :], VS[d][:, 0:129], True, True, r=[ktr, VSr[d]], w=[rST[d]])
                p.stt(CF[d][:, 0:129], CF[d][:, 0:129], ET[:, d, c:c + 1], STP, ALU.mult, ALU.add, r=[CFr[d], pr, rST[d]], w=[CFr[d]])
                p.cp("act", CB[d][:], CF[d][:, 0:128], r=[CFr[d]], w=[CBr[d]])
                p.cp(LE, NB[d][:], CF[d][:, 128:129].to_broadcast([128, 128]), r=[CFr[d]], w=[NBr[d]])

        nw = p.sbuf("nw", [128, 1], F32); nwr = p.res("nw")
        p.dma("sp", nw[:], normw_ap.rearrange("(a b) -> a b", b=1), w=[nwr])
        lat, ctx, rs = src["oT"]
        p.dma("sp", stg[:, 0:256], ctx, r=[rs], w=[stgr]); p.dma("sp", stg[:, 256:], lat, r=[rs], w=[stgr])
        sq = p.sbuf("sq", [128, 512], F32); sqr = p.res("sq")
        rs_t = p.sbuf("rs_t", [128, 512], F32); rsr = p.res("rs_t")
        pn = [PB[0], PB[1]]; pnr = [p.res(), p.res()]
        ci = 0
        for t0 in range(0, TOK, 512):
            n = min(512, TOK - t0); ts_ = slice(t0, t0 + n)
            h = ci % 2; ci += 1
            p.act(sq[:, 0:n], H[:, ts_], AF.Square, r=[Hr], w=[sqr])
            p.mm(pn[h][:, 0:n], ones[:], sq[:, 0:n], True, True, r=[cr, sqr, rNUM[h], rDEN[h], rST[h]], w=[pnr[h], rNUM[h], rDEN[h], rST[h]])
            p.act(rs_t[:, 0:n], pn[h][:, 0:n], AF.Sqrt, r=[pnr[h]], w=[rsr], scale=1.0 / 128.0, bias=1e-6)
            p.op("dve", lambda e, o=rs_t[:, 0:n]: e.reciprocal(out=o, in_=o), r=[rsr], w=[rsr])
            p.tt("dve", H[:, ts_], H[:, ts_], rs_t[:, 0:n], ALU.mult, r=[Hr, rsr], w=[Hr])
            p.act(stg[:, ts_], stg[:, ts_], AF.Sigmoid, r=[stgr], w=[stgr])
            p.stt(H[:, ts_], H[:, ts_], nw[:, 0:1], stg[:, ts_], ALU.mult, ALU.mult, r=[Hr, nwr, stgr], w=[Hr])
        p.dma("sp", dst_ctx, H[:, 0:256], r=[Hr], w=[dst_res])
        p.dma("sp", dst_lat, H[:, 256:], r=[Hr], w=[dst_res])


D = 1024
EPS = 1e-6


def ttiles(NL, NC):
    out = [(t0, min(512, NL - t0), False) for t0 in range(0, NL, 512)]
    if NC:
        out.append((NL, NC, True))
    return out


def mod_stage(p, cv_ap, ada_w_ap, ada_b_ap, mod_d, mod_res, NM=12):
    with p.scope():
        cv = p.sbuf("cv", [128, 8, 2], F32); cvr = p.res("cv")
        p.dma("sp", cv[:], cv_ap.rearrange("(k p) n -> p k n", p=128), w=[cvr])
        p.act(cv[:], cv[:], AF.Silu, r=[cvr], w=[cvr])
        ab = p.sbuf("ab", [128, NM], F32); abr = p.res("ab")
        p.dma("sp", ab[:], ada_b_ap.rearrange("(m p) -> p m", p=128), w=[abr], slow=True)
        acc = p.sbuf("acc", [128, NM, 2], F32); accr = p.res("acc")
        for n in range(2):
            p.cp("dve", acc[:, :, n], ab[:], r=[abr], w=[accr])
        wt = [p.sbuf("wt%d" % i, [128, 128 * NM], F32) for i in range(2)]; wtr = [p.res("wt%d" % i) for i in range(2)]
        ps = p.psum("ps", [128, NM, 2]); psr = p.res("ps")
        for k in range(8):
            h = k % 2
            p.dma("sp" if h else "pool", wt[h][:], ada_w_ap[128 * k:128 * k + 128, :], w=[wtr[h]])
            for m in range(NM):
                p.mm(ps[:, m, :], wt[h][:, 128 * m:128 * m + 128], cv[:, k, :], True, True, r=[wtr[h], cvr], w=[psr])
            p.tt("dve", acc[:], acc[:], ps[:], ALU.add, r=[accr, psr], w=[accr])
        p.dma("sp", mod_d, acc[:], r=[accr], w=[mod_res])


def load_mod(p, mod_d, mod_res, nw_ap, sidx, scidx):
    modt = p.sbuf("modt", [128, 48, 2], F32); mr = p.res("modt")
    p.dma("sp", modt[:], mod_d, r=[mod_res], w=[mr], slow=True)
    nw = p.sbuf("nw", [128, 8], F32)
    p.dma("sp", nw[:], nw_ap.rearrange("(k p) -> p k", p=128), w=[mr], slow=True)
    A = p.sbuf("A", [128, 8, 2], F32)
    p.ts("dve", A[:], modt[:, 8 * scidx:8 * scidx + 8, :], 1.0, ALU.add, r=[mr], w=[mr])
    p.tt("dve", A[:], A[:], nw[:, :, None].to_broadcast([128, 8, 2]), ALU.mult, r=[mr], w=[mr])
    return modt, A, mr


def norm_mod_tile(p, xt, xr, n, A, SH, col, mr, ones, onr, ps, psr, sq, sqr, rstd, rsr, hb, hbr, hf=None, hfr=None):
    for k in range(8):
        p.act(sq[:, 0:n], xt[:, k, 0:n], AF.Square, r=[xr], w=[sqr])
        p.mm(ps[:, 0:n], ones[:], sq[:, 0:n], k == 0, k == 7, r=[onr, sqr], w=[psr])
    p.act(rstd[:, 0:n], ps[:, 0:n], AF.Sqrt, r=[psr], w=[rsr], scale=1.0 / D, bias=EPS)
    p.op("dve", lambda e, o=rstd[:, 0:n]: e.reciprocal(out=o, in_=o), r=[rsr], w=[rsr])
    for k in range(8):
        p.tt("dve", sq[:, 0:n], xt[:, k, 0:n], rstd[:, 0:n], ALU.mult, r=[xr, rsr, sqr], w=[sqr])
        if hf is not None:
            p.ts("dve", hf[:, k, 0:n], sq[:, 0:n], A[:, k, col:col + 1], ALU.mult, r=[sqr, mr], w=[hfr],
                 s2=SH[:, k, col:col + 1], op1=ALU.add)
            p.cp("act", hb[:, k, 0:n], hf[:, k, 0:n], r=[hfr], w=[hbr])
        else:
            p.ts("dve", hb[:, k, 0:n], sq[:, 0:n], A[:, k, col:col + 1], ALU.mult, r=[sqr, mr], w=[hbr],
                 s2=SH[:, k, col:col + 1], op1=ALU.add)


def norm1_stage(p, xT_d, x_res, NL, NC, mod_d, mod_res, nw_ap, hT_d, h_res):
    with p.scope():
        modt, A, mr = load_mod(p, mod_d, mod_res, nw_ap, 0, 1)
        SH = modt[:, 0:8, :]
        ones = p.sbuf("ones", [128, 128], F32); onr = p.res("ones"); p.memset("dve", ones[:], 1.0, w=[onr])
        xt = [p.sbuf("xt%d" % i, [128, 8, 512], F32) for i in range(2)]; xr = [p.res() for i in range(2)]
        hb = [p.sbuf("hb%d" % i, [128, 8, 512], BF16) for i in range(2)]; hbr = [p.res() for i in range(2)]
        sq = p.sbuf("sq", [128, 512], F32); sqr = p.res(); rstd = p.sbuf("rstd", [128, 512], F32); rsr = p.res()
        ps = p.psum("ps", [128, 512]); psr = p.res()
        for i, (t0, n, isc) in enumerate(ttiles(NL, NC)):
            h = i % 2
            p.dma("sp", xt[h][:, :, 0:n], xT_d[:, t0:t0 + n].rearrange("(k p) t -> p k t", p=128), r=[x_res], w=[xr[h]])
            norm_mod_tile(p, xt[h], xr[h], n, A, SH, 1 if isc else 0, mr, ones, onr, ps, psr, sq, sqr, rstd, rsr, hb[h], hbr[h])
            p.dma("pool", hT_d[i].rearrange("(k p) t -> p k t", p=128), hb[h][:, :, 0:n], r=[hbr[h]], w=[h_res[i]])


def load_w_bf16(p, w_cols_ap, m, wst, wstr, wb, wbr, eng="act", q="sp"):
    p.dma(q, wst[:, :, 0:m], w_cols_ap.rearrange("(k p) m -> p k m", p=128), w=[wstr])
    p.cp(eng, wb[:, :, 0:m], wst[:, :, 0:m], r=[wstr], w=[wbr])


def inproj_gate_stage(p, hT_d, h_res, NL, NC, w_in_ap, b_in_ap, off, nchunk, gT_d, g_res):
    NT = NL + NC
    GW = 4
    with p.scope():
        hT = p.sbuf("hT", [128, 8, NT], BF16); hr = p.res("hT")
        for i, (t0, n, isc) in enumerate(ttiles(NL, NC)):
            p.dma("sp", hT[:, :, t0:t0 + n], hT_d[i].rearrange("(k p) t -> p k t", p=128), r=[h_res[i]], w=[hr])
        bias = p.sbuf("bias", [128, nchunk], F32); br = p.res("bias")
        p.dma("sp", bias[:], b_in_ap[off:off + 128 * nchunk].rearrange("(m p) -> p m", p=128), w=[br], slow=True)
        wst = [p.sbuf("wst%d" % i, [128, 8, 128 * GW], F32) for i in range(2)]; wstr = [p.res() for i in range(2)]
        wb = [p.sbuf("wb%d" % i, [128, 8, 128 * GW], BF16) for i in range(2)]; wbr = [p.res() for i in range(2)]
        ot = [p.sbuf("ot%d" % i, [128, NT], BF16) for i in range(2)]; otr = [p.res() for i in range(2)]
        ps = [p.psum("ps%d" % i, [128, 512]) for i in range(6)]; psr = [p.res() for i in range(6)]
        kk = 0
        ngrp = nchunk // GW
        def load(g):
            h = g % 2
            c0 = off + 128 * GW * g
            p.dma("sp" if h else "pool", wst[h][:], w_in_ap[:, c0:c0 + 128 * GW].rearrange("(k p) m -> p k m", p=128), w=[wstr[h]])
            p.cp("act", wb[h][:], wst[h][:], r=[wstr[h]], w=[wbr[h]])
        load(0)
        for g in range(ngrp):
            h = g % 2
            if g + 1 < ngrp:
                load(g + 1)
            for mi in range(GW):
                m = g * GW + mi
                o = m % 2
                for (t0, n, isc) in ttiles(NL, NC):
                    b_ = kk % 6; kk += 1
                    for k in range(8):
                        p.mm(ps[b_][:, 0:n], wb[h][:, k, 128 * mi:128 * mi + 128], hT[:, k, t0:t0 + n], k == 0, k == 7, r=[wbr[h], hr], w=[psr[b_]])
                    p.act(ot[o][:, t0:t0 + n], ps[b_][:, 0:n], AF.Sigmoid, r=[psr[b_], br], w=[otr[o]], bias=bias[:, m:m + 1])
                p.dma("sp", gT_d[128 * m:128 * m + 128, :], ot[o][:], r=[otr[o]], w=[g_res])


def inproj_mix_stage(p, hall_d, hall_res, NL, NC, w_in_ap, b_in_ap, col_list, zlat_d, zctx_d, z_res):
    NT = NL + NC
    nchunk = len(col_list)
    with p.scope():
        wst = p.sbuf("wst", [128, 8, 128], F32); wstr = p.res()
        W = p.sbuf("W", [128, nchunk, 8, 128], BF16); Wr = p.res("W")
        bias = p.sbuf("bias", [128, nchunk], F32); br = p.res("bias")
        p.memset("dve", bias[:], 0.0, w=[br])
        p.memset("dve", W[:], 0.0, w=[Wr])
        for i, (c0, m) in enumerate(col_list):
            p.dma("sp", wst[:, :, 0:m], w_in_ap[:, c0:c0 + m].rearrange("(k p) m -> p k m", p=128), w=[wstr], slow=(m < 128))
            p.cp("act" if i % 2 else "dve", W[:, i, :, 0:m], wst[:, :, 0:m], r=[wstr], w=[Wr])
            p.dma("pool", bias[0:m, i:i + 1], b_in_ap[c0:c0 + m].rearrange("(a b) -> a b", b=1), w=[br])
        hT = [p.sbuf("hT%d" % i, [128, 8, 512], BF16) for i in range(2)]; hr = [p.res() for i in range(2)]
        ot = [p.sbuf("ot%d" % i, [128, nchunk, 512], F32) for i in range(2)]; otr = [p.res() for i in range(2)]
        ps = [p.psum("ps%d" % i, [128, 512]) for i in range(4)]; psr = [p.res() for i in range(4)]
        kk = 0; ti = 0
        for r in range(4):
            for i_t, (t0, n, isc) in enumerate(ttiles(NL, NC)):
                h = ti % 2; ti += 1
                p.dma("sp", hT[h][:, :, 0:n], hall_d[i_t][1024 * r:1024 * r + 1024, :].rearrange("(k p) t -> p k t", p=128),
                      r=[hall_res[i_t]], w=[hr[h]])
                for i in range(nchunk):
                    b_ = kk % 4; kk += 1
                    for k in range(8):
                        p.mm(ps[b_][:, 0:n], W[:, i, k, :], hT[h][:, k, 0:n], k == 0, k == 7, r=[Wr, hr[h]], w=[psr[b_]])
                    p.act(ot[h][:, i, 0:n], ps[b_][:, 0:n], AF.Identity, r=[psr[b_], br], w=[otr[h]], bias=bias[:, i:i + 1])
                if isc:
                    dst = zctx_d[:, :, NC * r:NC * r + n]
                else:
                    dst = zlat_d[:, :, NL * r + t0:NL * r + t0 + n]
                p.dma("pool", dst.rearrange("i p t -> p i t"), ot[h][:, :, 0:n], r=[otr[h]], w=[z_res])


def yasm_stage(p, srcs, skip_ap, y_own_d, y_res, LL, LC):
    with p.scope():
        sk = p.sbuf("sk", [128, 1], F32); skr = p.res("sk")
        p.dma("sp", sk[:], skip_ap.rearrange("(a b) -> a b", b=1), w=[skr])
        CW = 2048
        A = [p.sbuf("A%d" % i, [128, CW], F32) for i in range(3)]; Ar = [p.res() for i in range(3)]
        O = [p.sbuf("O%d" % i, [128, CW], BF16) for i in range(2)]; Or = [p.res() for i in range(2)]
        kk = 0
        pieces = [(0, t0, min(CW, LL - t0), t0) for t0 in range(0, LL, CW)] + ([(1, 0, LC, LL)] if LC else [])
        for (which, t0, n, o0) in pieces:
            for i, nm in enumerate(("z", "c2", "x2")):
                ap = srcs[nm][which]
                p.dma("sp", A[i][:, 0:n], ap[:, t0:t0 + n], r=[srcs[nm][2]], w=[Ar[i]])
            p.stt(A[0][:, 0:n], A[0][:, 0:n], sk[:, 0:1], A[1][:, 0:n], ALU.mult, ALU.add, r=[Ar[0], Ar[1], skr], w=[Ar[0]])
            h = kk % 2; kk += 1
            p.tt("dve", O[h][:, 0:n], A[0][:, 0:n], A[2][:, 0:n], ALU.mult, r=[Ar[0], Ar[2]], w=[Or[h]])
            for (ap_, rs_, a0, an) in y_own_d(0, o0, n):
                p.dma("pool", ap_, O[h][:, a0:a0 + an], r=[Or[h]], w=[rs_])
            for bi, nm in ((1, "fn"), (2, "ml")):
                ap = srcs[nm][which]
                p.dma("sp", A[bi][:, 0:n], ap[:, t0:t0 + n], r=[srcs[nm][2]], w=[Ar[bi]])
                h = kk % 2; kk += 1
                p.cp("act", O[h][:, 0:n], A[bi][:, 0:n], r=[Ar[bi]], w=[Or[h]])
                for (ap_, rs_, a0, an) in y_own_d(bi, o0, n):
                    p.dma("pool", ap_, O[h][:, a0:a0 + an], r=[Or[h]], w=[rs_])


def merge_stage(p, y_all_d, y_res, gT_d, g_res, oh_ap, xT_d, x_res, NL, NC, mod_d, mod_res, wbr_ap, wout_ap, do_ctx):
    LL = 4 * NL
    with p.scope():
        modt = p.sbuf("modt", [128, 48, 2], F32); mr = p.res("modt")
        p.dma("sp", modt[:], mod_d, r=[mod_res], w=[mr], slow=True)
        oh = p.sbuf("oh", [128, 4], F32); ohr = p.res("oh")
        p.dma("sp", oh[:], oh_ap, w=[ohr])
        wst = p.sbuf("wst", [128, 8, 1024], F32); wstr = p.res()
        WB = p.sbuf("WB", [128, 12, 1024], BF16); WO = p.sbuf("WO", [128, 8, 1024], BF16); Wr = p.res("W")
        for br in range(3):
            p.dma("sp", wst[:, 0:4, :], wbr_ap[br].rearrange("(r p) d -> p r d", p=128), w=[wstr])
            p.cp("act" if br % 2 else "dve", WB[:, 4 * br:4 * br + 4, :], wst[:, 0:4, :], r=[wstr], w=[Wr])
        p.dma("sp", wst[:], wout_ap.rearrange("(k p) d -> p k d", p=128), w=[wstr])
        p.cp("act", WO[:], wst[:], r=[wstr], w=[Wr])
        Yc = [p.sbuf("Yc%d" % i, [128, 12, 512], BF16) for i in range(2)]; Ycr = [p.res() for i in range(2)]
        Y = p.sbuf("Y", [128, 12, 512], BF16); Yr = p.res("Y")
        Gt = p.sbuf("Gt", [128, 24, 512], BF16); Gr = p.res("G")
        xt = p.sbuf("xt", [128, 8, 512], F32); xr = p.res("xt")
        mg = p.sbuf("mg", [128, 8, 512], BF16); mgr = p.res("mg")
        t1 = p.sbuf("t1", [128, 512], F32); t1r = p.res(); t2 = p.sbuf("t2", [128, 512], F32); t2r = p.res()
        ps = [p.psum("ps%d" % i, [128, 512]) for i in range(6)]; psr = [p.res() for i in range(6)]
        kk = 0
        tiles = ttiles(NL, NC if do_ctx else 0)
        for (t0, n, isc) in tiles:
            col = 1 if isc else 0
            for jj in range(4):
                c0 = (LL + NC * jj) if isc else (NL * jj + t0)
                h = jj % 2
                for br in range(3):
                    ap_, rs_ = y_all_d(br, c0, n)
                    p.dma("sp" if br % 2 else "pool", Yc[h][:, 4 * br:4 * br + 4, 0:n],
                          ap_.rearrange("(q p) t -> p q t", p=128), r=[rs_], w=[Ycr[h]])
                if jj == 0:
                    p.ts("dve", Y[:, :, 0:n], Yc[h][:, :, 0:n], oh[:, 0:1], ALU.mult, r=[Ycr[h], ohr], w=[Yr])
                else:
                    p.stt(Y[:, :, 0:n], Yc[h][:, :, 0:n], oh[:, jj:jj + 1], Y[:, :, 0:n], ALU.mult, ALU.add, r=[Ycr[h], ohr, Yr], w=[Yr])
            p.dma("sp", Gt[:, :, 0:n], gT_d[:, t0:t0 + n].rearrange("(q p) t -> p q t", p=128), r=[g_res], w=[Gr])
            p.dma("pool", xt[:, :, 0:n], xT_d[:, t0:t0 + n].rearrange("(k p) t -> p k t", p=128), r=[x_res], w=[xr])
            for m in range(8):
                pb = []
                for br in range(3):
                    b_ = kk % 6; kk += 1; pb.append(b_)
                    for r in range(4):
                        p.mm(ps[b_][:, 0:n], WB[:, 4 * br + r, 128 * m:128 * m + 128], Y[:, 4 * br + r, 0:n], r == 0, r == 3,
                             r=[Wr, Yr], w=[psr[b_]])
                p.tt("dve", t1[:, 0:n], ps[pb[0]][:, 0:n], Gt[:, m, 0:n], ALU.mult, r=[psr[pb[0]], Gr], w=[t1r])
                p.tt("dve", t2[:, 0:n], ps[pb[1]][:, 0:n], Gt[:, 8 + m, 0:n], ALU.mult, r=[psr[pb[1]], Gr], w=[t2r])
                p.tt("dve", t1[:, 0:n], t1[:, 0:n], t2[:, 0:n], ALU.add, r=[t1r, t2r], w=[t1r])
                p.tt("dve", t2[:, 0:n], ps[pb[2]][:, 0:n], Gt[:, 16 + m, 0:n], ALU.mult, r=[psr[pb[2]], Gr], w=[t2r])
                p.tt("dve", mg[:, m, 0:n], t1[:, 0:n], t2[:, 0:n], ALU.add, r=[t1r, t2r], w=[mgr])
            for m in range(8):
                b_ = kk % 6; kk += 1
                for k in range(8):
                    p.mm(ps[b_][:, 0:n], WO[:, k, 128 * m:128 * m + 128], mg[:, k, 0:n], k == 0, k == 7, r=[Wr, mgr], w=[psr[b_]])
                p.stt(xt[:, m, 0:n], ps[b_][:, 0:n], modt[:, 16 + m, col:col + 1], xt[:, m, 0:n], ALU.mult, ALU.add,
                      r=[psr[b_], mr, xr], w=[xr])
            p.dma("sp", xT_d[:, t0:t0 + n].rearrange("(k p) t -> p k t", p=128), xt[:, :, 0:n], r=[xr], w=[x_res])


def moe_stage(p, xT_d, x_res, NL, NC, mod_d, mod_res, nw_ap, wr_ap, br_ap, wg_ap, wu_ap, wd_ap, do_ctx,
              final_nw_ap=None, out_d=None, out_res=None):
    NCX = NC if do_ctx else 0
    NT = NL + NCX
    tiles = ttiles(NL, NCX)
    nsub = (NT + 127) // 128
    with p.scope():
        modt, A, mr = load_mod(p, mod_d, mod_res, nw_ap, 3, 4)
        SH = modt[:, 24:32, :]
        ones = p.sbuf("ones", [128, 128], F32); onr = p.res("ones"); p.memset("dve", ones[:], 1.0, w=[onr])
        identf, idr = make_ident(p, F32, "idf")
        sq = p.sbuf("sq", [128, 512], F32); sqr = p.res(); rstd = p.sbuf("rstd", [128, 512], F32); rsr = p.res()
        xt = p.sbuf("xt", [128, 8, 512], F32); xr = p.res("xt")
        hf = p.sbuf("hf", [128, 8, 512], F32); hfr = p.res("hf")
        H2 = p.sbuf("H2", [128, 8, NT], BF16); h2r = p.res("H2")
        WR = p.sbuf("WR", [128, 8, 20], F32); wrr = p.res("WR")
        p.dma("sp", WR[:], wr_ap.rearrange("(k p) n -> p k n", p=128), w=[wrr], slow=True)
        BR = p.sbuf("BR", [128, 20], F32)
        p.dma("sp", BR[:], br_ap.partition_broadcast(128), w=[wrr])
        CWt = p.sbuf("CWt", [128, nsub, 16], F32); cwr = p.res("CW")
        ps = [p.psum("ps%d" % i, [128, 512]) for i in range(6)]; psr = [p.res() for i in range(6)]
        pr_ = p.psum("pr", [128, 32]); prr = p.res("pr")
        def st(nm, w):
            return p.sbuf(nm, [128, w], F32)
        L_ = st("L", 20); gm = st("gm", 1); ge = st("ge", 4); gs = st("gs", 1); gmask = st("gmask", 4)
        tmp16 = st("tmp16", 16); eg = st("eg", 4); m1 = st("m1", 1); mk1 = st("mk1", 4); eg2 = st("eg2", 4); m2 = st("m2", 1)
        mk2 = st("mk2", 4); w1 = st("w1", 1); w2 = st("w2", 1); cwe = st("cwe", 4)
        rr = p.res("route")
        for (t0, n, isc) in tiles:
            col = 1 if isc else 0
            p.dma("sp", xt[:, :, 0:n], xT_d[:, t0:t0 + n].rearrange("(k p) t -> p k t", p=128), r=[x_res], w=[xr])
            norm_mod_tile(p, xt, xr, n, A, SH, col, mr, ones, onr, ps[0], psr[0], sq, sqr, rstd, rsr,
                          H2[:, :, t0:t0 + n], h2r, hf=hf, hfr=hfr)
            for s0 in range(0, n, 128):
                sn = min(128, n - s0); si = (t0 + s0) // 128
                for k in range(8):
                    p.mm(pr_[0:sn, 0:20], hf[:, k, s0:s0 + sn], WR[:, k, :], k == 0, k == 7, r=[hfr, wrr], w=[prr])
                R = [rr]
                p.tt("dve", L_[0:sn], pr_[0:sn, 0:20], BR[0:sn], ALU.add, r=[prr, wrr, rr], w=R)
                p.op("dve", lambda e, o=gm[0:sn], i=L_[0:sn, 0:4]: e.tensor_reduce(out=o, in_=i, axis=AX.X, op=ALU.max), r=R, w=R)
                p.ts("dve", gmask[0:sn], L_[0:sn, 0:4], gm[0:sn, 0:1], ALU.is_equal, r=R, w=R)
                p.ts("dve", gm[0:sn], gm[0:sn], -1.0, ALU.mult, r=R, w=R)
                p.act(ge[0:sn], L_[0:sn, 0:4], AF.Exp, r=R, w=R, bias=gm[0:sn, 0:1])
                p.op("dve", lambda e, o=gs[0:sn], i=ge[0:sn]: e.tensor_reduce(out=o, in_=i, axis=AX.X, op=ALU.add), r=R, w=R)
                p.op("dve", lambda e, o=gs[0:sn]: e.reciprocal(out=o, in_=o), r=R, w=R)
                p.tt("dve", tmp16[0:sn].rearrange("p (g e) -> p g e", e=4), L_[0:sn, 4:20].rearrange("p (g e) -> p g e", e=4),
                     gmask[0:sn, :, None].to_broadcast([sn, 4, 4]), ALU.mult, r=R, w=R)
                p.op("dve", lambda e, o=eg[0:sn], i=tmp16[0:sn].rearrange("p (g e) -> p e g", e=4): e.tensor_reduce(out=o, in_=i, axis=AX.X, op=ALU.add), r=R, w=R)
                p.op("dve", lambda e, o=m1[0:sn], i=eg[0:sn]: e.tensor_reduce(out=o, in_=i, axis=AX.X, op=ALU.max), r=R, w=R)
                p.ts("dve", mk1[0:sn], eg[0:sn], m1[0:sn, 0:1], ALU.is_equal, r=R, w=R)
                p.stt(eg2[0:sn], mk1[0:sn], -1e30, eg[0:sn], ALU.mult, ALU.add, r=R, w=R)
                p.op("dve", lambda e, o=m2[0:sn], i=eg2[0:sn]: e.tensor_reduce(out=o, in_=i, axis=AX.X, op=ALU.max), r=R, w=R)
                p.ts("dve", mk2[0:sn], eg2[0:sn], m2[0:sn, 0:1], ALU.is_equal, r=R, w=R)
                p.tt("dve", w1[0:sn], m2[0:sn], m1[0:sn], ALU.subtract, r=R, w=R)
                p.act(w1[0:sn], w1[0:sn], AF.Exp, r=R, w=R)
                p.ts("dve", w1[0:sn], w1[0:sn], 1.0, ALU.add, r=R, w=R)
                p.op("dve", lambda e, o=w1[0:sn]: e.reciprocal(out=o, in_=o), r=R, w=R)
                p.ts("dve", w2[0:sn], w1[0:sn], -1.0, ALU.mult, r=R, w=R, s2=1.0, op1=ALU.add)
                p.tt("dve", w1[0:sn], w1[0:sn], gs[0:sn], ALU.mult, r=R, w=R)
                p.tt("dve", w2[0:sn], w2[0:sn], gs[0:sn], ALU.mult, r=R, w=R)
                p.ts("dve", cwe[0:sn], mk1[0:sn], w1[0:sn, 0:1], ALU.mult, r=R, w=R)
                p.stt(cwe[0:sn], mk2[0:sn], w2[0:sn, 0:1], cwe[0:sn], ALU.mult, ALU.add, r=R, w=R)
                p.cp("dve", tmp16[0:sn].rearrange("p (g e) -> p g e", e=4), cwe[0:sn, None, :].to_broadcast([sn, 4, 4]), r=R, w=R)
                p.tt("dve", CWt[0:sn, si, :].rearrange("p (g e) -> p g e", e=4), tmp16[0:sn].rearrange("p (g e) -> p g e", e=4),
                     gmask[0:sn, :, None].to_broadcast([sn, 4, 4]), ALU.mult, r=R, w=[cwr, rr])
        ACC = p.sbuf("ACC", [128, nsub, 1024], F32); accr = p.res("ACC")
        p.memset("dve", ACC[:], 0.0, w=[accr])
        wst = [p.sbuf("wst%d" % i, [128, 8, 256], F32) for i in range(2)]; wstr = [p.res() for i in range(2)]
        WG = [p.sbuf("WG%d" % i, [128, 8, 256], BF16) for i in range(2)]; WU = [p.sbuf("WU%d" % i, [128, 8, 256], BF16) for i in range(2)]
        WD = [p.sbuf("WD%d" % i, [128, 2, 1024], BF16) for i in range(2)]
        wer = [p.res() for i in range(2)]
        SG = p.sbuf("SG", [128, 512], BF16); sgr = p.res()
        AA = p.sbuf("AA", [128, 2, 512], BF16); aar = p.res()
        kk = 0
        for e_ in range(16):
            h = e_ % 2
            p.dma("sp", wst[0][:], wg_ap[e_].rearrange("(k p) m -> p k m", p=128), w=[wstr[0]])
            p.cp("act", WG[h][:], wst[0][:], r=[wstr[0]], w=[wer[h]])
            p.dma("pool", wst[1][:], wu_ap[e_].rearrange("(k p) m -> p k m", p=128), w=[wstr[1]])
            p.cp("act", WU[h][:], wst[1][:], r=[wstr[1]], w=[wer[h]])
            p.dma("sp", wst[0][:].rearrange("p a b -> p (a b)").rearrange("p (c d) -> p c d", c=2), wd_ap[e_].rearrange("(c p) d -> p c d", p=128), w=[wstr[0]])
            p.cp("act", WD[h][:], wst[0][:].rearrange("p a b -> p (a b)").rearrange("p (c d) -> p c d", c=2), r=[wstr[0]], w=[wer[h]])
            for (t0, n, isc) in tiles:
                for hc in range(2):
                    bg = kk % 6; kk += 1; bu = kk % 6; kk += 1
                    for k in range(8):
                        p.mm(ps[bg][:, 0:n], WG[h][:, k, 128 * hc:128 * hc + 128], H2[:, k, t0:t0 + n], k == 0, k == 7, r=[wer[h], h2r], w=[psr[bg]])
                    for k in range(8):
                        p.mm(ps[bu][:, 0:n], WU[h][:, k, 128 * hc:128 * hc + 128], H2[:, k, t0:t0 + n], k == 0, k == 7, r=[wer[h], h2r], w=[psr[bu]])
                    p.act(SG[:, 0:n], ps[bg][:, 0:n], AF.Silu, r=[psr[bg]], w=[sgr])
                    p.tt("dve", AA[:, hc, 0:n], ps[bu][:, 0:n], SG[:, 0:n], ALU.mult, r=[psr[bu], sgr], w=[aar])
                for s0 in range(0, n, 128):
                    sn = min(128, n - s0); si = (t0 + s0) // 128
                    for dh in range(2):
                        b_ = kk % 6; kk += 1
                        for hc in range(2):
                            p.mm(ps[b_][0:sn, :], AA[:, hc, s0:s0 + sn], WD[h][:, hc, 512 * dh:512 * dh + 512], hc == 0, hc == 1,
                                 r=[aar, wer[h]], w=[psr[b_]])
                        p.stt(ACC[0:sn, si, 512 * dh:512 * dh + 512], ps[b_][0:sn, :], CWt[0:sn, si, e_:e_ + 1],
                              ACC[0:sn, si, 512 * dh:512 * dh + 512], ALU.mult, ALU.add, r=[psr[b_], cwr, accr], w=[accr])
        if final_nw_ap is not None:
            fw_ = p.sbuf("fw", [128, 8], F32); fwr = p.res()
            p.dma("sp", fw_[:], final_nw_ap.rearrange("(k p) -> p k", p=128), w=[fwr], slow=True)
        for (t0, n, isc) in tiles:
            col = 1 if isc else 0
            p.dma("sp", xt[:, :, 0:n], xT_d[:, t0:t0 + n].rearrange("(k p) t -> p k t", p=128), r=[x_res], w=[xr])
            for m in range(8):
                b_ = kk % 6; kk += 1
                for s0 in range(0, n, 128):
                    sn = min(128, n - s0); si = (t0 + s0) // 128
                    p.tr(ps[b_][:, s0:s0 + sn], ACC[0:sn, si, 128 * m:128 * m + 128], identf[0:sn, 0:sn], r=[accr, idr], w=[psr[b_]])
                p.stt(xt[:, m, 0:n], ps[b_][:, 0:n], modt[:, 40 + m, col:col + 1], xt[:, m, 0:n], ALU.mult, ALU.add,
                      r=[psr[b_], mr, xr], w=[xr])
            if final_nw_ap is None:
                p.dma("pool", xT_d[:, t0:t0 + n].rearrange("(k p) t -> p k t", p=128), xt[:, :, 0:n], r=[xr], w=[x_res])
            elif not isc:
                for k in range(8):
                    p.act(sq[:, 0:n], xt[:, k, 0:n], AF.Square, r=[xr], w=[sqr])
                    p.mm(ps[0][:, 0:n], ones[:], sq[:, 0:n], k == 0, k == 7, r=[onr, sqr], w=[psr[0]])
                p.act(rstd[:, 0:n], ps[0][:, 0:n], AF.Sqrt, r=[psr[0]], w=[rsr], scale=1.0 / D, bias=EPS)
                p.op("dve", lambda e, o=rstd[:, 0:n]: e.reciprocal(out=o, in_=o), r=[rsr], w=[rsr])
                for k in range(8):
                    p.stt(hf[:, k, 0:n], xt[:, k, 0:n], fw_[:, k:k + 1], rstd[:, 0:n], ALU.mult, ALU.mult, r=[xr, fwr, rsr], w=[hfr])
                p.dma("pool", out_d[:, t0:t0 + n].rearrange("(k p) t -> p k t", p=128), hf[:, :, 0:n], r=[hfr], w=[out_res])

NLAT, NCTX = 2048, 64
LLAT, LCTX = 8192, 256
GROUPS = [[0, 1, 2, 3], [4, 5, 6, 7]]
DEPTH = 2
OFF_FN, OFF_ML, OFF_MLG, OFF_GATE = 1536, 2048, 4096, 4112


def build_program(const_np):
    p = Prog()
    I = {}

    def inp(name, shape, dt=F32):
        I[name] = p.dram(name, shape, dt, "ExternalInput")
        return I[name]

    inp("xT0", [1024, NLAT]); inp("cT0", [1024, NCTX]); inp("cv", [1024, 2]); inp("oh", [128, 4]); inp("norm_f", [1024])
    for k, v in const_np.items():
        inp(k, v.shape, F32 if v.dtype == np.float32 else BF16)
    for l in range(DEPTH):
        L = "_%d" % l
        inp("ada_w" + L, [1024, 1536]); inp("ada_b" + L, [1536]); inp("n1w" + L, [1024]); inp("n2w" + L, [1024])
        inp("w_gate_in" + L, [1024, 3072]); inp("b_gate_in" + L, [3072]); inp("w_mix" + L, [1024, 1028]); inp("b_mix" + L, [1028])
        inp("hy_cw" + L, [3, 3, 384]); inp("hy_cb" + L, [384]); inp("ml_cw" + L, [3, 3, 256]); inp("ml_cb" + L, [256])
        inp("f_w1" + L, [33, 64]); inp("f_b1" + L, [64]); inp("f_freq" + L, [64]); inp("f_w2" + L, [64, 64]); inp("f_b2" + L, [64])
        inp("f_w3" + L, [64, 4, 128]); inp("decay" + L, [128, 4]); inp("skip" + L, [2, 128]); inp("mlnw" + L, [128])
        inp("wbr" + L, [3, 512, 1024]); inp("wout" + L, [1024, 1024])
        inp("wr" + L, [1024, 20]); inp("br" + L, [20]); inp("wg" + L, [16, 1024, 256]); inp("wu" + L, [16, 1024, 256]); inp("wd" + L, [16, 256, 1024])
    outT = p.dram("outT", [1024, NLAT], F32, "ExternalOutput"); out_res = p.res("outT")

    NT = NLAT + NCTX
    S = {}

    def scr(name, shape, dt=F32):
        S[name] = (p.dram("s_" + name, shape, dt), p.res("s_" + name))
        return S[name]

    scr("mod_own", [128, 12, 2]); scr("mod_all", [4 * 128, 24]); scr("mod_full", [128, 48, 2])
    scr("xT", [1024, NT]); scr("gT", [3072, NT], BF16)
    TT = ttiles(NLAT, NCTX)
    hown = [scr("hown%d" % i, [1024, n], BF16) for i, (t0, n, isc) in enumerate(TT)]
    hall = [scr("hall%d" % i, [4096, n], BF16) for i, (t0, n, isc) in enumerate(TT)]
    YCH = [(0, 3072), (3072, 3072), (6144, 2304)]
    yown = [[scr("yown%d_%d" % (br, ck), [128, w_], BF16) for ck, (c0_, w_) in enumerate(YCH)] for br in range(3)]
    yall = [[scr("yall%d_%d" % (br, ck), [512, w_], BF16) for ck, (c0_, w_) in enumerate(YCH)] for br in range(3)]

    def y_own_fn(br, col0, n):
        out = []
        for ck, (c0_, w_) in enumerate(YCH):
            lo = max(col0, c0_); hi = min(col0 + n, c0_ + w_)
            if hi > lo:
                out.append((yown[br][ck][0][:, lo - c0_:hi - c0_], yown[br][ck][1], lo - col0, hi - lo))
        return out

    def y_all_fn(br, col0, n):
        for ck, (c0_, w_) in enumerate(YCH):
            if c0_ <= col0 and col0 + n <= c0_ + w_:
                return yall[br][ck][0][:, col0 - c0_:col0 - c0_ + n], yall[br][ck][1]
        raise AssertionError("y tile straddles chunks")
    scr("zlat", [9, 128, LLAT]); scr("zctx", [9, 128, LCTX]); scr("cvl", [5, 128, LLAT]); scr("cvc", [5, 128, LCTX])
    scr("fl", [4, 128, LLAT]); scr("fc", [4, 128, LCTX])
    for nm in ("c1", "c2", "zz", "fn", "ml"):
        scr(nm + "l", [128, LLAT]); scr(nm + "c", [128, LCTX])

    tabs = {k: I[k] for k in const_np}
    xT, xres = S["xT"]
    with p.scope():
        t = p.sbuf("t", [128, 8, NT], F32); tr = p.res()
        p.dma("sp", t[:, :, 0:NLAT], I["xT0"].rearrange("(k p) t -> p k t", p=128), w=[tr])
        p.dma("pool", t[:, :, NLAT:NT], I["cT0"].rearrange("(k p) t -> p k t", p=128), w=[tr])
        p.dma("sp", xT.rearrange("(k p) t -> p k t", p=128), t[:], r=[tr], w=[xres])

    mod_view = S["mod_all"][0].rearrange("(r p) (m n) -> p r m n", p=128, n=2)

    for l in range(DEPTH):
        L = "_%d" % l
        last = (l == DEPTH - 1)
        W = lambda nm: I[nm + L]
        p.label = 'mod_stage'; mod_stage(p, I["cv"], W("ada_w"), W("ada_b"), S["mod_own"][0], S["mod_own"][1], NM=12)
        p.coll("AllGather", S["mod_all"][0], S["mod_own"][0].rearrange("p m n -> p (m n)"), GROUPS, r=[S["mod_own"][1]], w=[S["mod_all"][1]])
        with p.scope():
            mt = p.sbuf("mt", [128, 48, 2], F32); mtr = p.res()
            p.dma("sp", mt[:].rearrange("p (r m) n -> p r m n", r=4), mod_view, r=[S["mod_all"][1]], w=[mtr], slow=True)
            p.dma("sp", S["mod_full"][0], mt[:], r=[mtr], w=[S["mod_full"][1]])
        modv, modr = S["mod_full"]
        p.label = 'norm1_stage'; norm1_stage(p, xT, xres, NLAT, NCTX, modv, modr, W("n1w"), [h_[0] for h_ in hown], [h_[1] for h_ in hown])
        for i in range(len(TT)):
            p.coll("AllGather", hall[i][0], hown[i][0], GROUPS, r=[hown[i][1]], w=[hall[i][1]])
        p.label = 'inproj_gate_stage'; inproj_gate_stage(p, [h_[0] for h_ in hown], [h_[1] for h_ in hown], NLAT, NCTX, W("w_gate_in"), W("b_gate_in"), 0, 24, S["gT"][0], S["gT"][1])
        cols = [(128 * i, 128) for i in range(8)] + [(1024, 4)]
        p.label = 'inproj_mix_stage'; inproj_mix_stage(p, [h_[0] for h_ in hall], [h_[1] for h_ in hall], NLAT, NCTX, W("w_mix"), W("b_mix"), cols, S["zlat"][0], S["zctx"][0], S["zlat"][1])
        zl, zc, zr = S["zlat"][0], S["zctx"][0], S["zlat"][1]
        cvl, cvc = S["cvl"][0], S["cvc"][0]
        cvr = S["cvl"][1]
        jl = [(zl[0], zr, W("hy_cw"), W("hy_cb"), 0, False, cvl[0], cvr), (zl[1], zr, W("hy_cw"), W("hy_cb"), 128, False, cvl[1], cvr),
              (zl[2], zr, W("hy_cw"), W("hy_cb"), 256, False, cvl[2], cvr), (zl[4], zr, W("ml_cw"), W("ml_cb"), 0, True, cvl[3], cvr),
              (zl[5], zr, W("ml_cw"), W("ml_cb"), 128, True, cvl[4], cvr)]
        p.label = 'conv_stage'; conv_stage(p, jl, 128, 64)
        jc = [(zc[4], zr, W("ml_cw"), W("ml_cb"), 0, True, cvc[3], cvr), (zc[5], zr, W("ml_cw"), W("ml_cb"), 128, True, cvc[4], cvr)]
        if not last:
            jc += [(zc[0], zr, W("hy_cw"), W("hy_cb"), 0, False, cvc[0], cvr), (zc[1], zr, W("hy_cw"), W("hy_cb"), 128, False, cvc[1], cvr),
                   (zc[2], zr, W("hy_cw"), W("hy_cb"), 256, False, cvc[2], cvr)]
        p.label = 'conv_stage'; conv_stage(p, jc, 1, 256)
        fw = {k: W(k) for k in ("f_w1", "f_b1", "f_freq", "f_w2", "f_b2", "f_w3", "decay")}
        variants = [("l", LLAT, cvl, "featsT_l", "trow_l")] + ([] if last else [("c", LCTX, cvc, "featsT_c", "trow_c")])
        for (sfx, Lx, cvx, fnm, tnm) in variants:
            filt, fr = S["f" + sfx]
            p.label = 'hfilt_stage'; hfilt_stage(p, I[fnm], I[tnm], fw, 0, Lx, filt, fr)
            src1 = {"v": (cvx[0], cvr), "kf": (filt[0], fr), "kg": (filt[1], fr)}
            p.label = 'hconv_stage'; hconv_stage(p, tabs, src1, S["c1" + sfx][0], S["c1" + sfx][1], 128, Lx, True)
            src2 = {"v": (cvx[0], cvr), "kf": (filt[2], fr), "kg": (filt[3], fr), "x1": (cvx[1], cvr),
                    "c1": S["c1" + sfx], "skip": (W("skip")[0], p.res())}
            p.label = 'hconv_stage'; hconv_stage(p, tabs, src2, S["c2" + sfx][0], S["c2" + sfx][1], 128, Lx, False, u_out=S["zz" + sfx][0], u_res=S["zz" + sfx][1])
            zsrc = zl if sfx == "l" else zc
            p.label = 'fnet_stage'; fnet_stage(p, tabs, zsrc[3], zr, S["fn" + sfx][0], S["fn" + sfx][1], Lx)
        msrc = {"qT": (cvl[3], cvc[3], cvr), "kT": (cvl[4], cvc[4], cvr), "vT": (zl[6], zc[6], zr), "oT": (zl[7], zc[7], zr),
                "g": (zl[8][0:4], zc[8][0:4], zr)}
        p.label = 'mlstm_stage'; mlstm_stage(p, msrc, S["mll"][0], S["mlc"][0], S["mll"][1], W("mlnw"), 0)
        S["mlc"] = (S["mlc"][0], S["mll"][1])
        ys = {"x2": (cvl[2], cvc[2], cvr)}
        for nm, key in (("c2", "c2"), ("z", "zz"), ("fn", "fn"), ("ml", "ml")):
            ys[nm] = (S[key + "l"][0], S[key + "c"][0], S[key + "l"][1])
        p.label = 'yasm_stage'; yasm_stage(p, ys, W("skip")[1], y_own_fn, None, LLAT, LCTX if not last else 0)
        for br in range(3):
            for ck in range(len(YCH)):
                p.coll("AllGather", yall[br][ck][0], yown[br][ck][0], GROUPS, r=[yown[br][ck][1]], w=[yall[br][ck][1]])
        p.label = 'merge_stage'; merge_stage(p, y_all_fn, None, S["gT"][0], S["gT"][1], I["oh"], xT, xres, NLAT, NCTX, modv, modr,
                    W("wbr"), W("wout"), not last)
        if last:
            p.label = 'moe_stage'; moe_stage(p, xT, xres, NLAT, NCTX, modv, modr, W("n2w"), W("wr"), W("br"), W("wg"), W("wu"), W("wd"), False,
                      I["norm_f"], outT, out_res)
        else:
            p.label = 'moe_stage'; moe_stage(p, xT, xres, NLAT, NCTX, modv, modr, W("n2w"), W("wr"), W("br"), W("wg"), W("wu"), W("wd"), True)
    return p.finish(), p


_CACHE = {}


def _consts():
    c = {}
    c.update(hconv_tables()); c.update(fnet_tables())
    fl, tl = hfilt_consts(LLAT); fc, tc = hfilt_consts(LCTX)
    c["featsT_l"] = fl; c["trow_l"] = tl; c["featsT_c"] = fc; c["trow_c"] = tc
    return c


def kernel(x, c, ctx, c_ctx, ada_w, ada_b, norm1_w, norm2_w, w_in, b_in,
           hy_conv_w, hy_conv_b, hy_f_w1, hy_f_b1, hy_f_w2, hy_f_b2, hy_f_w3, hy_f_freq,
           hy_decay, hy_skip, ml_conv_w, ml_conv_b, ml_norm_w, w_branch, w_out,
           moe_rg_w, moe_rg_b, moe_re_w, moe_re_b, moe_w_gate, moe_w_up, moe_w_down, norm_f_w):
    f32 = lambda a: np.ascontiguousarray(np.asarray(a, dtype=np.float32))
    x, c, ctx, c_ctx = f32(x), f32(c), f32(ctx), f32(c_ctx)
    if "nc" not in _CACHE:
        _CACHE["const"] = _consts()
        _CACHE["nc"] = build_program(_CACHE["const"])[0]
    const = _CACHE["const"]
    nc = _CACHE["nc"]
    in_maps = []
    for core in range(8):
        b, j = core // 4, core % 4
        m = dict(const)
        m["xT0"] = f32(x[b, NLAT * j:NLAT * (j + 1), :].T)
        m["cT0"] = f32(ctx[b, NCTX * j:NCTX * (j + 1), :].T)
        m["cv"] = f32(np.stack([c[b], c_ctx], axis=1))
        oh = np.zeros((128, 4), np.float32); oh[:, j] = 1.0
        m["oh"] = oh
        m["norm_f"] = f32(norm_f_w)
        sl = slice(128 * j, 128 * j + 128)
        for l in range(DEPTH):
            L = "_%d" % l
            m["ada_w" + L] = f32(ada_w[l][:, 1536 * j:1536 * (j + 1)]); m["ada_b" + L] = f32(ada_b[l][1536 * j:1536 * (j + 1)])
            m["n1w" + L] = f32(norm1_w[l]); m["n2w" + L] = f32(norm2_w[l])
            m["w_gate_in" + L] = f32(w_in[l][:, OFF_GATE:]); m["b_gate_in" + L] = f32(b_in[l][OFF_GATE:])
            mixcols = np.concatenate([np.arange(128 * j, 128 * j + 128) + o for o in
                                      (0, 512, 1024, OFF_FN, OFF_ML, OFF_ML + 512, OFF_ML + 1024, OFF_ML + 1536)]
                                     + [np.array([OFF_MLG + j, OFF_MLG + 4 + j, OFF_MLG + 8 + j, OFF_MLG + 12 + j])])
            m["w_mix" + L] = f32(w_in[l][:, mixcols]); m["b_mix" + L] = f32(b_in[l][mixcols])
            hyc = np.concatenate([np.arange(128 * j, 128 * j + 128) + o for o in (0, 512, 1024)])
            m["hy_cw" + L] = f32(hy_conv_w[l][:, :, hyc]); m["hy_cb" + L] = f32(hy_conv_b[l][hyc])
            mlc = np.concatenate([np.arange(128 * j, 128 * j + 128) + o for o in (0, 512)])
            m["ml_cw" + L] = f32(ml_conv_w[l][:, :, mlc]); m["ml_cb" + L] = f32(ml_conv_b[l][mlc])
            m["f_w1" + L] = f32(hy_f_w1[l]); m["f_b1" + L] = f32(hy_f_b1[l]); m["f_freq" + L] = f32(hy_f_freq[l])
            m["f_w2" + L] = f32(hy_f_w2[l]); m["f_b2" + L] = f32(hy_f_b2[l])
            m["f_w3" + L] = f32(np.asarray(hy_f_w3[l]).reshape(64, 4, 512)[:, :, sl])
            m["decay" + L] = f32(np.asarray(hy_decay[l]).reshape(4, 512)[:, sl].T)
            m["skip" + L] = f32(hy_skip[l][:, sl]); m["mlnw" + L] = f32(ml_norm_w[l][sl])
            m["wbr" + L] = f32(w_branch[l]); m["wout" + L] = f32(w_out[l])
            m["wr" + L] = f32(np.concatenate([moe_rg_w[l], moe_re_w[l]], axis=1)); m["br" + L] = f32(np.concatenate([moe_rg_b[l], moe_re_b[l]]))
            m["wg" + L] = f32(moe_w_gate[l]); m["wu" + L] = f32(moe_w_up[l]); m["wd" + L] = f32(moe_w_down[l])
        in_maps.append(m)
    res = run_bass_kernel_spmd(nc, in_maps, core_ids=list(range(8)))
    out = np.empty((2, LLAT, 1024), np.float32)
    for core in range(8):
        b, j = core // 4, core % 4
        out[b, NLAT * j:NLAT * (j + 1), :] = np.asarray(res.results[core]["outT"], dtype=np.float32).T
    return out
```

```python
import os
import numpy as np
import ml_dtypes

from contextlib import ExitStack, contextmanager
import concourse.bass as bass
import concourse.mybir as mybir
from concourse.bass_utils import run_bass_kernel_spmd

F32 = mybir.dt.float32
BF16 = mybir.dt.bfloat16
ALU = mybir.AluOpType
AF = mybir.ActivationFunctionType
AX = mybir.AxisListType


class SemCtr:
    __slots__ = ("sem", "count")

    def __init__(self, sem):
        self.sem = sem
        self.count = 0


class Res:
    __slots__ = ("name", "last_w", "readers", "dsem")

    def __init__(self, name):
        self.name = name
        self.last_w = None
        self.readers = {}
        self.dsem = None


class Prog:
    ENG = ("pe", "dve", "act", "pool", "sp")

    def __init__(self, same_engine_sync=True):
        self.nc = bass.Bass("TRN2", target_bir_lowering=False)
        self.st = ExitStack()
        self.stk = [self.st]
        self.q = {e: [] for e in self.ENG}
        self.cnt = {e: 0 for e in self.ENG}
        self.seen = {e: {} for e in self.ENG}
        self.esem = {e: self.st.enter_context(self.nc.semaphore("es_" + e)) for e in self.ENG}
        self.esem_ids = {id(s) for s in self.esem.values()}
        self.own_ids = {e: {id(self.esem[e])} for e in self.ENG}
        self.same = same_engine_sync
        self.dma_events = {}
        self.sem_pool = []
        self.block_log = []
        self.label = ''
        self.scope_res = [[]]
        self.uid = 0
        self.ninst = 0
        self.flush_every = int(os.environ.get("FLUSH_EVERY", "1200"))

    def dram(self, name, shape, dt, kind=None):
        if kind is None:
            t = self.nc.dram_tensor(name, list(shape), dt)
        else:
            t = self.nc.dram_tensor(name, list(shape), dt, kind=kind)
        return t.ap()

    def sbuf(self, name, shape, dt):
        self.uid += 1
        return self.stk[-1].enter_context(self.nc.sbuf_tensor("%s_%d" % (name, self.uid), list(shape), dt))

    def psum(self, name, shape, dt=F32):
        self.uid += 1
        return self.stk[-1].enter_context(self.nc.psum_tensor("%s_%d" % (name, self.uid), list(shape), dt))

    def res(self, name=None):
        self.uid += 1
        r = Res("%s_%d" % (name or "r", self.uid))
        self.scope_res[-1].append(r)
        return r

    def semctr(self):
        while self.sem_pool:
            sc = self.sem_pool.pop()
            if sc.count < 20000:
                return sc
        return SemCtr(self.newsem("ds"))

    def newsem(self, name):
        self.uid += 1
        return self.st.enter_context(self.nc.semaphore("%s_%d" % (name, self.uid)))

    def _waits(self, eng, r, w, dma=False):
        waits = {}

        def need(ev):
            if ev is None:
                return
            s, v = ev
            if (not self.same or eng == "pe") and id(s) in self.own_ids[eng]:
                return
            if self.seen[eng].get(id(s), (None, 0))[1] < v:
                if waits.get(id(s), (None, 0))[1] < v:
                    waits[id(s)] = (s, v)

        for x in r:
            need(x.last_w)
            if dma:
                for ev in x.readers.values():
                    if id(ev[0]) not in self.esem_ids:
                        need(ev)
        for x in w:
            need(x.last_w)
            for ev in x.readers.values():
                need(ev)
        for k, sv in waits.items():
            self.seen[eng][k] = sv
        return list(waits.values())

    def op(self, eng, fn, r=(), w=()):
        waits = self._waits(eng, r, w)
        if self.cnt[eng] >= 20000:
            self.esem[eng] = self.newsem("es_" + eng)
            self.esem_ids.add(id(self.esem[eng]))
            self.own_ids[eng].add(id(self.esem[eng]))
            self.cnt[eng] = 0
        self.cnt[eng] += 1
        ev = (self.esem[eng], self.cnt[eng])
        self.q[eng].append((waits, fn, ev, 1))
        for x in r:
            x.readers[id(ev[0])] = ev
        for x in w:
            x.last_w = ev
            x.readers = {}
        self.ninst += 1
        self._autoflush()
        return ev

    def _autoflush(self):
        if sum(len(v) for v in self.q.values()) >= self.flush_every:
            self.flush()

    def _async(self, eng, fn, inc, r, w):
        dst = w[0]
        waits = self._waits(eng, r, w, dma=True)
        if dst.dsem is None:
            dst.dsem = self.semctr()
        dst.dsem.count += inc
        ev = (dst.dsem.sem, dst.dsem.count)
        self.q[eng].append((waits, fn, ev, inc))
        for x in r:
            x.readers[id(ev[0])] = ev
        dst.last_w = ev
        dst.readers = {}
        self.dma_events[id(ev[0])] = ev
        self.ninst += 1
        self._autoflush()
        return ev

    def dma(self, eng, out, in_, r=(), w=(), slow=False):
        if slow:
            return self._async(eng, lambda e: e.dma_start(out=out, in_=in_, allow_slow_non_contiguous=True), 16, r, w)
        return self._async(eng, lambda e: e.dma_start(out=out, in_=in_), 16, r, w)

    def coll(self, kind, out, in_, groups, r=(), w=()):
        return self._async("pool", lambda e: e.collective_compute(
            kind, ALU.bypass, replica_groups=groups, ins=[in_.opt()], outs=[out.opt()]), 1, r, w)

    def barrier(self):
        for eng in self.ENG:
            waits = []
            evs = [(self.esem[x], self.cnt[x]) for x in self.ENG if x != eng and self.cnt[x] > 0]
            evs += list(self.dma_events.values())
            for s, v in evs:
                if self.seen[eng].get(id(s), (None, 0))[1] < v:
                    waits.append((s, v))
                    self.seen[eng][id(s)] = (s, v)
            if waits:
                self.q[eng].append((waits, None, None, 0))
        self.dma_events = {}

    def flush(self):
        nc = self.nc
        with nc.Block() as block:
            def mk(name):
                def run(e):
                    for waits, fn, ev, inc in self.q[name]:
                        for s, v in waits:
                            e.wait_ge(s, v)
                        if fn is not None:
                            fn(e).then_inc(ev[0], inc)
                return run
            block.sync(mk("sp"))
            block.tensor(mk("pe"))
            block.vector(mk("dve"))
            block.scalar(mk("act"))
            block.gpsimd(mk("pool"))
        self.block_log.append((self.label, sum(len(v) for v in self.q.values())))
        self.q = {e: [] for e in self.ENG}

    @contextmanager
    def scope(self):
        st = ExitStack()
        self.stk.append(st)
        self.scope_res.append([])
        try:
            yield
            self.barrier()
            self.flush()
        finally:
            self.stk.pop()
            st.close()
            for r in self.scope_res.pop():
                if r.dsem is not None:
                    self.sem_pool.append(r.dsem)
                    r.dsem = None

    def finish(self):
        self.barrier()
        self.flush()
        self.st.close()
        return self.nc

    def mm(self, out, lhsT, rhs, start, stop, r, w):
        return self.op("pe", lambda e: e.matmul(out, lhsT=lhsT, rhs=rhs, start=start, stop=stop), r, w)

    def tr(self, out, in_, ident, r, w):
        return self.op("pe", lambda e: e.transpose(out, in_, ident), r, w)

    def act(self, out, in_, func, r, w, bias=0.0, scale=1.0, accum_out=None):
        if accum_out is None:
            return self.op("act", lambda e: e.activation(out=out, in_=in_, func=func, bias=bias, scale=scale), r, w)
        return self.op("act", lambda e: e.activation(out=out, in_=in_, func=func, bias=bias, scale=scale, accum_out=accum_out), r, w)

    def tt(self, eng, out, a, b, op, r, w):
        return self.op(eng, lambda e: e.tensor_tensor(out=out, in0=a, in1=b, op=op), r, w)

    def ts(self, eng, out, a, s1, op0, r, w, s2=None, op1=None):
        if op1 is None:
            return self.op(eng, lambda e: e.tensor_scalar(out=out, in0=a, scalar1=s1, scalar2=None, op0=op0), r, w)
        return self.op(eng, lambda e: e.tensor_scalar(out=out, in0=a, scalar1=s1, scalar2=s2, op0=op0, op1=op1), r, w)

    def stt(self, out, in0, scalar, in1, op0, op1, r, w):
        return self.op("dve", lambda e: e.scalar_tensor_tensor(out=out, in0=in0, scalar=scalar, in1=in1, op0=op0, op1=op1), r, w)

    def cp(self, eng, out, in_, r, w):
        if eng == "act":
            return self.op("act", lambda e: e.copy(out=out, in_=in_), r, w)
        return self.op(eng, lambda e: e.tensor_copy(out=out, in_=in_), r, w)

    def memset(self, eng, out, val, w):
        return self.op(eng, lambda e: e.memset(out, val), (), w)


def make_ident(p, dt=BF16, name="ident"):
    idf = p.sbuf(name + "f", [128, 128], F32); r = p.res(name)
    p.memset("dve", idf[:], 0.0, w=[r])
    p.op("pool", lambda e: e.affine_select(out=idf[:], in_=idf[:], pattern=[[-1, 128]], compare_op=ALU.not_equal,
                                           fill=1.0, base=0, channel_multiplier=1), r=[r], w=[r])
    if dt == F32:
        return idf, r
    idb = p.sbuf(name + "b", [128, 128], dt)
    p.cp("dve", idb[:], idf[:], r=[r], w=[r])
    return idb, r


def conv_stage(p, jobs, R, W):
    L = R * W
    PAD = W + 1
    CW = min(512, L)
    with p.scope():
        ident, idr = make_ident(p)
        stg = p.sbuf("stg", [128, L], F32); stgr = p.res("stg")
        outt = p.sbuf("outt", [128, L], F32); outr = p.res("outt")
        nver = 3 if R > 1 else 1
        P = [p.sbuf("P%d" % i, [128, L + 2 * PAD], BF16) for i in range(nver)]
        Pr = [p.res("P%d" % i) for i in range(nver)]
        for i in range(nver):
            p.memset("pool", P[i][:, 0:PAD], 0.0, w=[Pr[i]])
            p.memset("pool", P[i][:, PAD + L:], 0.0, w=[Pr[i]])
        wsb = p.sbuf("wsb", [128, 9], F32); bsb = p.sbuf("bsb", [128, 1], F32); wr = p.res("wsb")
        D = p.sbuf("D", [128, 9, 128], BF16); Dr = p.res("D")
        ps = [p.psum("ps%d" % i, [128, 512]) for i in range(4)]; psr = [p.res("ps%d" % i) for i in range(4)]
        k = 0
        for (src, srcr, w_ap, b_ap, c0, silu, dst, dstr) in jobs:
            p.dma("sp", stg[:], src, r=[srcr], w=[stgr])
            p.dma("pool", wsb[:], w_ap.rearrange("a b c -> c (a b)")[c0:c0 + 128, :], w=[wr], slow=True)
            p.dma("pool", bsb[:], b_ap.rearrange("(a b) -> a b", b=1)[c0:c0 + 128, :], w=[wr])
            p.cp("act", P[0][:, PAD:PAD + L], stg[:], r=[stgr], w=[Pr[0]])
            if nver == 3:
                p.cp("pool", P[1][:, PAD:PAD + L], P[0][:, PAD:PAD + L], r=[Pr[0]], w=[Pr[1]])
                p.cp("dve", P[2][:, PAD:PAD + L], P[0][:, PAD:PAD + L], r=[Pr[0]], w=[Pr[2]])
                v1 = P[1][:, PAD:PAD + L].rearrange("p (r w) -> p r w", w=W)
                v2 = P[2][:, PAD:PAD + L].rearrange("p (r w) -> p r w", w=W)
                p.memset("pool", v1[:, :, W - 1:W], 0.0, w=[Pr[1]])
                p.memset("dve", v2[:, :, 0:1], 0.0, w=[Pr[2]])
            for tap in range(9):
                p.ts("dve", D[:, tap, :], ident[:], wsb[:, tap:tap + 1], ALU.mult, r=[idr, wr], w=[Dr])
            taps = [(dy, dx) for dy in (-1, 0, 1) for dx in (-1, 0, 1) if (R > 1 or dy == 0)]
            for c in range(L // CW):
                h = k % 4; k += 1
                for ti, (dy, dx) in enumerate(taps):
                    ver = 0 if nver == 1 else (1 if dx == -1 else (2 if dx == 1 else 0))
                    o = PAD + c * CW + W * dy + dx
                    p.mm(ps[h][:, 0:CW], D[:, (dy + 1) * 3 + dx + 1, :], P[ver][:, o:o + CW], ti == 0, ti == len(taps) - 1,
                         r=[Dr, Pr[ver]], w=[psr[h]])
                p.act(outt[:, c * CW:(c + 1) * CW], ps[h][:, 0:CW], AF.Silu if silu else AF.Identity,
                      r=[psr[h], wr], w=[outr], bias=bsb[:, 0:1])
            p.dma("sp", dst, outt[:], r=[outr], w=[dstr])


HY_BANDS = 16


def hfilt_consts(L):
    t = np.arange(L, dtype=np.float64) / L
    bands = np.linspace(1e-4, HY_BANDS - 1, HY_BANDS)
    ang = 2 * np.pi * t[:, None] * bands[None, :]
    feats = np.concatenate([t[:, None], np.cos(ang), np.sin(ang)], axis=-1)
    return np.ascontiguousarray(feats.T).astype(np.float32), t.astype(np.float32)


def hfilt_stage(p, featsT, trow, w, j, L, dst, dst_res):
    CW = min(512, L)
    NCH = L // CW
    TWO_PI = 2 * np.pi
    with p.scope():
        wr = p.res("w")
        ft = p.sbuf("ft", [33, L], F32); tb = p.sbuf("tb", [128, L], F32)
        w1 = p.sbuf("w1", [33, 64], F32); w2 = p.sbuf("w2", [64, 64], F32); w3 = p.sbuf("w3", [64, 4, 128], F32)
        sc = p.sbuf("sc", [64, 4], F32)
        dc = p.sbuf("dc", [128, 4], F32)
        p.dma("sp", ft[:], featsT, w=[wr]); p.dma("sp", tb[:], trow.partition_broadcast(128), w=[wr])
        p.dma("sp", w1[:], w["f_w1"], w=[wr]); p.dma("sp", w2[:], w["f_w2"], w=[wr])
        p.dma("sp", w3[:], w["f_w3"], w=[wr])
        p.dma("sp", sc[:, 0:1], w["f_b1"].rearrange("(a b) -> a b", b=1), w=[wr])
        p.dma("sp", sc[:, 1:2], w["f_freq"].rearrange("(a b) -> a b", b=1), w=[wr])
        p.dma("sp", sc[:, 2:3], w["f_b2"].rearrange("(a b) -> a b", b=1), w=[wr])
        p.dma("sp", dc[:], w["decay"], w=[wr])
        p.act(dc[:], dc[:], AF.Abs, r=[wr], w=[wr])
        p.ts("dve", dc[:], dc[:], -1.0, ALU.mult, r=[wr], w=[wr])
        h1f = p.sbuf("h1", [128, L], F32); h1 = h1f[0:64]; h2 = p.sbuf("h2", [64, L], F32)
        h1r, h2r = p.res("h1"), p.res("h2")
        ps = [p.psum("ps%d" % i, [128, 512]) for i in range(2)]; psr = [p.res("ps%d" % i) for i in range(2)]
        tmp = p.sbuf("tmp", [64, 512], F32); tmpr = p.res("tmp")
        tm1 = p.sbuf("tm1", [64, 512], F32); tm1r = p.res("tm1"); tm2 = p.sbuf("tm2", [64, 512], F32); tm2r = p.res("tm2")
        k = 0
        for (src, srcr, lw, bcol, dsth, dstr) in ((ft[:], wr, w1, 0, h1, h1r), (h1, h1r, w2, 2, h2[:], h2r)):
            for c in range(NCH):
                cs = slice(c * CW, (c + 1) * CW)
                h = k % 2; k += 1
                p.mm(ps[h][0:64, 0:CW], lw[:], src[:, cs], True, True, r=[wr, srcr], w=[psr[h]])
                p.ts("dve", tmp[:, 0:CW], ps[h][0:64, 0:CW], sc[:, bcol:bcol + 1], ALU.add, r=[psr[h], wr], w=[tmpr],
                     s2=sc[:, 1:2], op1=ALU.mult)
                tv = tmp[:, 0:CW]
                p.ts("dve", tm1[:, 0:CW], tv, float(np.pi), ALU.is_gt, r=[tmpr], w=[tm1r], s2=-TWO_PI, op1=ALU.mult)
                p.ts("pool", tm2[:, 0:CW], tv, float(-np.pi), ALU.is_lt, r=[tmpr], w=[tm2r], s2=TWO_PI, op1=ALU.mult)
                p.tt("dve", tv, tv, tm1[:, 0:CW], ALU.add, r=[tmpr, tm1r], w=[tmpr])
                p.tt("dve", tv, tv, tm2[:, 0:CW], ALU.add, r=[tmpr, tm2r], w=[tmpr])
                p.ts("dve", tv, tv, float(np.pi), ALU.min, r=[tmpr], w=[tmpr], s2=float(-np.pi), op1=ALU.max)
                p.act(dsth[:, cs], tmp[:, 0:CW], AF.Sin, r=[tmpr], w=[dstr])
        F = [p.sbuf("F%d" % i, [128, L], F32) for i in range(2)]; Fr = [p.res("F%d" % i) for i in range(2)]
        wnd = p.sbuf("wnd", [128, 512], F32); wndr = p.res("wnd")
        ss = p.sbuf("ss", [128, 4], F32); ssr = p.res("ss")
        junk = h1f; junkr = h1r
        for o in range(2):
            for d in range(2):
                od = o * 2 + d
                for c in range(NCH):
                    cs = slice(c * CW, (c + 1) * CW)
                    h = k % 2; k += 1
                    p.mm(ps[h][:, 0:CW], w3[:, od, :], h2[:, cs], True, True, r=[wr, h2r], w=[psr[h]])
                    p.act(wnd[:, 0:CW], tb[:, cs], AF.Exp, r=[wr], w=[wndr], scale=dc[:, od:od + 1])
                    p.tt("dve", F[d][:, cs], ps[h][:, 0:CW], wnd[:, 0:CW], ALU.mult, r=[psr[h], wndr], w=[Fr[d]])
                if d == 1:
                    p.memset("dve", F[d][:, 0:1], 0.0, w=[Fr[d]])
                p.act(junk[:], F[d][:], AF.Square, r=[Fr[d]], w=[junkr], accum_out=ss[:, od:od + 1])
                ssr.last_w = junkr.last_w
            tot = ss[:, 2 * o:2 * o + 1]
            p.tt("dve", tot, tot, ss[:, 2 * o + 1:2 * o + 2], ALU.add, r=[ssr], w=[ssr])
            p.act(tot, tot, AF.Sqrt, r=[ssr], w=[ssr], bias=1e-6)
            p.op("dve", lambda e, t_=tot: e.reciprocal(out=t_, in_=t_), r=[ssr], w=[ssr])
            for d in range(2):
                p.ts("dve", F[d][:], F[d][:], tot, ALU.mult, r=[Fr[d], ssr], w=[Fr[d]])
                p.dma("sp", dst[o * 2 + d], F[d][:], r=[Fr[d]], w=[dst_res])


NFFT = 16384


def hconv_tables():
    a = np.arange(128, dtype=np.float64)
    th128 = 2 * np.pi * np.outer(a, a) / 128.0
    C = np.cos(th128); S = np.sin(th128)
    thN = 2 * np.pi * np.outer(a, a) / NFFT
    bf = lambda x: np.ascontiguousarray(x.astype(np.float32)).astype(ml_dtypes.bfloat16)
    t = {}
    t["hc_w1"] = bf(np.concatenate([C[:64], -S[:64]], axis=1))
    t["hc_sq"] = bf(np.stack([C, -C, S, -S], axis=1))
    t["hc_wab"] = bf(np.stack([np.concatenate([C, S], 1), np.concatenate([-C, -S], 1),
                                np.concatenate([-S, C], 1)], axis=1))
    t["hc_i2"] = bf(np.stack([C[:, :64], S[:, :64], -S[:, :64]], axis=1))
    t["hc_tw"] = np.stack([np.cos(thN), -np.sin(thN)], axis=1).astype(np.float32)
    return t


def hconv_stage(p, tabs, srcs, dst, dst_res, CH, L_in, first, G=32, u_out=None, u_res=None):
    NP = L_in // 128
    with p.scope():
        w1 = p.sbuf("w1", [64, 256], BF16); sq = p.sbuf("sq", [128, 4, 128], BF16)
        wab = p.sbuf("wab", [128, 3, 256], BF16); i2 = p.sbuf("i2", [128, 3, 64], BF16)
        tw = p.sbuf("tw", [128, 2, 128], F32)
        tr = p.res("tabs")
        for sb_, nm in ((w1, "hc_w1"), (sq, "hc_sq"), (wab, "hc_wab"), (i2, "hc_i2"), (tw, "hc_tw")):
            p.dma("sp", sb_[:], tabs[nm], w=[tr])
        PS = p.psum("ps", [128, 4096])
        PH = [PS[:, 0:2048], PS[:, 2048:4096]]
        phr = [p.res("psA"), p.res("psB")]
        stg = [p.sbuf("stg%d" % i, [64, G, 128], F32) for i in range(3)]
        stgr = [p.res("stg%d" % i) for i in range(3)]
        skb = p.sbuf("skb", [64, G], F32); skr = p.res("skb")
        XB = [p.sbuf("X%d" % i, [64, G, 128], BF16) for i in range(2)]; xrB = [p.res("X%d" % i) for i in range(2)]
        NS4 = G // 4
        T = [p.sbuf("T%d" % i, [128, G, 128], BF16) for i in range(4)]
        TrS = [[p.res("T%d_%d" % (i, s)) for s in range(NS4)] for i in range(4)]
        Kr = p.sbuf("Kr", [128, G, 128], F32); Ki = p.sbuf("Ki", [128, G, 128], F32)
        kresS = [p.res("K%d" % s) for s in range(NS4)]
        Q = [p.sbuf("Q%d" % i, [128, G, 128], BF16) for i in range(4)]
        QrS = [[p.res("Q%d_%d" % (i, s)) for s in range(NS4)] for i in range(4)]
        OUT2 = p.sbuf("OUT2", [64, G, 128], F32); outr = p.res("OUT2")
        Trb = tw[:, 0:1, :].to_broadcast([128, 4, 128])
        Tib = tw[:, 1:2, :].to_broadcast([128, 4, 128])
        PF = [PS[:, 0:1024], PS[:, 1024:2048]]; pfr = [p.res("pfA"), p.res("pfB")]
        PX = [PS[:, 2048:3072], PS[:, 3072:4096]]; pxr = [p.res("pxA"), p.res("pxB")]
        cnt = {"f": 0, "x": 0, "i": 0}

        def t2d(ap, c0):
            return ap[c0:c0 + G, :].rearrange("g (a b) -> a g b", b=128)

        def load_u(c0, which, xi):
            X = XB[xi]; xr = xrB[xi]
            if NP < 64:
                p.memset("pool", X[:], 0.0, w=[xr])
            if which != "u" or first:
                ap, rs = srcs["v" if which == "u" else which]
                p.dma("sp" if xi else "pool", stg[xi][0:NP], t2d(ap, c0), r=[rs], w=[stgr[xi]])
                p.cp("act", X[0:NP], stg[xi][0:NP], r=[stgr[xi]], w=[xr])
            else:
                for i, nm in enumerate(("v", "c1", "x1")):
                    ap, rs = srcs[nm]
                    p.dma("sp" if i % 2 else "pool", stg[i][0:NP], t2d(ap, c0), r=[rs], w=[stgr[i]])
                ap, rs = srcs["skip"]
                p.dma("sp", skb[0:NP], ap[c0:c0 + G].partition_broadcast(NP), r=[rs], w=[skr])
                p.tt("pool", stg[0][0:NP], stg[0][0:NP], skb[0:NP, :, None].to_broadcast([NP, G, 128]), ALU.mult,
                     r=[stgr[0], skr], w=[stgr[0]])
                p.tt("pool", stg[0][0:NP], stg[0][0:NP], stg[1][0:NP], ALU.add, r=[stgr[0], stgr[1]], w=[stgr[0]])
                p.tt("pool", stg[0][0:NP], stg[0][0:NP], stg[2][0:NP], ALU.mult, r=[stgr[0], stgr[2]], w=[stgr[0]])
                p.cp("act", X[0:NP], stg[0][0:NP], r=[stgr[0]], w=[xr])
                if u_out is not None:
                    p.dma("pool", t2d(u_out, c0), stg[0][0:NP], r=[stgr[0]], w=[u_res])

        Cm, Cn, Sm, Sn = (sq[:, k, :] for k in range(4))

        def f1_tw(s4, xi):
            X = XB[xi]; xr = xrB[xi]
            h = cnt["f"] % 2; cnt["f"] += 1
            pv = PF[h].rearrange("p (c k) -> p c k", k=256)
            for c in range(4):
                p.mm(pv[:, c, :], X[:, s4 * 4 + c, :], w1[:], True, True, r=[xr, tr], w=[pfr[h]])
            Ar = pv[:, :, 0:128]; Ai = pv[:, :, 128:256]
            gs = slice(s4 * 4, s4 * 4 + 4)
            p.tt("dve", T[0][:, gs, :], Ar, Trb, ALU.mult, r=[pfr[h], tr], w=[TrS[0][s4]])
            p.tt("dve", T[1][:, gs, :], Ai, Tib, ALU.mult, r=[pfr[h], tr], w=[TrS[1][s4]])
            p.tt("dve", T[2][:, gs, :], Ar, Tib, ALU.mult, r=[pfr[h], tr], w=[TrS[2][s4]])
            p.tt("dve", T[3][:, gs, :], Ai, Trb, ALU.mult, r=[pfr[h], tr], w=[TrS[3][s4]])

        def f2_post(s4, which):
            h = cnt["x"] % 2; cnt["x"] += 1
            gs = slice(s4 * 4, s4 * 4 + 4)
            xr_ps = PX[h][:, 0:512]; xi_ps = PX[h][:, 512:1024]
            rr = [tr] + [TrS[k][s4] for k in range(4)]
            p.mm(xr_ps, Cm, T[0][:, gs, :], True, False, r=rr, w=[pxr[h]])
            p.mm(xr_ps, Cn, T[1][:, gs, :], False, False, r=rr, w=[pxr[h]])
            p.mm(xr_ps, Sm, T[2][:, gs, :], False, False, r=rr, w=[pxr[h]])
            p.mm(xr_ps, Sm, T[3][:, gs, :], False, True, r=rr, w=[pxr[h]])
            p.mm(xi_ps, Cm, T[2][:, gs, :], True, False, r=rr, w=[pxr[h]])
            p.mm(xi_ps, Cm, T[3][:, gs, :], False, False, r=rr, w=[pxr[h]])
            p.mm(xi_ps, Sn, T[0][:, gs, :], False, False, r=rr, w=[pxr[h]])
            p.mm(xi_ps, Sm, T[1][:, gs, :], False, True, r=rr, w=[pxr[h]])
            xr3 = xr_ps.rearrange("p (c k) -> p c k", k=128); xi3 = xi_ps.rearrange("p (c k) -> p c k", k=128)
            kres = kresS[s4]
            if which == "kf":
                p.cp("act", Kr[:, gs, :], xr3, r=[pxr[h]], w=[kres])
                p.cp("act", Ki[:, gs, :], xi3, r=[pxr[h]], w=[kres])
            elif which == "kg":
                p.tt("dve", Kr[:, gs, :], xr3, Kr[:, gs, :], ALU.add, r=[pxr[h], kres], w=[kres])
                p.stt(Ki[:, gs, :], xi3, -1.0, Ki[:, gs, :], ALU.mult, ALU.add, r=[pxr[h], kres], w=[kres])
            else:
                p.tt("dve", Q[0][:, gs, :], xr3, Kr[:, gs, :], ALU.mult, r=[pxr[h], kres], w=[QrS[0][s4]])
                p.tt("dve", Q[1][:, gs, :], xi3, Ki[:, gs, :], ALU.mult, r=[pxr[h], kres], w=[QrS[1][s4]])
                p.tt("dve", Q[2][:, gs, :], xr3, Ki[:, gs, :], ALU.mult, r=[pxr[h], kres], w=[QrS[2][s4]])
                p.tt("dve", Q[3][:, gs, :], xi3, Kr[:, gs, :], ALU.mult, r=[pxr[h], kres], w=[QrS[3][s4]])

        def fwd(which, xi):
            for s4 in range(NS4 + 1):
                if s4 < NS4:
                    f1_tw(s4, xi)
                if s4 >= 1:
                    f2_post(s4 - 1, which)

        def inv(c0):
            Wa, Wan, Wb = (wab[:, k, :] for k in range(3))
            for s4 in range(NS4 + 1):
                if s4 >= 1:
                    i2_step(s4 - 1, c0)
                if s4 == NS4:
                    break
                h = cnt["f"] % 2; cnt["f"] += 1
                pv = PF[h].rearrange("p (c k) -> p c k", k=256)
                rr = [tr] + [QrS[k][s4] for k in range(4)]
                for c in range(4):
                    g = s4 * 4 + c
                    p.mm(pv[:, c, :], Q[0][:, g, :], Wa, True, False, r=rr, w=[pfr[h]])
                    p.mm(pv[:, c, :], Q[1][:, g, :], Wan, False, False, r=rr, w=[pfr[h]])
                    p.mm(pv[:, c, :], Q[2][:, g, :], Wb, False, False, r=rr, w=[pfr[h]])
                    p.mm(pv[:, c, :], Q[3][:, g, :], Wb, False, True, r=rr, w=[pfr[h]])
                Br = pv[:, :, 0:128]; Bi = pv[:, :, 128:256]
                gs = slice(s4 * 4, s4 * 4 + 4)
                p.tt("dve", T[0][:, gs, :], Br, Trb, ALU.mult, r=[pfr[h], tr], w=[TrS[0][s4]])
                p.tt("dve", T[1][:, gs, :], Bi, Tib, ALU.mult, r=[pfr[h], tr], w=[TrS[1][s4]])
                p.tt("dve", T[2][:, gs, :], Br, Tib, ALU.mult, r=[pfr[h], tr], w=[TrS[2][s4]])
                p.tt("dve", T[3][:, gs, :], Bi, Trb, ALU.mult, r=[pfr[h], tr], w=[TrS[3][s4]])
            p.dma("sp", t2d(dst, c0), OUT2[0:NP], r=[outr], w=[dst_res])

        CI, SI, SIn = (i2[:, k, :] for k in range(3))

        def i2_step(s4, c0):
            h = cnt["x"] % 2; cnt["x"] += 1
            gs = slice(s4 * 4, s4 * 4 + 4)
            o_ps = PX[h][0:NP, 0:512]
            rr = [tr] + [TrS[k][s4] for k in range(4)]
            p.mm(o_ps, CI[:, 0:NP], T[0][:, gs, :], True, False, r=rr, w=[pxr[h]])
            p.mm(o_ps, CI[:, 0:NP], T[1][:, gs, :], False, False, r=rr, w=[pxr[h]])
            p.mm(o_ps, SI[:, 0:NP], T[2][:, gs, :], False, False, r=rr, w=[pxr[h]])
            p.mm(o_ps, SIn[:, 0:NP], T[3][:, gs, :], False, True, r=rr, w=[pxr[h]])
            p.act(OUT2[0:NP, gs, :], o_ps.rearrange("p (c k) -> p c k", k=128), AF.Copy, r=[pxr[h]], w=[outr], scale=1.0 / NFFT)

        items = [(c0, which) for c0 in range(0, CH, G) for which in ("kf", "kg", "u")]
        load_u(items[0][0], items[0][1], 0)
        for i, (c0, which) in enumerate(items):
            if i + 1 < len(items):
                load_u(items[i + 1][0], items[i + 1][1], (i + 1) % 2)
            fwd(which, i % 2)
            if which == "u":
                inv(c0)


LF = 8192


def fnet_tables():
    a = np.arange(128, dtype=np.float64)
    th = 2 * np.pi * np.outer(a, a) / 128.0
    C = np.cos(th); S = np.sin(th)
    b = np.arange(64, dtype=np.float64)
    th64 = 2 * np.pi * np.outer(b, b) / 64.0
    C64 = np.cos(th64); S64 = np.sin(th64)
    thL = 2 * np.pi * np.outer(b, a) / LF
    bf = lambda x: np.ascontiguousarray(x.astype(np.float32)).astype(ml_dtypes.bfloat16)
    return {"fn_w": bf(np.stack([np.concatenate([C, -S], 1), np.concatenate([S, C], 1)], axis=1)),
            "fn_64": bf(np.stack([C64, -C64, S64], axis=1)),
            "fn_tw": np.stack([np.cos(thL), -np.sin(thL)], axis=1).astype(np.float32)}


def fnet_stage(p, tabs, src, src_res, dst, dst_res, L_in):
    scale = 1.0 / np.sqrt(L_in * 128.0)
    f1_list = list(range(128)) if L_in == LF else [0, 32, 64, 96]
    with p.scope():
        wt = p.sbuf("wt", [128, 2, 256], BF16); t64 = p.sbuf("t64", [64, 3, 64], BF16); tw = p.sbuf("tw", [64, 2, 128], F32)
        tr = p.res("tabs")
        p.dma("sp", wt[:], tabs["fn_w"], w=[tr]); p.dma("sp", t64[:], tabs["fn_64"], w=[tr]); p.dma("sp", tw[:], tabs["fn_tw"], w=[tr])
        stg = p.sbuf("stg", [128, LF], F32); stgr = p.res("stg")
        ub = p.sbuf("ub", [128, LF], BF16); ubr = p.res("ub")
        V = p.sbuf("V", [128, 64, 256], BF16); Vr = p.res("V")
        T = [p.sbuf("T%d" % i, [64, 64, 128], BF16) for i in range(4)]; Tr_ = [p.res("T%d" % i) for i in range(4)]
        PS = p.psum("ps", [128, 4096]); PH = [PS[:, 0:2048], PS[:, 2048:4096]]; phr = [p.res("psA"), p.res("psB")]
        if L_in < LF:
            p.memset("pool", stg[:], 0.0, w=[stgr])
        p.dma("sp", stg[:, 0:L_in], src, r=[src_res], w=[stgr])
        p.cp("act", ub[:], stg[:], r=[stgr], w=[ubr])
        tog = 0
        uv = ub[:].rearrange("p (a b) -> p b a", b=64)
        for s8 in range(8):
            h = tog; tog ^= 1
            pv = PH[h].rearrange("p (c k) -> p c k", k=256)
            for c in range(8):
                n2 = s8 * 8 + c
                p.mm(pv[:, c, :], uv[:, n2, :], wt[:, 0, :], True, True, r=[ubr, tr], w=[phr[h]])
            p.cp("act" if s8 % 2 else "dve", V[:, s8 * 8:(s8 + 1) * 8, :], pv, r=[phr[h]], w=[Vr])
        Trb = tw[:, 0:1, :].to_broadcast([64, 8, 128]); Tib = tw[:, 1:2, :].to_broadcast([64, 8, 128])
        OUT = stg; outr = stgr
        for half in range(2):
            for s8 in range(8):
                h = tog; tog ^= 1
                pv = PH[h][0:64, :].rearrange("p (c k) -> p c k", k=256)
                for c in range(8):
                    ch = half * 64 + s8 * 8 + c
                    p.mm(pv[:, c, :], V[:, :, ch], wt[:, 0, :], True, False, r=[Vr, tr], w=[phr[h]])
                    p.mm(pv[:, c, :], V[:, :, 128 + ch], wt[:, 1, :], False, True, r=[Vr, tr], w=[phr[h]])
                Ar = pv[:, :, 0:128]; Ai = pv[:, :, 128:256]
                gs = slice(s8 * 8, s8 * 8 + 8)
                p.tt("dve", T[0][:, gs, :], Ar, Trb, ALU.mult, r=[phr[h], tr], w=[Tr_[0]])
                p.tt("dve", T[1][:, gs, :], Ai, Tib, ALU.mult, r=[phr[h], tr], w=[Tr_[1]])
                p.tt("dve", T[2][:, gs, :], Ar, Tib, ALU.mult, r=[phr[h], tr], w=[Tr_[2]])
                p.tt("dve", T[3][:, gs, :], Ai, Trb, ALU.mult, r=[phr[h], tr], w=[Tr_[3]])
            C64, C64n, S64 = (t64[:, k, :] for k in range(3))
            rr = [tr] + Tr_
            for s32 in range(0, len(f1_list), 32):
                fl = f1_list[s32:s32 + 32]
                h = tog; tog ^= 1
                pv = PH[h][0:64, :].rearrange("p (m k) -> p m k", k=64)
                for i, f1 in enumerate(fl):
                    p.mm(pv[:, i, :], T[0][:, :, f1], C64, True, False, r=rr, w=[phr[h]])
                    p.mm(pv[:, i, :], T[1][:, :, f1], C64n, False, False, r=rr, w=[phr[h]])
                    p.mm(pv[:, i, :], T[2][:, :, f1], S64, False, False, r=rr, w=[phr[h]])
                    p.mm(pv[:, i, :], T[3][:, :, f1], S64, False, True, r=rr, w=[phr[h]])
                ov = OUT[half * 64:half * 64 + 64, :].rearrange("p (a b) -> p a b", b=128)
                step = fl[1] - fl[0] if len(fl) > 1 else 1
                osl = ov[:, :, fl[0]:fl[-1] + 1:step].rearrange("p a b -> p b a")
                p.act(osl, pv[:, 0:len(fl), :], AF.Copy, r=[phr[h]], w=[outr], scale=float(scale))
        if L_in == LF:
            p.dma("sp", dst, OUT[:], r=[outr], w=[dst_res])
        else:
            ov = OUT[:].rearrange("p (a b) -> p a b", b=128)[:, :, 0:128:32]
            p.dma("sp", dst.rearrange("p (a b) -> p a b", b=4), ov, r=[outr], w=[dst_res], slow=True)


STOP = 99


NCK = 66
NEG = -30000.0


def mlstm_stage(p, src, dst_lat, dst_ctx, dst_res, normw_ap, j):
    TOK = NCK * 128
    with p.scope():
        ident, idr = make_ident(p, F32, "idf")
        identb = p.sbuf("identb", [128, 128], BF16)
        p.cp("dve", identb[:], ident[:], r=[idr], w=[idr])
        cr = p.res("consts")
        ones = p.sbuf("ones", [128, 128], F32); onesb = p.sbuf("onesb", [128, 128], BF16)
        p.memset("dve", ones[:], 1.0, w=[cr]); p.memset("dve", onesb[:], 1.0, w=[cr])
        tri = p.sbuf("tri", [128, 2, 128], F32); mneg = p.sbuf("mneg", [128, 2, 128], F32)
        p.memset("dve", tri[:], 1.0, w=[cr]); p.memset("dve", mneg[:], 0.0, w=[cr])
        p.op("pool", lambda e: e.affine_select(out=tri[:, 0, :], in_=tri[:, 0, :], pattern=[[1, 128]], compare_op=ALU.is_ge, fill=0.0, base=0, channel_multiplier=-1), r=[cr], w=[cr])
        p.op("pool", lambda e: e.affine_select(out=tri[:, 1, :], in_=tri[:, 1, :], pattern=[[-1, 128]], compare_op=ALU.is_ge, fill=0.0, base=0, channel_multiplier=1), r=[cr], w=[cr])
        p.op("pool", lambda e: e.affine_select(out=mneg[:, 0, :], in_=mneg[:, 0, :], pattern=[[1, 128]], compare_op=ALU.is_ge, fill=NEG, base=0, channel_multiplier=-1), r=[cr], w=[cr])
        p.op("pool", lambda e: e.affine_select(out=mneg[:, 1, :], in_=mneg[:, 1, :], pattern=[[-1, 128]], compare_op=ALU.is_ge, fill=NEG, base=0, channel_multiplier=1), r=[cr], w=[cr])

        stg = p.sbuf("stg", [128, TOK], F32); stgr = p.res("stg")
        qT = p.sbuf("qT", [128, TOK], BF16); kT = p.sbuf("kT", [128, TOK], BF16); vT = p.sbuf("vT", [128, TOK], BF16)
        qr, kr, vr = p.res("qT"), p.res("kT"), p.res("vT")
        for nm, t_, r_, sc in (("qT", qT, qr, 128.0 ** -0.5), ("kT", kT, kr, 1.0), ("vT", vT, vr, 1.0)):
            lat, ctx, rs = src[nm]
            p.dma("sp", stg[:, 0:256], ctx, r=[rs], w=[stgr]); p.dma("sp", stg[:, 256:], lat, r=[rs], w=[stgr])
            p.act(t_[:], stg[:], AF.Copy, r=[stgr], w=[r_], scale=sc)
        ktok = p.sbuf("ktok", [128, NCK, 128], BF16); vtok = p.sbuf("vtok", [128, NCK, 128], BF16)
        ktr, vtr = p.res("ktok"), p.res("vtok")
        ptb = [p.psum("ptb%d" % i, [128, 8, 128], BF16) for i in range(2)]; ptbr = [p.res("ptb%d" % i) for i in range(2)]
        kk = 0
        for srcT, sr, dstt, dr in ((kT, kr, ktok, ktr), (vT, vr, vtok, vtr)):
            for c8 in range(0, NCK, 8):
                n = min(8, NCK - c8)
                h = kk % 2; kk += 1
                for c in range(n):
                    p.tr(ptb[h][:, c, :], srcT[:, (c8 + c) * 128:(c8 + c + 1) * 128], identb[:], r=[sr, idr], w=[ptbr[h]])
                p.cp("act" if h else "dve", dstt[:, c8:c8 + n, :], ptb[h][:, 0:n, :], r=[ptbr[h]], w=[dr])
        if STOP == 1:
            p.dma('sp', dst_ctx, stg[:, 0:256], r=[stgr], w=[dst_res]); return
        gst = p.sbuf("gst", [NCK, 4, 128], F32); gr = p.res("gst")
        lat, ctx, rs = src["g"]
        p.dma("sp", gst[0:2], ctx.rearrange("g (c s) -> c g s", s=128), r=[rs], w=[gr])
        p.dma("sp", gst[2:NCK], lat.rearrange("g (c s) -> c g s", s=128), r=[rs], w=[gr])
        G = p.sbuf("G", [128, 4, NCK], F32); Gr = p.res("G")
        pg = p.psum("pg", [128, 4, 128]); pgr = p.res("pg")
        for g in range(4):
            p.tr(pg[:, g, 0:NCK], gst[:, g, :], ident[0:NCK, 0:NCK], r=[gr, idr], w=[pgr])
        p.cp("dve", G[:], pg[:, :, 0:NCK], r=[pgr], w=[Gr])
        LFt = p.sbuf("LF", [128, 2, NCK], F32); lfr = p.res("LF")
        for d in range(2):
            p.act(LFt[:, d, :], G[:, 2 * d + 1, :], AF.Exp, r=[Gr], w=[lfr], scale=-1.0)
        p.act(LFt[:], LFt[:], AF.Ln, r=[lfr], w=[lfr], bias=1.0)
        p.ts("dve", LFt[:], LFt[:], -1.0, ALU.mult, r=[lfr], w=[lfr])
        CUM = p.sbuf("CUM", [128, 2, NCK], F32); IB = p.sbuf("IB", [128, 2, NCK], F32)
        SC = p.sbuf("SC", [128, 2, NCK], F32); ET = p.sbuf("ET", [128, 2, NCK], F32)
        pr = p.res("pre")
        pc = p.psum("pc", [128, 4, 128]); pcr = p.res("pc")
        for d in range(2):
            p.mm(pc[:, d, 0:NCK], tri[:, d, :], LFt[:, d, :], True, True, r=[cr, lfr], w=[pcr])
            p.mm(pc[:, 2 + d, 0:NCK], ones[:], LFt[:, d, :], True, True, r=[cr, lfr], w=[pcr])
        p.cp("dve", CUM[:], pc[:, 0:2, 0:NCK], r=[pcr], w=[pr])
        for d in range(2):
            p.tt("dve", IB[:, d, :], G[:, 2 * d, :], CUM[:, d, :], ALU.subtract, r=[Gr, pr], w=[pr])
            p.tt("dve", SC[:, d, :], pc[:, 2 + d, 0:NCK], IB[:, d, :], ALU.add, r=[pcr, pr], w=[pr])
        p.act(SC[:], SC[:], AF.Exp, r=[pr], w=[pr])
        p.act(ET[:], pc[:, 2:4, 0:NCK], AF.Exp, r=[pcr], w=[pr])

        if STOP == 2:
            p.dma('sp', dst_ctx[:, 0:66], SC[:, 0, :], r=[pr], w=[dst_res]); return
        H = p.sbuf("H", [128, TOK], F32); Hr = p.res("H")
        p.memset("pool", H[:], 0.0, w=[Hr])
        PA = [p.psum("pa%d" % d, [128, 4, 128]) for d in range(2)]
        PB = [p.psum("pb%d" % d, [128, 512]) for d in range(2)]
        rP1 = [p.res() for _ in range(2)]; rP2 = [p.res() for _ in range(2)]; rKQ = [p.res() for _ in range(2)]
        rNUM = [p.res() for _ in range(2)]; rDEN = [p.res() for _ in range(2)]; rST = [p.res() for _ in range(2)]
        def T(nm, shape, dt):
            return [p.sbuf("%s%d" % (nm, d), shape, dt) for d in range(2)], [p.res("%s%d" % (nm, d)) for d in range(2)]
        DG, DGr = T("DG", [128, 128], F32); WT, WTr = T("WT", [128, 128], F32); EC, ECr = T("EC", [128, 128], F32)
        QE, QEr = T("QE", [128, 128], BF16); STt, STr = T("ST", [128, 128], BF16)
        DD, DDr = T("DD", [128, 128], F32); HT, HTr = T("HT", [128, 128], F32)
        VS, VSr = T("VS", [128, 132], BF16); CF, CFr = T("CF", [128, 132], F32)
        CB, CBr = T("CB", [128, 128], BF16); NB, NBr = T("NB", [128, 128], BF16)
        for d in range(2):
            p.memset("pool", CF[d][:], 0.0, w=[CFr[d]]); p.memset("pool", CB[d][:], 0.0, w=[CBr[d]]); p.memset("pool", NB[d][:], 0.0, w=[NBr[d]])
        LE = 'dve'
        order = [list(range(NCK)), [1, 0] + list(range(NCK - 1, 1, -1))]
        for step in range(NCK):
            for d in range(2):
                c = order[d][step]
                cs = slice(c * 128, (c + 1) * 128)
                P1 = PA[d][:, 0, :]; P2 = PA[d][:, 1, :]; KQ = PA[d][:, 2, :]
                NUM = PB[d][:, 0:128]; DEN = PB[d][:, 128:256]; STP = PB[d][:, 256:256 + 129]
                p.ts(LE, DG[d][:], ident[:], CUM[:, d, c:c + 1], ALU.mult, r=[idr, pr], w=[DGr[d]])
                p.mm(P1, ones[:], DG[d][:], True, True, r=[cr, DGr[d]], w=[rP1[d]])
                p.mm(P2, ones[:], DG[d][:], True, False, r=[cr, DGr[d]], w=[rP2[d]])
                p.mm(P2, ident[:], mneg[:, d, :], False, True, r=[cr, idr], w=[rP2[d]])
                p.mm(KQ, kT[:, cs], qT[:, cs], True, True, r=[kr, qr], w=[rKQ[d]])
                p.act(WT[d][:], P2, AF.Exp, r=[rP2[d], pr], w=[WTr[d]], bias=IB[:, d, c:c + 1])
                p.act(EC[d][:], P1, AF.Exp, r=[rP1[d]], w=[ECr[d]])
                p.tt("dve", STt[d][:], KQ, WT[d][:], ALU.mult, r=[rKQ[d], WTr[d]], w=[STr[d]])
                p.tt(LE, QE[d][:], qT[:, cs], EC[d][:], ALU.mult, r=[qr, ECr[d]], w=[QEr[d]])
                p.mm(NUM, vtok[:, c, :], STt[d][:], True, False, r=[vtr, STr[d]], w=[rNUM[d]])
                p.mm(NUM, CB[d][:], QE[d][:], False, True, r=[CBr[d], QEr[d]], w=[rNUM[d]])
                p.mm(DEN, onesb[:], STt[d][:], True, False, r=[cr, STr[d]], w=[rDEN[d]])
                p.mm(DEN, NB[d][:], QE[d][:], False, True, r=[NBr[d], QEr[d]], w=[rDEN[d]])
                p.act(DD[d][:], DEN, AF.Abs, r=[rDEN[d]], w=[DDr[d]])
                p.ts("dve", DD[d][:], DD[d][:], 1.0, ALU.max, r=[DDr[d]], w=[DDr[d]])
                p.op("dve", lambda e, t_=DD[d]: e.reciprocal(out=t_[:], in_=t_[:]), r=[DDr[d]], w=[DDr[d]])
                p.tt("dve", HT[d][:], NUM, DD[d][:], ALU.mult, r=[rNUM[d], DDr[d]], w=[HTr[d]])
                p.tt(LE, H[:, cs], H[:, cs], HT[d][:], ALU.add, r=[Hr, HTr[d]], w=[Hr])
                p.ts(LE, VS[d][:, 0:128], vtok[:, c, :], SC[:, d, c:c + 1], ALU.mult, r=[vtr, pr], w=[VSr[d]])
                p.cp(LE, VS[d][:, 128:129], SC[:, d, c:c + 1], r=[pr], w=[VSr[d]])
                p.mm(STP, ktok[:, c, :], VS[d][:, 0:129], True, True, r=[ktr, VSr[d]], w=[rST[d]])
                p.stt(CF[d][:, 0:129], CF[d][:, 0:129], ET[:, d, c:c + 1], STP, ALU.mult, ALU.add, r=[CFr[d], pr, rST[d]], w=[CFr[d]])
                p.cp("act", CB[d][:], CF[d][:, 0:128], r=[CFr[d]], w=[CBr[d]])
                p.cp(LE, NB[d][:], CF[d][:, 128:129].to_broadcast([128, 128]), r=[CFr[d]], w=[NBr[d]])

        nw = p.sbuf("nw", [128, 1], F32); nwr = p.res("nw")
        p.dma("sp", nw[:], normw_ap.rearrange("(a b) -> a b", b=1), w=[nwr])
        lat, ctx, rs = src["oT"]
        p.dma("sp", stg[:, 0:256], ctx, r=[rs], w=[stgr]); p.dma("sp", stg[:, 256:], lat, r=[rs], w=[stgr])
        sq = p.sbuf("sq", [128, 512], F32); sqr = p.res("sq")
        rs_t = p.sbuf("rs_t", [128, 512], F32); rsr = p.res("rs_t")
        pn = [PB[0], PB[1]]; pnr = [p.res(), p.res()]
        ci = 0
        for t0 in range(0, TOK, 512):
            n = min(512, TOK - t0); ts_ = slice(t0, t0 + n)
            h = ci % 2; ci += 1
            p.act(sq[:, 0:n], H[:, ts_], AF.Square, r=[Hr], w=[sqr])
            p.mm(pn[h][:, 0:n], ones[:], sq[:, 0:n], True, True, r=[cr, sqr, rNUM[h], rDEN[h], rST[h]], w=[pnr[h], rNUM[h], rDEN[h], rST[h]])
            p.act(rs_t[:, 0:n], pn[h][:, 0:n], AF.Sqrt, r=[pnr[h]], w=[rsr], scale=1.0 / 128.0, bias=1e-6)
            p.op("dve", lambda e, o=rs_t[:, 0:n]: e.reciprocal(out=o, in_=o), r=[rsr], w=[rsr])
            p.tt("dve", H[:, ts_], H[:, ts_], rs_t[:, 0:n], ALU.mult, r=[Hr, rsr], w=[Hr])
            p.act(stg[:, ts_], stg[:, ts_], AF.Sigmoid, r=[stgr], w=[stgr])
            p.stt(H[:, ts_], H[:, ts_], nw[:, 0:1], stg[:, ts_], ALU.mult, ALU.mult, r=[Hr, nwr, stgr], w=[Hr])
        p.dma("sp", dst_ctx, H[:, 0:256], r=[Hr], w=[dst_res])
        p.dma("sp", dst_lat, H[:, 256:], r=[Hr], w=[dst_res])


D = 1024
EPS = 1e-6


def ttiles(NL, NC):
    out = [(t0, min(512, NL - t0), False) for t0 in range(0, NL, 512)]
    if NC:
        out.append((NL, NC, True))
    return out


def mod_stage(p, cv_ap, ada_w_ap, ada_b_ap, mod_d, mod_res, NM=12):
    with p.scope():
        cv = p.sbuf("cv", [128, 8, 2], F32); cvr = p.res("cv")
        p.dma("sp", cv[:], cv_ap.rearrange("(k p) n -> p k n", p=128), w=[cvr])
        p.act(cv[:], cv[:], AF.Silu, r=[cvr], w=[cvr])
        ab = p.sbuf("ab", [128, NM], F32); abr = p.res("ab")
        p.dma("sp", ab[:], ada_b_ap.rearrange("(m p) -> p m", p=128), w=[abr], slow=True)
        acc = p.sbuf("acc", [128, NM, 2], F32); accr = p.res("acc")
        for n in range(2):
            p.cp("dve", acc[:, :, n], ab[:], r=[abr], w=[accr])
        wt = [p.sbuf("wt%d" % i, [128, 128 * NM], F32) for i in range(2)]; wtr = [p.res("wt%d" % i) for i in range(2)]
        ps = p.psum("ps", [128, NM, 2]); psr = p.res("ps")
        for k in range(8):
            h = k % 2
            p.dma("sp" if h else "pool", wt[h][:], ada_w_ap[128 * k:128 * k + 128, :], w=[wtr[h]])
            for m in range(NM):
                p.mm(ps[:, m, :], wt[h][:, 128 * m:128 * m + 128], cv[:, k, :], True, True, r=[wtr[h], cvr], w=[psr])
            p.tt("dve", acc[:], acc[:], ps[:], ALU.add, r=[accr, psr], w=[accr])
        p.dma("sp", mod_d, acc[:], r=[accr], w=[mod_res])


def load_mod(p, mod_d, mod_res, nw_ap, sidx, scidx):
    modt = p.sbuf("modt", [128, 48, 2], F32); mr = p.res("modt")
    p.dma("sp", modt[:], mod_d, r=[mod_res], w=[mr], slow=True)
    nw = p.sbuf("nw", [128, 8], F32)
    p.dma("sp", nw[:], nw_ap.rearrange("(k p) -> p k", p=128), w=[mr], slow=True)
    A = p.sbuf("A", [128, 8, 2], F32)
    p.ts("dve", A[:], modt[:, 8 * scidx:8 * scidx + 8, :], 1.0, ALU.add, r=[mr], w=[mr])
    p.tt("dve", A[:], A[:], nw[:, :, None].to_broadcast([128, 8, 2]), ALU.mult, r=[mr], w=[mr])
    return modt, A, mr


def norm_mod_tile(p, xt, xr, n, A, SH, col, mr, ones, onr, ps, psr, sq, sqr, rstd, rsr, hb, hbr, hf=None, hfr=None):
    for k in range(8):
        p.act(sq[:, 0:n], xt[:, k, 0:n], AF.Square, r=[xr], w=[sqr])
        p.mm(ps[:, 0:n], ones[:], sq[:, 0:n], k == 0, k == 7, r=[onr, sqr], w=[psr])
    p.act(rstd[:, 0:n], ps[:, 0:n], AF.Sqrt, r=[psr], w=[rsr], scale=1.0 / D, bias=EPS)
    p.op("dve", lambda e, o=rstd[:, 0:n]: e.reciprocal(out=o, in_=o), r=[rsr], w=[rsr])
    for k in range(8):
        p.tt("dve", sq[:, 0:n], xt[:, k, 0:n], rstd[:, 0:n], ALU.mult, r=[xr, rsr, sqr], w=[sqr])
        if hf is not None:
            p.ts("dve", hf[:, k, 0:n], sq[:, 0:n], A[:, k, col:col + 1], ALU.mult, r=[sqr, mr], w=[hfr],
                 s2=SH[:, k, col:col + 1], op1=ALU.add)
            p.cp("act", hb[:, k, 0:n], hf[:, k, 0:n], r=[hfr], w=[hbr])
        else:
            p.ts("dve", hb[:, k, 0:n], sq[:, 0:n], A[:, k, col:col + 1], ALU.mult, r=[sqr, mr], w=[hbr],
                 s2=SH[:, k, col:col + 1], op1=ALU.add)


def norm1_stage(p, xT_d, x_res, NL, NC, mod_d, mod_res, nw_ap, hT_d, h_res):
    with p.scope():
        modt, A, mr = load_mod(p, mod_d, mod_res, nw_ap, 0, 1)
        SH = modt[:, 0:8, :]
        ones = p.sbuf("ones", [128, 128], F32); onr = p.res("ones"); p.memset("dve", ones[:], 1.0, w=[onr])
        xt = [p.sbuf("xt%d" % i, [128, 8, 512], F32) for i in range(2)]; xr = [p.res() for i in range(2)]
        hb = [p.sbuf("hb%d" % i, [128, 8, 512], BF16) for i in range(2)]; hbr = [p.res() for i in range(2)]
        sq = p.sbuf("sq", [128, 512], F32); sqr = p.res(); rstd = p.sbuf("rstd", [128, 512], F32); rsr = p.res()
        ps = p.psum("ps", [128, 512]); psr = p.res()
        for i, (t0, n, isc) in enumerate(ttiles(NL, NC)):
            h = i % 2
            p.dma("sp", xt[h][:, :, 0:n], xT_d[:, t0:t0 + n].rearrange("(k p) t -> p k t", p=128), r=[x_res], w=[xr[h]])
            norm_mod_tile(p, xt[h], xr[h], n, A, SH, 1 if isc else 0, mr, ones, onr, ps, psr, sq, sqr, rstd, rsr, hb[h], hbr[h])
            p.dma("pool", hT_d[i].rearrange("(k p) t -> p k t", p=128), hb[h][:, :, 0:n], r=[hbr[h]], w=[h_res[i]])


def load_w_bf16(p, w_cols_ap, m, wst, wstr, wb, wbr, eng="act", q="sp"):
    p.dma(q, wst[:, :, 0:m], w_cols_ap.rearrange("(k p) m -> p k m", p=128), w=[wstr])
    p.cp(eng, wb[:, :, 0:m], wst[:, :, 0:m], r=[wstr], w=[wbr])


def inproj_gate_stage(p, hT_d, h_res, NL, NC, w_in_ap, b_in_ap, off, nchunk, gT_d, g_res):
    NT = NL + NC
    GW = 4
    with p.scope():
        hT = p.sbuf("hT", [128, 8, NT], BF16); hr = p.res("hT")
        for i, (t0, n, isc) in enumerate(ttiles(NL, NC)):
            p.dma("sp", hT[:, :, t0:t0 + n], hT_d[i].rearrange("(k p) t -> p k t", p=128), r=[h_res[i]], w=[hr])
        bias = p.sbuf("bias", [128, nchunk], F32); br = p.res("bias")
        p.dma("sp", bias[:], b_in_ap[off:off + 128 * nchunk].rearrange("(m p) -> p m", p=128), w=[br], slow=True)
        wst = [p.sbuf("wst%d" % i, [128, 8, 128 * GW], F32) for i in range(2)]; wstr = [p.res() for i in range(2)]
        wb = [p.sbuf("wb%d" % i, [128, 8, 128 * GW], BF16) for i in range(2)]; wbr = [p.res() for i in range(2)]
        ot = [p.sbuf("ot%d" % i, [128, NT], BF16) for i in range(2)]; otr = [p.res() for i in range(2)]
        ps = [p.psum("ps%d" % i, [128, 512]) for i in range(6)]; psr = [p.res() for i in range(6)]
        kk = 0
        ngrp = nchunk // GW
        def load(g):
            h = g % 2
            c0 = off + 128 * GW * g
            p.dma("sp" if h else "pool", wst[h][:], w_in_ap[:, c0:c0 + 128 * GW].rearrange("(k p) m -> p k m", p=128), w=[wstr[h]])
            p.cp("act", wb[h][:], wst[h][:], r=[wstr[h]], w=[wbr[h]])
        load(0)
        for g in range(ngrp):
            h = g % 2
            if g + 1 < ngrp:
                load(g + 1)
            for mi in range(GW):
                m = g * GW + mi
                o = m % 2
                for (t0, n, isc) in ttiles(NL, NC):
                    b_ = kk % 6; kk += 1
                    for k in range(8):
                        p.mm(ps[b_][:, 0:n], wb[h][:, k, 128 * mi:128 * mi + 128], hT[:, k, t0:t0 + n], k == 0, k == 7, r=[wbr[h], hr], w=[psr[b_]])
                    p.act(ot[o][:, t0:t0 + n], ps[b_][:, 0:n], AF.Sigmoid, r=[psr[b_], br], w=[otr[o]], bias=bias[:, m:m + 1])
                p.dma("sp", gT_d[128 * m:128 * m + 128, :], ot[o][:], r=[otr[o]], w=[g_res])


def inproj_mix_stage(p, hall_d, hall_res, NL, NC, w_in_ap, b_in_ap, col_list, zlat_d, zctx_d, z_res):
    NT = NL + NC
    nchunk = len(col_list)
    with p.scope():
        wst = p.sbuf("wst", [128, 8, 128], F32); wstr = p.res()
        W = p.sbuf("W", [128, nchunk, 8, 128], BF16); Wr = p.res("W")
        bias = p.sbuf("bias", [128, nchunk], F32); br = p.res("bias")
        p.memset("dve", bias[:], 0.0, w=[br])
        p.memset("dve", W[:], 0.0, w=[Wr])
        for i, (c0, m) in enumerate(col_list):
            p.dma("sp", wst[:, :, 0:m], w_in_ap[:, c0:c0 + m].rearrange("(k p) m -> p k m", p=128), w=[wstr], slow=(m < 128))
            p.cp("act" if i % 2 else "dve", W[:, i, :, 0:m], wst[:, :, 0:m], r=[wstr], w=[Wr])
            p.dma("pool", bias[0:m, i:i + 1], b_in_ap[c0:c0 + m].rearrange("(a b) -> a b", b=1), w=[br])
        hT = [p.sbuf("hT%d" % i, [128, 8, 512], BF16) for i in range(2)]; hr = [p.res() for i in range(2)]
        ot = [p.sbuf("ot%d" % i, [128, nchunk, 512], F32) for i in range(2)]; otr = [p.res() for i in range(2)]
        ps = [p.psum("ps%d" % i, [128, 512]) for i in range(4)]; psr = [p.res() for i in range(4)]
        kk = 0; ti = 0
        for r in range(4):
            for i_t, (t0, n, isc) in enumerate(ttiles(NL, NC)):
                h = ti % 2; ti += 1
                p.dma("sp", hT[h][:, :, 0:n], hall_d[i_t][1024 * r:1024 * r + 1024, :].rearrange("(k p) t -> p k t", p=128),
                      r=[hall_res[i_t]], w=[hr[h]])
                for i in range(nchunk):
                    b_ = kk % 4; kk += 1
                    for k in range(8):
                        p.mm(ps[b_][:, 0:n], W[:, i, k, :], hT[h][:, k, 0:n], k == 0, k == 7, r=[Wr, hr[h]], w=[psr[b_]])
                    p.act(ot[h][:, i, 0:n], ps[b_][:, 0:n], AF.Identity, r=[psr[b_], br], w=[otr[h]], bias=bias[:, i:i + 1])
                if isc:
                    dst = zctx_d[:, :, NC * r:NC * r + n]
                else:
                    dst = zlat_d[:, :, NL * r + t0:NL * r + t0 + n]
                p.dma("pool", dst.rearrange("i p t -> p i t"), ot[h][:, :, 0:n], r=[otr[h]], w=[z_res])


def yasm_stage(p, srcs, skip_ap, y_own_d, y_res, LL, LC):
    with p.scope():
        sk = p.sbuf("sk", [128, 1], F32); skr = p.res("sk")
        p.dma("sp", sk[:], skip_ap.rearrange("(a b) -> a b", b=1), w=[skr])
        CW = 2048
        A = [p.sbuf("A%d" % i, [128, CW], F32) for i in range(3)]; Ar = [p.res() for i in range(3)]
        O = [p.sbuf("O%d" % i, [128, CW], BF16) for i in range(2)]; Or = [p.res() for i in range(2)]
        kk = 0
        pieces = [(0, t0, min(CW, LL - t0), t0) for t0 in range(0, LL, CW)] + ([(1, 0, LC, LL)] if LC else [])
        for (which, t0, n, o0) in pieces:
            for i, nm in enumerate(("z", "c2", "x2")):
                ap = srcs[nm][which]
                p.dma("sp", A[i][:, 0:n], ap[:, t0:t0 + n], r=[srcs[nm][2]], w=[Ar[i]])
            p.stt(A[0][:, 0:n], A[0][:, 0:n], sk[:, 0:1], A[1][:, 0:n], ALU.mult, ALU.add, r=[Ar[0], Ar[1], skr], w=[Ar[0]])
            h = kk % 2; kk += 1
            p.tt("dve", O[h][:, 0:n], A[0][:, 0:n], A[2][:, 0:n], ALU.mult, r=[Ar[0], Ar[2]], w=[Or[h]])
            for (ap_, rs_, a0, an) in y_own_d(0, o0, n):
                p.dma("pool", ap_, O[h][:, a0:a0 + an], r=[Or[h]], w=[rs_])
            for bi, nm in ((1, "fn"), (2, "ml")):
                ap = srcs[nm][which]
                p.dma("sp", A[bi][:, 0:n], ap[:, t0:t0 + n], r=[srcs[nm][2]], w=[Ar[bi]])
                h = kk % 2; kk += 1
                p.cp("act", O[h][:, 0:n], A[bi][:, 0:n], r=[Ar[bi]], w=[Or[h]])
                for (ap_, rs_, a0, an) in y_own_d(bi, o0, n):
                    p.dma("pool", ap_, O[h][:, a0:a0 + an], r=[Or[h]], w=[rs_])


def merge_stage(p, y_all_d, y_res, gT_d, g_res, oh_ap, xT_d, x_res, NL, NC, mod_d, mod_res, wbr_ap, wout_ap, do_ctx):
    LL = 4 * NL
    with p.scope():
        modt = p.sbuf("modt", [128, 48, 2], F32); mr = p.res("modt")
        p.dma("sp", modt[:], mod_d, r=[mod_res], w=[mr], slow=True)
        oh = p.sbuf("oh", [128, 4], F32); ohr = p.res("oh")
        p.dma("sp", oh[:], oh_ap, w=[ohr])
        wst = p.sbuf("wst", [128, 8, 1024], F32); wstr = p.res()
        WB = p.sbuf("WB", [128, 12, 1024], BF16); WO = p.sbuf("WO", [128, 8, 1024], BF16); Wr = p.res("W")
        for br in range(3):
            p.dma("sp", wst[:, 0:4, :], wbr_ap[br].rearrange("(r p) d -> p r d", p=128), w=[wstr])
            p.cp("act" if br % 2 else "dve", WB[:, 4 * br:4 * br + 4, :], wst[:, 0:4, :], r=[wstr], w=[Wr])
        p.dma("sp", wst[:], wout_ap.rearrange("(k p) d -> p k d", p=128), w=[wstr])
        p.cp("act", WO[:], wst[:], r=[wstr], w=[Wr])
        Yc = [p.sbuf("Yc%d" % i, [128, 12, 512], BF16) for i in range(2)]; Ycr = [p.res() for i in range(2)]
        Y = p.sbuf("Y", [128, 12, 512], BF16); Yr = p.res("Y")
        Gt = p.sbuf("Gt", [128, 24, 512], BF16); Gr = p.res("G")
        xt = p.sbuf("xt", [128, 8, 512], F32); xr = p.res("xt")
        mg = p.sbuf("mg", [128, 8, 512], BF16); mgr = p.res("mg")
        t1 = p.sbuf("t1", [128, 512], F32); t1r = p.res(); t2 = p.sbuf("t2", [128, 512], F32); t2r = p.res()
        ps = [p.psum("ps%d" % i, [128, 512]) for i in range(6)]; psr = [p.res() for i in range(6)]
        kk = 0
        tiles = ttiles(NL, NC if do_ctx else 0)
        for (t0, n, isc) in tiles:
            col = 1 if isc else 0
            for jj in range(4):
                c0 = (LL + NC * jj) if isc else (NL * jj + t0)
                h = jj % 2
                for br in range(3):
                    ap_, rs_ = y_all_d(br, c0, n)
                    p.dma("sp" if br % 2 else "pool", Yc[h][:, 4 * br:4 * br + 4, 0:n],
                          ap_.rearrange("(q p) t -> p q t", p=128), r=[rs_], w=[Ycr[h]])
                if jj == 0:
                    p.ts("dve", Y[:, :, 0:n], Yc[h][:, :, 0:n], oh[:, 0:1], ALU.mult, r=[Ycr[h], ohr], w=[Yr])
                else:
                    p.stt(Y[:, :, 0:n], Yc[h][:, :, 0:n], oh[:, jj:jj + 1], Y[:, :, 0:n], ALU.mult, ALU.add, r=[Ycr[h], ohr, Yr], w=[Yr])
            p.dma("sp", Gt[:, :, 0:n], gT_d[:, t0:t0 + n].rearrange("(q p) t -> p q t", p=128), r=[g_res], w=[Gr])
            p.dma("pool", xt[:, :, 0:n], xT_d[:, t0:t0 + n].rearrange("(k p) t -> p k t", p=128), r=[x_res], w=[xr])
            for m in range(8):
                pb = []
                for br in range(3):
                    b_ = kk % 6; kk += 1; pb.append(b_)
                    for r in range(4):
                        p.mm(ps[b_][:, 0:n], WB[:, 4 * br + r, 128 * m:128 * m + 128], Y[:, 4 * br + r, 0:n], r == 0, r == 3,
                             r=[Wr, Yr], w=[psr[b_]])
                p.tt("dve", t1[:, 0:n], ps[pb[0]][:, 0:n], Gt[:, m, 0:n], ALU.mult, r=[psr[pb[0]], Gr], w=[t1r])
                p.tt("dve", t2[:, 0:n], ps[pb[1]][:, 0:n], Gt[:, 8 + m, 0:n], ALU.mult, r=[psr[pb[1]], Gr], w=[t2r])
                p.tt("dve", t1[:, 0:n], t1[:, 0:n], t2[:, 0:n], ALU.add, r=[t1r, t2r], w=[t1r])
                p.tt("dve", t2[:, 0:n], ps[pb[2]][:, 0:n], Gt[:, 16 + m, 0:n], ALU.mult, r=[psr[pb[2]], Gr], w=[t2r])
                p.tt("dve", mg[:, m, 0:n], t1[:, 0:n], t2[:, 0:n], ALU.add, r=[t1r, t2r], w=[mgr])
            for m in range(8):
                b_ = kk % 6; kk += 1
                for k in range(8):
                    p.mm(ps[b_][:, 0:n], WO[:, k, 128 * m:128 * m + 128], mg[:, k, 0:n], k == 0, k == 7, r=[Wr, mgr], w=[psr[b_]])
                p.stt(xt[:, m, 0:n], ps[b_][:, 0:n], modt[:, 16 + m, col:col + 1], xt[:, m, 0:n], ALU.mult, ALU.add,
                      r=[psr[b_], mr, xr], w=[xr])
            p.dma("sp", xT_d[:, t0:t0 + n].rearrange("(k p) t -> p k t", p=128), xt[:, :, 0:n], r=[xr], w=[x_res])


def moe_stage(p, xT_d, x_res, NL, NC, mod_d, mod_res, nw_ap, wr_ap, br_ap, wg_ap, wu_ap, wd_ap, do_ctx,
              final_nw_ap=None, out_d=None, out_res=None):
    NCX = NC if do_ctx else 0
    NT = NL + NCX
    tiles = ttiles(NL, NCX)
    nsub = (NT + 127) // 128
    with p.scope():
        modt, A, mr = load_mod(p, mod_d, mod_res, nw_ap, 3, 4)
        SH = modt[:, 24:32, :]
        ones = p.sbuf("ones", [128, 128], F32); onr = p.res("ones"); p.memset("dve", ones[:], 1.0, w=[onr])
        identf, idr = make_ident(p, F32, "idf")
        sq = p.sbuf("sq", [128, 512], F32); sqr = p.res(); rstd = p.sbuf("rstd", [128, 512], F32); rsr = p.res()
        xt = p.sbuf("xt", [128, 8, 512], F32); xr = p.res("xt")
        hf = p.sbuf("hf", [128, 8, 512], F32); hfr = p.res("hf")
        H2 = p.sbuf("H2", [128, 8, NT], BF16); h2r = p.res("H2")
        WR = p.sbuf("WR", [128, 8, 20], F32); wrr = p.res("WR")
        p.dma("sp", WR[:], wr_ap.rearrange("(k p) n -> p k n", p=128), w=[wrr], slow=True)
        BR = p.sbuf("BR", [128, 20], F32)
        p.dma("sp", BR[:], br_ap.partition_broadcast(128), w=[wrr])
        CWt = p.sbuf("CWt", [128, nsub, 16], F32); cwr = p.res("CW")
        ps = [p.psum("ps%d" % i, [128, 512]) for i in range(6)]; psr = [p.res() for i in range(6)]
        pr_ = p.psum("pr", [128, 32]); prr = p.res("pr")
        def st(nm, w):
            return p.sbuf(nm, [128, w], F32)
        L_ = st("L", 20); gm = st("gm", 1); ge = st("ge", 4); gs = st("gs", 1); gmask = st("gmask", 4)
        tmp16 = st("tmp16", 16); eg = st("eg", 4); m1 = st("m1", 1); mk1 = st("mk1", 4); eg2 = st("eg2", 4); m2 = st("m2", 1)
        mk2 = st("mk2", 4); w1 = st("w1", 1); w2 = st("w2", 1); cwe = st("cwe", 4)
        rr = p.res("route")
        for (t0, n, isc) in tiles:
            col = 1 if isc else 0
            p.dma("sp", xt[:, :, 0:n], xT_d[:, t0:t0 + n].rearrange("(k p) t -> p k t", p=128), r=[x_res], w=[xr])
            norm_mod_tile(p, xt, xr, n, A, SH, col, mr, ones, onr, ps[0], psr[0], sq, sqr, rstd, rsr,
                          H2[:, :, t0:t0 + n], h2r, hf=hf, hfr=hfr)
            for s0 in range(0, n, 128):
                sn = min(128, n - s0); si = (t0 + s0) // 128
                for k in range(8):
                    p.mm(pr_[0:sn, 0:20], hf[:, k, s0:s0 + sn], WR[:, k, :], k == 0, k == 7, r=[hfr, wrr], w=[prr])
                R = [rr]
                p.tt("dve", L_[0:sn], pr_[0:sn, 0:20], BR[0:sn], ALU.add, r=[prr, wrr, rr], w=R)
                p.op("dve", lambda e, o=gm[0:sn], i=L_[0:sn, 0:4]: e.tensor_reduce(out=o, in_=i, axis=AX.X, op=ALU.max), r=R, w=R)
                p.ts("dve", gmask[0:sn], L_[0:sn, 0:4], gm[0:sn, 0:1], ALU.is_equal, r=R, w=R)
                p.ts("dve", gm[0:sn], gm[0:sn], -1.0, ALU.mult, r=R, w=R)
                p.act(ge[0:sn], L_[0:sn, 0:4], AF.Exp, r=R, w=R, bias=gm[0:sn, 0:1])
                p.op("dve", lambda e, o=gs[0:sn], i=ge[0:sn]: e.tensor_reduce(out=o, in_=i, axis=AX.X, op=ALU.add), r=R, w=R)
                p.op("dve", lambda e, o=gs[0:sn]: e.reciprocal(out=o, in_=o), r=R, w=R)
                p.tt("dve", tmp16[0:sn].rearrange("p (g e) -> p g e", e=4), L_[0:sn, 4:20].rearrange("p (g e) -> p g e", e=4),
                     gmask[0:sn, :, None].to_broadcast([sn, 4, 4]), ALU.mult, r=R, w=R)
                p.op("dve", lambda e, o=eg[0:sn], i=tmp16[0:sn].rearrange("p (g e) -> p e g", e=4): e.tensor_reduce(out=o, in_=i, axis=AX.X, op=ALU.add), r=R, w=R)
                p.op("dve", lambda e, o=m1[0:sn], i=eg[0:sn]: e.tensor_reduce(out=o, in_=i, axis=AX.X, op=ALU.max), r=R, w=R)
                p.ts("dve", mk1[0:sn], eg[0:sn], m1[0:sn, 0:1], ALU.is_equal, r=R, w=R)
                p.stt(eg2[0:sn], mk1[0:sn], -1e30, eg[0:sn], ALU.mult, ALU.add, r=R, w=R)
                p.op("dve", lambda e, o=m2[0:sn], i=eg2[0:sn]: e.tensor_reduce(out=o, in_=i, axis=AX.X, op=ALU.max), r=R, w=R)
                p.ts("dve", mk2[0:sn], eg2[0:sn], m2[0:sn, 0:1], ALU.is_equal, r=R, w=R)
                p.tt("dve", w1[0:sn], m2[0:sn], m1[0:sn], ALU.subtract, r=R, w=R)
                p.act(w1[0:sn], w1[0:sn], AF.Exp, r=R, w=R)
                p.ts("dve", w1[0:sn], w1[0:sn], 1.0, ALU.add, r=R, w=R)
                p.op("dve", lambda e, o=w1[0:sn]: e.reciprocal(out=o, in_=o), r=R, w=R)
                p.ts("dve", w2[0:sn], w1[0:sn], -1.0, ALU.mult, r=R, w=R, s2=1.0, op1=ALU.add)
                p.tt("dve", w1[0:sn], w1[0:sn], gs[0:sn], ALU.mult, r=R, w=R)
                p.tt("dve", w2[0:sn], w2[0:sn], gs[0:sn], ALU.mult, r=R, w=R)
                p.ts("dve", cwe[0:sn], mk1[0:sn], w1[0:sn, 0:1], ALU.mult, r=R, w=R)
                p.stt(cwe[0:sn], mk2[0:sn], w2[0:sn, 0:1], cwe[0:sn], ALU.mult, ALU.add, r=R, w=R)
                p.cp("dve", tmp16[0:sn].rearrange("p (g e) -> p g e", e=4), cwe[0:sn, None, :].to_broadcast([sn, 4, 4]), r=R, w=R)
                p.tt("dve", CWt[0:sn, si, :].rearrange("p (g e) -> p g e", e=4), tmp16[0:sn].rearrange("p (g e) -> p g e", e=4),
                     gmask[0:sn, :, None].to_broadcast([sn, 4, 4]), ALU.mult, r=R, w=[cwr, rr])
        ACC = p.sbuf("ACC", [128, nsub, 1024], F32); accr = p.res("ACC")
        p.memset("dve", ACC[:], 0.0, w=[accr])
        wst = [p.sbuf("wst%d" % i, [128, 8, 256], F32) for i in range(2)]; wstr = [p.res() for i in range(2)]
        WG = [p.sbuf("WG%d" % i, [128, 8, 256], BF16) for i in range(2)]; WU = [p.sbuf("WU%d" % i, [128, 8, 256], BF16) for i in range(2)]
        WD = [p.sbuf("WD%d" % i, [128, 2, 1024], BF16) for i in range(2)]
        wer = [p.res() for i in range(2)]
        SG = p.sbuf("SG", [128, 512], BF16); sgr = p.res()
        AA = p.sbuf("AA", [128, 2, 512], BF16); aar = p.res()
        kk = 0
        for e_ in range(16):
            h = e_ % 2
            p.dma("sp", wst[0][:], wg_ap[e_].rearrange("(k p) m -> p k m", p=128), w=[wstr[0]])
            p.cp("act", WG[h][:], wst[0][:], r=[wstr[0]], w=[wer[h]])
            p.dma("pool", wst[1][:], wu_ap[e_].rearrange("(k p) m -> p k m", p=128), w=[wstr[1]])
            p.cp("act", WU[h][:], wst[1][:], r=[wstr[1]], w=[wer[h]])
            p.dma("sp", wst[0][:].rearrange("p a b -> p (a b)").rearrange("p (c d) -> p c d", c=2), wd_ap[e_].rearrange("(c p) d -> p c d", p=128), w=[wstr[0]])
            p.cp("act", WD[h][:], wst[0][:].rearrange("p a b -> p (a b)").rearrange("p (c d) -> p c d", c=2), r=[wstr[0]], w=[wer[h]])
            for (t0, n, isc) in tiles:
                for hc in range(2):
                    bg = kk % 6; kk += 1; bu = kk % 6; kk += 1
                    for k in range(8):
                        p.mm(ps[bg][:, 0:n], WG[h][:, k, 128 * hc:128 * hc + 128], H2[:, k, t0:t0 + n], k == 0, k == 7, r=[wer[h], h2r], w=[psr[bg]])
                    for k in range(8):
                        p.mm(ps[bu][:, 0:n], WU[h][:, k, 128 * hc:128 * hc + 128], H2[:, k, t0:t0 + n], k == 0, k == 7, r=[wer[h], h2r], w=[psr[bu]])
                    p.act(SG[:, 0:n], ps[bg][:, 0:n], AF.Silu, r=[psr[bg]], w=[sgr])
                    p.tt("dve", AA[:, hc, 0:n], ps[bu][:, 0:n], SG[:, 0:n], ALU.mult, r=[psr[bu], sgr], w=[aar])
                for s0 in range(0, n, 128):
                    sn = min(128, n - s0); si = (t0 + s0) // 128
                    for dh in range(2):
                        b_ = kk % 6; kk += 1
                        for hc in range(2):
                            p.mm(ps[b_][0:sn, :], AA[:, hc, s0:s0 + sn], WD[h][:, hc, 512 * dh:512 * dh + 512], hc == 0, hc == 1,
                                 r=[aar, wer[h]], w=[psr[b_]])
                        p.stt(ACC[0:sn, si, 512 * dh:512 * dh + 512], ps[b_][0:sn, :], CWt[0:sn, si, e_:e_ + 1],
                              ACC[0:sn, si, 512 * dh:512 * dh + 512], ALU.mult, ALU.add, r=[psr[b_], cwr, accr], w=[accr])
        if final_nw_ap is not None:
            fw_ = p.sbuf("fw", [128, 8], F32); fwr = p.res()
            p.dma("sp", fw_[:], final_nw_ap.rearrange("(k p) -> p k", p=128), w=[fwr], slow=True)
        for (t0, n, isc) in tiles:
            col = 1 if isc else 0
            p.dma("sp", xt[:, :, 0:n], xT_d[:, t0:t0 + n].rearrange("(k p) t -> p k t", p=128), r=[x_res], w=[xr])
            for m in range(8):
                b_ = kk % 6; kk += 1
                for s0 in range(0, n, 128):
                    sn = min(128, n - s0); si = (t0 + s0) // 128
                    p.tr(ps[b_][:, s0:s0 + sn], ACC[0:sn, si, 128 * m:128 * m + 128], identf[0:sn, 0:sn], r=[accr, idr], w=[psr[b_]])
                p.stt(xt[:, m, 0:n], ps[b_][:, 0:n], modt[:, 40 + m, col:col + 1], xt[:, m, 0:n], ALU.mult, ALU.add,
                      r=[psr[b_], mr, xr], w=[xr])
            if final_nw_ap is None:
                p.dma("pool", xT_d[:, t0:t0 + n].rearrange("(k p) t -> p k t", p=128), xt[:, :, 0:n], r=[xr], w=[x_res])
            elif not isc:
                for k in range(8):
                    p.act(sq[:, 0:n], xt[:, k, 0:n], AF.Square, r=[xr], w=[sqr])
                    p.mm(ps[0][:, 0:n], ones[:], sq[:, 0:n], k == 0, k == 7, r=[onr, sqr], w=[psr[0]])
                p.act(rstd[:, 0:n], ps[0][:, 0:n], AF.Sqrt, r=[psr[0]], w=[rsr], scale=1.0 / D, bias=EPS)
                p.op("dve", lambda e, o=rstd[:, 0:n]: e.reciprocal(out=o, in_=o), r=[rsr], w=[rsr])
                for k in range(8):
                    p.stt(hf[:, k, 0:n], xt[:, k, 0:n], fw_[:, k:k + 1], rstd[:, 0:n], ALU.mult, ALU.mult, r=[xr, fwr, rsr], w=[hfr])
                p.dma("pool", out_d[:, t0:t0 + n].rearrange("(k p) t -> p k t", p=128), hf[:, :, 0:n], r=[hfr], w=[out_res])

NLAT, NCTX = 2048, 64
LLAT, LCTX = 8192, 256
GROUPS = [[0, 1, 2, 3], [4, 5, 6, 7]]
DEPTH = 2
OFF_FN, OFF_ML, OFF_MLG, OFF_GATE = 1536, 2048, 4096, 4112


def build_program(const_np):
    p = Prog()
    I = {}

    def inp(name, shape, dt=F32):
        I[name] = p.dram(name, shape, dt, "ExternalInput")
        return I[name]

    inp("xT0", [1024, NLAT]); inp("cT0", [1024, NCTX]); inp("cv", [1024, 2]); inp("oh", [128, 4]); inp("norm_f", [1024])
    for k, v in const_np.items():
        inp(k, v.shape, F32 if v.dtype == np.float32 else BF16)
    for l in range(DEPTH):
        L = "_%d" % l
        inp("ada_w" + L, [1024, 1536]); inp("ada_b" + L, [1536]); inp("n1w" + L, [1024]); inp("n2w" + L, [1024])
        inp("w_gate_in" + L, [1024, 3072]); inp("b_gate_in" + L, [3072]); inp("w_mix" + L, [1024, 1028]); inp("b_mix" + L, [1028])
        inp("hy_cw" + L, [3, 3, 384]); inp("hy_cb" + L, [384]); inp("ml_cw" + L, [3, 3, 256]); inp("ml_cb" + L, [256])
        inp("f_w1" + L, [33, 64]); inp("f_b1" + L, [64]); inp("f_freq" + L, [64]); inp("f_w2" + L, [64, 64]); inp("f_b2" + L, [64])
        inp("f_w3" + L, [64, 4, 128]); inp("decay" + L, [128, 4]); inp("skip" + L, [2, 128]); inp("mlnw" + L, [128])
        inp("wbr" + L, [3, 512, 1024]); inp("wout" + L, [1024, 1024])
        inp("wr" + L, [1024, 20]); inp("br" + L, [20]); inp("wg" + L, [16, 1024, 256]); inp("wu" + L, [16, 1024, 256]); inp("wd" + L, [16, 256, 1024])
    outT = p.dram("outT", [1024, NLAT], F32, "ExternalOutput"); out_res = p.res("outT")

    NT = NLAT + NCTX
    S = {}

    def scr(name, shape, dt=F32):
        S[name] = (p.dram("s_" + name, shape, dt), p.res("s_" + name))
        return S[name]

    scr("mod_own", [128, 12, 2]); scr("mod_all", [4 * 128, 24]); scr("mod_full", [128, 48, 2])
    scr("xT", [1024, NT]); scr("gT", [3072, NT], BF16)
    TT = ttiles(NLAT, NCTX)
    hown = [scr("hown%d" % i, [1024, n], BF16) for i, (t0, n, isc) in enumerate(TT)]
    hall = [scr("hall%d" % i, [4096, n], BF16) for i, (t0, n, isc) in enumerate(TT)]
    YCH = [(0, 3072), (3072, 3072), (6144, 2304)]
    yown = [[scr("yown%d_%d" % (br, ck), [128, w_], BF16) for ck, (c0_, w_) in enumerate(YCH)] for br in range(3)]
    yall = [[scr("yall%d_%d" % (br, ck), [512, w_], BF16) for ck, (c0_, w_) in enumerate(YCH)] for br in range(3)]

    def y_own_fn(br, col0, n):
        out = []
        for ck, (c0_, w_) in enumerate(YCH):
            lo = max(col0, c0_); hi = min(col0 + n, c0_ + w_)
            if hi > lo:
                out.append((yown[br][ck][0][:, lo - c0_:hi - c0_], yown[br][ck][1], lo - col0, hi - lo))
        return out

    def y_all_fn(br, col0, n):
        for ck, (c0_, w_) in enumerate(YCH):
            if c0_ <= col0 and col0 + n <= c0_ + w_:
                return yall[br][ck][0][:, col0 - c0_:col0 - c0_ + n], yall[br][ck][1]
        raise AssertionError("y tile straddles chunks")
    scr("zlat", [9, 128, LLAT]); scr("zctx", [9, 128, LCTX]); scr("cvl", [5, 128, LLAT]); scr("cvc", [5, 128, LCTX])
    scr("fl", [4, 128, LLAT]); scr("fc", [4, 128, LCTX])
    for nm in ("c1", "c2", "zz", "fn", "ml"):
        scr(nm + "l", [128, LLAT]); scr(nm + "c", [128, LCTX])

    tabs = {k: I[k] for k in const_np}
    xT, xres = S["xT"]
    with p.scope():
        t = p.sbuf("t", [128, 8, NT], F32); tr = p.res()
        p.dma("sp", t[:, :, 0:NLAT], I["xT0"].rearrange("(k p) t -> p k t", p=128), w=[tr])
        p.dma("pool", t[:, :, NLAT:NT], I["cT0"].rearrange("(k p) t -> p k t", p=128), w=[tr])
        p.dma("sp", xT.rearrange("(k p) t -> p k t", p=128), t[:], r=[tr], w=[xres])

    mod_view = S["mod_all"][0].rearrange("(r p) (m n) -> p r m n", p=128, n=2)

    for l in range(DEPTH):
        L = "_%d" % l
        last = (l == DEPTH - 1)
        W = lambda nm: I[nm + L]
        p.label = 'mod_stage'; mod_stage(p, I["cv"], W("ada_w"), W("ada_b"), S["mod_own"][0], S["mod_own"][1], NM=12)
        p.coll("AllGather", S["mod_all"][0], S["mod_own"][0].rearrange("p m n -> p (m n)"), GROUPS, r=[S["mod_own"][1]], w=[S["mod_all"][1]])
        with p.scope():
            mt = p.sbuf("mt", [128, 48, 2], F32); mtr = p.res()
            p.dma("sp", mt[:].rearrange("p (r m) n -> p r m n", r=4), mod_view, r=[S["mod_all"][1]], w=[mtr], slow=True)
            p.dma("sp", S["mod_full"][0], mt[:], r=[mtr], w=[S["mod_full"][1]])
        modv, modr = S["mod_full"]
        p.label = 'norm1_stage'; norm1_stage(p, xT, xres, NLAT, NCTX, modv, modr, W("n1w"), [h_[0] for h_ in hown], [h_[1] for h_ in hown])
        for i in range(len(TT)):
            p.coll("AllGather", hall[i][0], hown[i][0], GROUPS, r=[hown[i][1]], w=[hall[i][1]])
        p.label = 'inproj_gate_stage'; inproj_gate_stage(p, [h_[0] for h_ in hown], [h_[1] for h_ in hown], NLAT, NCTX, W("w_gate_in"), W("b_gate_in"), 0, 24, S["gT"][0], S["gT"][1])
        cols = [(128 * i, 128) for i in range(8)] + [(1024, 4)]
        p.label = 'inproj_mix_stage'; inproj_mix_stage(p, [h_[0] for h_ in hall], [h_[1] for h_ in hall], NLAT, NCTX, W("w_mix"), W("b_mix"), cols, S["zlat"][0], S["zctx"][0], S["zlat"][1])
        zl, zc, zr = S["zlat"][0], S["zctx"][0], S["zlat"][1]
        cvl, cvc = S["cvl"][0], S["cvc"][0]
        cvr = S["cvl"][1]
        jl = [(zl[0], zr, W("hy_cw"), W("hy_cb"), 0, False, cvl[0], cvr), (zl[1], zr, W("hy_cw"), W("hy_cb"), 128, False, cvl[1], cvr),
              (zl[2], zr, W("hy_cw"), W("hy_cb"), 256, False, cvl[2], cvr), (zl[4], zr, W("ml_cw"), W("ml_cb"), 0, True, cvl[3], cvr),
              (zl[5], zr, W("ml_cw"), W("ml_cb"), 128, True, cvl[4], cvr)]
        p.label = 'conv_stage'; conv_stage(p, jl, 128, 64)
        jc = [(zc[4], zr, W("ml_cw"), W("ml_cb"), 0, True, cvc[3], cvr), (zc[5], zr, W("ml_cw"), W("ml_cb"), 128, True, cvc[4], cvr)]
        if not last:
            jc += [(zc[0], zr, W("hy_cw"), W("hy_cb"), 0, False, cvc[0], cvr), (zc[1], zr, W("hy_cw"), W("hy_cb"), 128, False, cvc[1], cvr),
                   (zc[2], zr, W("hy_cw"), W("hy_cb"), 256, False, cvc[2], cvr)]
        p.label = 'conv_stage'; conv_stage(p, jc, 1, 256)
        fw = {k: W(k) for k in ("f_w1", "f_b1", "f_freq", "f_w2", "f_b2", "f_w3", "decay")}
        variants = [("l", LLAT, cvl, "featsT_l", "trow_l")] + ([] if last else [("c", LCTX, cvc, "featsT_c", "trow_c")])
        for (sfx, Lx, cvx, fnm, tnm) in variants:
            filt, fr = S["f" + sfx]
            p.label = 'hfilt_stage'; hfilt_stage(p, I[fnm], I[tnm], fw, 0, Lx, filt, fr)
            src1 = {"v": (cvx[0], cvr), "kf": (filt[0], fr), "kg": (filt[1], fr)}
            p.label = 'hconv_stage'; hconv_stage(p, tabs, src1, S["c1" + sfx][0], S["c1" + sfx][1], 128, Lx, True)
            src2 = {"v": (cvx[0], cvr), "kf": (filt[2], fr), "kg": (filt[3], fr), "x1": (cvx[1], cvr),
                    "c1": S["c1" + sfx], "skip": (W("skip")[0], p.res())}
            p.label = 'hconv_stage'; hconv_stage(p, tabs, src2, S["c2" + sfx][0], S["c2" + sfx][1], 128, Lx, False, u_out=S["zz" + sfx][0], u_res=S["zz" + sfx][1])
            zsrc = zl if sfx == "l" else zc
            p.label = 'fnet_stage'; fnet_stage(p, tabs, zsrc[3], zr, S["fn" + sfx][0], S["fn" + sfx][1], Lx)
        msrc = {"qT": (cvl[3], cvc[3], cvr), "kT": (cvl[4], cvc[4], cvr), "vT": (zl[6], zc[6], zr), "oT": (zl[7], zc[7], zr),
                "g": (zl[8][0:4], zc[8][0:4], zr)}
        p.label = 'mlstm_stage'; mlstm_stage(p, msrc, S["mll"][0], S["mlc"][0], S["mll"][1], W("mlnw"), 0)
        S["mlc"] = (S["mlc"][0], S["mll"][1])
        ys = {"x2": (cvl[2], cvc[2], cvr)}
        for nm, key in (("c2", "c2"), ("z", "zz"), ("fn", "fn"), ("ml", "ml")):
            ys[nm] = (S[key + "l"][0], S[key + "c"][0], S[key + "l"][1])
        p.label = 'yasm_stage'; yasm_stage(p, ys, W("skip")[1], y_own_fn, None, LLAT, LCTX if not last else 0)
        for br in range(3):
            for ck in range(len(YCH)):
                p.coll("AllGather", yall[br][ck][0], yown[br][ck][0], GROUPS, r=[yown[br][ck][1]], w=[yall[br][ck][1]])
        p.label = 'merge_stage'; merge_stage(p, y_all_fn, None, S["gT"][0], S["gT"][1], I["oh"], xT, xres, NLAT, NCTX, modv, modr,
                    W("wbr"), W("wout"), not last)
        if last:
            p.label = 'moe_stage'; moe_stage(p, xT, xres, NLAT, NCTX, modv, modr, W("n2w"), W("wr"), W("br"), W("wg"), W("wu"), W("wd"), False,
                      I["norm_f"], outT, out_res)
        else:
            p.label = 'moe_stage'; moe_stage(p, xT, xres, NLAT, NCTX, modv, modr, W("n2w"), W("wr"), W("br"), W("wg"), W("wu"), W("wd"), True)
    return p.finish(), p


_CACHE = {}


def _consts():
    c = {}
    c.update(hconv_tables()); c.update(fnet_tables())
    fl, tl = hfilt_consts(LLAT); fc, tc = hfilt_consts(LCTX)
    c["featsT_l"] = fl; c["trow_l"] = tl; c["featsT_c"] = fc; c["trow_c"] = tc
    return c


def kernel(x, c, ctx, c_ctx, ada_w, ada_b, norm1_w, norm2_w, w_in, b_in,
           hy_conv_w, hy_conv_b, hy_f_w1, hy_f_b1, hy_f_w2, hy_f_b2, hy_f_w3, hy_f_freq,
           hy_decay, hy_skip, ml_conv_w, ml_conv_b, ml_norm_w, w_branch, w_out,
           moe_rg_w, moe_rg_b, moe_re_w, moe_re_b, moe_w_gate, moe_w_up, moe_w_down, norm_f_w):
    f32 = lambda a: np.ascontiguousarray(np.asarray(a, dtype=np.float32))
    x, c, ctx, c_ctx = f32(x), f32(c), f32(ctx), f32(c_ctx)
    if "nc" not in _CACHE:
        _CACHE["const"] = _consts()
        _CACHE["nc"] = build_program(_CACHE["const"])[0]
    const = _CACHE["const"]
    nc = _CACHE["nc"]
    in_maps = []
    for core in range(8):
        b, j = core // 4, core % 4
        m = dict(const)
        m["xT0"] = f32(x[b, NLAT * j:NLAT * (j + 1), :].T)
        m["cT0"] = f32(ctx[b, NCTX * j:NCTX * (j + 1), :].T)
        m["cv"] = f32(np.stack([c[b], c_ctx], axis=1))
        oh = np.zeros((128, 4), np.float32); oh[:, j] = 1.0
        m["oh"] = oh
        m["norm_f"] = f32(norm_f_w)
        sl = slice(128 * j, 128 * j + 128)
        for l in range(DEPTH):
            L = "_%d" % l
            m["ada_w" + L] = f32(ada_w[l][:, 1536 * j:1536 * (j + 1)]); m["ada_b" + L] = f32(ada_b[l][1536 * j:1536 * (j + 1)])
            m["n1w" + L] = f32(norm1_w[l]); m["n2w" + L] = f32(norm2_w[l])
            m["w_gate_in" + L] = f32(w_in[l][:, OFF_GATE:]); m["b_gate_in" + L] = f32(b_in[l][OFF_GATE:])
            mixcols = np.concatenate([np.arange(128 * j, 128 * j + 128) + o for o in
                                      (0, 512, 1024, OFF_FN, OFF_ML, OFF_ML + 512, OFF_ML + 1024, OFF_ML + 1536)]
                                     + [np.array([OFF_MLG + j, OFF_MLG + 4 + j, OFF_MLG + 8 + j, OFF_MLG + 12 + j])])
            m["w_mix" + L] = f32(w_in[l][:, mixcols]); m["b_mix" + L] = f32(b_in[l][mixcols])
            hyc = np.concatenate([np.arange(128 * j, 128 * j + 128) + o for o in (0, 512, 1024)])
            m["hy_cw" + L] = f32(hy_conv_w[l][:, :, hyc]); m["hy_cb" + L] = f32(hy_conv_b[l][hyc])
            mlc = np.concatenate([np.arange(128 * j, 128 * j + 128) + o for o in (0, 512)])
            m["ml_cw" + L] = f32(ml_conv_w[l][:, :, mlc]); m["ml_cb" + L] = f32(ml_conv_b[l][mlc])
            m["f_w1" + L] = f32(hy_f_w1[l]); m["f_b1" + L] = f32(hy_f_b1[l]); m["f_freq" + L] = f32(hy_f_freq[l])
            m["f_w2" + L] = f32(hy_f_w2[l]); m["f_b2" + L] = f32(hy_f_b2[l])
            m["f_w3" + L] = f32(np.asarray(hy_f_w3[l]).reshape(64, 4, 512)[:, :, sl])
            m["decay" + L] = f32(np.asarray(hy_decay[l]).reshape(4, 512)[:, sl].T)
            m["skip" + L] = f32(hy_skip[l][:, sl]); m["mlnw" + L] = f32(ml_norm_w[l][sl])
            m["wbr" + L] = f32(w_branch[l]); m["wout" + L] = f32(w_out[l])
            m["wr" + L] = f32(np.concatenate([moe_rg_w[l], moe_re_w[l]], axis=1)); m["br" + L] = f32(np.concatenate([moe_rg_b[l], moe_re_b[l]]))
            m["wg" + L] = f32(moe_w_gate[l]); m["wu" + L] = f32(moe_w_up[l]); m["wd" + L] = f32(moe_w_down[l])
        in_maps.append(m)
    res = run_bass_kernel_spmd(nc, in_maps, core_ids=list(range(8)))
    out = np.empty((2, LLAT, 1024), np.float32)
    for core in range(8):
        b, j = core // 4, core % 4
        out[b, NLAT * j:NLAT * (j + 1), :] = np.asarray(res.results[core]["outT"], dtype=np.float32).T
    return out
```

```python
import os
import numpy as np
import ml_dtypes

from contextlib import ExitStack, contextmanager
import concourse.bass as bass
import concourse.mybir as mybir
from concourse.bass_utils import run_bass_kernel_spmd

F32 = mybir.dt.float32
BF16 = mybir.dt.bfloat16
ALU = mybir.AluOpType
AF = mybir.ActivationFunctionType
AX = mybir.AxisListType


class SemCtr:
    __slots__ = ("sem", "count")

    def __init__(self, sem):
        self.sem = sem
        self.count = 0


class Res:
    __slots__ = ("name", "last_w", "readers", "dsem")

    def __init__(self, name):
        self.name = name
        self.last_w = None
        self.readers = {}
        self.dsem = None


class Prog:
    ENG = ("pe", "dve", "act", "pool", "sp")

    def __init__(self, same_engine_sync=True):
        self.nc = bass.Bass("TRN2", target_bir_lowering=False)
        self.st = ExitStack()
        self.stk = [self.st]
        self.q = {e: [] for e in self.ENG}
        self.cnt = {e: 0 for e in self.ENG}
        self.seen = {e: {} for e in self.ENG}
        self.esem = {e: self.st.enter_context(self.nc.semaphore("es_" + e)) for e in self.ENG}
        self.esem_ids = {id(s) for s in self.esem.values()}
        self.own_ids = {e: {id(self.esem[e])} for e in self.ENG}
        self.same = same_engine_sync
        self.dma_events = {}
        self.sem_pool = []
        self.block_log = []
        self.label = ''
        self.scope_res = [[]]
        self.uid = 0
        self.ninst = 0
        self.flush_every = int(os.environ.get("FLUSH_EVERY", "1200"))

    def dram(self, name, shape, dt, kind=None):
        if kind is None:
            t = self.nc.dram_tensor(name, list(shape), dt)
        else:
            t = self.nc.dram_tensor(name, list(shape), dt, kind=kind)
        return t.ap()

    def sbuf(self, name, shape, dt):
        self.uid += 1
        return self.stk[-1].enter_context(self.nc.sbuf_tensor("%s_%d" % (name, self.uid), list(shape), dt))

    def psum(self, name, shape, dt=F32):
        self.uid += 1
        return self.stk[-1].enter_context(self.nc.psum_tensor("%s_%d" % (name, self.uid), list(shape), dt))

    def res(self, name=None):
        self.uid += 1
        r = Res("%s_%d" % (name or "r", self.uid))
        self.scope_res[-1].append(r)
        return r

    def semctr(self):
        while self.sem_pool:
            sc = self.sem_pool.pop()
            if sc.count < 20000:
                return sc
        return SemCtr(self.newsem("ds"))

    def newsem(self, name):
        self.uid += 1
        return self.st.enter_context(self.nc.semaphore("%s_%d" % (name, self.uid)))

    def _waits(self, eng, r, w, dma=False):
        waits = {}

        def need(ev):
            if ev is None:
                return
            s, v = ev
            if (not self.same or eng == "pe") and id(s) in self.own_ids[eng]:
                return
            if self.seen[eng].get(id(s), (None, 0))[1] < v:
                if waits.get(id(s), (None, 0))[1] < v:
                    waits[id(s)] = (s, v)

        for x in r:
            need(x.last_w)
            if dma:
                for ev in x.readers.values():
                    if id(ev[0]) not in self.esem_ids:
                        need(ev)
        for x in w:
            need(x.last_w)
            for ev in x.readers.values():
                need(ev)
        for k, sv in waits.items():
            self.seen[eng][k] = sv
        return list(waits.values())

    def op(self, eng, fn, r=(), w=()):
        waits = self._waits(eng, r, w)
        if self.cnt[eng] >= 20000:
            self.esem[eng] = self.newsem("es_" + eng)
            self.esem_ids.add(id(self.esem[eng]))
            self.own_ids[eng].add(id(self.esem[eng]))
            self.cnt[eng] = 0
        self.cnt[eng] += 1
        ev = (self.esem[eng], self.cnt[eng])
        self.q[eng].append((waits, fn, ev, 1))
        for x in r:
            x.readers[id(ev[0])] = ev
        for x in w:
            x.last_w = ev
            x.readers = {}
        self.ninst += 1
        self._autoflush()
        return ev

    def _autoflush(self):
        if sum(len(v) for v in self.q.values()) >= self.flush_every:
            self.flush()

    def _async(self, eng, fn, inc, r, w):
        dst = w[0]
        waits = self._waits(eng, r, w, dma=True)
        if dst.dsem is None:
            dst.dsem = self.semctr()
        dst.dsem.count += inc
        ev = (dst.dsem.sem, dst.dsem.count)
        self.q[eng].append((waits, fn, ev, inc))
        for x in r:
            x.readers[id(ev[0])] = ev
        dst.last_w = ev
        dst.readers = {}
        self.dma_events[id(ev[0])] = ev
        self.ninst += 1
        self._autoflush()
        return ev

    def dma(self, eng, out, in_, r=(), w=(), slow=False):
        if slow:
            return self._async(eng, lambda e: e.dma_start(out=out, in_=in_, allow_slow_non_contiguous=True), 16, r, w)
        return self._async(eng, lambda e: e.dma_start(out=out, in_=in_), 16, r, w)

    def coll(self, kind, out, in_, groups, r=(), w=()):
        return self._async("pool", lambda e: e.collective_compute(
            kind, ALU.bypass, replica_groups=groups, ins=[in_.opt()], outs=[out.opt()]), 1, r, w)

    def barrier(self):
        for eng in self.ENG:
            waits = []
            evs = [(self.esem[x], self.cnt[x]) for x in self.ENG if x != eng and self.cnt[x] > 0]
            evs += list(self.dma_events.values())
            for s, v in evs:
                if self.seen[eng].get(id(s), (None, 0))[1] < v:
                    waits.append((s, v))
                    self.seen[eng][id(s)] = (s, v)
            if waits:
                self.q[eng].append((waits, None, None, 0))
        self.dma_events = {}

    def flush(self):
        nc = self.nc
        with nc.Block() as block:
            def mk(name):
                def run(e):
                    for waits, fn, ev, inc in self.q[name]:
                        for s, v in waits:
                            e.wait_ge(s, v)
                        if fn is not None:
                            fn(e).then_inc(ev[0], inc)
                return run
            block.sync(mk("sp"))
            block.tensor(mk("pe"))
            block.vector(mk("dve"))
            block.scalar(mk("act"))
            block.gpsimd(mk("pool"))
        self.block_log.append((self.label, sum(len(v) for v in self.q.values())))
        self.q = {e: [] for e in self.ENG}

    @contextmanager
    def scope(self):
        st = ExitStack()
        self.stk.append(st)
        self.scope_res.append([])
        try:
            yield
            self.barrier()
            self.flush()
        finally:
            self.stk.pop()
            st.close()
            for r in self.scope_res.pop():
                if r.dsem is not None:
                    self.sem_pool.append(r.dsem)
                    r.dsem = None

    def finish(self):
        self.barrier()
        self.flush()
        self.st.close()
        return self.nc

    def mm(self, out, lhsT, rhs, start, stop, r, w):
        return self.op("pe", lambda e: e.matmul(out, lhsT=lhsT, rhs=rhs, start=start, stop=stop), r, w)

    def tr(self, out, in_, ident, r, w):
        return self.op("pe", lambda e: e.transpose(out, in_, ident), r, w)

    def act(self, out, in_, func, r, w, bias=0.0, scale=1.0, accum_out=None):
        if accum_out is None:
            return self.op("act", lambda e: e.activation(out=out, in_=in_, func=func, bias=bias, scale=scale), r, w)
        return self.op("act", lambda e: e.activation(out=out, in_=in_, func=func, bias=bias, scale=scale, accum_out=accum_out), r, w)

    def tt(self, eng, out, a, b, op, r, w):
        return self.op(eng, lambda e: e.tensor_tensor(out=out, in0=a, in1=b, op=op), r, w)

    def ts(self, eng, out, a, s1, op0, r, w, s2=None, op1=None):
        if op1 is None:
            return self.op(eng, lambda e: e.tensor_scalar(out=out, in0=a, scalar1=s1, scalar2=None, op0=op0), r, w)
        return self.op(eng, lambda e: e.tensor_scalar(out=out, in0=a, scalar1=s1, scalar2=s2, op0=op0, op1=op1), r, w)

    def stt(self, out, in0, scalar, in1, op0, op1, r, w):
        return self.op("dve", lambda e: e.scalar_tensor_tensor(out=out, in0=in0, scalar=scalar, in1=in1, op0=op0, op1=op1), r, w)

    def cp(self, eng, out, in_, r, w):
        if eng == "act":
            return self.op("act", lambda e: e.copy(out=out, in_=in_), r, w)
        return self.op(eng, lambda e: e.tensor_copy(out=out, in_=in_), r, w)

    def memset(self, eng, out, val, w):
        return self.op(eng, lambda e: e.memset(out, val), (), w)


def make_ident(p, dt=BF16, name="ident"):
    idf = p.sbuf(name + "f", [128, 128], F32); r = p.res(name)
    p.memset("dve", idf[:], 0.0, w=[r])
    p.op("pool", lambda e: e.affine_select(out=idf[:], in_=idf[:], pattern=[[-1, 128]], compare_op=ALU.not_equal,
                                           fill=1.0, base=0, channel_multiplier=1), r=[r], w=[r])
    if dt == F32:
        return idf, r
    idb = p.sbuf(name + "b", [128, 128], dt)
    p.cp("dve", idb[:], idf[:], r=[r], w=[r])
    return idb, r


def conv_stage(p, jobs, R, W):
    L = R * W
    PAD = W + 1
    CW = min(512, L)
    with p.scope():
        ident, idr = make_ident(p)
        stgs = [p.sbuf("stg%d" % i, [128, L], F32) for i in range(2)]; stgrs = [p.res("stg%d" % i) for i in range(2)]
        outts = [p.sbuf("outt%d" % i, [128, L], F32) for i in range(2)]; outrs = [p.res("outt%d" % i) for i in range(2)]
        nver = 3 if R > 1 else 1
        P = [p.sbuf("P%d" % i, [128, L + 2 * PAD], BF16) for i in range(nver)]
        Pr = [p.res("P%d" % i) for i in range(nver)]
        for i in range(nver):
            p.memset("pool", P[i][:, 0:PAD], 0.0, w=[Pr[i]])
            p.memset("pool", P[i][:, PAD + L:], 0.0, w=[Pr[i]])
        wsb = p.sbuf("wsb", [128, 9], F32); bsb = p.sbuf("bsb", [128, 1], F32); wr = p.res("wsb")
        D = p.sbuf("D", [128, 9, 128], BF16); Dr = p.res("D")
        ps = [p.psum("ps%d" % i, [128, 512]) for i in range(4)]; psr = [p.res("ps%d" % i) for i in range(4)]
        k = 0
        for ji, (src, srcr, w_ap, b_ap, c0, silu, dst, dstr) in enumerate(jobs):
            stg = stgs[ji % 2]; stgr = stgrs[ji % 2]; outt = outts[ji % 2]; outr = outrs[ji % 2]
            p.dma("sp" if ji % 2 else "pool", stg[:], src, r=[srcr], w=[stgr])
            p.dma("sp", wsb[:], w_ap.rearrange("a b c -> c (a b)")[c0:c0 + 128, :], w=[wr], slow=True)
            p.dma("sp", bsb[:], b_ap.rearrange("(a b) -> a b", b=1)[c0:c0 + 128, :], w=[wr])
            p.cp("act", P[0][:, PAD:PAD + L], stg[:], r=[stgr], w=[Pr[0]])
            if nver == 3:
                p.cp("act", P[1][:, PAD:PAD + L], stg[:], r=[stgr], w=[Pr[1]])
                p.cp("dve", P[2][:, PAD:PAD + L], stg[:], r=[stgr], w=[Pr[2]])
                v1 = P[1][:, PAD:PAD + L].rearrange("p (r w) -> p r w", w=W)
                v2 = P[2][:, PAD:PAD + L].rearrange("p (r w) -> p r w", w=W)
                p.memset("dve", v1[:, :, W - 1:W], 0.0, w=[Pr[1]])
                p.memset("dve", v2[:, :, 0:1], 0.0, w=[Pr[2]])
            for tap in range(9):
                p.ts("dve", D[:, tap, :], ident[:], wsb[:, tap:tap + 1], ALU.mult, r=[idr, wr], w=[Dr])
            taps = [(dy, dx) for dy in (-1, 0, 1) for dx in (-1, 0, 1) if (R > 1 or dy == 0)]
            for c in range(L // CW):
                h = k % 4; k += 1
                for ti, (dy, dx) in enumerate(taps):
                    ver = 0 if nver == 1 else (1 if dx == -1 else (2 if dx == 1 else 0))
                    o = PAD + c * CW + W * dy + dx
                    p.mm(ps[h][:, 0:CW], D[:, (dy + 1) * 3 + dx + 1, :], P[ver][:, o:o + CW], ti == 0, ti == len(taps) - 1,
                         r=[Dr, Pr[ver]], w=[psr[h]])
                p.act(outt[:, c * CW:(c + 1) * CW], ps[h][:, 0:CW], AF.Silu if silu else AF.Identity,
                      r=[psr[h], wr], w=[outr], bias=bsb[:, 0:1])
            p.dma("pool" if ji % 2 else "sp", dst, outt[:], r=[outr], w=[dstr])


HY_BANDS = 16


def hfilt_consts(L):
    t = np.arange(L, dtype=np.float64) / L
    bands = np.linspace(1e-4, HY_BANDS - 1, HY_BANDS)
    ang = 2 * np.pi * t[:, None] * bands[None, :]
    feats = np.concatenate([t[:, None], np.cos(ang), np.sin(ang)], axis=-1)
    return np.ascontiguousarray(feats.T).astype(np.float32), t.astype(np.float32)


def hfilt_stage(p, featsT, trow, w, j, L, dst, dst_res):
    CW = min(512, L)
    NCH = L // CW
    TWO_PI = 2 * np.pi
    with p.scope():
        wr = p.res("w")
        ft = p.sbuf("ft", [33, L], F32); tb = p.sbuf("tb", [128, L], F32)
        w1 = p.sbuf("w1", [33, 64], F32); w2 = p.sbuf("w2", [64, 64], F32); w3 = p.sbuf("w3", [64, 4, 128], F32)
        sc = p.sbuf("sc", [64, 4], F32)
        dc = p.sbuf("dc", [128, 4], F32)
        p.dma("sp", ft[:], featsT, w=[wr]); p.dma("sp", tb[:], trow.partition_broadcast(128), w=[wr])
        p.dma("sp", w1[:], w["f_w1"], w=[wr]); p.dma("sp", w2[:], w["f_w2"], w=[wr])
        p.dma("sp", w3[:], w["f_w3"], w=[wr])
        p.dma("sp", sc[:, 0:1], w["f_b1"].rearrange("(a b) -> a b", b=1), w=[wr])
        p.dma("sp", sc[:, 1:2], w["f_freq"].rearrange("(a b) -> a b", b=1), w=[wr])
        p.dma("sp", sc[:, 2:3], w["f_b2"].rearrange("(a b) -> a b", b=1), w=[wr])
        p.dma("sp", dc[:], w["decay"], w=[wr])
        p.act(dc[:], dc[:], AF.Abs, r=[wr], w=[wr])
        p.ts("dve", dc[:], dc[:], -1.0, ALU.mult, r=[wr], w=[wr])
        h1f = p.sbuf("h1", [128, L], F32); h1 = h1f[0:64]; h2 = p.sbuf("h2", [64, L], F32)
        h1r, h2r = p.res("h1"), p.res("h2")
        ps = [p.psum("ps%d" % i, [128, 512]) for i in range(2)]; psr = [p.res("ps%d" % i) for i in range(2)]
        tmp = p.sbuf("tmp", [64, 512], F32); tmpr = p.res("tmp")
        tm1 = p.sbuf("tm1", [64, 512], F32); tm1r = p.res("tm1"); tm2 = p.sbuf("tm2", [64, 512], F32); tm2r = p.res("tm2")
        k = 0
        for (src, srcr, lw, bcol, dsth, dstr) in ((ft[:], wr, w1, 0, h1, h1r), (h1, h1r, w2, 2, h2[:], h2r)):
            for c in range(NCH):
                cs = slice(c * CW, (c + 1) * CW)
                h = k % 2; k += 1
                p.mm(ps[h][0:64, 0:CW], lw[:], src[:, cs], True, True, r=[wr, srcr], w=[psr[h]])
                p.ts("dve", tmp[:, 0:CW], ps[h][0:64, 0:CW], sc[:, bcol:bcol + 1], ALU.add, r=[psr[h], wr], w=[tmpr],
                     s2=sc[:, 1:2], op1=ALU.mult)
                tv = tmp[:, 0:CW]
                p.ts("dve", tm1[:, 0:CW], tv, float(np.pi), ALU.is_gt, r=[tmpr], w=[tm1r], s2=-TWO_PI, op1=ALU.mult)
                p.ts("dve", tm2[:, 0:CW], tv, float(-np.pi), ALU.is_lt, r=[tmpr], w=[tm2r], s2=TWO_PI, op1=ALU.mult)
                p.tt("dve", tv, tv, tm1[:, 0:CW], ALU.add, r=[tmpr, tm1r], w=[tmpr])
                p.tt("dve", tv, tv, tm2[:, 0:CW], ALU.add, r=[tmpr, tm2r], w=[tmpr])
                p.ts("dve", tv, tv, float(np.pi), ALU.min, r=[tmpr], w=[tmpr], s2=float(-np.pi), op1=ALU.max)
                p.act(dsth[:, cs], tmp[:, 0:CW], AF.Sin, r=[tmpr], w=[dstr])
        F = [p.sbuf("F%d" % i, [128, L], F32) for i in range(2)]; Fr = [p.res("F%d" % i) for i in range(2)]
        wnd = p.sbuf("wnd", [128, 512], F32); wndr = p.res("wnd")
        ss = p.sbuf("ss", [128, 4], F32); ssr = p.res("ss")
        junk = h1f; junkr = h1r
        for o in range(2):
            for d in range(2):
                od = o * 2 + d
                for c in range(NCH):
                    cs = slice(c * CW, (c + 1) * CW)
                    h = k % 2; k += 1
                    p.mm(ps[h][:, 0:CW], w3[:, od, :], h2[:, cs], True, True, r=[wr, h2r], w=[psr[h]])
                    p.act(wnd[:, 0:CW], tb[:, cs], AF.Exp, r=[wr], w=[wndr], scale=dc[:, od:od + 1])
                    p.tt("dve", F[d][:, cs], ps[h][:, 0:CW], wnd[:, 0:CW], ALU.mult, r=[psr[h], wndr], w=[Fr[d]])
                if d == 1:
                    p.memset("dve", F[d][:, 0:1], 0.0, w=[Fr[d]])
                p.act(junk[:], F[d][:], AF.Square, r=[Fr[d]], w=[junkr], accum_out=ss[:, od:od + 1])
                ssr.last_w = junkr.last_w
            tot = ss[:, 2 * o:2 * o + 1]
            p.tt("dve", tot, tot, ss[:, 2 * o + 1:2 * o + 2], ALU.add, r=[ssr], w=[ssr])
            p.act(tot, tot, AF.Sqrt, r=[ssr], w=[ssr], bias=1e-6)
            p.op("dve", lambda e, t_=tot: e.reciprocal(out=t_, in_=t_), r=[ssr], w=[ssr])
            for d in range(2):
                p.ts("dve", F[d][:], F[d][:], tot, ALU.mult, r=[Fr[d], ssr], w=[Fr[d]])
                p.dma("sp", dst[o * 2 + d], F[d][:], r=[Fr[d]], w=[dst_res])


def hconv_tables():
    a = np.arange(128, dtype=np.float64)
    th128 = 2 * np.pi * np.outer(a, a) / 128.0
    C = np.cos(th128); S = np.sin(th128)
    bf = lambda x: np.ascontiguousarray(x.astype(np.float32)).astype(ml_dtypes.bfloat16)
    t = {}
    t["hc_sq"] = bf(np.stack([C, -C, S, -S], axis=1))
    t["hc_wab"] = bf(np.stack([np.concatenate([C, S], 1), np.concatenate([-C, -S], 1),
                                np.concatenate([-S, C], 1)], axis=1))
    for pre, N1 in (("hc", 128), ("hcc", 4)):
        NF = 128 * N1
        f1 = np.arange(N1, dtype=np.float64)
        n1 = np.arange(64, dtype=np.float64)
        th1 = 2 * np.pi * np.outer(n1, f1) / N1
        t[pre + "_w1"] = bf(np.concatenate([np.cos(th1), -np.sin(th1)], axis=1))
        thN = 2 * np.pi * np.outer(a, f1) / NF
        t[pre + "_tw"] = np.stack([np.cos(thN), -np.sin(thN)], axis=1).astype(np.float32)
        thN2 = 2 * np.pi * np.outer(a, a) / NF
        t[pre + "_tw2"] = np.stack([np.cos(thN2), -np.sin(thN2)], axis=1).astype(np.float32)
        m1 = np.arange(64, dtype=np.float64)
        thI = 2 * np.pi * np.outer(a, m1) / N1
        t[pre + "_i2"] = bf(np.stack([np.cos(thI), np.sin(thI), -np.sin(thI)], axis=1))
    return t


def hconv_stage(p, tabs, srcs, dst, dst_res, CH, L_in, first, G=32, u_out=None, u_res=None):
    NP = L_in // 128
    N1 = 128 if L_in == 8192 else 4
    NFFT = 128 * N1
    pre = "hc" if N1 == 128 else "hcc"
    with p.scope():
        w1 = p.sbuf("w1", [64, 2 * N1], BF16); sq = p.sbuf("sq", [128, 4, 128], BF16)
        wab = p.sbuf("wab", [128, 3, 256], BF16); i2 = p.sbuf("i2", [128, 3, 64], BF16)
        tw = p.sbuf("tw", [128, 2, N1], F32); tw2 = p.sbuf("tw2", [128, 2, 128], F32)
        tr = p.res("tabs")
        for sb_, nm in ((w1, pre + "_w1"), (sq, "hc_sq"), (wab, "hc_wab"), (i2, pre + "_i2"), (tw, pre + "_tw"), (tw2, pre + "_tw2")):
            p.dma("sp", sb_[:], tabs[nm], w=[tr])
        PS = p.psum("ps", [128, 4096])
        PH = [PS[:, 0:2048], PS[:, 2048:4096]]
        phr = [p.res("psA"), p.res("psB")]
        stg = [p.sbuf("stg%d" % i, [64, G, 128], F32) for i in range(3)]
        stgr = [p.res("stg%d" % i) for i in range(3)]
        skb = p.sbuf("skb", [64, G], F32); skr = p.res("skb")
        XB = [p.sbuf("X%d" % i, [64, G, 128], BF16) for i in range(2)]; xrB = [p.res("X%d" % i) for i in range(2)]
        NS4 = G // 4
        T = [p.sbuf("T%d" % i, [128, G, N1], BF16) for i in range(4)]
        TI = T if N1 == 128 else [p.sbuf("TI%d" % i, [N1, G, 128], BF16) for i in range(4)]
        TrS = [[p.res("T%d_%d" % (i, s)) for s in range(NS4)] for i in range(4)]
        Kr = p.sbuf("Kr", [128, G, N1], F32); Ki = p.sbuf("Ki", [128, G, N1], F32)
        kresS = [p.res("K%d" % s) for s in range(NS4)]
        Q = [p.sbuf("Q%d" % i, [128, G, N1], BF16) for i in range(4)]
        QrS = [[p.res("Q%d_%d" % (i, s)) for s in range(NS4)] for i in range(4)]
        OUT2 = p.sbuf("OUT2", [64, G, 128], F32); outr = p.res("OUT2")
        Trb = tw[:, 0:1, :].to_broadcast([128, 4, N1])
        Tib = tw[:, 1:2, :].to_broadcast([128, 4, N1])
        Trb2 = tw2[0:N1, 0:1, :].to_broadcast([N1, 4, 128])
        Tib2 = tw2[0:N1, 1:2, :].to_broadcast([N1, 4, 128])
        PF = [PS[:, 0:1024], PS[:, 1024:2048]]; pfr = [p.res("pfA"), p.res("pfB")]
        PX = [PS[:, 2048:3072], PS[:, 3072:4096]]; pxr = [p.res("pxA"), p.res("pxB")]
        cnt = {"f": 0, "x": 0, "i": 0}

        def t2d(ap, c0):
            return ap[c0:c0 + G, :].rearrange("g (a b) -> a g b", b=128)

        def load_u(c0, which, xi):
            X = XB[xi]; xr = xrB[xi]
            if NP < 64:
                p.memset("pool", X[:], 0.0, w=[xr])
            if which != "u" or first:
                ap, rs = srcs["v" if which == "u" else which]
                p.dma("sp" if xi else "pool", stg[xi][0:NP], t2d(ap, c0), r=[rs], w=[stgr[xi]])
                p.cp("act", X[0:NP], stg[xi][0:NP], r=[stgr[xi]], w=[xr])
            else:
                for i, nm in enumerate(("v", "c1", "x1")):
                    ap, rs = srcs[nm]
                    p.dma("sp" if i % 2 else "pool", stg[i][0:NP], t2d(ap, c0), r=[rs], w=[stgr[i]])
                ap, rs = srcs["skip"]
                p.dma("sp", skb[0:NP], ap[c0:c0 + G].partition_broadcast(NP), r=[rs], w=[skr])
                p.tt("pool", stg[0][0:NP], stg[0][0:NP], skb[0:NP, :, None].to_broadcast([NP, G, 128]), ALU.mult,
                     r=[stgr[0], skr], w=[stgr[0]])
                p.tt("pool", stg[0][0:NP], stg[0][0:NP], stg[1][0:NP], ALU.add, r=[stgr[0], stgr[1]], w=[stgr[0]])
                p.tt("pool", stg[0][0:NP], stg[0][0:NP], stg[2][0:NP], ALU.mult, r=[stgr[0], stgr[2]], w=[stgr[0]])
                p.cp("act", X[0:NP], stg[0][0:NP], r=[stgr[0]], w=[xr])
                if u_out is not None:
                    p.dma("pool", t2d(u_out, c0), stg[0][0:NP], r=[stgr[0]], w=[u_res])

        Cm, Cn, Sm, Sn = (sq[:, k, :] for k in range(4))

        def f1_tw(s4, xi):
            X = XB[xi]; xr = xrB[xi]
            h = cnt["f"] % 2; cnt["f"] += 1
            pv = PF[h][:, 0:8 * N1].rearrange("p (c k) -> p c k", k=2 * N1)
            for c in range(4):
                p.mm(pv[:, c, :], X[:, s4 * 4 + c, :], w1[:], True, True, r=[xr, tr], w=[pfr[h]])
            Ar = pv[:, :, 0:N1]; Ai = pv[:, :, N1:2 * N1]
            gs = slice(s4 * 4, s4 * 4 + 4)
            p.tt("dve", T[0][:, gs, :], Ar, Trb, ALU.mult, r=[pfr[h], tr], w=[TrS[0][s4]])
            p.tt("dve", T[1][:, gs, :], Ai, Tib, ALU.mult, r=[pfr[h], tr], w=[TrS[1][s4]])
            p.tt("dve", T[2][:, gs, :], Ar, Tib, ALU.mult, r=[pfr[h], tr], w=[TrS[2][s4]])
            p.tt("dve", T[3][:, gs, :], Ai, Trb, ALU.mult, r=[pfr[h], tr], w=[TrS[3][s4]])

        def f2_post(s4, which):
            h = cnt["x"] % 2; cnt["x"] += 1
            gs = slice(s4 * 4, s4 * 4 + 4)
            xr_ps = PX[h][:, 0:4 * N1]; xi_ps = PX[h][:, 512:512 + 4 * N1]
            rr = [tr] + [TrS[k][s4] for k in range(4)]
            p.mm(xr_ps, Cm, T[0][:, gs, :], True, False, r=rr, w=[pxr[h]])
            p.mm(xr_ps, Cn, T[1][:, gs, :], False, False, r=rr, w=[pxr[h]])
            p.mm(xr_ps, Sm, T[2][:, gs, :], False, False, r=rr, w=[pxr[h]])
            p.mm(xr_ps, Sm, T[3][:, gs, :], False, True, r=rr, w=[pxr[h]])
            p.mm(xi_ps, Cm, T[2][:, gs, :], True, False, r=rr, w=[pxr[h]])
            p.mm(xi_ps, Cm, T[3][:, gs, :], False, False, r=rr, w=[pxr[h]])
            p.mm(xi_ps, Sn, T[0][:, gs, :], False, False, r=rr, w=[pxr[h]])
            p.mm(xi_ps, Sm, T[1][:, gs, :], False, True, r=rr, w=[pxr[h]])
            xr3 = xr_ps.rearrange("p (c k) -> p c k", k=N1); xi3 = xi_ps.rearrange("p (c k) -> p c k", k=N1)
            kres = kresS[s4]
            if which == "kf":
                p.cp("act", Kr[:, gs, :], xr3, r=[pxr[h]], w=[kres])
                p.cp("act", Ki[:, gs, :], xi3, r=[pxr[h]], w=[kres])
            elif which == "kg":
                p.tt("dve", Kr[:, gs, :], xr3, Kr[:, gs, :], ALU.add, r=[pxr[h], kres], w=[kres])
                p.stt(Ki[:, gs, :], xi3, -1.0, Ki[:, gs, :], ALU.mult, ALU.add, r=[pxr[h], kres], w=[kres])
            else:
                p.tt("dve", Q[0][:, gs, :], xr3, Kr[:, gs, :], ALU.mult, r=[pxr[h], kres], w=[QrS[0][s4]])
                p.tt("dve", Q[1][:, gs, :], xi3, Ki[:, gs, :], ALU.mult, r=[pxr[h], kres], w=[QrS[1][s4]])
                p.tt("dve", Q[2][:, gs, :], xr3, Ki[:, gs, :], ALU.mult, r=[pxr[h], kres], w=[QrS[2][s4]])
                p.tt("dve", Q[3][:, gs, :], xi3, Kr[:, gs, :], ALU.mult, r=[pxr[h], kres], w=[QrS[3][s4]])

        def fwd(which, xi):
            for s4 in range(NS4 + 1):
                if s4 < NS4:
                    f1_tw(s4, xi)
                if s4 >= 1:
                    f2_post(s4 - 1, which)

        def inv(c0):
            Wa, Wan, Wb = (wab[:, k, :] for k in range(3))
            for s4 in range(NS4 + 1):
                if s4 >= 1:
                    i2_step(s4 - 1, c0)
                if s4 == NS4:
                    break
                h = cnt["f"] % 2; cnt["f"] += 1
                pv = PF[h][0:N1, :].rearrange("p (c k) -> p c k", k=256)
                rr = [tr] + [QrS[k][s4] for k in range(4)]
                for c in range(4):
                    g = s4 * 4 + c
                    p.mm(pv[:, c, :], Q[0][:, g, :], Wa, True, False, r=rr, w=[pfr[h]])
                    p.mm(pv[:, c, :], Q[1][:, g, :], Wan, False, False, r=rr, w=[pfr[h]])
                    p.mm(pv[:, c, :], Q[2][:, g, :], Wb, False, False, r=rr, w=[pfr[h]])
                    p.mm(pv[:, c, :], Q[3][:, g, :], Wb, False, True, r=rr, w=[pfr[h]])
                Br = pv[:, :, 0:128]; Bi = pv[:, :, 128:256]
                gs = slice(s4 * 4, s4 * 4 + 4)
                p.tt("dve", TI[0][0:N1, gs, :], Br, Trb2, ALU.mult, r=[pfr[h], tr], w=[TrS[0][s4]])
                p.tt("dve", TI[1][0:N1, gs, :], Bi, Tib2, ALU.mult, r=[pfr[h], tr], w=[TrS[1][s4]])
                p.tt("dve", TI[2][0:N1, gs, :], Br, Tib2, ALU.mult, r=[pfr[h], tr], w=[TrS[2][s4]])
                p.tt("dve", TI[3][0:N1, gs, :], Bi, Trb2, ALU.mult, r=[pfr[h], tr], w=[TrS[3][s4]])
            p.dma("sp", t2d(dst, c0), OUT2[0:NP], r=[outr], w=[dst_res])

        CI, SI, SIn = (i2[:, k, :] for k in range(3))

        def i2_step(s4, c0):
            h = cnt["x"] % 2; cnt["x"] += 1
            gs = slice(s4 * 4, s4 * 4 + 4)
            o_ps = PX[h][0:NP, 0:512]
            rr = [tr] + [TrS[k][s4] for k in range(4)]
            p.mm(o_ps, CI[0:N1, 0:NP], TI[0][0:N1, gs, :], True, False, r=rr, w=[pxr[h]])
            p.mm(o_ps, CI[0:N1, 0:NP], TI[1][0:N1, gs, :], False, False, r=rr, w=[pxr[h]])
            p.mm(o_ps, SI[0:N1, 0:NP], TI[2][0:N1, gs, :], False, False, r=rr, w=[pxr[h]])
            p.mm(o_ps, SIn[0:N1, 0:NP], TI[3][0:N1, gs, :], False, True, r=rr, w=[pxr[h]])
            p.act(OUT2[0:NP, gs, :], o_ps.rearrange("p (c k) -> p c k", k=128), AF.Copy, r=[pxr[h]], w=[outr], scale=1.0 / NFFT)

        items = [(c0, which) for c0 in range(0, CH, G) for which in ("kf", "kg", "u")]
        load_u(items[0][0], items[0][1], 0)
        for i, (c0, which) in enumerate(items):
            if i + 1 < len(items):
                load_u(items[i + 1][0], items[i + 1][1], (i + 1) % 2)
            fwd(which, i % 2)
            if which == "u":
                inv(c0)


LF = 8192


def fnet_tables():
    a = np.arange(128, dtype=np.float64)
    th = 2 * np.pi * np.outer(a, a) / 128.0
    C = np.cos(th); S = np.sin(th)
    b = np.arange(64, dtype=np.float64)
    th64 = 2 * np.pi * np.outer(b, b) / 64.0
    C64 = np.cos(th64); S64 = np.sin(th64)
    thL = 2 * np.pi * np.outer(b, a) / LF
    bf = lambda x: np.ascontiguousarray(x.astype(np.float32)).astype(ml_dtypes.bfloat16)
    return {"fn_w": bf(np.stack([np.concatenate([C, -S], 1), np.concatenate([S, C], 1)], axis=1)),
            "fn_64": bf(np.stack([C64, -C64, S64], axis=1)),
            "fn_tw": np.stack([np.cos(thL), -np.sin(thL)], axis=1).astype(np.float32),
            "fn_256": bf(np.stack([np.stack([np.cos(2 * np.pi * np.outer(np.arange(128 * b_, 128 * b_ + 128), np.arange(256)) / 256.0),
                                             np.sin(2 * np.pi * np.outer(np.arange(128 * b_, 128 * b_ + 128), np.arange(256)) / 256.0)], axis=1)
                                   for b_ in range(2)], axis=1))}


def fnet_stage(p, tabs, src, src_res, dst, dst_res, L_in):
    scale = 1.0 / np.sqrt(L_in * 128.0)
    f1_list = list(range(128)) if L_in == LF else [0, 32, 64, 96]
    with p.scope():
        wt = p.sbuf("wt", [128, 2, 256], BF16); t64 = p.sbuf("t64", [64, 3, 64], BF16); tw = p.sbuf("tw", [64, 2, 128], F32)
        tr = p.res("tabs")
        p.dma("sp", wt[:], tabs["fn_w"], w=[tr]); p.dma("sp", t64[:], tabs["fn_64"], w=[tr]); p.dma("sp", tw[:], tabs["fn_tw"], w=[tr])
        stg = p.sbuf("stg", [128, LF], F32); stgr = p.res("stg")
        ub = p.sbuf("ub", [128, LF], BF16); ubr = p.res("ub")
        V = p.sbuf("V", [128, 64, 256], BF16); Vr = p.res("V")
        T = [p.sbuf("T%d" % i, [64, 64, 128], BF16) for i in range(4)]; Tr_ = [p.res("T%d" % i) for i in range(4)]
        PS = p.psum("ps", [128, 4096]); PH = [PS[:, 0:2048], PS[:, 2048:4096]]; phr = [p.res("psA"), p.res("psB")]
        if L_in < LF:
            p.memset("pool", stg[:], 0.0, w=[stgr])
        p.dma("sp", stg[:, 0:L_in], src, r=[src_res], w=[stgr])
        p.cp("act", ub[:], stg[:], r=[stgr], w=[ubr])
        tog = 0
        uv = ub[:].rearrange("p (a b) -> p b a", b=64)
        for s8 in range(8):
            h = tog; tog ^= 1
            pv = PH[h].rearrange("p (c k) -> p c k", k=256)
            for c in range(8):
                n2 = s8 * 8 + c
                p.mm(pv[:, c, :], uv[:, n2, :], wt[:, 0, :], True, True, r=[ubr, tr], w=[phr[h]])
            p.cp("act" if s8 % 2 else "dve", V[:, s8 * 8:(s8 + 1) * 8, :], pv, r=[phr[h]], w=[Vr])
        Trb = tw[:, 0:1, :].to_broadcast([64, 8, 128]); Tib = tw[:, 1:2, :].to_broadcast([64, 8, 128])
        OUT = stg; outr = stgr
        for half in range(2):
            for s8 in range(8):
                h = tog; tog ^= 1
                pv = PH[h][0:64, :].rearrange("p (c k) -> p c k", k=256)
                for c in range(8):
                    ch = half * 64 + s8 * 8 + c
                    p.mm(pv[:, c, :], V[:, :, ch], wt[:, 0, :], True, False, r=[Vr, tr], w=[phr[h]])
                    p.mm(pv[:, c, :], V[:, :, 128 + ch], wt[:, 1, :], False, True, r=[Vr, tr], w=[phr[h]])
                Ar = pv[:, :, 0:128]; Ai = pv[:, :, 128:256]
                gs = slice(s8 * 8, s8 * 8 + 8)
                p.tt("dve", T[0][:, gs, :], Ar, Trb, ALU.mult, r=[phr[h], tr], w=[Tr_[0]])
                p.tt("dve", T[1][:, gs, :], Ai, Tib, ALU.mult, r=[phr[h], tr], w=[Tr_[1]])
                p.tt("dve", T[2][:, gs, :], Ar, Tib, ALU.mult, r=[phr[h], tr], w=[Tr_[2]])
                p.tt("dve", T[3][:, gs, :], Ai, Trb, ALU.mult, r=[phr[h], tr], w=[Tr_[3]])
            C64, C64n, S64 = (t64[:, k, :] for k in range(3))
            rr = [tr] + Tr_
            for s32 in range(0, len(f1_list), 32):
                fl = f1_list[s32:s32 + 32]
                h = tog; tog ^= 1
                pv = PH[h][0:64, :].rearrange("p (m k) -> p m k", k=64)
                for i, f1 in enumerate(fl):
                    p.mm(pv[:, i, :], T[0][:, :, f1], C64, True, False, r=rr, w=[phr[h]])
                    p.mm(pv[:, i, :], T[1][:, :, f1], C64n, False, False, r=rr, w=[phr[h]])
                    p.mm(pv[:, i, :], T[2][:, :, f1], S64, False, False, r=rr, w=[phr[h]])
                    p.mm(pv[:, i, :], T[3][:, :, f1], S64, False, True, r=rr, w=[phr[h]])
                ov = OUT[half * 64:half * 64 + 64, :].rearrange("p (a b) -> p a b", b=128)
                step = fl[1] - fl[0] if len(fl) > 1 else 1
                osl = ov[:, :, fl[0]:fl[-1] + 1:step].rearrange("p a b -> p b a")
                p.act(osl, pv[:, 0:len(fl), :], AF.Copy, r=[phr[h]], w=[outr], scale=float(scale))
        if L_in == LF:
            p.dma("sp", dst, OUT[:], r=[outr], w=[dst_res])
        else:
            ov = OUT[:].rearrange("p (a b) -> p a b", b=128)[:, :, 0:128:32]
            p.dma("sp", dst.rearrange("p (a b) -> p a b", b=4), ov, r=[outr], w=[dst_res], slow=True)


def fnet_ctx_stage(p, tabs, src, src_res, dst, dst_res):
    scale = 1.0 / np.sqrt(256 * 128.0)
    with p.scope():
        wt = p.sbuf("wt", [128, 2, 256], BF16); t256 = p.sbuf("t256", [128, 2, 2, 256], BF16); tr = p.res("tabs")
        p.dma("sp", wt[:], tabs["fn_w"], w=[tr]); p.dma("sp", t256[:], tabs["fn_256"], w=[tr])
        stg = p.sbuf("stg", [128, 256], F32); stgr = p.res("stg")
        ub = p.sbuf("ub", [128, 256], BF16); ubr = p.res("ub")
        V = p.sbuf("V", [128, 2, 256], BF16); Vr = p.res("V")
        ps0 = p.psum("ps0", [128, 2, 256]); ps0r = p.res("ps0")
        ps1 = p.psum("ps1", [128, 256]); ps1r = p.res("ps1")
        p.dma("sp", stg[:], src, r=[src_res], w=[stgr])
        p.cp("act", ub[:], stg[:], r=[stgr], w=[ubr])
        for blk in range(2):
            p.mm(ps0[:, blk, :], ub[:, blk * 128:(blk + 1) * 128], wt[:, 0, :], True, True, r=[ubr, tr], w=[ps0r])
        p.cp("dve", V[:], ps0[:], r=[ps0r], w=[Vr])
        k = 0
        for blk in range(2):
            for ri in range(2):
                p.mm(ps1[:], V[:, blk, ri * 128:(ri + 1) * 128], t256[:, blk, ri, :], k == 0, k == 3, r=[Vr, tr], w=[ps1r])
                k += 1
        p.act(stg[:], ps1[:], AF.Copy, r=[ps1r], w=[stgr], scale=float(scale))
        p.dma("sp", dst, stg[:], r=[stgr], w=[dst_res])


STOP = 99


NCK = 66
NEG = -30000.0


def mlstm_stage(p, src, dst_lat, dst_ctx, dst_res, normw_ap, j):
    TOK = NCK * 128
    with p.scope():
        ident, idr = make_ident(p, F32, "idf")
        identb = p.sbuf("identb", [128, 128], BF16)
        p.cp("dve", identb[:], ident[:], r=[idr], w=[idr])
        cr = p.res("consts")
        ones = p.sbuf("ones", [128, 128], F32); onesb = p.sbuf("onesb", [128, 128], BF16)
        p.memset("dve", ones[:], 1.0, w=[cr]); p.memset("dve", onesb[:], 1.0, w=[cr])
        tri = p.sbuf("tri", [128, 2, 128], F32); mneg = p.sbuf("mneg", [128, 2, 128], F32)
        p.memset("dve", tri[:], 1.0, w=[cr]); p.memset("dve", mneg[:], 0.0, w=[cr])
        p.op("pool", lambda e: e.affine_select(out=tri[:, 0, :], in_=tri[:, 0, :], pattern=[[1, 128]], compare_op=ALU.is_ge, fill=0.0, base=0, channel_multiplier=-1), r=[cr], w=[cr])
        p.op("pool", lambda e: e.affine_select(out=tri[:, 1, :], in_=tri[:, 1, :], pattern=[[-1, 128]], compare_op=ALU.is_ge, fill=0.0, base=0, channel_multiplier=1), r=[cr], w=[cr])
        p.op("pool", lambda e: e.affine_select(out=mneg[:, 0, :], in_=mneg[:, 0, :], pattern=[[1, 128]], compare_op=ALU.is_ge, fill=NEG, base=0, channel_multiplier=-1), r=[cr], w=[cr])
        p.op("pool", lambda e: e.affine_select(out=mneg[:, 1, :], in_=mneg[:, 1, :], pattern=[[-1, 128]], compare_op=ALU.is_ge, fill=NEG, base=0, channel_multiplier=1), r=[cr], w=[cr])

        stg = p.sbuf("stg", [128, TOK], F32); stgr = p.res("stg")
        qT = p.sbuf("qT", [128, TOK], BF16); kT = p.sbuf("kT", [128, TOK], BF16); vT = p.sbuf("vT", [128, TOK], BF16)
        qr, kr, vr = p.res("qT"), p.res("kT"), p.res("vT")
        for nm, t_, r_, sc in (("qT", qT, qr, 128.0 ** -0.5), ("kT", kT, kr, 1.0), ("vT", vT, vr, 1.0)):
            lat, ctx, rs = src[nm]
            p.dma("sp", stg[:, 0:256], ctx, r=[rs], w=[stgr]); p.dma("sp", stg[:, 256:], lat, r=[rs], w=[stgr])
            p.act(t_[:], stg[:], AF.Copy, r=[stgr], w=[r_], scale=sc)
        ktok = p.sbuf("ktok", [128, NCK, 128], BF16); vtok = p.sbuf("vtok", [128, NCK, 128], BF16)
        ktr, vtr = p.res("ktok"), p.res("vtok")
        ptb = [p.psum("ptb%d" % i, [128, 8, 128], BF16) for i in range(2)]; ptbr = [p.res("ptb%d" % i) for i in range(2)]
        kk = 0
        for srcT, sr, dstt, dr in ((kT, kr, ktok, ktr), (vT, vr, vtok, vtr)):
            for c8 in range(0, NCK, 8):
                n = min(8, NCK - c8)
                h = kk % 2; kk += 1
                for c in range(n):
                    p.tr(ptb[h][:, c, :], srcT[:, (c8 + c) * 128:(c8 + c + 1) * 128], identb[:], r=[sr, idr], w=[ptbr[h]])
                p.cp("act" if h else "dve", dstt[:, c8:c8 + n, :], ptb[h][:, 0:n, :], r=[ptbr[h]], w=[dr])
        if STOP == 1:
            p.dma('sp', dst_ctx, stg[:, 0:256], r=[stgr], w=[dst_res]); return
        gst = p.sbuf("gst", [NCK, 4, 128], F32); gr = p.res("gst")
        lat, ctx, rs = src["g"]
        p.dma("sp", gst[0:2], ctx.rearrange("g (c s) -> c g s", s=128), r=[rs], w=[gr])
        p.dma("sp", gst[2:NCK], lat.rearrange("g (c s) -> c g s", s=128), r=[rs], w=[gr])
        G = p.sbuf("G", [128, 4, NCK], F32); Gr = p.res("G")
        pg = p.psum("pg", [128, 4, 128]); pgr = p.res("pg")
        for g in range(4):
            p.tr(pg[:, g, 0:NCK], gst[:, g, :], ident[0:NCK, 0:NCK], r=[gr, idr], w=[pgr])
        p.cp("dve", G[:], pg[:, :, 0:NCK], r=[pgr], w=[Gr])
        LFt = p.sbuf("LF", [128, 2, NCK], F32); lfr = p.res("LF")
        for d in range(2):
            p.act(LFt[:, d, :], G[:, 2 * d + 1, :], AF.Exp, r=[Gr], w=[lfr], scale=-1.0)
        p.act(LFt[:], LFt[:], AF.Ln, r=[lfr], w=[lfr], bias=1.0)
        p.ts("dve", LFt[:], LFt[:], -1.0, ALU.mult, r=[lfr], w=[lfr])
        CUM = p.sbuf("CUM", [128, 2, NCK], F32); IB = p.sbuf("IB", [128, 2, NCK], F32)
        SC = p.sbuf("SC", [128, 2, NCK], F32); ET = p.sbuf("ET", [128, 2, NCK], F32)
        pr = p.res("pre")
        pc = p.psum("pc", [128, 4, 128]); pcr = p.res("pc")
        for d in range(2):
            p.mm(pc[:, d, 0:NCK], tri[:, d, :], LFt[:, d, :], True, True, r=[cr, lfr], w=[pcr])
            p.mm(pc[:, 2 + d, 0:NCK], ones[:], LFt[:, d, :], True, True, r=[cr, lfr], w=[pcr])
        p.cp("dve", CUM[:], pc[:, 0:2, 0:NCK], r=[pcr], w=[pr])
        for d in range(2):
            p.tt("dve", IB[:, d, :], G[:, 2 * d, :], CUM[:, d, :], ALU.subtract, r=[Gr, pr], w=[pr])
            p.tt("dve", SC[:, d, :], pc[:, 2 + d, 0:NCK], IB[:, d, :], ALU.add, r=[pcr, pr], w=[pr])
        p.act(SC[:], SC[:], AF.Exp, r=[pr], w=[pr])
        p.act(ET[:], pc[:, 2:4, 0:NCK], AF.Exp, r=[pcr], w=[pr])

        if STOP == 2:
            p.dma('sp', dst_ctx[:, 0:66], SC[:, 0, :], r=[pr], w=[dst_res]); return
        H = p.sbuf("H", [128, TOK], F32); Hr = p.res("H")
        p.memset("pool", H[:], 0.0, w=[Hr])
        PA = [p.psum("pa%d" % d, [128, 4, 128]) for d in range(2)]
        PB = [p.psum("pb%d" % d, [128, 512]) for d in range(2)]
        rP1 = [p.res() for _ in range(2)]; rP2 = [p.res() for _ in range(2)]; rKQ = [p.res() for _ in range(2)]
        rNUM = [p.res() for _ in range(2)]; rDEN = [p.res() for _ in range(2)]; rST = [p.res() for _ in range(2)]
        def T(nm, shape, dt):
            return [p.sbuf("%s%d" % (nm, d), shape, dt) for d in range(2)], [p.res("%s%d" % (nm, d)) for d in range(2)]
        DG, DGr = T("DG", [128, 128], F32); WT, WTr = T("WT", [128, 128], F32); EC, ECr = T("EC", [128, 128], F32)
        QE, QEr = T("QE", [128, 128], BF16); STt, STr = T("ST", [128, 128], BF16)
        DD, DDr = T("DD", [128, 128], F32); HT, HTr = T("HT", [128, 128], F32)
        VS, VSr = T("VS", [128, 132], BF16); CF, CFr = T("CF", [128, 132], F32)
        CB, CBr = T("CB", [128, 128], BF16); NB, NBr = T("NB", [128, 128], BF16)
        for d in range(2):
            p.memset("pool", CF[d][:], 0.0, w=[CFr[d]]); p.memset("pool", CB[d][:], 0.0, w=[CBr[d]]); p.memset("pool", NB[d][:], 0.0, w=[NBr[d]])
        LE = 'dve'
        order = [list(range(NCK)), [1, 0] + list(range(NCK - 1, 1, -1))]
        for step in range(NCK):
            for d in range(2):
                c = order[d][step]
                cs = slice(c * 128, (c + 1) * 128)
                P1 = PA[d][:, 0, :]; P2 = PA[d][:, 1, :]; KQ = PA[d][:, 2, :]
                NUM = PB[d][:, 0:128]; DEN = PB[d][:, 128:256]; STP = PB[d][:, 256:256 + 129]
                p.ts(LE, DG[d][:], ident[:], CUM[:, d, c:c + 1], ALU.mult, r=[idr, pr], w=[DGr[d]])
                p.mm(P1, ones[:], DG[d][:], True, True, r=[cr, DGr[d]], w=[rP1[d]])
                p.mm(P2, ones[:], DG[d][:], True, False, r=[cr, DGr[d]], w=[rP2[d]])
                p.mm(P2, ident[:], mneg[:, d, :], False, True, r=[cr, idr], w=[rP2[d]])
                p.mm(KQ, kT[:, cs], qT[:, cs], True, True, r=[kr, qr], w=[rKQ[d]])
                p.act(WT[d][:], P2, AF.Exp, r=[rP2[d], pr], w=[WTr[d]], bias=IB[:, d, c:c + 1])
                p.act(EC[d][:], P1, AF.Exp, r=[rP1[d]], w=[ECr[d]])
                p.tt("dve", STt[d][:], KQ, WT[d][:], ALU.mult, r=[rKQ[d], WTr[d]], w=[STr[d]])
                p.tt(LE, QE[d][:], qT[:, cs], EC[d][:], ALU.mult, r=[qr, ECr[d]], w=[QEr[d]])
                p.mm(NUM, vtok[:, c, :], STt[d][:], True, False, r=[vtr, STr[d]], w=[rNUM[d]])
                p.mm(NUM, CB[d][:], QE[d][:], False, True, r=[CBr[d], QEr[d]], w=[rNUM[d]])
                p.mm(DEN, onesb[:], STt[d][:], True, False, r=[cr, STr[d]], w=[rDEN[d]])
                p.mm(DEN, NB[d][:], QE[d][:], False, True, r=[NBr[d], QEr[d]], w=[rDEN[d]])
                p.act(DD[d][:], DEN, AF.Abs, r=[rDEN[d]], w=[DDr[d]])
                p.ts("dve", DD[d][:], DD[d][:], 1.0, ALU.max, r=[DDr[d]], w=[DDr[d]])
                p.op("dve", lambda e, t_=DD[d]: e.reciprocal(out=t_[:], in_=t_[:]), r=[DDr[d]], w=[DDr[d]])
                p.tt("dve", HT[d][:], NUM, DD[d][:], ALU.mult, r=[rNUM[d], DDr[d]], w=[HTr[d]])
                p.tt(LE, H[:, cs], H[:, cs], HT[d][:], ALU.add, r=[Hr, HTr[d]], w=[Hr])
                p.ts(LE, VS[d][:, 0:128], vtok[:, c, :], SC[:, d, c:c + 1], ALU.mult, r=[vtr, pr], w=[VSr[d]])
                p.cp(LE, VS[d][:, 128:129], SC[:, d, c:c + 1], r=[pr], w=[VSr[d]])
                p.mm(STP, ktok[:, c, :], VS[d][:, 0:129], True, True, r=[ktr, VSr[d]], w=[rST[d]])
                p.stt(CF[d][:, 0:129], CF[d][:, 0:129], ET[:, d, c:c + 1], STP, ALU.mult, ALU.add, r=[CFr[d], pr, rST[d]], w=[CFr[d]])
                p.cp("act", CB[d][:], CF[d][:, 0:128], r=[CFr[d]], w=[CBr[d]])
                p.cp(LE, NB[d][:], CF[d][:, 128:129].to_broadcast([128, 128]), r=[CFr[d]], w=[NBr[d]])

        nw = p.sbuf("nw", [128, 1], F32); nwr = p.res("nw")
        p.dma("sp", nw[:], normw_ap.rearrange("(a b) -> a b", b=1), w=[nwr])
        lat, ctx, rs = src["oT"]
        p.dma("sp", stg[:, 0:256], ctx, r=[rs], w=[stgr]); p.dma("sp", stg[:, 256:], lat, r=[rs], w=[stgr])
        sq = p.sbuf("sq", [128, 512], F32); sqr = p.res("sq")
        rs_t = p.sbuf("rs_t", [128, 512], F32); rsr = p.res("rs_t")
        pn = [PB[0], PB[1]]; pnr = [p.res(), p.res()]
        ci = 0
        for t0 in range(0, TOK, 512):
            n = min(512, TOK - t0); ts_ = slice(t0, t0 + n)
            h = ci % 2; ci += 1
            p.act(sq[:, 0:n], H[:, ts_], AF.Square, r=[Hr], w=[sqr])
            p.mm(pn[h][:, 0:n], ones[:], sq[:, 0:n], True, True, r=[cr, sqr, rNUM[h], rDEN[h], rST[h]], w=[pnr[h], rNUM[h], rDEN[h], rST[h]])
            p.act(rs_t[:, 0:n], pn[h][:, 0:n], AF.Sqrt, r=[pnr[h]], w=[rsr], scale=1.0 / 128.0, bias=1e-6)
            p.op("dve", lambda e, o=rs_t[:, 0:n]: e.reciprocal(out=o, in_=o), r=[rsr], w=[rsr])
            p.tt("dve", H[:, ts_], H[:, ts_], rs_t[:, 0:n], ALU.mult, r=[Hr, rsr], w=[Hr])
            p.act(stg[:, ts_], stg[:, ts_], AF.Sigmoid, r=[stgr], w=[stgr])
            p.stt(H[:, ts_], H[:, ts_], nw[:, 0:1], stg[:, ts_], ALU.mult, ALU.mult, r=[Hr, nwr, stgr], w=[Hr])
        p.dma("sp", dst_ctx, H[:, 0:256], r=[Hr], w=[dst_res])
        p.dma("sp", dst_lat, H[:, 256:], r=[Hr], w=[dst_res])


D = 1024
EPS = 1e-6


def ttiles(NL, NC):
    out = [(t0, min(512, NL - t0), False) for t0 in range(0, NL, 512)]
    if NC:
        out.append((NL, NC, True))
    return out


def mod_stage(p, cv_ap, ada_w_ap, ada_b_ap, mod_d, mod_res, NM=12):
    with p.scope():
        cv = p.sbuf("cv", [128, 8, 2], F32); cvr = p.res("cv")
        p.dma("sp", cv[:], cv_ap.rearrange("(k p) n -> p k n", p=128), w=[cvr])
        p.act(cv[:], cv[:], AF.Silu, r=[cvr], w=[cvr])
        ab = p.sbuf("ab", [128, NM], F32); abr = p.res("ab")
        p.dma("sp", ab[:], ada_b_ap.rearrange("(m p) -> p m", p=128), w=[abr], slow=True)
        acc = p.sbuf("acc", [128, NM, 2], F32); accr = p.res("acc")
        for n in range(2):
            p.cp("dve", acc[:, :, n], ab[:], r=[abr], w=[accr])
        wt = [p.sbuf("wt%d" % i, [128, 128 * NM], F32) for i in range(2)]; wtr = [p.res("wt%d" % i) for i in range(2)]
        ps = p.psum("ps", [128, NM, 2]); psr = p.res("ps")
        for k in range(8):
            h = k % 2
            p.dma("sp" if h else "pool", wt[h][:], ada_w_ap[128 * k:128 * k + 128, :], w=[wtr[h]])
            for m in range(NM):
                p.mm(ps[:, m, :], wt[h][:, 128 * m:128 * m + 128], cv[:, k, :], True, True, r=[wtr[h], cvr], w=[psr])
            p.tt("dve", acc[:], acc[:], ps[:], ALU.add, r=[accr, psr], w=[accr])
        p.dma("sp", mod_d, acc[:], r=[accr], w=[mod_res])


def load_mod(p, mod_d, mod_res, nw_ap, sidx, scidx):
    modt = p.sbuf("modt", [128, 48, 2], F32); mr = p.res("modt")
    p.dma("sp", modt[:], mod_d, r=[mod_res], w=[mr], slow=True)
    nw = p.sbuf("nw", [128, 8], F32)
    p.dma("sp", nw[:], nw_ap.rearrange("(k p) -> p k", p=128), w=[mr], slow=True)
    A = p.sbuf("A", [128, 8, 2], F32)
    p.ts("dve", A[:], modt[:, 8 * scidx:8 * scidx + 8, :], 1.0, ALU.add, r=[mr], w=[mr])
    p.tt("dve", A[:], A[:], nw[:, :, None].to_broadcast([128, 8, 2]), ALU.mult, r=[mr], w=[mr])
    return modt, A, mr


def norm_mod_tile(p, xt, xr, n, A, SH, col, mr, ones, onr, ps, psr, sq, sqr, rstd, rsr, hb, hbr, hf=None, hfr=None):
    for k in range(8):
        p.act(sq[:, 0:n], xt[:, k, 0:n], AF.Square, r=[xr], w=[sqr])
        p.mm(ps[:, 0:n], ones[:], sq[:, 0:n], k == 0, k == 7, r=[onr, sqr], w=[psr])
    p.act(rstd[:, 0:n], ps[:, 0:n], AF.Sqrt, r=[psr], w=[rsr], scale=1.0 / D, bias=EPS)
    p.op("dve", lambda e, o=rstd[:, 0:n]: e.reciprocal(out=o, in_=o), r=[rsr], w=[rsr])
    for k in range(8):
        p.tt("dve", sq[:, 0:n], xt[:, k, 0:n], rstd[:, 0:n], ALU.mult, r=[xr, rsr, sqr], w=[sqr])
        if hf is not None:
            p.ts("dve", hf[:, k, 0:n], sq[:, 0:n], A[:, k, col:col + 1], ALU.mult, r=[sqr, mr], w=[hfr],
                 s2=SH[:, k, col:col + 1], op1=ALU.add)
            p.cp("act", hb[:, k, 0:n], hf[:, k, 0:n], r=[hfr], w=[hbr])
        else:
            p.ts("dve", hb[:, k, 0:n], sq[:, 0:n], A[:, k, col:col + 1], ALU.mult, r=[sqr, mr], w=[hbr],
                 s2=SH[:, k, col:col + 1], op1=ALU.add)


def norm1_stage(p, xT_d, x_res, NL, NC, mod_d, mod_res, nw_ap, hT_d, h_res):
    with p.scope():
        modt, A, mr = load_mod(p, mod_d, mod_res, nw_ap, 0, 1)
        SH = modt[:, 0:8, :]
        ones = p.sbuf("ones", [128, 128], F32); onr = p.res("ones"); p.memset("dve", ones[:], 1.0, w=[onr])
        xt = [p.sbuf("xt%d" % i, [128, 8, 512], F32) for i in range(2)]; xr = [p.res() for i in range(2)]
        hb = [p.sbuf("hb%d" % i, [128, 8, 512], BF16) for i in range(2)]; hbr = [p.res() for i in range(2)]
        sq = p.sbuf("sq", [128, 512], F32); sqr = p.res(); rstd = p.sbuf("rstd", [128, 512], F32); rsr = p.res()
        ps = p.psum("ps", [128, 512]); psr = p.res()
        for i, (t0, n, isc) in enumerate(ttiles(NL, NC)):
            h = i % 2
            p.dma("sp", xt[h][:, :, 0:n], xT_d[:, t0:t0 + n].rearrange("(k p) t -> p k t", p=128), r=[x_res], w=[xr[h]])
            norm_mod_tile(p, xt[h], xr[h], n, A, SH, 1 if isc else 0, mr, ones, onr, ps, psr, sq, sqr, rstd, rsr, hb[h], hbr[h])
            p.dma("pool", hT_d[i].rearrange("(k p) t -> p k t", p=128), hb[h][:, :, 0:n], r=[hbr[h]], w=[h_res[i]])


def load_w_bf16(p, w_cols_ap, m, wst, wstr, wb, wbr, eng="act", q="sp"):
    p.dma(q, wst[:, :, 0:m], w_cols_ap.rearrange("(k p) m -> p k m", p=128), w=[wstr])
    p.cp(eng, wb[:, :, 0:m], wst[:, :, 0:m], r=[wstr], w=[wbr])


def inproj_gate_stage(p, hT_d, h_res, NL, NC, w_in_ap, b_in_ap, off, nchunk, gT_d, g_res):
    NT = NL + NC
    GW = 4
    with p.scope():
        hT = p.sbuf("hT", [128, 8, NT], BF16); hr = p.res("hT")
        for i, (t0, n, isc) in enumerate(ttiles(NL, NC)):
            p.dma("sp", hT[:, :, t0:t0 + n], hT_d[i].rearrange("(k p) t -> p k t", p=128), r=[h_res[i]], w=[hr])
        bias = p.sbuf("bias", [128, nchunk], F32); br = p.res("bias")
        p.dma("sp", bias[:], b_in_ap[off:off + 128 * nchunk].rearrange("(m p) -> p m", p=128), w=[br], slow=True)
        wst = [p.sbuf("wst%d" % i, [128, 8, 128 * GW], F32) for i in range(2)]; wstr = [p.res() for i in range(2)]
        wb = [p.sbuf("wb%d" % i, [128, 8, 128 * GW], BF16) for i in range(2)]; wbr = [p.res() for i in range(2)]
        ot = [p.sbuf("ot%d" % i, [128, NT], BF16) for i in range(2)]; otr = [p.res() for i in range(2)]
        ps = [p.psum("ps%d" % i, [128, 512]) for i in range(6)]; psr = [p.res() for i in range(6)]
        kk = 0
        ngrp = nchunk // GW
        def load(g):
            h = g % 2
            c0 = off + 128 * GW * g
            p.dma("sp" if h else "pool", wst[h][:], w_in_ap[:, c0:c0 + 128 * GW].rearrange("(k p) m -> p k m", p=128), w=[wstr[h]])
            p.cp("act", wb[h][:], wst[h][:], r=[wstr[h]], w=[wbr[h]])
        load(0)
        for g in range(ngrp):
            h = g % 2
            if g + 1 < ngrp:
                load(g + 1)
            for mi in range(GW):
                m = g * GW + mi
                o = m % 2
                for (t0, n, isc) in ttiles(NL, NC):
                    b_ = kk % 6; kk += 1
                    for k in range(8):
                        p.mm(ps[b_][:, 0:n], wb[h][:, k, 128 * mi:128 * mi + 128], hT[:, k, t0:t0 + n], k == 0, k == 7, r=[wbr[h], hr], w=[psr[b_]])
                    p.act(ot[o][:, t0:t0 + n], ps[b_][:, 0:n], AF.Sigmoid, r=[psr[b_], br], w=[otr[o]], bias=bias[:, m:m + 1])
                p.dma("sp", gT_d[128 * m:128 * m + 128, :], ot[o][:], r=[otr[o]], w=[g_res])


def inproj_mix_stage(p, hall_d, hall_res, NL, NC, w_in_ap, b_in_ap, col_list, zlat_d, zctx_d, z_res):
    NT = NL + NC
    nchunk = len(col_list)
    with p.scope():
        wst = p.sbuf("wst", [128, 8, 128], F32); wstr = p.res()
        W = p.sbuf("W", [128, nchunk, 8, 128], BF16); Wr = p.res("W")
        bias = p.sbuf("bias", [128, nchunk], F32); br = p.res("bias")
        p.memset("dve", bias[:], 0.0, w=[br])
        p.memset("dve", W[:], 0.0, w=[Wr])
        for i, (c0, m) in enumerate(col_list):
            p.dma("sp", wst[:, :, 0:m], w_in_ap[:, c0:c0 + m].rearrange("(k p) m -> p k m", p=128), w=[wstr], slow=(m < 128))
            p.cp("act" if i % 2 else "dve", W[:, i, :, 0:m], wst[:, :, 0:m], r=[wstr], w=[Wr])
            p.dma("pool", bias[0:m, i:i + 1], b_in_ap[c0:c0 + m].rearrange("(a b) -> a b", b=1), w=[br])
        hT = [p.sbuf("hT%d" % i, [128, 8, 512], BF16) for i in range(2)]; hr = [p.res() for i in range(2)]
        ot = [p.sbuf("ot%d" % i, [128, nchunk, 512], F32) for i in range(2)]; otr = [p.res() for i in range(2)]
        ps = [p.psum("ps%d" % i, [128, 512]) for i in range(4)]; psr = [p.res() for i in range(4)]
        kk = 0; ti = 0
        for r in range(4):
            for i_t, (t0, n, isc) in enumerate(ttiles(NL, NC)):
                h = ti % 2; ti += 1
                p.dma("sp", hT[h][:, :, 0:n], hall_d[i_t][1024 * r:1024 * r + 1024, :].rearrange("(k p) t -> p k t", p=128),
                      r=[hall_res[i_t]], w=[hr[h]])
                for i in range(nchunk):
                    b_ = kk % 4; kk += 1
                    for k in range(8):
                        p.mm(ps[b_][:, 0:n], W[:, i, k, :], hT[h][:, k, 0:n], k == 0, k == 7, r=[Wr, hr[h]], w=[psr[b_]])
                    p.act(ot[h][:, i, 0:n], ps[b_][:, 0:n], AF.Identity, r=[psr[b_], br], w=[otr[h]], bias=bias[:, i:i + 1])
                if isc:
                    dst = zctx_d[:, :, NC * r:NC * r + n]
                else:
                    dst = zlat_d[:, :, NL * r + t0:NL * r + t0 + n]
                p.dma("pool", dst.rearrange("i p t -> p i t"), ot[h][:, :, 0:n], r=[otr[h]], w=[z_res])


def yasm_stage(p, srcs, skip_ap, y_own_d, y_res, LL, LC):
    with p.scope():
        sk = p.sbuf("sk", [128, 1], F32); skr = p.res("sk")
        p.dma("sp", sk[:], skip_ap.rearrange("(a b) -> a b", b=1), w=[skr])
        CW = 2048
        A = [p.sbuf("A%d" % i, [128, CW], F32) for i in range(3)]; Ar = [p.res() for i in range(3)]
        O = [p.sbuf("O%d" % i, [128, CW], BF16) for i in range(2)]; Or = [p.res() for i in range(2)]
        kk = 0
        pieces = [(0, t0, min(CW, LL - t0), t0) for t0 in range(0, LL, CW)] + ([(1, 0, LC, LL)] if LC else [])
        for (which, t0, n, o0) in pieces:
            for i, nm in enumerate(("z", "c2", "x2")):
                ap = srcs[nm][which]
                p.dma("sp", A[i][:, 0:n], ap[:, t0:t0 + n], r=[srcs[nm][2]], w=[Ar[i]])
            p.stt(A[0][:, 0:n], A[0][:, 0:n], sk[:, 0:1], A[1][:, 0:n], ALU.mult, ALU.add, r=[Ar[0], Ar[1], skr], w=[Ar[0]])
            h = kk % 2; kk += 1
            p.tt("dve", O[h][:, 0:n], A[0][:, 0:n], A[2][:, 0:n], ALU.mult, r=[Ar[0], Ar[2]], w=[Or[h]])
            for (ap_, rs_, a0, an) in y_own_d(0, o0, n):
                p.dma("pool", ap_, O[h][:, a0:a0 + an], r=[Or[h]], w=[rs_])
            for bi, nm in ((1, "fn"), (2, "ml")):
                ap = srcs[nm][which]
                p.dma("sp", A[bi][:, 0:n], ap[:, t0:t0 + n], r=[srcs[nm][2]], w=[Ar[bi]])
                h = kk % 2; kk += 1
                p.cp("act", O[h][:, 0:n], A[bi][:, 0:n], r=[Ar[bi]], w=[Or[h]])
                for (ap_, rs_, a0, an) in y_own_d(bi, o0, n):
                    p.dma("pool", ap_, O[h][:, a0:a0 + an], r=[Or[h]], w=[rs_])


def merge_stage(p, y_all_d, y_res, gT_d, g_res, oh_ap, xT_d, x_res, NL, NC, mod_d, mod_res, wbr_ap, wout_ap, do_ctx):
    LL = 4 * NL
    with p.scope():
        modt = p.sbuf("modt", [128, 48, 2], F32); mr = p.res("modt")
        p.dma("sp", modt[:], mod_d, r=[mod_res], w=[mr], slow=True)
        oh = p.sbuf("oh", [128, 4], F32); ohr = p.res("oh")
        p.dma("sp", oh[:], oh_ap, w=[ohr])
        wst = p.sbuf("wst", [128, 8, 1024], F32); wstr = p.res()
        WB = p.sbuf("WB", [128, 12, 1024], BF16); WO = p.sbuf("WO", [128, 8, 1024], BF16); Wr = p.res("W")
        for br in range(3):
            p.dma("sp", wst[:, 0:4, :], wbr_ap[br].rearrange("(r p) d -> p r d", p=128), w=[wstr])
            p.cp("act" if br % 2 else "dve", WB[:, 4 * br:4 * br + 4, :], wst[:, 0:4, :], r=[wstr], w=[Wr])
        p.dma("sp", wst[:], wout_ap.rearrange("(k p) d -> p k d", p=128), w=[wstr])
        p.cp("act", WO[:], wst[:], r=[wstr], w=[Wr])
        Yc = [p.sbuf("Yc%d" % i, [128, 12, 512], BF16) for i in range(2)]; Ycr = [p.res() for i in range(2)]
        Y = p.sbuf("Y", [128, 12, 512], BF16); Yr = p.res("Y")
        Gt = p.sbuf("Gt", [128, 24, 512], BF16); Gr = p.res("G")
        xt = p.sbuf("xt", [128, 8, 512], F32); xr = p.res("xt")
        mg = p.sbuf("mg", [128, 8, 512], BF16); mgr = p.res("mg")
        t1 = p.sbuf("t1", [128, 512], F32); t1r = p.res(); t2 = p.sbuf("t2", [128, 512], F32); t2r = p.res()
        ps = [p.psum("ps%d" % i, [128, 512]) for i in range(6)]; psr = [p.res() for i in range(6)]
        kk = 0
        tiles = ttiles(NL, NC if do_ctx else 0)
        for (t0, n, isc) in tiles:
            col = 1 if isc else 0
            for jj in range(4):
                c0 = (LL + NC * jj) if isc else (NL * jj + t0)
                h = jj % 2
                for br in range(3):
                    ap_, rs_ = y_all_d(br, c0, n)
                    p.dma("sp" if br % 2 else "pool", Yc[h][:, 4 * br:4 * br + 4, 0:n],
                          ap_.rearrange("(q p) t -> p q t", p=128), r=[rs_], w=[Ycr[h]])
                if jj == 0:
                    p.ts("dve", Y[:, :, 0:n], Yc[h][:, :, 0:n], oh[:, 0:1], ALU.mult, r=[Ycr[h], ohr], w=[Yr])
                else:
                    p.stt(Y[:, :, 0:n], Yc[h][:, :, 0:n], oh[:, jj:jj + 1], Y[:, :, 0:n], ALU.mult, ALU.add, r=[Ycr[h], ohr, Yr], w=[Yr])
            p.dma("sp", Gt[:, :, 0:n], gT_d[:, t0:t0 + n].rearrange("(q p) t -> p q t", p=128), r=[g_res], w=[Gr])
            p.dma("pool", xt[:, :, 0:n], xT_d[:, t0:t0 + n].rearrange("(k p) t -> p k t", p=128), r=[x_res], w=[xr])
            for m in range(8):
                pb = []
                for br in range(3):
                    b_ = kk % 6; kk += 1; pb.append(b_)
                    for r in range(4):
                        p.mm(ps[b_][:, 0:n], WB[:, 4 * br + r, 128 * m:128 * m + 128], Y[:, 4 * br + r, 0:n], r == 0, r == 3,
                             r=[Wr, Yr], w=[psr[b_]])
                p.tt("dve", t1[:, 0:n], ps[pb[0]][:, 0:n], Gt[:, m, 0:n], ALU.mult, r=[psr[pb[0]], Gr], w=[t1r])
                p.tt("dve", t2[:, 0:n], ps[pb[1]][:, 0:n], Gt[:, 8 + m, 0:n], ALU.mult, r=[psr[pb[1]], Gr], w=[t2r])
                p.tt("dve", t1[:, 0:n], t1[:, 0:n], t2[:, 0:n], ALU.add, r=[t1r, t2r], w=[t1r])
                p.tt("dve", t2[:, 0:n], ps[pb[2]][:, 0:n], Gt[:, 16 + m, 0:n], ALU.mult, r=[psr[pb[2]], Gr], w=[t2r])
                p.tt("dve", mg[:, m, 0:n], t1[:, 0:n], t2[:, 0:n], ALU.add, r=[t1r, t2r], w=[mgr])
            for m in range(8):
                b_ = kk % 6; kk += 1
                for k in range(8):
                    p.mm(ps[b_][:, 0:n], WO[:, k, 128 * m:128 * m + 128], mg[:, k, 0:n], k == 0, k == 7, r=[Wr, mgr], w=[psr[b_]])
                p.stt(xt[:, m, 0:n], ps[b_][:, 0:n], modt[:, 16 + m, col:col + 1], xt[:, m, 0:n], ALU.mult, ALU.add,
                      r=[psr[b_], mr, xr], w=[xr])
            p.dma("sp", xT_d[:, t0:t0 + n].rearrange("(k p) t -> p k t", p=128), xt[:, :, 0:n], r=[xr], w=[x_res])


def moe_stage(p, xT_d, x_res, NL, NC, mod_d, mod_res, nw_ap, wr_ap, br_ap, wg_ap, wu_ap, wd_ap, do_ctx,
              final_nw_ap=None, out_d=None, out_res=None):
    NCX = NC if do_ctx else 0
    NT = NL + NCX
    tiles = ttiles(NL, NCX)
    nsub = (NT + 127) // 128
    with p.scope():
        modt, A, mr = load_mod(p, mod_d, mod_res, nw_ap, 3, 4)
        SH = modt[:, 24:32, :]
        ones = p.sbuf("ones", [128, 128], F32); onr = p.res("ones"); p.memset("dve", ones[:], 1.0, w=[onr])
        identf, idr = make_ident(p, F32, "idf")
        sq = p.sbuf("sq", [128, 512], F32); sqr = p.res(); rstd = p.sbuf("rstd", [128, 512], F32); rsr = p.res()
        xt = p.sbuf("xt", [128, 8, 512], F32); xr = p.res("xt")
        hf = p.sbuf("hf", [128, 8, 512], F32); hfr = p.res("hf")
        H2 = p.sbuf("H2", [128, 8, NT], BF16); h2r = p.res("H2")
        WR = p.sbuf("WR", [128, 8, 20], F32); wrr = p.res("WR")
        p.dma("sp", WR[:], wr_ap.rearrange("(k p) n -> p k n", p=128), w=[wrr], slow=True)
        BR = p.sbuf("BR", [128, 20], F32)
        p.dma("sp", BR[:], br_ap.partition_broadcast(128), w=[wrr])
        CWt = p.sbuf("CWt", [128, nsub, 16], F32); cwr = p.res("CW")
        ps = [p.psum("ps%d" % i, [128, 512]) for i in range(6)]; psr = [p.res() for i in range(6)]
        pr_ = p.psum("pr", [128, 32]); prr = p.res("pr")
        def st(nm, w):
            return p.sbuf(nm, [128, w], F32)
        L_ = st("L", 20); gm = st("gm", 1); ge = st("ge", 4); gs = st("gs", 1); gmask = st("gmask", 4)
        tmp16 = st("tmp16", 16); eg = st("eg", 4); m1 = st("m1", 1); mk1 = st("mk1", 4); eg2 = st("eg2", 4); m2 = st("m2", 1)
        mk2 = st("mk2", 4); w1 = st("w1", 1); w2 = st("w2", 1); cwe = st("cwe", 4)
        rr = p.res("route")
        for (t0, n, isc) in tiles:
            col = 1 if isc else 0
            p.dma("sp", xt[:, :, 0:n], xT_d[:, t0:t0 + n].rearrange("(k p) t -> p k t", p=128), r=[x_res], w=[xr])
            norm_mod_tile(p, xt, xr, n, A, SH, col, mr, ones, onr, ps[0], psr[0], sq, sqr, rstd, rsr,
                          H2[:, :, t0:t0 + n], h2r, hf=hf, hfr=hfr)
            for s0 in range(0, n, 128):
                sn = min(128, n - s0); si = (t0 + s0) // 128
                for k in range(8):
                    p.mm(pr_[0:sn, 0:20], hf[:, k, s0:s0 + sn], WR[:, k, :], k == 0, k == 7, r=[hfr, wrr], w=[prr])
                R = [rr]
                p.tt("dve", L_[0:sn], pr_[0:sn, 0:20], BR[0:sn], ALU.add, r=[prr, wrr, rr], w=R)
                p.op("dve", lambda e, o=gm[0:sn], i=L_[0:sn, 0:4]: e.tensor_reduce(out=o, in_=i, axis=AX.X, op=ALU.max), r=R, w=R)
                p.ts("dve", gmask[0:sn], L_[0:sn, 0:4], gm[0:sn, 0:1], ALU.is_equal, r=R, w=R)
                p.ts("dve", gm[0:sn], gm[0:sn], -1.0, ALU.mult, r=R, w=R)
                p.act(ge[0:sn], L_[0:sn, 0:4], AF.Exp, r=R, w=R, bias=gm[0:sn, 0:1])
                p.op("dve", lambda e, o=gs[0:sn], i=ge[0:sn]: e.tensor_reduce(out=o, in_=i, axis=AX.X, op=ALU.add), r=R, w=R)
                p.op("dve", lambda e, o=gs[0:sn]: e.reciprocal(out=o, in_=o), r=R, w=R)
                p.tt("dve", tmp16[0:sn].rearrange("p (g e) -> p g e", e=4), L_[0:sn, 4:20].rearrange("p (g e) -> p g e", e=4),
                     gmask[0:sn, :, None].to_broadcast([sn, 4, 4]), ALU.mult, r=R, w=R)
                p.op("dve", lambda e, o=eg[0:sn], i=tmp16[0:sn].rearrange("p (g e) -> p e g", e=4): e.tensor_reduce(out=o, in_=i, axis=AX.X, op=ALU.add), r=R, w=R)
                p.op("dve", lambda e, o=m1[0:sn], i=eg[0:sn]: e.tensor_reduce(out=o, in_=i, axis=AX.X, op=ALU.max), r=R, w=R)
                p.ts("dve", mk1[0:sn], eg[0:sn], m1[0:sn, 0:1], ALU.is_equal, r=R, w=R)
                p.stt(eg2[0:sn], mk1[0:sn], -1e30, eg[0:sn], ALU.mult, ALU.add, r=R, w=R)
                p.op("dve", lambda e, o=m2[0:sn], i=eg2[0:sn]: e.tensor_reduce(out=o, in_=i, axis=AX.X, op=ALU.max), r=R, w=R)
                p.ts("dve", mk2[0:sn], eg2[0:sn], m2[0:sn, 0:1], ALU.is_equal, r=R, w=R)
                p.tt("dve", w1[0:sn], m2[0:sn], m1[0:sn], ALU.subtract, r=R, w=R)
                p.act(w1[0:sn], w1[0:sn], AF.Exp, r=R, w=R)
                p.ts("dve", w1[0:sn], w1[0:sn], 1.0, ALU.add, r=R, w=R)
                p.op("dve", lambda e, o=w1[0:sn]: e.reciprocal(out=o, in_=o), r=R, w=R)
                p.ts("dve", w2[0:sn], w1[0:sn], -1.0, ALU.mult, r=R, w=R, s2=1.0, op1=ALU.add)
                p.tt("dve", w1[0:sn], w1[0:sn], gs[0:sn], ALU.mult, r=R, w=R)
                p.tt("dve", w2[0:sn], w2[0:sn], gs[0:sn], ALU.mult, r=R, w=R)
                p.ts("dve", cwe[0:sn], mk1[0:sn], w1[0:sn, 0:1], ALU.mult, r=R, w=R)
                p.stt(cwe[0:sn], mk2[0:sn], w2[0:sn, 0:1], cwe[0:sn], ALU.mult, ALU.add, r=R, w=R)
                p.cp("dve", tmp16[0:sn].rearrange("p (g e) -> p g e", e=4), cwe[0:sn, None, :].to_broadcast([sn, 4, 4]), r=R, w=R)
                p.tt("dve", CWt[0:sn, si, :].rearrange("p (g e) -> p g e", e=4), tmp16[0:sn].rearrange("p (g e) -> p g e", e=4),
                     gmask[0:sn, :, None].to_broadcast([sn, 4, 4]), ALU.mult, r=R, w=[cwr, rr])
        ACC = p.sbuf("ACC", [128, nsub, 1024], F32); accr = p.res("ACC")
        p.memset("dve", ACC[:], 0.0, w=[accr])
        wst = [p.sbuf("wst%d" % i, [128, 8, 256], F32) for i in range(2)]; wstr = [p.res() for i in range(2)]
        WG = [p.sbuf("WG%d" % i, [128, 8, 256], BF16) for i in range(2)]; WU = [p.sbuf("WU%d" % i, [128, 8, 256], BF16) for i in range(2)]
        WD = [p.sbuf("WD%d" % i, [128, 2, 1024], BF16) for i in range(2)]
        wer = [p.res() for i in range(2)]
        SG = p.sbuf("SG", [128, 512], BF16); sgr = p.res()
        AA = p.sbuf("AA", [128, 2, 512], BF16); aar = p.res()
        kk = 0
        for e_ in range(16):
            h = e_ % 2
            p.dma("sp", wst[0][:], wg_ap[e_].rearrange("(k p) m -> p k m", p=128), w=[wstr[0]])
            p.cp("act", WG[h][:], wst[0][:], r=[wstr[0]], w=[wer[h]])
            p.dma("pool", wst[1][:], wu_ap[e_].rearrange("(k p) m -> p k m", p=128), w=[wstr[1]])
            p.cp("act", WU[h][:], wst[1][:], r=[wstr[1]], w=[wer[h]])
            p.dma("sp", wst[0][:].rearrange("p a b -> p (a b)").rearrange("p (c d) -> p c d", c=2), wd_ap[e_].rearrange("(c p) d -> p c d", p=128), w=[wstr[0]])
            p.cp("act", WD[h][:], wst[0][:].rearrange("p a b -> p (a b)").rearrange("p (c d) -> p c d", c=2), r=[wstr[0]], w=[wer[h]])
            for (t0, n, isc) in tiles:
                for hc in range(2):
                    bg = kk % 6; kk += 1; bu = kk % 6; kk += 1
                    for k in range(8):
                        p.mm(ps[bg][:, 0:n], WG[h][:, k, 128 * hc:128 * hc + 128], H2[:, k, t0:t0 + n], k == 0, k == 7, r=[wer[h], h2r], w=[psr[bg]])
                    for k in range(8):
                        p.mm(ps[bu][:, 0:n], WU[h][:, k, 128 * hc:128 * hc + 128], H2[:, k, t0:t0 + n], k == 0, k == 7, r=[wer[h], h2r], w=[psr[bu]])
                    p.act(SG[:, 0:n], ps[bg][:, 0:n], AF.Silu, r=[psr[bg]], w=[sgr])
                    p.tt("dve", AA[:, hc, 0:n], ps[bu][:, 0:n], SG[:, 0:n], ALU.mult, r=[psr[bu], sgr], w=[aar])
                for s0 in range(0, n, 128):
                    sn = min(128, n - s0); si = (t0 + s0) // 128
                    for dh in range(2):
                        b_ = kk % 6; kk += 1
                        for hc in range(2):
                            p.mm(ps[b_][0:sn, :], AA[:, hc, s0:s0 + sn], WD[h][:, hc, 512 * dh:512 * dh + 512], hc == 0, hc == 1,
                                 r=[aar, wer[h]], w=[psr[b_]])
                        p.stt(ACC[0:sn, si, 512 * dh:512 * dh + 512], ps[b_][0:sn, :], CWt[0:sn, si, e_:e_ + 1],
                              ACC[0:sn, si, 512 * dh:512 * dh + 512], ALU.mult, ALU.add, r=[psr[b_], cwr, accr], w=[accr])
        if final_nw_ap is not None:
            fw_ = p.sbuf("fw", [128, 8], F32); fwr = p.res()
            p.dma("sp", fw_[:], final_nw_ap.rearrange("(k p) -> p k", p=128), w=[fwr], slow=True)
        for (t0, n, isc) in tiles:
            col = 1 if isc else 0
            p.dma("sp", xt[:, :, 0:n], xT_d[:, t0:t0 + n].rearrange("(k p) t -> p k t", p=128), r=[x_res], w=[xr])
            for m in range(8):
                b_ = kk % 6; kk += 1
                for s0 in range(0, n, 128):
                    sn = min(128, n - s0); si = (t0 + s0) // 128
                    p.tr(ps[b_][:, s0:s0 + sn], ACC[0:sn, si, 128 * m:128 * m + 128], identf[0:sn, 0:sn], r=[accr, idr], w=[psr[b_]])
                p.stt(xt[:, m, 0:n], ps[b_][:, 0:n], modt[:, 40 + m, col:col + 1], xt[:, m, 0:n], ALU.mult, ALU.add,
                      r=[psr[b_], mr, xr], w=[xr])
            if final_nw_ap is None:
                p.dma("pool", xT_d[:, t0:t0 + n].rearrange("(k p) t -> p k t", p=128), xt[:, :, 0:n], r=[xr], w=[x_res])
            elif not isc:
                for k in range(8):
                    p.act(sq[:, 0:n], xt[:, k, 0:n], AF.Square, r=[xr], w=[sqr])
                    p.mm(ps[0][:, 0:n], ones[:], sq[:, 0:n], k == 0, k == 7, r=[onr, sqr], w=[psr[0]])
                p.act(rstd[:, 0:n], ps[0][:, 0:n], AF.Sqrt, r=[psr[0]], w=[rsr], scale=1.0 / D, bias=EPS)
                p.op("dve", lambda e, o=rstd[:, 0:n]: e.reciprocal(out=o, in_=o), r=[rsr], w=[rsr])
                for k in range(8):
                    p.stt(hf[:, k, 0:n], xt[:, k, 0:n], fw_[:, k:k + 1], rstd[:, 0:n], ALU.mult, ALU.mult, r=[xr, fwr, rsr], w=[hfr])
                p.dma("pool", out_d[:, t0:t0 + n].rearrange("(k p) t -> p k t", p=128), hf[:, :, 0:n], r=[hfr], w=[out_res])

NLAT, NCTX = 2048, 64
LLAT, LCTX = 8192, 256
GROUPS = [[0, 1, 2, 3], [4, 5, 6, 7]]
DEPTH = 2
OFF_FN, OFF_ML, OFF_MLG, OFF_GATE = 1536, 2048, 4096, 4112


def build_program(const_np):
    p = Prog()
    I = {}

    def inp(name, shape, dt=F32):
        I[name] = p.dram(name, shape, dt, "ExternalInput")
        return I[name]

    inp("xT0", [1024, NLAT]); inp("cT0", [1024, NCTX]); inp("cv", [1024, 2]); inp("oh", [128, 4]); inp("norm_f", [1024])
    for k, v in const_np.items():
        inp(k, v.shape, F32 if v.dtype == np.float32 else BF16)
    for l in range(DEPTH):
        L = "_%d" % l
        inp("ada_w" + L, [1024, 1536]); inp("ada_b" + L, [1536]); inp("n1w" + L, [1024]); inp("n2w" + L, [1024])
        inp("w_gate_in" + L, [1024, 3072]); inp("b_gate_in" + L, [3072]); inp("w_mix" + L, [1024, 1028]); inp("b_mix" + L, [1028])
        inp("hy_cw" + L, [3, 3, 384]); inp("hy_cb" + L, [384]); inp("ml_cw" + L, [3, 3, 256]); inp("ml_cb" + L, [256])
        inp("f_w1" + L, [33, 64]); inp("f_b1" + L, [64]); inp("f_freq" + L, [64]); inp("f_w2" + L, [64, 64]); inp("f_b2" + L, [64])
        inp("f_w3" + L, [64, 4, 128]); inp("decay" + L, [128, 4]); inp("skip" + L, [2, 128]); inp("mlnw" + L, [128])
        inp("wbr" + L, [3, 512, 1024]); inp("wout" + L, [1024, 1024])
        inp("wr" + L, [1024, 20]); inp("br" + L, [20]); inp("wg" + L, [16, 1024, 256]); inp("wu" + L, [16, 1024, 256]); inp("wd" + L, [16, 256, 1024])
    outT = p.dram("outT", [1024, NLAT], F32, "ExternalOutput"); out_res = p.res("outT")

    NT = NLAT + NCTX
    S = {}

    def scr(name, shape, dt=F32):
        S[name] = (p.dram("s_" + name, shape, dt), p.res("s_" + name))
        return S[name]

    scr("mod_own", [128, 12, 2]); scr("mod_all", [4 * 128, 24]); scr("mod_full", [128, 48, 2])
    scr("xT", [1024, NT]); scr("gT", [3072, NT], BF16)
    TT = ttiles(NLAT, NCTX)
    hown = [scr("hown%d" % i, [1024, n], BF16) for i, (t0, n, isc) in enumerate(TT)]
    hall = [scr("hall%d" % i, [4096, n], BF16) for i, (t0, n, isc) in enumerate(TT)]
    YCH = [(0, 3072), (3072, 3072), (6144, 2304)]
    yown = [[scr("yown%d_%d" % (br, ck), [128, w_], BF16) for ck, (c0_, w_) in enumerate(YCH)] for br in range(3)]
    yall = [[scr("yall%d_%d" % (br, ck), [512, w_], BF16) for ck, (c0_, w_) in enumerate(YCH)] for br in range(3)]

    def y_own_fn(br, col0, n):
        out = []
        for ck, (c0_, w_) in enumerate(YCH):
            lo = max(col0, c0_); hi = min(col0 + n, c0_ + w_)
            if hi > lo:
                out.append((yown[br][ck][0][:, lo - c0_:hi - c0_], yown[br][ck][1], lo - col0, hi - lo))
        return out

    def y_all_fn(br, col0, n):
        for ck, (c0_, w_) in enumerate(YCH):
            if c0_ <= col0 and col0 + n <= c0_ + w_:
                return yall[br][ck][0][:, col0 - c0_:col0 - c0_ + n], yall[br][ck][1]
        raise AssertionError("y tile straddles chunks")
    scr("zlat", [9, 128, LLAT]); scr("zctx", [9, 128, LCTX]); scr("cvl", [5, 128, LLAT]); scr("cvc", [5, 128, LCTX])
    scr("fl", [4, 128, LLAT]); scr("fc", [4, 128, LCTX])
    for nm in ("c1", "c2", "zz", "fn", "ml"):
        scr(nm + "l", [128, LLAT]); scr(nm + "c", [128, LCTX])

    tabs = {k: I[k] for k in const_np}
    xT, xres = S["xT"]
    with p.scope():
        t = p.sbuf("t", [128, 8, NT], F32); tr = p.res()
        p.dma("sp", t[:, :, 0:NLAT], I["xT0"].rearrange("(k p) t -> p k t", p=128), w=[tr])
        p.dma("pool", t[:, :, NLAT:NT], I["cT0"].rearrange("(k p) t -> p k t", p=128), w=[tr])
        p.dma("sp", xT.rearrange("(k p) t -> p k t", p=128), t[:], r=[tr], w=[xres])

    mod_view = S["mod_all"][0].rearrange("(r p) (m n) -> p r m n", p=128, n=2)

    for l in range(DEPTH):
        L = "_%d" % l
        last = (l == DEPTH - 1)
        W = lambda nm: I[nm + L]
        p.label = 'mod_stage'; mod_stage(p, I["cv"], W("ada_w"), W("ada_b"), S["mod_own"][0], S["mod_own"][1], NM=12)
        p.coll("AllGather", S["mod_all"][0], S["mod_own"][0].rearrange("p m n -> p (m n)"), GROUPS, r=[S["mod_own"][1]], w=[S["mod_all"][1]])
        with p.scope():
            mt = p.sbuf("mt", [128, 48, 2], F32); mtr = p.res()
            p.dma("sp", mt[:].rearrange("p (r m) n -> p r m n", r=4), mod_view, r=[S["mod_all"][1]], w=[mtr], slow=True)
            p.dma("sp", S["mod_full"][0], mt[:], r=[mtr], w=[S["mod_full"][1]])
        modv, modr = S["mod_full"]
        p.label = 'norm1_stage'; norm1_stage(p, xT, xres, NLAT, NCTX, modv, modr, W("n1w"), [h_[0] for h_ in hown], [h_[1] for h_ in hown])
        for i in range(len(TT)):
            p.coll("AllGather", hall[i][0], hown[i][0], GROUPS, r=[hown[i][1]], w=[hall[i][1]])
        p.label = 'inproj_gate_stage'; inproj_gate_stage(p, [h_[0] for h_ in hown], [h_[1] for h_ in hown], NLAT, NCTX, W("w_gate_in"), W("b_gate_in"), 0, 24, S["gT"][0], S["gT"][1])
        cols = [(128 * i, 128) for i in range(8)] + [(1024, 4)]
        p.label = 'inproj_mix_stage'; inproj_mix_stage(p, [h_[0] for h_ in hall], [h_[1] for h_ in hall], NLAT, NCTX, W("w_mix"), W("b_mix"), cols, S["zlat"][0], S["zctx"][0], S["zlat"][1])
        zl, zc, zr = S["zlat"][0], S["zctx"][0], S["zlat"][1]
        cvl, cvc = S["cvl"][0], S["cvc"][0]
        cvr = S["cvl"][1]
        jl = [(zl[0], zr, W("hy_cw"), W("hy_cb"), 0, False, cvl[0], cvr), (zl[1], zr, W("hy_cw"), W("hy_cb"), 128, False, cvl[1], cvr),
              (zl[2], zr, W("hy_cw"), W("hy_cb"), 256, False, cvl[2], cvr), (zl[4], zr, W("ml_cw"), W("ml_cb"), 0, True, cvl[3], cvr),
              (zl[5], zr, W("ml_cw"), W("ml_cb"), 128, True, cvl[4], cvr)]
        p.label = 'conv_stage'; conv_stage(p, jl, 128, 64)
        jc = [(zc[4], zr, W("ml_cw"), W("ml_cb"), 0, True, cvc[3], cvr), (zc[5], zr, W("ml_cw"), W("ml_cb"), 128, True, cvc[4], cvr)]
        if not last:
            jc += [(zc[0], zr, W("hy_cw"), W("hy_cb"), 0, False, cvc[0], cvr), (zc[1], zr, W("hy_cw"), W("hy_cb"), 128, False, cvc[1], cvr),
                   (zc[2], zr, W("hy_cw"), W("hy_cb"), 256, False, cvc[2], cvr)]
        p.label = 'conv_stage'; conv_stage(p, jc, 1, 256)
        fw = {k: W(k) for k in ("f_w1", "f_b1", "f_freq", "f_w2", "f_b2", "f_w3", "decay")}
        variants = [("l", LLAT, cvl, "featsT_l", "trow_l")] + ([] if last else [("c", LCTX, cvc, "featsT_c", "trow_c")])
        for (sfx, Lx, cvx, fnm, tnm) in variants:
            filt, fr = S["f" + sfx]
            p.label = 'hfilt_stage'; hfilt_stage(p, I[fnm], I[tnm], fw, 0, Lx, filt, fr)
            src1 = {"v": (cvx[0], cvr), "kf": (filt[0], fr), "kg": (filt[1], fr)}
            p.label = 'hconv_stage'; hconv_stage(p, tabs, src1, S["c1" + sfx][0], S["c1" + sfx][1], 128, Lx, True)
            src2 = {"v": (cvx[0], cvr), "kf": (filt[2], fr), "kg": (filt[3], fr), "x1": (cvx[1], cvr),
                    "c1": S["c1" + sfx], "skip": (W("skip")[0], p.res())}
            p.label = 'hconv_stage'; hconv_stage(p, tabs, src2, S["c2" + sfx][0], S["c2" + sfx][1], 128, Lx, False, u_out=S["zz" + sfx][0], u_res=S["zz" + sfx][1])
            zsrc = zl if sfx == "l" else zc
            p.label = 'fnet_stage'
            if sfx == "l":
                fnet_stage(p, tabs, zsrc[3], zr, S["fn" + sfx][0], S["fn" + sfx][1], Lx)
            else:
                fnet_ctx_stage(p, tabs, zsrc[3], zr, S["fn" + sfx][0], S["fn" + sfx][1])
        msrc = {"qT": (cvl[3], cvc[3], cvr), "kT": (cvl[4], cvc[4], cvr), "vT": (zl[6], zc[6], zr), "oT": (zl[7], zc[7], zr),
                "g": (zl[8][0:4], zc[8][0:4], zr)}
        p.label = 'mlstm_stage'; mlstm_stage(p, msrc, S["mll"][0], S["mlc"][0], S["mll"][1], W("mlnw"), 0)
        S["mlc"] = (S["mlc"][0], S["mll"][1])
        ys = {"x2": (cvl[2], cvc[2], cvr)}
        for nm, key in (("c2", "c2"), ("z", "zz"), ("fn", "fn"), ("ml", "ml")):
            ys[nm] = (S[key + "l"][0], S[key + "c"][0], S[key + "l"][1])
        p.label = 'yasm_stage'; yasm_stage(p, ys, W("skip")[1], y_own_fn, None, LLAT, LCTX if not last else 0)
        for br in range(3):
            for ck in range(len(YCH)):
                p.coll("AllGather", yall[br][ck][0], yown[br][ck][0], GROUPS, r=[yown[br][ck][1]], w=[yall[br][ck][1]])
        p.label = 'merge_stage'; merge_stage(p, y_all_fn, None, S["gT"][0], S["gT"][1], I["oh"], xT, xres, NLAT, NCTX, modv, modr,
                    W("wbr"), W("wout"), not last)
        if last:
            p.label = 'moe_stage'; moe_stage(p, xT, xres, NLAT, NCTX, modv, modr, W("n2w"), W("wr"), W("br"), W("wg"), W("wu"), W("wd"), False,
                      I["norm_f"], outT, out_res)
        else:
            p.label = 'moe_stage'; moe_stage(p, xT, xres, NLAT, NCTX, modv, modr, W("n2w"), W("wr"), W("br"), W("wg"), W("wu"), W("wd"), True)
    return p.finish(), p


_CACHE = {}


def _consts():
    c = {}
    c.update(hconv_tables()); c.update(fnet_tables())
    fl, tl = hfilt_consts(LLAT); fc, tc = hfilt_consts(LCTX)
    c["featsT_l"] = fl; c["trow_l"] = tl; c["featsT_c"] = fc; c["trow_c"] = tc
    return c


def kernel(x, c, ctx, c_ctx, ada_w, ada_b, norm1_w, norm2_w, w_in, b_in,
           hy_conv_w, hy_conv_b, hy_f_w1, hy_f_b1, hy_f_w2, hy_f_b2, hy_f_w3, hy_f_freq,
           hy_decay, hy_skip, ml_conv_w, ml_conv_b, ml_norm_w, w_branch, w_out,
           moe_rg_w, moe_rg_b, moe_re_w, moe_re_b, moe_w_gate, moe_w_up, moe_w_down, norm_f_w):
    f32 = lambda a: np.ascontiguousarray(np.asarray(a, dtype=np.float32))
    x, c, ctx, c_ctx = f32(x), f32(c), f32(ctx), f32(c_ctx)
    if "nc" not in _CACHE:
        _CACHE["const"] = _consts()
        _CACHE["nc"] = build_program(_CACHE["const"])[0]
    const = _CACHE["const"]
    nc = _CACHE["nc"]
    in_maps = []
    for core in range(8):
        b, j = core // 4, core % 4
        m = dict(const)
        m["xT0"] = f32(x[b, NLAT * j:NLAT * (j + 1), :].T)
        m["cT0"] = f32(ctx[b, NCTX * j:NCTX * (j + 1), :].T)
        m["cv"] = f32(np.stack([c[b], c_ctx], axis=1))
        oh = np.zeros((128, 4), np.float32); oh[:, j] = 1.0
        m["oh"] = oh
        m["norm_f"] = f32(norm_f_w)
        sl = slice(128 * j, 128 * j + 128)
        for l in range(DEPTH):
            L = "_%d" % l
            m["ada_w" + L] = f32(ada_w[l][:, 1536 * j:1536 * (j + 1)]); m["ada_b" + L] = f32(ada_b[l][1536 * j:1536 * (j + 1)])
            m["n1w" + L] = f32(norm1_w[l]); m["n2w" + L] = f32(norm2_w[l])
            m["w_gate_in" + L] = f32(w_in[l][:, OFF_GATE:]); m["b_gate_in" + L] = f32(b_in[l][OFF_GATE:])
            mixcols = np.concatenate([np.arange(128 * j, 128 * j + 128) + o for o in
                                      (0, 512, 1024, OFF_FN, OFF_ML, OFF_ML + 512, OFF_ML + 1024, OFF_ML + 1536)]
                                     + [np.array([OFF_MLG + j, OFF_MLG + 4 + j, OFF_MLG + 8 + j, OFF_MLG + 12 + j])])
            m["w_mix" + L] = f32(w_in[l][:, mixcols]); m["b_mix" + L] = f32(b_in[l][mixcols])
            hyc = np.concatenate([np.arange(128 * j, 128 * j + 128) + o for o in (0, 512, 1024)])
            m["hy_cw" + L] = f32(hy_conv_w[l][:, :, hyc]); m["hy_cb" + L] = f32(hy_conv_b[l][hyc])
            mlc = np.concatenate([np.arange(128 * j, 128 * j + 128) + o for o in (0, 512)])
            m["ml_cw" + L] = f32(ml_conv_w[l][:, :, mlc]); m["ml_cb" + L] = f32(ml_conv_b[l][mlc])
            m["f_w1" + L] = f32(hy_f_w1[l]); m["f_b1" + L] = f32(hy_f_b1[l]); m["f_freq" + L] = f32(hy_f_freq[l])
            m["f_w2" + L] = f32(hy_f_w2[l]); m["f_b2" + L] = f32(hy_f_b2[l])
            m["f_w3" + L] = f32(np.asarray(hy_f_w3[l]).reshape(64, 4, 512)[:, :, sl])
            m["decay" + L] = f32(np.asarray(hy_decay[l]).reshape(4, 512)[:, sl].T)
            m["skip" + L] = f32(hy_skip[l][:, sl]); m["mlnw" + L] = f32(ml_norm_w[l][sl])
            m["wbr" + L] = f32(w_branch[l]); m["wout" + L] = f32(w_out[l])
            m["wr" + L] = f32(np.concatenate([moe_rg_w[l], moe_re_w[l]], axis=1)); m["br" + L] = f32(np.concatenate([moe_rg_b[l], moe_re_b[l]]))
            m["wg" + L] = f32(moe_w_gate[l]); m["wu" + L] = f32(moe_w_up[l]); m["wd" + L] = f32(moe_w_down[l])
        in_maps.append(m)
    res = run_bass_kernel_spmd(nc, in_maps, core_ids=list(range(8)))
    out = np.empty((2, LLAT, 1024), np.float32)
    for core in range(8):
        b, j = core // 4, core % 4
        out[b, NLAT * j:NLAT * (j + 1), :] = np.asarray(res.results[core]["outT"], dtype=np.float32).T
    return out
```

```python
import os
import numpy as np
import ml_dtypes

from contextlib import ExitStack, contextmanager
import concourse.bass as bass
import concourse.mybir as mybir
from concourse.bass_utils import run_bass_kernel_spmd

F32 = mybir.dt.float32
BF16 = mybir.dt.bfloat16
ALU = mybir.AluOpType
AF = mybir.ActivationFunctionType
AX = mybir.AxisListType


class SemCtr:
    __slots__ = ("sem", "count")

    def __init__(self, sem):
        self.sem = sem
        self.count = 0


class Res:
    __slots__ = ("name", "last_w", "readers", "dsem")

    def __init__(self, name):
        self.name = name
        self.last_w = None
        self.readers = {}
        self.dsem = None


class Prog:
    ENG = ("pe", "dve", "act", "pool", "sp")

    def __init__(self, same_engine_sync=True):
        self.nc = bass.Bass("TRN2", target_bir_lowering=False)
        self.st = ExitStack()
        self.stk = [self.st]
        self.q = {e: [] for e in self.ENG}
        self.cnt = {e: 0 for e in self.ENG}
        self.seen = {e: {} for e in self.ENG}
        self.esem = {e: self.st.enter_context(self.nc.semaphore("es_" + e)) for e in self.ENG}
        self.esem_ids = {id(s) for s in self.esem.values()}
        self.own_ids = {e: {id(self.esem[e])} for e in self.ENG}
        self.same = same_engine_sync
        self.dma_events = {}
        self.sem_pool = []
        self.block_log = []
        self.label = ''
        self.scope_res = [[]]
        self.uid = 0
        self.ninst = 0
        self.flush_every = int(os.environ.get("FLUSH_EVERY", "1200"))

    def dram(self, name, shape, dt, kind=None):
        if kind is None:
            t = self.nc.dram_tensor(name, list(shape), dt)
        else:
            t = self.nc.dram_tensor(name, list(shape), dt, kind=kind)
        return t.ap()

    def sbuf(self, name, shape, dt):
        self.uid += 1
        return self.stk[-1].enter_context(self.nc.sbuf_tensor("%s_%d" % (name, self.uid), list(shape), dt))

    def psum(self, name, shape, dt=F32):
        self.uid += 1
        return self.stk[-1].enter_context(self.nc.psum_tensor("%s_%d" % (name, self.uid), list(shape), dt))

    def res(self, name=None):
        self.uid += 1
        r = Res("%s_%d" % (name or "r", self.uid))
        self.scope_res[-1].append(r)
        return r

    def semctr(self):
        while self.sem_pool:
            sc = self.sem_pool.pop()
            if sc.count < 20000:
                return sc
        return SemCtr(self.newsem("ds"))

    def newsem(self, name):
        self.uid += 1
        return self.st.enter_context(self.nc.semaphore("%s_%d" % (name, self.uid)))

    def _waits(self, eng, r, w, dma=False):
        waits = {}

        def need(ev):
            if ev is None:
                return
            s, v = ev
            if (not self.same or eng == "pe") and id(s) in self.own_ids[eng]:
                return
            if self.seen[eng].get(id(s), (None, 0))[1] < v:
                if waits.get(id(s), (None, 0))[1] < v:
                    waits[id(s)] = (s, v)

        for x in r:
            need(x.last_w)
            if dma:
                for ev in x.readers.values():
                    if id(ev[0]) not in self.esem_ids:
                        need(ev)
        for x in w:
            need(x.last_w)
            for ev in x.readers.values():
                need(ev)
        for k, sv in waits.items():
            self.seen[eng][k] = sv
        return list(waits.values())

    def op(self, eng, fn, r=(), w=()):
        waits = self._waits(eng, r, w)
        if self.cnt[eng] >= 20000:
            self.esem[eng] = self.newsem("es_" + eng)
            self.esem_ids.add(id(self.esem[eng]))
            self.own_ids[eng].add(id(self.esem[eng]))
            self.cnt[eng] = 0
        self.cnt[eng] += 1
        ev = (self.esem[eng], self.cnt[eng])
        self.q[eng].append((waits, fn, ev, 1))
        for x in r:
            x.readers[id(ev[0])] = ev
        for x in w:
            x.last_w = ev
            x.readers = {}
        self.ninst += 1
        self._autoflush()
        return ev

    def _autoflush(self):
        if sum(len(v) for v in self.q.values()) >= self.flush_every:
            self.flush()

    def _async(self, eng, fn, inc, r, w):
        dst = w[0]
        waits = self._waits(eng, r, w, dma=True)
        if dst.dsem is None:
            dst.dsem = self.semctr()
        dst.dsem.count += inc
        ev = (dst.dsem.sem, dst.dsem.count)
        self.q[eng].append((waits, fn, ev, inc))
        for x in r:
            x.readers[id(ev[0])] = ev
        dst.last_w = ev
        dst.readers = {}
        self.dma_events[id(ev[0])] = ev
        self.ninst += 1
        self._autoflush()
        return ev

    def dma(self, eng, out, in_, r=(), w=(), slow=False):
        if slow:
            return self._async(eng, lambda e: e.dma_start(out=out, in_=in_, allow_slow_non_contiguous=True), 16, r, w)
        return self._async(eng, lambda e: e.dma_start(out=out, in_=in_), 16, r, w)

    def coll(self, kind, out, in_, groups, r=(), w=()):
        return self._async("pool", lambda e: e.collective_compute(
            kind, ALU.bypass, replica_groups=groups, ins=[in_.opt()], outs=[out.opt()]), 1, r, w)

    def barrier(self):
        for eng in self.ENG:
            waits = []
            evs = [(self.esem[x], self.cnt[x]) for x in self.ENG if x != eng and self.cnt[x] > 0]
            evs += list(self.dma_events.values())
            for s, v in evs:
                if self.seen[eng].get(id(s), (None, 0))[1] < v:
                    waits.append((s, v))
                    self.seen[eng][id(s)] = (s, v)
            if waits:
                self.q[eng].append((waits, None, None, 0))
        self.dma_events = {}

    def flush(self):
        nc = self.nc
        with nc.Block() as block:
            def mk(name):
                def run(e):
                    for waits, fn, ev, inc in self.q[name]:
                        for s, v in waits:
                            e.wait_ge(s, v)
                        if fn is not None:
                            fn(e).then_inc(ev[0], inc)
                return run
            block.sync(mk("sp"))
            block.tensor(mk("pe"))
            block.vector(mk("dve"))
            block.scalar(mk("act"))
            block.gpsimd(mk("pool"))
        self.block_log.append((self.label, sum(len(v) for v in self.q.values())))
        self.q = {e: [] for e in self.ENG}

    @contextmanager
    def scope(self):
        st = ExitStack()
        self.stk.append(st)
        self.scope_res.append([])
        try:
            yield
            self.barrier()
            self.flush()
        finally:
            self.stk.pop()
            st.close()
            for r in self.scope_res.pop():
                if r.dsem is not None:
                    self.sem_pool.append(r.dsem)
                    r.dsem = None

    def finish(self):
        self.barrier()
        self.flush()
        self.st.close()
        return self.nc

    def mm(self, out, lhsT, rhs, start, stop, r, w):
        return self.op("pe", lambda e: e.matmul(out, lhsT=lhsT, rhs=rhs, start=start, stop=stop), r, w)

    def tr(self, out, in_, ident, r, w):
        return self.op("pe", lambda e: e.transpose(out, in_, ident), r, w)

    def act(self, out, in_, func, r, w, bias=0.0, scale=1.0, accum_out=None):
        if accum_out is None:
            return self.op("act", lambda e: e.activation(out=out, in_=in_, func=func, bias=bias, scale=scale), r, w)
        return self.op("act", lambda e: e.activation(out=out, in_=in_, func=func, bias=bias, scale=scale, accum_out=accum_out), r, w)

    def tt(self, eng, out, a, b, op, r, w):
        return self.op(eng, lambda e: e.tensor_tensor(out=out, in0=a, in1=b, op=op), r, w)

    def ts(self, eng, out, a, s1, op0, r, w, s2=None, op1=None):
        if op1 is None:
            return self.op(eng, lambda e: e.tensor_scalar(out=out, in0=a, scalar1=s1, scalar2=None, op0=op0), r, w)
        return self.op(eng, lambda e: e.tensor_scalar(out=out, in0=a, scalar1=s1, scalar2=s2, op0=op0, op1=op1), r, w)

    def stt(self, out, in0, scalar, in1, op0, op1, r, w):
        return self.op("dve", lambda e: e.scalar_tensor_tensor(out=out, in0=in0, scalar=scalar, in1=in1, op0=op0, op1=op1), r, w)

    def cp(self, eng, out, in_, r, w):
        if eng == "act":
            return self.op("act", lambda e: e.copy(out=out, in_=in_), r, w)
        return self.op(eng, lambda e: e.tensor_copy(out=out, in_=in_), r, w)

    def memset(self, eng, out, val, w):
        return self.op(eng, lambda e: e.memset(out, val), (), w)


def make_ident(p, dt=BF16, name="ident"):
    idf = p.sbuf(name + "f", [128, 128], F32); r = p.res(name)
    p.memset("dve", idf[:], 0.0, w=[r])
    p.op("pool", lambda e: e.affine_select(out=idf[:], in_=idf[:], pattern=[[-1, 128]], compare_op=ALU.not_equal,
                                           fill=1.0, base=0, channel_multiplier=1), r=[r], w=[r])
    if dt == F32:
        return idf, r
    idb = p.sbuf(name + "b", [128, 128], dt)
    p.cp("dve", idb[:], idf[:], r=[r], w=[r])
    return idb, r


def conv_stage(p, jobs, R, W):
    L = R * W
    PAD = W + 1
    CW = min(512, L)
    with p.scope():
        ident, idr = make_ident(p)
        stgs = [p.sbuf("stg%d" % i, [128, L], F32) for i in range(2)]; stgrs = [p.res("stg%d" % i) for i in range(2)]
        outts = [p.sbuf("outt%d" % i, [128, L], F32) for i in range(2)]; outrs = [p.res("outt%d" % i) for i in range(2)]
        nver = 3 if R > 1 else 1
        P = [p.sbuf("P%d" % i, [128, L + 2 * PAD], BF16) for i in range(nver)]
        Pr = [p.res("P%d" % i) for i in range(nver)]
        for i in range(nver):
            p.memset("pool", P[i][:, 0:PAD], 0.0, w=[Pr[i]])
            p.memset("pool", P[i][:, PAD + L:], 0.0, w=[Pr[i]])
        wsb = p.sbuf("wsb", [128, 9], F32); bsb = p.sbuf("bsb", [128, 1], F32); wr = p.res("wsb")
        D = p.sbuf("D", [128, 9, 128], BF16); Dr = p.res("D")
        ps = [p.psum("ps%d" % i, [128, 512]) for i in range(4)]; psr = [p.res("ps%d" % i) for i in range(4)]
        k = 0
        for ji, (src, srcr, w_ap, b_ap, c0, silu, dst, dstr) in enumerate(jobs):
            stg = stgs[ji % 2]; stgr = stgrs[ji % 2]; outt = outts[ji % 2]; outr = outrs[ji % 2]
            p.dma("sp" if ji % 2 else "pool", stg[:], src, r=[srcr], w=[stgr])
            p.dma("sp", wsb[:], w_ap.rearrange("a b c -> c (a b)")[c0:c0 + 128, :], w=[wr], slow=True)
            p.dma("sp", bsb[:], b_ap.rearrange("(a b) -> a b", b=1)[c0:c0 + 128, :], w=[wr])
            p.cp("act", P[0][:, PAD:PAD + L], stg[:], r=[stgr], w=[Pr[0]])
            if nver == 3:
                p.cp("act", P[1][:, PAD:PAD + L], stg[:], r=[stgr], w=[Pr[1]])
                p.cp("dve", P[2][:, PAD:PAD + L], stg[:], r=[stgr], w=[Pr[2]])
                v1 = P[1][:, PAD:PAD + L].rearrange("p (r w) -> p r w", w=W)
                v2 = P[2][:, PAD:PAD + L].rearrange("p (r w) -> p r w", w=W)
                p.memset("dve", v1[:, :, W - 1:W], 0.0, w=[Pr[1]])
                p.memset("dve", v2[:, :, 0:1], 0.0, w=[Pr[2]])
            for tap in range(9):
                p.ts("dve", D[:, tap, :], ident[:], wsb[:, tap:tap + 1], ALU.mult, r=[idr, wr], w=[Dr])
            taps = [(dy, dx) for dy in (-1, 0, 1) for dx in (-1, 0, 1) if (R > 1 or dy == 0)]
            for c in range(L // CW):
                h = k % 4; k += 1
                for ti, (dy, dx) in enumerate(taps):
                    ver = 0 if nver == 1 else (1 if dx == -1 else (2 if dx == 1 else 0))
                    o = PAD + c * CW + W * dy + dx
                    p.mm(ps[h][:, 0:CW], D[:, (dy + 1) * 3 + dx + 1, :], P[ver][:, o:o + CW], ti == 0, ti == len(taps) - 1,
                         r=[Dr, Pr[ver]], w=[psr[h]])
                p.act(outt[:, c * CW:(c + 1) * CW], ps[h][:, 0:CW], AF.Silu if silu else AF.Identity,
                      r=[psr[h], wr], w=[outr], bias=bsb[:, 0:1])
            p.dma("pool" if ji % 2 else "sp", dst, outt[:], r=[outr], w=[dstr])


HY_BANDS = 16


def hfilt_consts(L):
    t = np.arange(L, dtype=np.float64) / L
    bands = np.linspace(1e-4, HY_BANDS - 1, HY_BANDS)
    ang = 2 * np.pi * t[:, None] * bands[None, :]
    feats = np.concatenate([t[:, None], np.cos(ang), np.sin(ang)], axis=-1)
    return np.ascontiguousarray(feats.T).astype(np.float32), t.astype(np.float32)


def hfilt_stage(p, featsT, trow, w, j, L, dst, dst_res):
    CW = min(512, L)
    NCH = L // CW
    TWO_PI = 2 * np.pi
    with p.scope():
        wr = p.res("w")
        ft = p.sbuf("ft", [33, L], F32); tb = p.sbuf("tb", [128, L], F32)
        w1 = p.sbuf("w1", [33, 64], F32); w2 = p.sbuf("w2", [64, 64], F32); w3 = p.sbuf("w3", [64, 4, 128], F32)
        sc = p.sbuf("sc", [64, 4], F32)
        dc = p.sbuf("dc", [128, 4], F32)
        p.dma("sp", ft[:], featsT, w=[wr]); p.dma("sp", tb[:], trow.partition_broadcast(128), w=[wr])
        p.dma("sp", w1[:], w["f_w1"], w=[wr]); p.dma("sp", w2[:], w["f_w2"], w=[wr])
        p.dma("sp", w3[:], w["f_w3"], w=[wr])
        p.dma("sp", sc[:, 0:1], w["f_b1"].rearrange("(a b) -> a b", b=1), w=[wr])
        p.dma("sp", sc[:, 1:2], w["f_freq"].rearrange("(a b) -> a b", b=1), w=[wr])
        p.dma("sp", sc[:, 2:3], w["f_b2"].rearrange("(a b) -> a b", b=1), w=[wr])
        p.dma("sp", dc[:], w["decay"], w=[wr])
        p.act(dc[:], dc[:], AF.Abs, r=[wr], w=[wr])
        p.ts("dve", dc[:], dc[:], -1.0, ALU.mult, r=[wr], w=[wr])
        h1f = p.sbuf("h1", [128, L], F32); h1 = h1f[0:64]; h2 = p.sbuf("h2", [64, L], F32)
        h1r, h2r = p.res("h1"), p.res("h2")
        ps = [p.psum("ps%d" % i, [128, 512]) for i in range(2)]; psr = [p.res("ps%d" % i) for i in range(2)]
        tmp = p.sbuf("tmp", [64, 512], F32); tmpr = p.res("tmp")
        tm1 = p.sbuf("tm1", [64, 512], F32); tm1r = p.res("tm1"); tm2 = p.sbuf("tm2", [64, 512], F32); tm2r = p.res("tm2")
        k = 0
        for (src, srcr, lw, bcol, dsth, dstr) in ((ft[:], wr, w1, 0, h1, h1r), (h1, h1r, w2, 2, h2[:], h2r)):
            for c in range(NCH):
                cs = slice(c * CW, (c + 1) * CW)
                h = k % 2; k += 1
                p.mm(ps[h][0:64, 0:CW], lw[:], src[:, cs], True, True, r=[wr, srcr], w=[psr[h]])
                p.ts("dve", tmp[:, 0:CW], ps[h][0:64, 0:CW], sc[:, bcol:bcol + 1], ALU.add, r=[psr[h], wr], w=[tmpr],
                     s2=sc[:, 1:2], op1=ALU.mult)
                tv = tmp[:, 0:CW]
                p.ts("dve", tm1[:, 0:CW], tv, float(np.pi), ALU.is_gt, r=[tmpr], w=[tm1r], s2=-TWO_PI, op1=ALU.mult)
                p.ts("dve", tm2[:, 0:CW], tv, float(-np.pi), ALU.is_lt, r=[tmpr], w=[tm2r], s2=TWO_PI, op1=ALU.mult)
                p.tt("dve", tv, tv, tm1[:, 0:CW], ALU.add, r=[tmpr, tm1r], w=[tmpr])
                p.tt("dve", tv, tv, tm2[:, 0:CW], ALU.add, r=[tmpr, tm2r], w=[tmpr])
                p.ts("dve", tv, tv, float(np.pi), ALU.min, r=[tmpr], w=[tmpr], s2=float(-np.pi), op1=ALU.max)
                p.act(dsth[:, cs], tmp[:, 0:CW], AF.Sin, r=[tmpr], w=[dstr])
        F = [p.sbuf("F%d" % i, [128, L], F32) for i in range(2)]; Fr = [p.res("F%d" % i) for i in range(2)]
        wnd = p.sbuf("wnd", [128, 512], F32); wndr = p.res("wnd")
        ss = p.sbuf("ss", [128, 4], F32); ssr = p.res("ss")
        junk = h1f; junkr = h1r
        for o in range(2):
            for d in range(2):
                od = o * 2 + d
                for c in range(NCH):
                    cs = slice(c * CW, (c + 1) * CW)
                    h = k % 2; k += 1
                    p.mm(ps[h][:, 0:CW], w3[:, od, :], h2[:, cs], True, True, r=[wr, h2r], w=[psr[h]])
                    p.act(wnd[:, 0:CW], tb[:, cs], AF.Exp, r=[wr], w=[wndr], scale=dc[:, od:od + 1])
                    p.tt("dve", F[d][:, cs], ps[h][:, 0:CW], wnd[:, 0:CW], ALU.mult, r=[psr[h], wndr], w=[Fr[d]])
                if d == 1:
                    p.memset("dve", F[d][:, 0:1], 0.0, w=[Fr[d]])
                p.act(junk[:], F[d][:], AF.Square, r=[Fr[d]], w=[junkr], accum_out=ss[:, od:od + 1])
                ssr.last_w = junkr.last_w
            tot = ss[:, 2 * o:2 * o + 1]
            p.tt("dve", tot, tot, ss[:, 2 * o + 1:2 * o + 2], ALU.add, r=[ssr], w=[ssr])
            p.act(tot, tot, AF.Sqrt, r=[ssr], w=[ssr], bias=1e-6)
            p.op("dve", lambda e, t_=tot: e.reciprocal(out=t_, in_=t_), r=[ssr], w=[ssr])
            for d in range(2):
                p.ts("dve", F[d][:], F[d][:], tot, ALU.mult, r=[Fr[d], ssr], w=[Fr[d]])
                p.dma("sp", dst[o * 2 + d], F[d][:], r=[Fr[d]], w=[dst_res])


def hconv_tables():
    a = np.arange(128, dtype=np.float64)
    th128 = 2 * np.pi * np.outer(a, a) / 128.0
    C = np.cos(th128); S = np.sin(th128)
    bf = lambda x: np.ascontiguousarray(x.astype(np.float32)).astype(ml_dtypes.bfloat16)
    t = {}
    t["hc_sq"] = bf(np.stack([C, -C, S, -S], axis=1))
    t["hc_wab"] = bf(np.stack([np.concatenate([C, S], 1), np.concatenate([-C, -S], 1),
                                np.concatenate([-S, C], 1)], axis=1))
    for pre, N1 in (("hc", 128), ("hcc", 4)):
        NF = 128 * N1
        f1 = np.arange(N1, dtype=np.float64)
        n1 = np.arange(64, dtype=np.float64)
        th1 = 2 * np.pi * np.outer(n1, f1) / N1
        t[pre + "_w1"] = bf(np.concatenate([np.cos(th1), -np.sin(th1)], axis=1))
        thN = 2 * np.pi * np.outer(a, f1) / NF
        t[pre + "_tw"] = np.stack([np.cos(thN), -np.sin(thN)], axis=1).astype(np.float32)
        thN2 = 2 * np.pi * np.outer(a, a) / NF
        t[pre + "_tw2"] = np.stack([np.cos(thN2), -np.sin(thN2)], axis=1).astype(np.float32)
        m1 = np.arange(64, dtype=np.float64)
        thI = 2 * np.pi * np.outer(a, m1) / N1
        t[pre + "_i2"] = bf(np.stack([np.cos(thI), np.sin(thI), -np.sin(thI)], axis=1))
    return t


def hconv_stage(p, tabs, srcs, dst, dst_res, CH, L_in, first, G=32, u_out=None, u_res=None):
    NP = L_in // 128
    N1 = 128 if L_in == 8192 else 4
    NFFT = 128 * N1
    pre = "hc" if N1 == 128 else "hcc"
    with p.scope():
        w1 = p.sbuf("w1", [64, 2 * N1], BF16); sq = p.sbuf("sq", [128, 4, 128], BF16)
        wab = p.sbuf("wab", [128, 3, 256], BF16); i2 = p.sbuf("i2", [128, 3, 64], BF16)
        tw = p.sbuf("tw", [128, 2, N1], F32); tw2 = p.sbuf("tw2", [128, 2, 128], F32)
        tr = p.res("tabs")
        for sb_, nm in ((w1, pre + "_w1"), (sq, "hc_sq"), (wab, "hc_wab"), (i2, pre + "_i2"), (tw, pre + "_tw"), (tw2, pre + "_tw2")):
            p.dma("sp", sb_[:], tabs[nm], w=[tr])
        PS = p.psum("ps", [128, 4096])
        PH = [PS[:, 0:2048], PS[:, 2048:4096]]
        phr = [p.res("psA"), p.res("psB")]
        stg = [p.sbuf("stg%d" % i, [64, G, 128], F32) for i in range(3)]
        stgr = [p.res("stg%d" % i) for i in range(3)]
        skb = p.sbuf("skb", [64, G], F32); skr = p.res("skb")
        XB = [p.sbuf("X%d" % i, [64, G, 128], BF16) for i in range(2)]; xrB = [p.res("X%d" % i) for i in range(2)]
        NS4 = G // 4
        T = [p.sbuf("T%d" % i, [128, G, N1], BF16) for i in range(4)]
        TI = T if N1 == 128 else [p.sbuf("TI%d" % i, [N1, G, 128], BF16) for i in range(4)]
        TrS = [[p.res("T%d_%d" % (i, s)) for s in range(NS4)] for i in range(4)]
        Kr = p.sbuf("Kr", [128, G, N1], F32); Ki = p.sbuf("Ki", [128, G, N1], F32)
        kresS = [p.res("K%d" % s) for s in range(NS4)]
        Q = [p.sbuf("Q%d" % i, [128, G, N1], BF16) for i in range(4)]
        QrS = [[p.res("Q%d_%d" % (i, s)) for s in range(NS4)] for i in range(4)]
        OUT2 = p.sbuf("OUT2", [64, G, 128], F32); outr = p.res("OUT2")
        Trb = tw[:, 0:1, :].to_broadcast([128, 4, N1])
        Tib = tw[:, 1:2, :].to_broadcast([128, 4, N1])
        Trb2 = tw2[0:N1, 0:1, :].to_broadcast([N1, 4, 128])
        Tib2 = tw2[0:N1, 1:2, :].to_broadcast([N1, 4, 128])
        PF = [PS[:, 0:1024], PS[:, 1024:2048]]; pfr = [p.res("pfA"), p.res("pfB")]
        PX = [PS[:, 2048:3072], PS[:, 3072:4096]]; pxr = [p.res("pxA"), p.res("pxB")]
        cnt = {"f": 0, "x": 0, "i": 0}

        def t2d(ap, c0):
            return ap[c0:c0 + G, :].rearrange("g (a b) -> a g b", b=128)

        def load_u(c0, which, xi):
            X = XB[xi]; xr = xrB[xi]
            if NP < 64:
                p.memset("pool", X[:], 0.0, w=[xr])
            if which != "u" or first:
                ap, rs = srcs["v" if which == "u" else which]
                p.dma("sp" if xi else "pool", stg[xi][0:NP], t2d(ap, c0), r=[rs], w=[stgr[xi]])
                p.cp("act", X[0:NP], stg[xi][0:NP], r=[stgr[xi]], w=[xr])
            else:
                for i, nm in enumerate(("v", "c1", "x1")):
                    ap, rs = srcs[nm]
                    p.dma("sp" if i % 2 else "pool", stg[i][0:NP], t2d(ap, c0), r=[rs], w=[stgr[i]])
                ap, rs = srcs["skip"]
                p.dma("sp", skb[0:NP], ap[c0:c0 + G].partition_broadcast(NP), r=[rs], w=[skr])
                p.tt("pool", stg[0][0:NP], stg[0][0:NP], skb[0:NP, :, None].to_broadcast([NP, G, 128]), ALU.mult,
                     r=[stgr[0], skr], w=[stgr[0]])
                p.tt("pool", stg[0][0:NP], stg[0][0:NP], stg[1][0:NP], ALU.add, r=[stgr[0], stgr[1]], w=[stgr[0]])
                p.tt("pool", stg[0][0:NP], stg[0][0:NP], stg[2][0:NP], ALU.mult, r=[stgr[0], stgr[2]], w=[stgr[0]])
                p.cp("act", X[0:NP], stg[0][0:NP], r=[stgr[0]], w=[xr])
                if u_out is not None:
                    p.dma("pool", t2d(u_out, c0), stg[0][0:NP], r=[stgr[0]], w=[u_res])

        Cm, Cn, Sm, Sn = (sq[:, k, :] for k in range(4))

        def f1_tw(s4, xi):
            X = XB[xi]; xr = xrB[xi]
            h = cnt["f"] % 2; cnt["f"] += 1
            pv = PF[h][:, 0:8 * N1].rearrange("p (c k) -> p c k", k=2 * N1)
            for c in range(4):
                p.mm(pv[:, c, :], X[:, s4 * 4 + c, :], w1[:], True, True, r=[xr, tr], w=[pfr[h]])
            Ar = pv[:, :, 0:N1]; Ai = pv[:, :, N1:2 * N1]
            gs = slice(s4 * 4, s4 * 4 + 4)
            p.tt("dve", T[0][:, gs, :], Ar, Trb, ALU.mult, r=[pfr[h], tr], w=[TrS[0][s4]])
            p.tt("dve", T[1][:, gs, :], Ai, Tib, ALU.mult, r=[pfr[h], tr], w=[TrS[1][s4]])
            p.tt("dve", T[2][:, gs, :], Ar, Tib, ALU.mult, r=[pfr[h], tr], w=[TrS[2][s4]])
            p.tt("dve", T[3][:, gs, :], Ai, Trb, ALU.mult, r=[pfr[h], tr], w=[TrS[3][s4]])

        def f2_post(s4, which):
            h = cnt["x"] % 2; cnt["x"] += 1
            gs = slice(s4 * 4, s4 * 4 + 4)
            xr_ps = PX[h][:, 0:4 * N1]; xi_ps = PX[h][:, 512:512 + 4 * N1]
            rr = [tr] + [TrS[k][s4] for k in range(4)]
            p.mm(xr_ps, Cm, T[0][:, gs, :], True, False, r=rr, w=[pxr[h]])
            p.mm(xr_ps, Cn, T[1][:, gs, :], False, False, r=rr, w=[pxr[h]])
            p.mm(xr_ps, Sm, T[2][:, gs, :], False, False, r=rr, w=[pxr[h]])
            p.mm(xr_ps, Sm, T[3][:, gs, :], False, True, r=rr, w=[pxr[h]])
            p.mm(xi_ps, Cm, T[2][:, gs, :], True, False, r=rr, w=[pxr[h]])
            p.mm(xi_ps, Cm, T[3][:, gs, :], False, False, r=rr, w=[pxr[h]])
            p.mm(xi_ps, Sn, T[0][:, gs, :], False, False, r=rr, w=[pxr[h]])
            p.mm(xi_ps, Sm, T[1][:, gs, :], False, True, r=rr, w=[pxr[h]])
            xr3 = xr_ps.rearrange("p (c k) -> p c k", k=N1); xi3 = xi_ps.rearrange("p (c k) -> p c k", k=N1)
            kres = kresS[s4]
            if which == "kf":
                p.cp("act", Kr[:, gs, :], xr3, r=[pxr[h]], w=[kres])
                p.cp("act", Ki[:, gs, :], xi3, r=[pxr[h]], w=[kres])
            elif which == "kg":
                p.tt("dve", Kr[:, gs, :], xr3, Kr[:, gs, :], ALU.add, r=[pxr[h], kres], w=[kres])
                p.stt(Ki[:, gs, :], xi3, -1.0, Ki[:, gs, :], ALU.mult, ALU.add, r=[pxr[h], kres], w=[kres])
            else:
                p.tt("dve", Q[0][:, gs, :], xr3, Kr[:, gs, :], ALU.mult, r=[pxr[h], kres], w=[QrS[0][s4]])
                p.tt("dve", Q[1][:, gs, :], xi3, Ki[:, gs, :], ALU.mult, r=[pxr[h], kres], w=[QrS[1][s4]])
                p.tt("dve", Q[2][:, gs, :], xr3, Ki[:, gs, :], ALU.mult, r=[pxr[h], kres], w=[QrS[2][s4]])
                p.tt("dve", Q[3][:, gs, :], xi3, Kr[:, gs, :], ALU.mult, r=[pxr[h], kres], w=[QrS[3][s4]])

        def fwd(which, xi):
            for s4 in range(NS4 + 1):
                if s4 < NS4:
                    f1_tw(s4, xi)
                if s4 >= 1:
                    f2_post(s4 - 1, which)

        def inv(c0):
            Wa, Wan, Wb = (wab[:, k, :] for k in range(3))
            for s4 in range(NS4 + 1):
                if s4 >= 1:
                    i2_step(s4 - 1, c0)
                if s4 == NS4:
                    break
                h = cnt["f"] % 2; cnt["f"] += 1
                pv = PF[h][0:N1, :].rearrange("p (c k) -> p c k", k=256)
                rr = [tr] + [QrS[k][s4] for k in range(4)]
                for c in range(4):
                    g = s4 * 4 + c
                    p.mm(pv[:, c, :], Q[0][:, g, :], Wa, True, False, r=rr, w=[pfr[h]])
                    p.mm(pv[:, c, :], Q[1][:, g, :], Wan, False, False, r=rr, w=[pfr[h]])
                    p.mm(pv[:, c, :], Q[2][:, g, :], Wb, False, False, r=rr, w=[pfr[h]])
                    p.mm(pv[:, c, :], Q[3][:, g, :], Wb, False, True, r=rr, w=[pfr[h]])
                Br = pv[:, :, 0:128]; Bi = pv[:, :, 128:256]
                gs = slice(s4 * 4, s4 * 4 + 4)
                p.tt("dve", TI[0][0:N1, gs, :], Br, Trb2, ALU.mult, r=[pfr[h], tr], w=[TrS[0][s4]])
                p.tt("dve", TI[1][0:N1, gs, :], Bi, Tib2, ALU.mult, r=[pfr[h], tr], w=[TrS[1][s4]])
                p.tt("dve", TI[2][0:N1, gs, :], Br, Tib2, ALU.mult, r=[pfr[h], tr], w=[TrS[2][s4]])
                p.tt("dve", TI[3][0:N1, gs, :], Bi, Trb2, ALU.mult, r=[pfr[h], tr], w=[TrS[3][s4]])
            p.dma("sp", t2d(dst, c0), OUT2[0:NP], r=[outr], w=[dst_res])

        CI, SI, SIn = (i2[:, k, :] for k in range(3))

        def i2_step(s4, c0):
            h = cnt["x"] % 2; cnt["x"] += 1
            gs = slice(s4 * 4, s4 * 4 + 4)
            o_ps = PX[h][0:NP, 0:512]
            rr = [tr] + [TrS[k][s4] for k in range(4)]
            p.mm(o_ps, CI[0:N1, 0:NP], TI[0][0:N1, gs, :], True, False, r=rr, w=[pxr[h]])
            p.mm(o_ps, CI[0:N1, 0:NP], TI[1][0:N1, gs, :], False, False, r=rr, w=[pxr[h]])
            p.mm(o_ps, SI[0:N1, 0:NP], TI[2][0:N1, gs, :], False, False, r=rr, w=[pxr[h]])
            p.mm(o_ps, SIn[0:N1, 0:NP], TI[3][0:N1, gs, :], False, True, r=rr, w=[pxr[h]])
            p.act(OUT2[0:NP, gs, :], o_ps.rearrange("p (c k) -> p c k", k=128), AF.Copy, r=[pxr[h]], w=[outr], scale=1.0 / NFFT)

        items = [(c0, which) for c0 in range(0, CH, G) for which in ("kf", "kg", "u")]
        load_u(items[0][0], items[0][1], 0)
        for i, (c0, which) in enumerate(items):
            if i + 1 < len(items):
                load_u(items[i + 1][0], items[i + 1][1], (i + 1) % 2)
            fwd(which, i % 2)
            if which == "u":
                inv(c0)


LF = 8192


def fnet_tables():
    a = np.arange(128, dtype=np.float64)
    th = 2 * np.pi * np.outer(a, a) / 128.0
    C = np.cos(th); S = np.sin(th)
    b = np.arange(64, dtype=np.float64)
    th64 = 2 * np.pi * np.outer(b, b) / 64.0
    C64 = np.cos(th64); S64 = np.sin(th64)
    thL = 2 * np.pi * np.outer(b, a) / LF
    bf = lambda x: np.ascontiguousarray(x.astype(np.float32)).astype(ml_dtypes.bfloat16)
    return {"fn_w": bf(np.stack([np.concatenate([C, -S], 1), np.concatenate([S, C], 1)], axis=1)),
            "fn_64": bf(np.stack([C64, -C64, S64], axis=1)),
            "fn_tw": np.stack([np.cos(thL), -np.sin(thL)], axis=1).astype(np.float32),
            "fn_256": bf(np.stack([np.stack([np.cos(2 * np.pi * np.outer(np.arange(128 * b_, 128 * b_ + 128), np.arange(256)) / 256.0),
                                             np.sin(2 * np.pi * np.outer(np.arange(128 * b_, 128 * b_ + 128), np.arange(256)) / 256.0)], axis=1)
                                   for b_ in range(2)], axis=1))}


def fnet_stage(p, tabs, src, src_res, dst, dst_res, L_in):
    scale = 1.0 / np.sqrt(L_in * 128.0)
    f1_list = list(range(128)) if L_in == LF else [0, 32, 64, 96]
    with p.scope():
        wt = p.sbuf("wt", [128, 2, 256], BF16); t64 = p.sbuf("t64", [64, 3, 64], BF16); tw = p.sbuf("tw", [64, 2, 128], F32)
        tr = p.res("tabs")
        p.dma("sp", wt[:], tabs["fn_w"], w=[tr]); p.dma("sp", t64[:], tabs["fn_64"], w=[tr]); p.dma("sp", tw[:], tabs["fn_tw"], w=[tr])
        stg = p.sbuf("stg", [128, LF], F32); stgr = p.res("stg")
        ub = p.sbuf("ub", [128, LF], BF16); ubr = p.res("ub")
        V = p.sbuf("V", [128, 64, 256], BF16); Vr = p.res("V")
        T = [p.sbuf("T%d" % i, [64, 64, 128], BF16) for i in range(4)]; Tr_ = [p.res("T%d" % i) for i in range(4)]
        PS = p.psum("ps", [128, 4096]); PH = [PS[:, 0:2048], PS[:, 2048:4096]]; phr = [p.res("psA"), p.res("psB")]
        if L_in < LF:
            p.memset("pool", stg[:], 0.0, w=[stgr])
        p.dma("sp", stg[:, 0:L_in], src, r=[src_res], w=[stgr])
        p.cp("act", ub[:], stg[:], r=[stgr], w=[ubr])
        tog = 0
        uv = ub[:].rearrange("p (a b) -> p b a", b=64)
        for s8 in range(8):
            h = tog; tog ^= 1
            pv = PH[h].rearrange("p (c k) -> p c k", k=256)
            for c in range(8):
                n2 = s8 * 8 + c
                p.mm(pv[:, c, :], uv[:, n2, :], wt[:, 0, :], True, True, r=[ubr, tr], w=[phr[h]])
            p.cp("act" if s8 % 2 else "dve", V[:, s8 * 8:(s8 + 1) * 8, :], pv, r=[phr[h]], w=[Vr])
        Trb = tw[:, 0:1, :].to_broadcast([64, 8, 128]); Tib = tw[:, 1:2, :].to_broadcast([64, 8, 128])
        OUT = stg; outr = stgr
        for half in range(2):
            for s8 in range(8):
                h = tog; tog ^= 1
                pv = PH[h][0:64, :].rearrange("p (c k) -> p c k", k=256)
                for c in range(8):
                    ch = half * 64 + s8 * 8 + c
                    p.mm(pv[:, c, :], V[:, :, ch], wt[:, 0, :], True, False, r=[Vr, tr], w=[phr[h]])
                    p.mm(pv[:, c, :], V[:, :, 128 + ch], wt[:, 1, :], False, True, r=[Vr, tr], w=[phr[h]])
                Ar = pv[:, :, 0:128]; Ai = pv[:, :, 128:256]
                gs = slice(s8 * 8, s8 * 8 + 8)
                p.tt("dve", T[0][:, gs, :], Ar, Trb, ALU.mult, r=[phr[h], tr], w=[Tr_[0]])
                p.tt("dve", T[1][:, gs, :], Ai, Tib, ALU.mult, r=[phr[h], tr], w=[Tr_[1]])
                p.tt("dve", T[2][:, gs, :], Ar, Tib, ALU.mult, r=[phr[h], tr], w=[Tr_[2]])
                p.tt("dve", T[3][:, gs, :], Ai, Trb, ALU.mult, r=[phr[h], tr], w=[Tr_[3]])
            C64, C64n, S64 = (t64[:, k, :] for k in range(3))
            rr = [tr] + Tr_
            for s32 in range(0, len(f1_list), 32):
                fl = f1_list[s32:s32 + 32]
                h = tog; tog ^= 1
                pv = PH[h][0:64, :].rearrange("p (m k) -> p m k", k=64)
                for i, f1 in enumerate(fl):
                    p.mm(pv[:, i, :], T[0][:, :, f1], C64, True, False, r=rr, w=[phr[h]])
                    p.mm(pv[:, i, :], T[1][:, :, f1], C64n, False, False, r=rr, w=[phr[h]])
                    p.mm(pv[:, i, :], T[2][:, :, f1], S64, False, False, r=rr, w=[phr[h]])
                    p.mm(pv[:, i, :], T[3][:, :, f1], S64, False, True, r=rr, w=[phr[h]])
                ov = OUT[half * 64:half * 64 + 64, :].rearrange("p (a b) -> p a b", b=128)
                step = fl[1] - fl[0] if len(fl) > 1 else 1
                osl = ov[:, :, fl[0]:fl[-1] + 1:step].rearrange("p a b -> p b a")
                p.act(osl, pv[:, 0:len(fl), :], AF.Copy, r=[phr[h]], w=[outr], scale=float(scale))
        if L_in == LF:
            p.dma("sp", dst, OUT[:], r=[outr], w=[dst_res])
        else:
            ov = OUT[:].rearrange("p (a b) -> p a b", b=128)[:, :, 0:128:32]
            p.dma("sp", dst.rearrange("p (a b) -> p a b", b=4), ov, r=[outr], w=[dst_res], slow=True)


def fnet_ctx_stage(p, tabs, src, src_res, dst, dst_res):
    scale = 1.0 / np.sqrt(256 * 128.0)
    with p.scope():
        wt = p.sbuf("wt", [128, 2, 256], BF16); t256 = p.sbuf("t256", [128, 2, 2, 256], BF16); tr = p.res("tabs")
        p.dma("sp", wt[:], tabs["fn_w"], w=[tr]); p.dma("sp", t256[:], tabs["fn_256"], w=[tr])
        stg = p.sbuf("stg", [128, 256], F32); stgr = p.res("stg")
        ub = p.sbuf("ub", [128, 256], BF16); ubr = p.res("ub")
        V = p.sbuf("V", [128, 2, 256], BF16); Vr = p.res("V")
        ps0 = p.psum("ps0", [128, 2, 256]); ps0r = p.res("ps0")
        ps1 = p.psum("ps1", [128, 256]); ps1r = p.res("ps1")
        p.dma("sp", stg[:], src, r=[src_res], w=[stgr])
        p.cp("act", ub[:], stg[:], r=[stgr], w=[ubr])
        for blk in range(2):
            p.mm(ps0[:, blk, :], ub[:, blk * 128:(blk + 1) * 128], wt[:, 0, :], True, True, r=[ubr, tr], w=[ps0r])
        p.cp("dve", V[:], ps0[:], r=[ps0r], w=[Vr])
        k = 0
        for blk in range(2):
            for ri in range(2):
                p.mm(ps1[:], V[:, blk, ri * 128:(ri + 1) * 128], t256[:, blk, ri, :], k == 0, k == 3, r=[Vr, tr], w=[ps1r])
                k += 1
        p.act(stg[:], ps1[:], AF.Copy, r=[ps1r], w=[stgr], scale=float(scale))
        p.dma("sp", dst, stg[:], r=[stgr], w=[dst_res])


STOP = 99


NCK = 66
NEG = -30000.0


def mlstm_stage(p, src, dst_lat, dst_ctx, dst_res, normw_ap, j):
    TOK = NCK * 128
    with p.scope():
        ident, idr = make_ident(p, F32, "idf")
        identb = p.sbuf("identb", [128, 128], BF16)
        p.cp("dve", identb[:], ident[:], r=[idr], w=[idr])
        cr = p.res("consts")
        ones = p.sbuf("ones", [128, 128], F32); onesb = p.sbuf("onesb", [128, 128], BF16)
        p.memset("dve", ones[:], 1.0, w=[cr]); p.memset("dve", onesb[:], 1.0, w=[cr])
        tri = p.sbuf("tri", [128, 2, 128], F32); mneg = p.sbuf("mneg", [128, 2, 128], F32)
        p.memset("dve", tri[:], 1.0, w=[cr]); p.memset("dve", mneg[:], 0.0, w=[cr])
        p.op("pool", lambda e: e.affine_select(out=tri[:, 0, :], in_=tri[:, 0, :], pattern=[[1, 128]], compare_op=ALU.is_ge, fill=0.0, base=0, channel_multiplier=-1), r=[cr], w=[cr])
        p.op("pool", lambda e: e.affine_select(out=tri[:, 1, :], in_=tri[:, 1, :], pattern=[[-1, 128]], compare_op=ALU.is_ge, fill=0.0, base=0, channel_multiplier=1), r=[cr], w=[cr])
        p.op("pool", lambda e: e.affine_select(out=mneg[:, 0, :], in_=mneg[:, 0, :], pattern=[[1, 128]], compare_op=ALU.is_ge, fill=NEG, base=0, channel_multiplier=-1), r=[cr], w=[cr])
        p.op("pool", lambda e: e.affine_select(out=mneg[:, 1, :], in_=mneg[:, 1, :], pattern=[[-1, 128]], compare_op=ALU.is_ge, fill=NEG, base=0, channel_multiplier=1), r=[cr], w=[cr])

        stg = p.sbuf("stg", [128, TOK], F32); stgr = p.res("stg")
        qT = p.sbuf("qT", [128, TOK], BF16); kT = p.sbuf("kT", [128, TOK], BF16); vT = p.sbuf("vT", [128, TOK], BF16)
        qr, kr, vr = p.res("qT"), p.res("kT"), p.res("vT")
        for nm, t_, r_, sc in (("qT", qT, qr, 128.0 ** -0.5), ("kT", kT, kr, 1.0), ("vT", vT, vr, 1.0)):
            lat, ctx, rs = src[nm]
            p.dma("sp", stg[:, 0:256], ctx, r=[rs], w=[stgr]); p.dma("sp", stg[:, 256:], lat, r=[rs], w=[stgr])
            p.act(t_[:], stg[:], AF.Copy, r=[stgr], w=[r_], scale=sc)
        ktok = p.sbuf("ktok", [128, NCK, 128], BF16); vtok = p.sbuf("vtok", [128, NCK, 128], BF16)
        ktr, vtr = p.res("ktok"), p.res("vtok")
        ptb = [p.psum("ptb%d" % i, [128, 8, 128], BF16) for i in range(2)]; ptbr = [p.res("ptb%d" % i) for i in range(2)]
        kk = 0
        for srcT, sr, dstt, dr in ((kT, kr, ktok, ktr), (vT, vr, vtok, vtr)):
            for c8 in range(0, NCK, 8):
                n = min(8, NCK - c8)
                h = kk % 2; kk += 1
                for c in range(n):
                    p.tr(ptb[h][:, c, :], srcT[:, (c8 + c) * 128:(c8 + c + 1) * 128], identb[:], r=[sr, idr], w=[ptbr[h]])
                p.cp("act" if h else "dve", dstt[:, c8:c8 + n, :], ptb[h][:, 0:n, :], r=[ptbr[h]], w=[dr])
        if STOP == 1:
            p.dma('sp', dst_ctx, stg[:, 0:256], r=[stgr], w=[dst_res]); return
        gst = p.sbuf("gst", [NCK, 4, 128], F32); gr = p.res("gst")
        lat, ctx, rs = src["g"]
        p.dma("sp", gst[0:2], ctx.rearrange("g (c s) -> c g s", s=128), r=[rs], w=[gr])
        p.dma("sp", gst[2:NCK], lat.rearrange("g (c s) -> c g s", s=128), r=[rs], w=[gr])
        G = p.sbuf("G", [128, 4, NCK], F32); Gr = p.res("G")
        pg = p.psum("pg", [128, 4, 128]); pgr = p.res("pg")
        for g in range(4):
            p.tr(pg[:, g, 0:NCK], gst[:, g, :], ident[0:NCK, 0:NCK], r=[gr, idr], w=[pgr])
        p.cp("dve", G[:], pg[:, :, 0:NCK], r=[pgr], w=[Gr])
        LFt = p.sbuf("LF", [128, 2, NCK], F32); lfr = p.res("LF")
        for d in range(2):
            p.act(LFt[:, d, :], G[:, 2 * d + 1, :], AF.Exp, r=[Gr], w=[lfr], scale=-1.0)
        p.act(LFt[:], LFt[:], AF.Ln, r=[lfr], w=[lfr], bias=1.0)
        p.ts("dve", LFt[:], LFt[:], -1.0, ALU.mult, r=[lfr], w=[lfr])
        CUM = p.sbuf("CUM", [128, 2, NCK], F32); IB = p.sbuf("IB", [128, 2, NCK], F32)
        SC = p.sbuf("SC", [128, 2, NCK], F32); ET = p.sbuf("ET", [128, 2, NCK], F32)
        pr = p.res("pre")
        pc = p.psum("pc", [128, 4, 128]); pcr = p.res("pc")
        for d in range(2):
            p.mm(pc[:, d, 0:NCK], tri[:, d, :], LFt[:, d, :], True, True, r=[cr, lfr], w=[pcr])
            p.mm(pc[:, 2 + d, 0:NCK], ones[:], LFt[:, d, :], True, True, r=[cr, lfr], w=[pcr])
        p.cp("dve", CUM[:], pc[:, 0:2, 0:NCK], r=[pcr], w=[pr])
        for d in range(2):
            p.tt("dve", IB[:, d, :], G[:, 2 * d, :], CUM[:, d, :], ALU.subtract, r=[Gr, pr], w=[pr])
            p.tt("dve", SC[:, d, :], pc[:, 2 + d, 0:NCK], IB[:, d, :], ALU.add, r=[pcr, pr], w=[pr])
        p.act(SC[:], SC[:], AF.Exp, r=[pr], w=[pr])
        p.act(ET[:], pc[:, 2:4, 0:NCK], AF.Exp, r=[pcr], w=[pr])

        if STOP == 2:
            p.dma('sp', dst_ctx[:, 0:66], SC[:, 0, :], r=[pr], w=[dst_res]); return
        H = p.sbuf("H", [128, TOK], F32); Hr = p.res("H")
        p.memset("pool", H[:], 0.0, w=[Hr])
        PA = [p.psum("pa%d" % d, [128, 4, 128]) for d in range(2)]
        PB = [p.psum("pb%d" % d, [128, 512]) for d in range(2)]
        rP1 = [p.res() for _ in range(2)]; rP2 = [p.res() for _ in range(2)]; rKQ = [p.res() for _ in range(2)]
        rNUM = [p.res() for _ in range(2)]; rDEN = [p.res() for _ in range(2)]; rST = [p.res() for _ in range(2)]
        def T(nm, shape, dt):
            return [p.sbuf("%s%d" % (nm, d), shape, dt) for d in range(2)], [p.res("%s%d" % (nm, d)) for d in range(2)]
        DG, DGr = T("DG", [128, 128], F32); WT, WTr = T("WT", [128, 128], F32); EC, ECr = T("EC", [128, 128], F32)
        QE, QEr = T("QE", [128, 128], BF16); STt, STr = T("ST", [128, 128], BF16)
        DD, DDr = T("DD", [128, 128], F32); HT, HTr = T("HT", [128, 128], F32)
        VS, VSr = T("VS", [128, 132], BF16); CF, CFr = T("CF", [128, 132], F32)
        CB, CBr = T("CB", [128, 128], BF16); NB, NBr = T("NB", [128, 128], BF16)
        for d in range(2):
            p.memset("pool", CF[d][:], 0.0, w=[CFr[d]]); p.memset("pool", CB[d][:], 0.0, w=[CBr[d]]); p.memset("pool", NB[d][:], 0.0, w=[NBr[d]])
        LE = 'dve'
        order = [list(range(NCK)), [1, 0] + list(range(NCK - 1, 1, -1))]
        for step in range(NCK):
            for d in range(2):
                c = order[d][step]
                cs = slice(c * 128, (c + 1) * 128)
                P1 = PA[d][:, 0, :]; P2 = PA[d][:, 1, :]; KQ = PA[d][:, 2, :]
                NUM = PB[d][:, 0:128]; DEN = PB[d][:, 128:256]; STP = PB[d][:, 256:256 + 129]
                p.ts(LE, DG[d][:], ident[:], CUM[:, d, c:c + 1], ALU.mult, r=[idr, pr], w=[DGr[d]])
                p.mm(P1, ones[:], DG[d][:], True, True, r=[cr, DGr[d]], w=[rP1[d]])
                p.mm(P2, ones[:], DG[d][:], True, False, r=[cr, DGr[d]], w=[rP2[d]])
                p.mm(P2, ident[:], mneg[:, d, :], False, True, r=[cr, idr], w=[rP2[d]])
                p.mm(KQ, kT[:, cs], qT[:, cs], True, True, r=[kr, qr], w=[rKQ[d]])
                p.act(WT[d][:], P2, AF.Exp, r=[rP2[d], pr], w=[WTr[d]], bias=IB[:, d, c:c + 1])
                p.act(EC[d][:], P1, AF.Exp, r=[rP1[d]], w=[ECr[d]])
                p.tt("dve", STt[d][:], KQ, WT[d][:], ALU.mult, r=[rKQ[d], WTr[d]], w=[STr[d]])
                p.tt(LE, QE[d][:], qT[:, cs], EC[d][:], ALU.mult, r=[qr, ECr[d]], w=[QEr[d]])
                p.mm(NUM, vtok[:, c, :], STt[d][:], True, False, r=[vtr, STr[d]], w=[rNUM[d]])
                p.mm(NUM, CB[d][:], QE[d][:], False, True, r=[CBr[d], QEr[d]], w=[rNUM[d]])
                p.mm(DEN, onesb[:], STt[d][:], True, False, r=[cr, STr[d]], w=[rDEN[d]])
                p.mm(DEN, NB[d][:], QE[d][:], False, True, r=[NBr[d], QEr[d]], w=[rDEN[d]])
                p.act(DD[d][:], DEN, AF.Abs, r=[rDEN[d]], w=[DDr[d]])
                p.ts("dve", DD[d][:], DD[d][:], 1.0, ALU.max, r=[DDr[d]], w=[DDr[d]])
                p.op("dve", lambda e, t_=DD[d]: e.reciprocal(out=t_[:], in_=t_[:]), r=[DDr[d]], w=[DDr[d]])
                p.tt("dve", HT[d][:], NUM, DD[d][:], ALU.mult, r=[rNUM[d], DDr[d]], w=[HTr[d]])
                p.tt(LE, H[:, cs], H[:, cs], HT[d][:], ALU.add, r=[Hr, HTr[d]], w=[Hr])
                p.ts(LE, VS[d][:, 0:128], vtok[:, c, :], SC[:, d, c:c + 1], ALU.mult, r=[vtr, pr], w=[VSr[d]])
                p.cp(LE, VS[d][:, 128:129], SC[:, d, c:c + 1], r=[pr], w=[VSr[d]])
                p.mm(STP, ktok[:, c, :], VS[d][:, 0:129], True, True, r=[ktr, VSr[d]], w=[rST[d]])
                p.stt(CF[d][:, 0:129], CF[d][:, 0:129], ET[:, d, c:c + 1], STP, ALU.mult, ALU.add, r=[CFr[d], pr, rST[d]], w=[CFr[d]])
                p.cp("act", CB[d][:], CF[d][:, 0:128], r=[CFr[d]], w=[CBr[d]])
                p.cp(LE, NB[d][:], CF[d][:, 128:129].to_broadcast([128, 128]), r=[CFr[d]], w=[NBr[d]])

        nw = p.sbuf("nw", [128, 1], F32); nwr = p.res("nw")
        p.dma("sp", nw[:], normw_ap.rearrange("(a b) -> a b", b=1), w=[nwr])
        lat, ctx, rs = src["oT"]
        p.dma("sp", stg[:, 0:256], ctx, r=[rs], w=[stgr]); p.dma("sp", stg[:, 256:], lat, r=[rs], w=[stgr])
        sq = p.sbuf("sq", [128, 512], F32); sqr = p.res("sq")
        rs_t = p.sbuf("rs_t", [128, 512], F32); rsr = p.res("rs_t")
        pn = [PB[0], PB[1]]; pnr = [p.res(), p.res()]
        ci = 0
        for t0 in range(0, TOK, 512):
            n = min(512, TOK - t0); ts_ = slice(t0, t0 + n)
            h = ci % 2; ci += 1
            p.act(sq[:, 0:n], H[:, ts_], AF.Square, r=[Hr], w=[sqr])
            p.mm(pn[h][:, 0:n], ones[:], sq[:, 0:n], True, True, r=[cr, sqr, rNUM[h], rDEN[h], rST[h]], w=[pnr[h], rNUM[h], rDEN[h], rST[h]])
            p.act(rs_t[:, 0:n], pn[h][:, 0:n], AF.Sqrt, r=[pnr[h]], w=[rsr], scale=1.0 / 128.0, bias=1e-6)
            p.op("dve", lambda e, o=rs_t[:, 0:n]: e.reciprocal(out=o, in_=o), r=[rsr], w=[rsr])
            p.tt("dve", H[:, ts_], H[:, ts_], rs_t[:, 0:n], ALU.mult, r=[Hr, rsr], w=[Hr])
            p.act(stg[:, ts_], stg[:, ts_], AF.Sigmoid, r=[stgr], w=[stgr])
            p.stt(H[:, ts_], H[:, ts_], nw[:, 0:1], stg[:, ts_], ALU.mult, ALU.mult, r=[Hr, nwr, stgr], w=[Hr])
        p.dma("sp", dst_ctx, H[:, 0:256], r=[Hr], w=[dst_res])
        p.dma("sp", dst_lat, H[:, 256:], r=[Hr], w=[dst_res])


D = 1024
EPS = 1e-6


def ttiles(NL, NC):
    out = [(t0, min(512, NL - t0), False) for t0 in range(0, NL, 512)]
    if NC:
        out.append((NL, NC, True))
    return out


def mod_stage(p, cv_ap, ada_w_ap, ada_b_ap, mod_d, mod_res, NM=12):
    with p.scope():
        cv = p.sbuf("cv", [128, 8, 2], F32); cvr = p.res("cv")
        p.dma("sp", cv[:], cv_ap.rearrange("(k p) n -> p k n", p=128), w=[cvr])
        p.act(cv[:], cv[:], AF.Silu, r=[cvr], w=[cvr])
        ab = p.sbuf("ab", [128, NM], F32); abr = p.res("ab")
        p.dma("sp", ab[:], ada_b_ap.rearrange("(m p) -> p m", p=128), w=[abr], slow=True)
        acc = p.sbuf("acc", [128, NM, 2], F32); accr = p.res("acc")
        for n in range(2):
            p.cp("dve", acc[:, :, n], ab[:], r=[abr], w=[accr])
        wt = [p.sbuf("wt%d" % i, [128, 128 * NM], F32) for i in range(2)]; wtr = [p.res("wt%d" % i) for i in range(2)]
        ps = p.psum("ps", [128, NM, 2]); psr = p.res("ps")
        for k in range(8):
            h = k % 2
            p.dma("sp" if h else "pool", wt[h][:], ada_w_ap[128 * k:128 * k + 128, :], w=[wtr[h]])
            for m in range(NM):
                p.mm(ps[:, m, :], wt[h][:, 128 * m:128 * m + 128], cv[:, k, :], True, True, r=[wtr[h], cvr], w=[psr])
            p.tt("dve", acc[:], acc[:], ps[:], ALU.add, r=[accr, psr], w=[accr])
        p.dma("sp", mod_d, acc[:], r=[accr], w=[mod_res])


def load_mod(p, mod_d, mod_res, nw_ap, sidx, scidx):
    modt = p.sbuf("modt", [128, 48, 2], F32); mr = p.res("modt")
    p.dma("sp", modt[:], mod_d, r=[mod_res], w=[mr], slow=True)
    nw = p.sbuf("nw", [128, 8], F32)
    p.dma("sp", nw[:], nw_ap.rearrange("(k p) -> p k", p=128), w=[mr], slow=True)
    A = p.sbuf("A", [128, 8, 2], F32)
    p.ts("dve", A[:], modt[:, 8 * scidx:8 * scidx + 8, :], 1.0, ALU.add, r=[mr], w=[mr])
    p.tt("dve", A[:], A[:], nw[:, :, None].to_broadcast([128, 8, 2]), ALU.mult, r=[mr], w=[mr])
    return modt, A, mr


def norm_mod_tile(p, xt, xr, n, A, SH, col, mr, ones, onr, ps, psr, sq, sqr, rstd, rsr, hb, hbr, hf=None, hfr=None):
    for k in range(8):
        p.act(sq[:, 0:n], xt[:, k, 0:n], AF.Square, r=[xr], w=[sqr])
        p.mm(ps[:, 0:n], ones[:], sq[:, 0:n], k == 0, k == 7, r=[onr, sqr], w=[psr])
    p.act(rstd[:, 0:n], ps[:, 0:n], AF.Sqrt, r=[psr], w=[rsr], scale=1.0 / D, bias=EPS)
    p.op("dve", lambda e, o=rstd[:, 0:n]: e.reciprocal(out=o, in_=o), r=[rsr], w=[rsr])
    for k in range(8):
        p.tt("dve", sq[:, 0:n], xt[:, k, 0:n], rstd[:, 0:n], ALU.mult, r=[xr, rsr, sqr], w=[sqr])
        if hf is not None:
            p.ts("dve", hf[:, k, 0:n], sq[:, 0:n], A[:, k, col:col + 1], ALU.mult, r=[sqr, mr], w=[hfr],
                 s2=SH[:, k, col:col + 1], op1=ALU.add)
            p.cp("act", hb[:, k, 0:n], hf[:, k, 0:n], r=[hfr], w=[hbr])
        else:
            p.ts("dve", hb[:, k, 0:n], sq[:, 0:n], A[:, k, col:col + 1], ALU.mult, r=[sqr, mr], w=[hbr],
                 s2=SH[:, k, col:col + 1], op1=ALU.add)


def norm1_stage(p, xT_d, x_res, NL, NC, mod_d, mod_res, nw_ap, hT_d, h_res):
    with p.scope():
        modt, A, mr = load_mod(p, mod_d, mod_res, nw_ap, 0, 1)
        SH = modt[:, 0:8, :]
        ones = p.sbuf("ones", [128, 128], F32); onr = p.res("ones"); p.memset("dve", ones[:], 1.0, w=[onr])
        xt = [p.sbuf("xt%d" % i, [128, 8, 512], F32) for i in range(2)]; xr = [p.res() for i in range(2)]
        hb = [p.sbuf("hb%d" % i, [128, 8, 512], BF16) for i in range(2)]; hbr = [p.res() for i in range(2)]
        sq = p.sbuf("sq", [128, 512], F32); sqr = p.res(); rstd = p.sbuf("rstd", [128, 512], F32); rsr = p.res()
        ps = p.psum("ps", [128, 512]); psr = p.res()
        for i, (t0, n, isc) in enumerate(ttiles(NL, NC)):
            h = i % 2
            p.dma("sp", xt[h][:, :, 0:n], xT_d[:, t0:t0 + n].rearrange("(k p) t -> p k t", p=128), r=[x_res], w=[xr[h]])
            norm_mod_tile(p, xt[h], xr[h], n, A, SH, 1 if isc else 0, mr, ones, onr, ps, psr, sq, sqr, rstd, rsr, hb[h], hbr[h])
            p.dma("pool", hT_d[i].rearrange("(k p) t -> p k t", p=128), hb[h][:, :, 0:n], r=[hbr[h]], w=[h_res[i]])


def load_w_bf16(p, w_cols_ap, m, wst, wstr, wb, wbr, eng="act", q="sp"):
    p.dma(q, wst[:, :, 0:m], w_cols_ap.rearrange("(k p) m -> p k m", p=128), w=[wstr])
    p.cp(eng, wb[:, :, 0:m], wst[:, :, 0:m], r=[wstr], w=[wbr])


def inproj_gate_stage(p, hT_d, h_res, NL, NC, w_in_ap, b_in_ap, off, nchunk, gT_d, g_res):
    NT = NL + NC
    GW = 4
    with p.scope():
        hT = p.sbuf("hT", [128, 8, NT], BF16); hr = p.res("hT")
        for i, (t0, n, isc) in enumerate(ttiles(NL, NC)):
            p.dma("sp", hT[:, :, t0:t0 + n], hT_d[i].rearrange("(k p) t -> p k t", p=128), r=[h_res[i]], w=[hr])
        bias = p.sbuf("bias", [128, nchunk], F32); br = p.res("bias")
        p.dma("sp", bias[:], b_in_ap[off:off + 128 * nchunk].rearrange("(m p) -> p m", p=128), w=[br], slow=True)
        wst = [p.sbuf("wst%d" % i, [128, 8, 128 * GW], F32) for i in range(2)]; wstr = [p.res() for i in range(2)]
        wb = [p.sbuf("wb%d" % i, [128, 8, 128 * GW], BF16) for i in range(2)]; wbr = [p.res() for i in range(2)]
        ot = [p.sbuf("ot%d" % i, [128, NT], BF16) for i in range(2)]; otr = [p.res() for i in range(2)]
        ps = [p.psum("ps%d" % i, [128, 512]) for i in range(6)]; psr = [p.res() for i in range(6)]
        kk = 0
        ngrp = nchunk // GW
        def load(g):
            h = g % 2
            c0 = off + 128 * GW * g
            p.dma("sp" if h else "pool", wst[h][:], w_in_ap[:, c0:c0 + 128 * GW].rearrange("(k p) m -> p k m", p=128), w=[wstr[h]])
            p.cp("act", wb[h][:], wst[h][:], r=[wstr[h]], w=[wbr[h]])
        load(0)
        for g in range(ngrp):
            h = g % 2
            if g + 1 < ngrp:
                load(g + 1)
            for mi in range(GW):
                m = g * GW + mi
                o = m % 2
                for (t0, n, isc) in ttiles(NL, NC):
                    b_ = kk % 6; kk += 1
                    for k in range(8):
                        p.mm(ps[b_][:, 0:n], wb[h][:, k, 128 * mi:128 * mi + 128], hT[:, k, t0:t0 + n], k == 0, k == 7, r=[wbr[h], hr], w=[psr[b_]])
                    p.act(ot[o][:, t0:t0 + n], ps[b_][:, 0:n], AF.Sigmoid, r=[psr[b_], br], w=[otr[o]], bias=bias[:, m:m + 1])
                p.dma("sp", gT_d[128 * m:128 * m + 128, :], ot[o][:], r=[otr[o]], w=[g_res])


def inproj_mix_stage(p, hall_d, hall_res, NL, NC, w_in_ap, b_in_ap, col_list, zlat_d, zctx_d, z_res):
    NT = NL + NC
    nchunk = len(col_list)
    with p.scope():
        wst = p.sbuf("wst", [128, 8, 128], F32); wstr = p.res()
        W = p.sbuf("W", [128, nchunk, 8, 128], BF16); Wr = p.res("W")
        bias = p.sbuf("bias", [128, nchunk], F32); br = p.res("bias")
        p.memset("dve", bias[:], 0.0, w=[br])
        p.memset("dve", W[:], 0.0, w=[Wr])
        for i, (c0, m) in enumerate(col_list):
            p.dma("sp", wst[:, :, 0:m], w_in_ap[:, c0:c0 + m].rearrange("(k p) m -> p k m", p=128), w=[wstr], slow=(m < 128))
            p.cp("act" if i % 2 else "dve", W[:, i, :, 0:m], wst[:, :, 0:m], r=[wstr], w=[Wr])
            p.dma("pool", bias[0:m, i:i + 1], b_in_ap[c0:c0 + m].rearrange("(a b) -> a b", b=1), w=[br])
        hT = [p.sbuf("hT%d" % i, [128, 8, 512], BF16) for i in range(2)]; hr = [p.res() for i in range(2)]
        ot = [p.sbuf("ot%d" % i, [128, nchunk, 512], F32) for i in range(2)]; otr = [p.res() for i in range(2)]
        ps = [p.psum("ps%d" % i, [128, 512]) for i in range(4)]; psr = [p.res() for i in range(4)]
        kk = 0; ti = 0
        for r in range(4):
            for i_t, (t0, n, isc) in enumerate(ttiles(NL, NC)):
                h = ti % 2; ti += 1
                p.dma("sp", hT[h][:, :, 0:n], hall_d[i_t][1024 * r:1024 * r + 1024, :].rearrange("(k p) t -> p k t", p=128),
                      r=[hall_res[i_t]], w=[hr[h]])
                for i in range(nchunk):
                    b_ = kk % 4; kk += 1
                    for k in range(8):
                        p.mm(ps[b_][:, 0:n], W[:, i, k, :], hT[h][:, k, 0:n], k == 0, k == 7, r=[Wr, hr[h]], w=[psr[b_]])
                    p.act(ot[h][:, i, 0:n], ps[b_][:, 0:n], AF.Identity, r=[psr[b_], br], w=[otr[h]], bias=bias[:, i:i + 1])
                if isc:
                    dst = zctx_d[:, :, NC * r:NC * r + n]
                else:
                    dst = zlat_d[:, :, NL * r + t0:NL * r + t0 + n]
                p.dma("pool", dst.rearrange("i p t -> p i t"), ot[h][:, :, 0:n], r=[otr[h]], w=[z_res])


def yasm_stage(p, srcs, skip_ap, y_own_d, y_res, LL, LC):
    with p.scope():
        sk = p.sbuf("sk", [128, 1], F32); skr = p.res("sk")
        p.dma("sp", sk[:], skip_ap.rearrange("(a b) -> a b", b=1), w=[skr])
        CW = 2048
        A = [p.sbuf("A%d" % i, [128, CW], F32) for i in range(3)]; Ar = [p.res() for i in range(3)]
        O = [p.sbuf("O%d" % i, [128, CW], BF16) for i in range(2)]; Or = [p.res() for i in range(2)]
        kk = 0
        pieces = [(0, t0, min(CW, LL - t0), t0) for t0 in range(0, LL, CW)] + ([(1, 0, LC, LL)] if LC else [])
        for (which, t0, n, o0) in pieces:
            for i, nm in enumerate(("z", "c2", "x2")):
                ap = srcs[nm][which]
                p.dma("sp", A[i][:, 0:n], ap[:, t0:t0 + n], r=[srcs[nm][2]], w=[Ar[i]])
            p.stt(A[0][:, 0:n], A[0][:, 0:n], sk[:, 0:1], A[1][:, 0:n], ALU.mult, ALU.add, r=[Ar[0], Ar[1], skr], w=[Ar[0]])
            h = kk % 2; kk += 1
            p.tt("dve", O[h][:, 0:n], A[0][:, 0:n], A[2][:, 0:n], ALU.mult, r=[Ar[0], Ar[2]], w=[Or[h]])
            for (ap_, rs_, a0, an) in y_own_d(0, o0, n):
                p.dma("pool", ap_, O[h][:, a0:a0 + an], r=[Or[h]], w=[rs_])
            for bi, nm in ((1, "fn"), (2, "ml")):
                ap = srcs[nm][which]
                p.dma("sp", A[bi][:, 0:n], ap[:, t0:t0 + n], r=[srcs[nm][2]], w=[Ar[bi]])
                h = kk % 2; kk += 1
                p.cp("act", O[h][:, 0:n], A[bi][:, 0:n], r=[Ar[bi]], w=[Or[h]])
                for (ap_, rs_, a0, an) in y_own_d(bi, o0, n):
                    p.dma("pool", ap_, O[h][:, a0:a0 + an], r=[Or[h]], w=[rs_])


def merge_stage(p, y_all_d, y_res, hT_d, h_res, wgi_ap, bgi_ap, oh_ap, xT_d, x_res, NL, NC, mod_d, mod_res, wbr_ap, wout_ap, do_ctx):
    LL = 4 * NL
    with p.scope():
        modt = p.sbuf("modt", [128, 48, 2], F32); mr = p.res("modt")
        p.dma("sp", modt[:], mod_d, r=[mod_res], w=[mr])
        oh = p.sbuf("oh", [128, 4], F32); ohr = p.res("oh")
        p.dma("sp", oh[:], oh_ap, w=[ohr])
        bias = p.sbuf("bias", [128, 24], F32)
        p.dma("sp", bias[:], bgi_ap.rearrange("(m p) -> p m", p=128), w=[ohr], slow=True)
        wst = p.sbuf("wst", [128, 8, 1024], F32); wstr = p.res()
        WB = p.sbuf("WB", [128, 12, 1024], BF16); WO = p.sbuf("WO", [128, 8, 1024], BF16); Wr = p.res("W")
        WG = p.sbuf("WG", [128, 8, 3072], BF16)
        for br in range(3):
            p.dma("sp", wst[:, 0:4, :], wbr_ap[br].rearrange("(r p) d -> p r d", p=128), w=[wstr])
            p.cp("act" if br % 2 else "dve", WB[:, 4 * br:4 * br + 4, :], wst[:, 0:4, :], r=[wstr], w=[Wr])
        p.dma("sp", wst[:], wout_ap.rearrange("(k p) d -> p k d", p=128), w=[wstr])
        p.cp("act", WO[:], wst[:], r=[wstr], w=[Wr])
        for c in range(3):
            p.dma("sp" if c % 2 else "pool", wst[:], wgi_ap[:, 1024 * c:1024 * c + 1024].rearrange("(k p) m -> p k m", p=128), w=[wstr])
            p.cp("dve" if c % 2 else "act", WG[:, :, 1024 * c:1024 * c + 1024], wst[:], r=[wstr], w=[Wr])
        Yc = [p.sbuf("Yc%d" % i, [128, 12, 512], BF16) for i in range(2)]; Ycr = [p.res() for i in range(2)]
        Y = p.sbuf("Y", [128, 12, 512], BF16); Yr = p.res("Y")
        hT = p.sbuf("hT", [128, 8, 512], BF16); hr = p.res("hT")
        Gs = [p.sbuf("Gs%d" % i, [128, 512], BF16) for i in range(6)]; Gsr = [p.res() for i in range(6)]
        xt = p.sbuf("xt", [128, 8, 512], F32); xr = p.res("xt")
        mg = p.sbuf("mg", [128, 8, 512], BF16); mgr = p.res("mg")
        t1 = p.sbuf("t1", [128, 512], F32); t1r = p.res(); t2 = p.sbuf("t2", [128, 512], F32); t2r = p.res()
        ps = [p.psum("ps%d" % i, [128, 512]) for i in range(8)]; psr = [p.res() for i in range(8)]
        kk = 0; gi = 0
        tiles = ttiles(NL, NC if do_ctx else 0)
        for ti_, (t0, n, isc) in enumerate(tiles):
            col = 1 if isc else 0
            for jj in range(4):
                c0 = (LL + NC * jj) if isc else (NL * jj + t0)
                h = jj % 2
                for br in range(3):
                    ap_, rs_ = y_all_d(br, c0, n)
                    p.dma("sp" if br % 2 else "pool", Yc[h][:, 4 * br:4 * br + 4, 0:n],
                          ap_.rearrange("(q p) t -> p q t", p=128), r=[rs_], w=[Ycr[h]])
                if jj == 0:
                    p.ts("dve", Y[:, :, 0:n], Yc[h][:, :, 0:n], oh[:, 0:1], ALU.mult, r=[Ycr[h], ohr], w=[Yr])
                else:
                    p.stt(Y[:, :, 0:n], Yc[h][:, :, 0:n], oh[:, jj:jj + 1], Y[:, :, 0:n], ALU.mult, ALU.add, r=[Ycr[h], ohr, Yr], w=[Yr])
            p.dma("sp", hT[:, :, 0:n], hT_d[ti_].rearrange("(k p) t -> p k t", p=128), r=[h_res[ti_]], w=[hr])
            p.dma("pool", xt[:, :, 0:n], xT_d[:, t0:t0 + n].rearrange("(k p) t -> p k t", p=128), r=[x_res], w=[xr])
            for m in range(8):
                gsel = []
                for br in range(3):
                    b_ = kk % 8; kk += 1
                    g_ = gi % 6; gi += 1; gsel.append(g_)
                    cg = (br * 8 + m) * 128
                    for k in range(8):
                        p.mm(ps[b_][:, 0:n], WG[:, k, cg:cg + 128], hT[:, k, 0:n], k == 0, k == 7, r=[Wr, hr], w=[psr[b_]])
                    p.act(Gs[g_][:, 0:n], ps[b_][:, 0:n], AF.Sigmoid, r=[psr[b_], ohr], w=[Gsr[g_]], bias=bias[:, br * 8 + m:br * 8 + m + 1])
                pb = []
                for br in range(3):
                    b_ = kk % 8; kk += 1; pb.append(b_)
                    for r in range(4):
                        p.mm(ps[b_][:, 0:n], WB[:, 4 * br + r, 128 * m:128 * m + 128], Y[:, 4 * br + r, 0:n], r == 0, r == 3,
                             r=[Wr, Yr], w=[psr[b_]])
                p.tt("dve", t1[:, 0:n], ps[pb[0]][:, 0:n], Gs[gsel[0]][:, 0:n], ALU.mult, r=[psr[pb[0]], Gsr[gsel[0]]], w=[t1r])
                p.tt("dve", t2[:, 0:n], ps[pb[1]][:, 0:n], Gs[gsel[1]][:, 0:n], ALU.mult, r=[psr[pb[1]], Gsr[gsel[1]]], w=[t2r])
                p.tt("dve", t1[:, 0:n], t1[:, 0:n], t2[:, 0:n], ALU.add, r=[t1r, t2r], w=[t1r])
                p.tt("dve", t2[:, 0:n], ps[pb[2]][:, 0:n], Gs[gsel[2]][:, 0:n], ALU.mult, r=[psr[pb[2]], Gsr[gsel[2]]], w=[t2r])
                p.tt("dve", mg[:, m, 0:n], t1[:, 0:n], t2[:, 0:n], ALU.add, r=[t1r, t2r], w=[mgr])
            for m in range(8):
                b_ = kk % 8; kk += 1
                for k in range(8):
                    p.mm(ps[b_][:, 0:n], WO[:, k, 128 * m:128 * m + 128], mg[:, k, 0:n], k == 0, k == 7, r=[Wr, mgr], w=[psr[b_]])
                p.stt(xt[:, m, 0:n], ps[b_][:, 0:n], modt[:, 16 + m, col:col + 1], xt[:, m, 0:n], ALU.mult, ALU.add,
                      r=[psr[b_], mr, xr], w=[xr])
            p.dma("sp", xT_d[:, t0:t0 + n].rearrange("(k p) t -> p k t", p=128), xt[:, :, 0:n], r=[xr], w=[x_res])


def moe_stage(p, xT_d, x_res, NL, NC, mod_d, mod_res, nw_ap, wr_ap, br_ap, wg_ap, wu_ap, wd_ap, do_ctx,
              final_nw_ap=None, out_d=None, out_res=None):
    NCX = NC if do_ctx else 0
    NT = NL + NCX
    tiles = ttiles(NL, NCX)
    nsub = (NT + 127) // 128
    with p.scope():
        modt, A, mr = load_mod(p, mod_d, mod_res, nw_ap, 3, 4)
        SH = modt[:, 24:32, :]
        ones = p.sbuf("ones", [128, 128], F32); onr = p.res("ones"); p.memset("dve", ones[:], 1.0, w=[onr])
        identf, idr = make_ident(p, F32, "idf")
        sq = p.sbuf("sq", [128, 512], F32); sqr = p.res(); rstd = p.sbuf("rstd", [128, 512], F32); rsr = p.res()
        xt = p.sbuf("xt", [128, 8, 512], F32); xr = p.res("xt")
        hf = p.sbuf("hf", [128, 8, 512], F32); hfr = p.res("hf")
        H2 = p.sbuf("H2", [128, 8, NT], BF16); h2r = p.res("H2")
        WR = p.sbuf("WR", [128, 8, 20], F32); wrr = p.res("WR")
        p.dma("sp", WR[:], wr_ap.rearrange("(k p) n -> p k n", p=128), w=[wrr], slow=True)
        BR = p.sbuf("BR", [128, 20], F32)
        p.dma("sp", BR[:], br_ap.partition_broadcast(128), w=[wrr])
        CWt = p.sbuf("CWt", [128, nsub, 16], F32); cwr = p.res("CW")
        ps = [p.psum("ps%d" % i, [128, 512]) for i in range(6)]; psr = [p.res() for i in range(6)]
        pr_ = p.psum("pr", [128, 32]); prr = p.res("pr")
        def st(nm, w):
            return p.sbuf(nm, [128, w], F32)
        L_ = st("L", 20); gm = st("gm", 1); ge = st("ge", 4); gs = st("gs", 1); gmask = st("gmask", 4)
        tmp16 = st("tmp16", 16); eg = st("eg", 4); m1 = st("m1", 1); mk1 = st("mk1", 4); eg2 = st("eg2", 4); m2 = st("m2", 1)
        mk2 = st("mk2", 4); w1 = st("w1", 1); w2 = st("w2", 1); cwe = st("cwe", 4)
        rr = p.res("route")
        for (t0, n, isc) in tiles:
            col = 1 if isc else 0
            p.dma("sp", xt[:, :, 0:n], xT_d[:, t0:t0 + n].rearrange("(k p) t -> p k t", p=128), r=[x_res], w=[xr])
            norm_mod_tile(p, xt, xr, n, A, SH, col, mr, ones, onr, ps[0], psr[0], sq, sqr, rstd, rsr,
                          H2[:, :, t0:t0 + n], h2r, hf=hf, hfr=hfr)
            for s0 in range(0, n, 128):
                sn = min(128, n - s0); si = (t0 + s0) // 128
                for k in range(8):
                    p.mm(pr_[0:sn, 0:20], hf[:, k, s0:s0 + sn], WR[:, k, :], k == 0, k == 7, r=[hfr, wrr], w=[prr])
                R = [rr]
                p.tt("dve", L_[0:sn], pr_[0:sn, 0:20], BR[0:sn], ALU.add, r=[prr, wrr, rr], w=R)
                p.op("dve", lambda e, o=gm[0:sn], i=L_[0:sn, 0:4]: e.tensor_reduce(out=o, in_=i, axis=AX.X, op=ALU.max), r=R, w=R)
                p.ts("dve", gmask[0:sn], L_[0:sn, 0:4], gm[0:sn, 0:1], ALU.is_equal, r=R, w=R)
                p.ts("dve", gm[0:sn], gm[0:sn], -1.0, ALU.mult, r=R, w=R)
                p.act(ge[0:sn], L_[0:sn, 0:4], AF.Exp, r=R, w=R, bias=gm[0:sn, 0:1])
                p.op("dve", lambda e, o=gs[0:sn], i=ge[0:sn]: e.tensor_reduce(out=o, in_=i, axis=AX.X, op=ALU.add), r=R, w=R)
                p.op("dve", lambda e, o=gs[0:sn]: e.reciprocal(out=o, in_=o), r=R, w=R)
                p.tt("dve", tmp16[0:sn].rearrange("p (g e) -> p g e", e=4), L_[0:sn, 4:20].rearrange("p (g e) -> p g e", e=4),
                     gmask[0:sn, :, None].to_broadcast([sn, 4, 4]), ALU.mult, r=R, w=R)
                p.op("dve", lambda e, o=eg[0:sn], i=tmp16[0:sn].rearrange("p (g e) -> p e g", e=4): e.tensor_reduce(out=o, in_=i, axis=AX.X, op=ALU.add), r=R, w=R)
                p.op("dve", lambda e, o=m1[0:sn], i=eg[0:sn]: e.tensor_reduce(out=o, in_=i, axis=AX.X, op=ALU.max), r=R, w=R)
                p.ts("dve", mk1[0:sn], eg[0:sn], m1[0:sn, 0:1], ALU.is_equal, r=R, w=R)
                p.stt(eg2[0:sn], mk1[0:sn], -1e30, eg[0:sn], ALU.mult, ALU.add, r=R, w=R)
                p.op("dve", lambda e, o=m2[0:sn], i=eg2[0:sn]: e.tensor_reduce(out=o, in_=i, axis=AX.X, op=ALU.max), r=R, w=R)
                p.ts("dve", mk2[0:sn], eg2[0:sn], m2[0:sn, 0:1], ALU.is_equal, r=R, w=R)
                p.tt("dve", w1[0:sn], m2[0:sn], m1[0:sn], ALU.subtract, r=R, w=R)
                p.act(w1[0:sn], w1[0:sn], AF.Exp, r=R, w=R)
                p.ts("dve", w1[0:sn], w1[0:sn], 1.0, ALU.add, r=R, w=R)
                p.op("dve", lambda e, o=w1[0:sn]: e.reciprocal(out=o, in_=o), r=R, w=R)
                p.ts("dve", w2[0:sn], w1[0:sn], -1.0, ALU.mult, r=R, w=R, s2=1.0, op1=ALU.add)
                p.tt("dve", w1[0:sn], w1[0:sn], gs[0:sn], ALU.mult, r=R, w=R)
                p.tt("dve", w2[0:sn], w2[0:sn], gs[0:sn], ALU.mult, r=R, w=R)
                p.ts("dve", cwe[0:sn], mk1[0:sn], w1[0:sn, 0:1], ALU.mult, r=R, w=R)
                p.stt(cwe[0:sn], mk2[0:sn], w2[0:sn, 0:1], cwe[0:sn], ALU.mult, ALU.add, r=R, w=R)
                p.cp("dve", tmp16[0:sn].rearrange("p (g e) -> p g e", e=4), cwe[0:sn, None, :].to_broadcast([sn, 4, 4]), r=R, w=R)
                p.tt("dve", CWt[0:sn, si, :].rearrange("p (g e) -> p g e", e=4), tmp16[0:sn].rearrange("p (g e) -> p g e", e=4),
                     gmask[0:sn, :, None].to_broadcast([sn, 4, 4]), ALU.mult, r=R, w=[cwr, rr])
        ACC = p.sbuf("ACC", [128, nsub, 1024], F32); accr = p.res("ACC")
        p.memset("dve", ACC[:], 0.0, w=[accr])
        wst = [p.sbuf("wst%d" % i, [128, 8, 256], F32) for i in range(2)]; wstr = [p.res() for i in range(2)]
        WG = [p.sbuf("WG%d" % i, [128, 8, 256], BF16) for i in range(2)]; WU = [p.sbuf("WU%d" % i, [128, 8, 256], BF16) for i in range(2)]
        WD = [p.sbuf("WD%d" % i, [128, 2, 1024], BF16) for i in range(2)]
        wer = [p.res() for i in range(2)]
        SG = p.sbuf("SG", [128, 512], BF16); sgr = p.res()
        AA = p.sbuf("AA", [128, 2, 512], BF16); aar = p.res()
        kk = 0
        for e_ in range(16):
            h = e_ % 2
            p.dma("sp", wst[0][:], wg_ap[e_].rearrange("(k p) m -> p k m", p=128), w=[wstr[0]])
            p.cp("act", WG[h][:], wst[0][:], r=[wstr[0]], w=[wer[h]])
            p.dma("pool", wst[1][:], wu_ap[e_].rearrange("(k p) m -> p k m", p=128), w=[wstr[1]])
            p.cp("act", WU[h][:], wst[1][:], r=[wstr[1]], w=[wer[h]])
            p.dma("sp", wst[0][:].rearrange("p a b -> p (a b)").rearrange("p (c d) -> p c d", c=2), wd_ap[e_].rearrange("(c p) d -> p c d", p=128), w=[wstr[0]])
            p.cp("act", WD[h][:], wst[0][:].rearrange("p a b -> p (a b)").rearrange("p (c d) -> p c d", c=2), r=[wstr[0]], w=[wer[h]])
            for (t0, n, isc) in tiles:
                for hc in range(2):
                    bg = kk % 6; kk += 1; bu = kk % 6; kk += 1
                    for k in range(8):
                        p.mm(ps[bg][:, 0:n], WG[h][:, k, 128 * hc:128 * hc + 128], H2[:, k, t0:t0 + n], k == 0, k == 7, r=[wer[h], h2r], w=[psr[bg]])
                    for k in range(8):
                        p.mm(ps[bu][:, 0:n], WU[h][:, k, 128 * hc:128 * hc + 128], H2[:, k, t0:t0 + n], k == 0, k == 7, r=[wer[h], h2r], w=[psr[bu]])
                    p.act(SG[:, 0:n], ps[bg][:, 0:n], AF.Silu, r=[psr[bg]], w=[sgr])
                    p.tt("dve", AA[:, hc, 0:n], ps[bu][:, 0:n], SG[:, 0:n], ALU.mult, r=[psr[bu], sgr], w=[aar])
                for s0 in range(0, n, 128):
                    sn = min(128, n - s0); si = (t0 + s0) // 128
                    for dh in range(2):
                        b_ = kk % 6; kk += 1
                        for hc in range(2):
                            p.mm(ps[b_][0:sn, :], AA[:, hc, s0:s0 + sn], WD[h][:, hc, 512 * dh:512 * dh + 512], hc == 0, hc == 1,
                                 r=[aar, wer[h]], w=[psr[b_]])
                        p.stt(ACC[0:sn, si, 512 * dh:512 * dh + 512], ps[b_][0:sn, :], CWt[0:sn, si, e_:e_ + 1],
                              ACC[0:sn, si, 512 * dh:512 * dh + 512], ALU.mult, ALU.add, r=[psr[b_], cwr, accr], w=[accr])
        if final_nw_ap is not None:
            fw_ = p.sbuf("fw", [128, 8], F32); fwr = p.res()
            p.dma("sp", fw_[:], final_nw_ap.rearrange("(k p) -> p k", p=128), w=[fwr], slow=True)
        for (t0, n, isc) in tiles:
            col = 1 if isc else 0
            p.dma("sp", xt[:, :, 0:n], xT_d[:, t0:t0 + n].rearrange("(k p) t -> p k t", p=128), r=[x_res], w=[xr])
            for m in range(8):
                b_ = kk % 6; kk += 1
                for s0 in range(0, n, 128):
                    sn = min(128, n - s0); si = (t0 + s0) // 128
                    p.tr(ps[b_][:, s0:s0 + sn], ACC[0:sn, si, 128 * m:128 * m + 128], identf[0:sn, 0:sn], r=[accr, idr], w=[psr[b_]])
                p.stt(xt[:, m, 0:n], ps[b_][:, 0:n], modt[:, 40 + m, col:col + 1], xt[:, m, 0:n], ALU.mult, ALU.add,
                      r=[psr[b_], mr, xr], w=[xr])
            if final_nw_ap is None:
                p.dma("pool", xT_d[:, t0:t0 + n].rearrange("(k p) t -> p k t", p=128), xt[:, :, 0:n], r=[xr], w=[x_res])
            elif not isc:
                for k in range(8):
                    p.act(sq[:, 0:n], xt[:, k, 0:n], AF.Square, r=[xr], w=[sqr])
                    p.mm(ps[0][:, 0:n], ones[:], sq[:, 0:n], k == 0, k == 7, r=[onr, sqr], w=[psr[0]])
                p.act(rstd[:, 0:n], ps[0][:, 0:n], AF.Sqrt, r=[psr[0]], w=[rsr], scale=1.0 / D, bias=EPS)
                p.op("dve", lambda e, o=rstd[:, 0:n]: e.reciprocal(out=o, in_=o), r=[rsr], w=[rsr])
                for k in range(8):
                    p.stt(hf[:, k, 0:n], xt[:, k, 0:n], fw_[:, k:k + 1], rstd[:, 0:n], ALU.mult, ALU.mult, r=[xr, fwr, rsr], w=[hfr])
                p.dma("pool", out_d[:, t0:t0 + n].rearrange("(k p) t -> p k t", p=128), hf[:, :, 0:n], r=[hfr], w=[out_res])

NLAT, NCTX = 2048, 64
LLAT, LCTX = 8192, 256
GROUPS = [[0, 1, 2, 3], [4, 5, 6, 7]]
DEPTH = 2
OFF_FN, OFF_ML, OFF_MLG, OFF_GATE = 1536, 2048, 4096, 4112


def build_program(const_np):
    p = Prog()
    I = {}

    def inp(name, shape, dt=F32):
        I[name] = p.dram(name, shape, dt, "ExternalInput")
        return I[name]

    inp("xT0", [1024, NLAT]); inp("cT0", [1024, NCTX]); inp("cv", [1024, 2]); inp("oh", [128, 4]); inp("norm_f", [1024])
    for k, v in const_np.items():
        inp(k, v.shape, F32 if v.dtype == np.float32 else BF16)
    for l in range(DEPTH):
        L = "_%d" % l
        inp("ada_w" + L, [1024, 1536]); inp("ada_b" + L, [1536]); inp("n1w" + L, [1024]); inp("n2w" + L, [1024])
        inp("w_gate_in" + L, [1024, 3072]); inp("b_gate_in" + L, [3072]); inp("w_mix" + L, [1024, 1028]); inp("b_mix" + L, [1028])
        inp("hy_cw" + L, [3, 3, 384]); inp("hy_cb" + L, [384]); inp("ml_cw" + L, [3, 3, 256]); inp("ml_cb" + L, [256])
        inp("f_w1" + L, [33, 64]); inp("f_b1" + L, [64]); inp("f_freq" + L, [64]); inp("f_w2" + L, [64, 64]); inp("f_b2" + L, [64])
        inp("f_w3" + L, [64, 4, 128]); inp("decay" + L, [128, 4]); inp("skip" + L, [2, 128]); inp("mlnw" + L, [128])
        inp("wbr" + L, [3, 512, 1024]); inp("wout" + L, [1024, 1024])
        inp("wr" + L, [1024, 20]); inp("br" + L, [20]); inp("wg" + L, [16, 1024, 256]); inp("wu" + L, [16, 1024, 256]); inp("wd" + L, [16, 256, 1024])
    outT = p.dram("outT", [1024, NLAT], F32, "ExternalOutput"); out_res = p.res("outT")

    NT = NLAT + NCTX
    S = {}

    def scr(name, shape, dt=F32):
        S[name] = (p.dram("s_" + name, shape, dt), p.res("s_" + name))
        return S[name]

    scr("mod_own", [128, 12, 2]); scr("mod_all", [4 * 128, 24]); scr("mod_full", [128, 48, 2])
    scr("xT", [1024, NT])
    TT = ttiles(NLAT, NCTX)
    hown = [scr("hown%d" % i, [1024, n], BF16) for i, (t0, n, isc) in enumerate(TT)]
    hall = [scr("hall%d" % i, [4096, n], BF16) for i, (t0, n, isc) in enumerate(TT)]
    YCH = [(0, 3072), (3072, 3072), (6144, 2304)]
    yown = [[scr("yown%d_%d" % (br, ck), [128, w_], BF16) for ck, (c0_, w_) in enumerate(YCH)] for br in range(3)]
    yall = [[scr("yall%d_%d" % (br, ck), [512, w_], BF16) for ck, (c0_, w_) in enumerate(YCH)] for br in range(3)]

    def y_own_fn(br, col0, n):
        out = []
        for ck, (c0_, w_) in enumerate(YCH):
            lo = max(col0, c0_); hi = min(col0 + n, c0_ + w_)
            if hi > lo:
                out.append((yown[br][ck][0][:, lo - c0_:hi - c0_], yown[br][ck][1], lo - col0, hi - lo))
        return out

    def y_all_fn(br, col0, n):
        for ck, (c0_, w_) in enumerate(YCH):
            if c0_ <= col0 and col0 + n <= c0_ + w_:
                return yall[br][ck][0][:, col0 - c0_:col0 - c0_ + n], yall[br][ck][1]
        raise AssertionError("y tile straddles chunks")
    scr("zlat", [9, 128, LLAT]); scr("zctx", [9, 128, LCTX]); scr("cvl", [5, 128, LLAT]); scr("cvc", [5, 128, LCTX])
    scr("fl", [4, 128, LLAT]); scr("fc", [4, 128, LCTX])
    for nm in ("c1", "c2", "zz", "fn", "ml"):
        scr(nm + "l", [128, LLAT]); scr(nm + "c", [128, LCTX])

    tabs = {k: I[k] for k in const_np}
    xT, xres = S["xT"]
    with p.scope():
        t = p.sbuf("t", [128, 8, NT], F32); tr = p.res()
        p.dma("sp", t[:, :, 0:NLAT], I["xT0"].rearrange("(k p) t -> p k t", p=128), w=[tr])
        p.dma("pool", t[:, :, NLAT:NT], I["cT0"].rearrange("(k p) t -> p k t", p=128), w=[tr])
        p.dma("sp", xT.rearrange("(k p) t -> p k t", p=128), t[:], r=[tr], w=[xres])

    mod_view = S["mod_all"][0].rearrange("(r p) (m n) -> p r m n", p=128, n=2)

    for l in range(DEPTH):
        L = "_%d" % l
        last = (l == DEPTH - 1)
        W = lambda nm: I[nm + L]
        p.label = 'mod_stage'; mod_stage(p, I["cv"], W("ada_w"), W("ada_b"), S["mod_own"][0], S["mod_own"][1], NM=12)
        p.coll("AllGather", S["mod_all"][0], S["mod_own"][0].rearrange("p m n -> p (m n)"), GROUPS, r=[S["mod_own"][1]], w=[S["mod_all"][1]])
        with p.scope():
            mt = p.sbuf("mt", [128, 48, 2], F32); mtr = p.res()
            p.dma("sp", mt[:].rearrange("p (r m) n -> p r m n", r=4), mod_view, r=[S["mod_all"][1]], w=[mtr], slow=True)
            p.dma("sp", S["mod_full"][0], mt[:], r=[mtr], w=[S["mod_full"][1]])
        modv, modr = S["mod_full"]
        p.label = 'norm1_stage'; norm1_stage(p, xT, xres, NLAT, NCTX, modv, modr, W("n1w"), [h_[0] for h_ in hown], [h_[1] for h_ in hown])
        for i in range(len(TT)):
            p.coll("AllGather", hall[i][0], hown[i][0], GROUPS, r=[hown[i][1]], w=[hall[i][1]])
        fw = {k: W(k) for k in ("f_w1", "f_b1", "f_freq", "f_w2", "f_b2", "f_w3", "decay")}
        p.label = 'hfilt_stage'; hfilt_stage(p, I["featsT_l"], I["trow_l"], fw, 0, LLAT, S["fl"][0], S["fl"][1])
        if not last:
            p.label = 'hfilt_stage'; hfilt_stage(p, I["featsT_c"], I["trow_c"], fw, 0, LCTX, S["fc"][0], S["fc"][1])
        cols = [(128 * i, 128) for i in range(8)] + [(1024, 4)]
        p.label = 'inproj_mix_stage'; inproj_mix_stage(p, [h_[0] for h_ in hall], [h_[1] for h_ in hall], NLAT, NCTX, W("w_mix"), W("b_mix"), cols, S["zlat"][0], S["zctx"][0], S["zlat"][1])
        zl, zc, zr = S["zlat"][0], S["zctx"][0], S["zlat"][1]
        cvl, cvc = S["cvl"][0], S["cvc"][0]
        cvr = S["cvl"][1]
        jl = [(zl[0], zr, W("hy_cw"), W("hy_cb"), 0, False, cvl[0], cvr), (zl[1], zr, W("hy_cw"), W("hy_cb"), 128, False, cvl[1], cvr),
              (zl[2], zr, W("hy_cw"), W("hy_cb"), 256, False, cvl[2], cvr), (zl[4], zr, W("ml_cw"), W("ml_cb"), 0, True, cvl[3], cvr),
              (zl[5], zr, W("ml_cw"), W("ml_cb"), 128, True, cvl[4], cvr)]
        p.label = 'conv_stage'; conv_stage(p, jl, 128, 64)
        jc = [(zc[4], zr, W("ml_cw"), W("ml_cb"), 0, True, cvc[3], cvr), (zc[5], zr, W("ml_cw"), W("ml_cb"), 128, True, cvc[4], cvr)]
        if not last:
            jc += [(zc[0], zr, W("hy_cw"), W("hy_cb"), 0, False, cvc[0], cvr), (zc[1], zr, W("hy_cw"), W("hy_cb"), 128, False, cvc[1], cvr),
                   (zc[2], zr, W("hy_cw"), W("hy_cb"), 256, False, cvc[2], cvr)]
        p.label = 'conv_stage'; conv_stage(p, jc, 1, 256)
        variants = [("l", LLAT, cvl, "featsT_l", "trow_l")] + ([] if last else [("c", LCTX, cvc, "featsT_c", "trow_c")])
        for (sfx, Lx, cvx, fnm, tnm) in variants:
            filt, fr = S["f" + sfx]
            src1 = {"v": (cvx[0], cvr), "kf": (filt[0], fr), "kg": (filt[1], fr)}
            p.label = 'hconv_stage'; hconv_stage(p, tabs, src1, S["c1" + sfx][0], S["c1" + sfx][1], 128, Lx, True)
            src2 = {"v": (cvx[0], cvr), "kf": (filt[2], fr), "kg": (filt[3], fr), "x1": (cvx[1], cvr),
                    "c1": S["c1" + sfx], "skip": (W("skip")[0], p.res())}
            p.label = 'hconv_stage'; hconv_stage(p, tabs, src2, S["c2" + sfx][0], S["c2" + sfx][1], 128, Lx, False, u_out=S["zz" + sfx][0], u_res=S["zz" + sfx][1])
            zsrc = zl if sfx == "l" else zc
            p.label = 'fnet_stage'
            if sfx == "l":
                fnet_stage(p, tabs, zsrc[3], zr, S["fn" + sfx][0], S["fn" + sfx][1], Lx)
            else:
                fnet_ctx_stage(p, tabs, zsrc[3], zr, S["fn" + sfx][0], S["fn" + sfx][1])
        msrc = {"qT": (cvl[3], cvc[3], cvr), "kT": (cvl[4], cvc[4], cvr), "vT": (zl[6], zc[6], zr), "oT": (zl[7], zc[7], zr),
                "g": (zl[8][0:4], zc[8][0:4], zr)}
        p.label = 'mlstm_stage'; mlstm_stage(p, msrc, S["mll"][0], S["mlc"][0], S["mll"][1], W("mlnw"), 0)
        S["mlc"] = (S["mlc"][0], S["mll"][1])
        ys = {"x2": (cvl[2], cvc[2], cvr)}
        for nm, key in (("c2", "c2"), ("z", "zz"), ("fn", "fn"), ("ml", "ml")):
            ys[nm] = (S[key + "l"][0], S[key + "c"][0], S[key + "l"][1])
        p.label = 'yasm_stage'; yasm_stage(p, ys, W("skip")[1], y_own_fn, None, LLAT, LCTX if not last else 0)
        for br in range(3):
            for ck in range(len(YCH)):
                p.coll("AllGather", yall[br][ck][0], yown[br][ck][0], GROUPS, r=[yown[br][ck][1]], w=[yall[br][ck][1]])
        p.label = 'merge_stage'; merge_stage(p, y_all_fn, None, [h_[0] for h_ in hown], [h_[1] for h_ in hown], W("w_gate_in"), W("b_gate_in"),
                                             I["oh"], xT, xres, NLAT, NCTX, modv, modr, W("wbr"), W("wout"), not last)
        if last:
            p.label = 'moe_stage'; moe_stage(p, xT, xres, NLAT, NCTX, modv, modr, W("n2w"), W("wr"), W("br"), W("wg"), W("wu"), W("wd"), False,
                      I["norm_f"], outT, out_res)
        else:
            p.label = 'moe_stage'; moe_stage(p, xT, xres, NLAT, NCTX, modv, modr, W("n2w"), W("wr"), W("br"), W("wg"), W("wu"), W("wd"), True)
    return p.finish(), p


_CACHE = {}


def _consts():
    c = {}
    c.update(hconv_tables()); c.update(fnet_tables())
    fl, tl = hfilt_consts(LLAT); fc, tc = hfilt_consts(LCTX)
    c["featsT_l"] = fl; c["trow_l"] = tl; c["featsT_c"] = fc; c["trow_c"] = tc
    return c


def kernel(x, c, ctx, c_ctx, ada_w, ada_b, norm1_w, norm2_w, w_in, b_in,
           hy_conv_w, hy_conv_b, hy_f_w1, hy_f_b1, hy_f_w2, hy_f_b2, hy_f_w3, hy_f_freq,
           hy_decay, hy_skip, ml_conv_w, ml_conv_b, ml_norm_w, w_branch, w_out,
           moe_rg_w, moe_rg_b, moe_re_w, moe_re_b, moe_w_gate, moe_w_up, moe_w_down, norm_f_w):
    f32 = lambda a: np.ascontiguousarray(np.asarray(a, dtype=np.float32))
    x, c, ctx, c_ctx = f32(x), f32(c), f32(ctx), f32(c_ctx)
    if "nc" not in _CACHE:
        _CACHE["const"] = _consts()
        _CACHE["nc"] = build_program(_CACHE["const"])[0]
    const = _CACHE["const"]
    nc = _CACHE["nc"]
    in_maps = []
    for core in range(8):
        b, j = core // 4, core % 4
        m = dict(const)
        m["xT0"] = f32(x[b, NLAT * j:NLAT * (j + 1), :].T)
        m["cT0"] = f32(ctx[b, NCTX * j:NCTX * (j + 1), :].T)
        m["cv"] = f32(np.stack([c[b], c_ctx], axis=1))
        oh = np.zeros((128, 4), np.float32); oh[:, j] = 1.0
        m["oh"] = oh
        m["norm_f"] = f32(norm_f_w)
        sl = slice(128 * j, 128 * j + 128)
        for l in range(DEPTH):
            L = "_%d" % l
            m["ada_w" + L] = f32(ada_w[l][:, 1536 * j:1536 * (j + 1)]); m["ada_b" + L] = f32(ada_b[l][1536 * j:1536 * (j + 1)])
            m["n1w" + L] = f32(norm1_w[l]); m["n2w" + L] = f32(norm2_w[l])
            m["w_gate_in" + L] = f32(w_in[l][:, OFF_GATE:]); m["b_gate_in" + L] = f32(b_in[l][OFF_GATE:])
            mixcols = np.concatenate([np.arange(128 * j, 128 * j + 128) + o for o in
                                      (0, 512, 1024, OFF_FN, OFF_ML, OFF_ML + 512, OFF_ML + 1024, OFF_ML + 1536)]
                                     + [np.array([OFF_MLG + j, OFF_MLG + 4 + j, OFF_MLG + 8 + j, OFF_MLG + 12 + j])])
            m["w_mix" + L] = f32(w_in[l][:, mixcols]); m["b_mix" + L] = f32(b_in[l][mixcols])
            hyc = np.concatenate([np.arange(128 * j, 128 * j + 128) + o for o in (0, 512, 1024)])
            m["hy_cw" + L] = f32(hy_conv_w[l][:, :, hyc]); m["hy_cb" + L] = f32(hy_conv_b[l][hyc])
            mlc = np.concatenate([np.arange(128 * j, 128 * j + 128) + o for o in (0, 512)])
            m["ml_cw" + L] = f32(ml_conv_w[l][:, :, mlc]); m["ml_cb" + L] = f32(ml_conv_b[l][mlc])
            m["f_w1" + L] = f32(hy_f_w1[l]); m["f_b1" + L] = f32(hy_f_b1[l]); m["f_freq" + L] = f32(hy_f_freq[l])
            m["f_w2" + L] = f32(hy_f_w2[l]); m["f_b2" + L] = f32(hy_f_b2[l])
            m["f_w3" + L] = f32(np.asarray(hy_f_w3[l]).reshape(64, 4, 512)[:, :, sl])
            m["decay" + L] = f32(np.asarray(hy_decay[l]).reshape(4, 512)[:, sl].T)
            m["skip" + L] = f32(hy_skip[l][:, sl]); m["mlnw" + L] = f32(ml_norm_w[l][sl])
            m["wbr" + L] = f32(w_branch[l]); m["wout" + L] = f32(w_out[l])
            m["wr" + L] = f32(np.concatenate([moe_rg_w[l], moe_re_w[l]], axis=1)); m["br" + L] = f32(np.concatenate([moe_rg_b[l], moe_re_b[l]]))
            m["wg" + L] = f32(moe_w_gate[l]); m["wu" + L] = f32(moe_w_up[l]); m["wd" + L] = f32(moe_w_down[l])
        in_maps.append(m)
    res = run_bass_kernel_spmd(nc, in_maps, core_ids=list(range(8)))
    out = np.empty((2, LLAT, 1024), np.float32)
    for core in range(8):
        b, j = core // 4, core % 4
        out[b, NLAT * j:NLAT * (j + 1), :] = np.asarray(res.results[core]["outT"], dtype=np.float32).T
    return out
```

```python
import os
import numpy as np
import ml_dtypes

from contextlib import ExitStack, contextmanager
import concourse.bass as bass
import concourse.mybir as mybir
from concourse.bass_utils import run_bass_kernel_spmd

F32 = mybir.dt.float32
BF16 = mybir.dt.bfloat16
ALU = mybir.AluOpType
AF = mybir.ActivationFunctionType
AX = mybir.AxisListType


class SemCtr:
    __slots__ = ("sem", "count")

    def __init__(self, sem):
        self.sem = sem
        self.count = 0


class Res:
    __slots__ = ("name", "last_w", "readers", "dsem")

    def __init__(self, name):
        self.name = name
        self.last_w = None
        self.readers = {}
        self.dsem = None


class Prog:
    ENG = ("pe", "dve", "act", "pool", "sp")

    def __init__(self, same_engine_sync=True):
        self.nc = bass.Bass("TRN2", target_bir_lowering=False)
        self.st = ExitStack()
        self.stk = [self.st]
        self.q = {e: [] for e in self.ENG}
        self.cnt = {e: 0 for e in self.ENG}
        self.seen = {e: {} for e in self.ENG}
        self.esem = {e: self.st.enter_context(self.nc.semaphore("es_" + e)) for e in self.ENG}
        self.esem_ids = {id(s) for s in self.esem.values()}
        self.own_ids = {e: {id(self.esem[e])} for e in self.ENG}
        self.same = same_engine_sync
        self.dma_events = {}
        self.sem_pool = []
        self.block_log = []
        self.label = ''
        self.scope_res = [[]]
        self.uid = 0
        self.ninst = 0
        self.flush_every = int(os.environ.get("FLUSH_EVERY", "1200"))

    def dram(self, name, shape, dt, kind=None):
        if kind is None:
            t = self.nc.dram_tensor(name, list(shape), dt)
        else:
            t = self.nc.dram_tensor(name, list(shape), dt, kind=kind)
        return t.ap()

    def sbuf(self, name, shape, dt):
        self.uid += 1
        return self.stk[-1].enter_context(self.nc.sbuf_tensor("%s_%d" % (name, self.uid), list(shape), dt))

    def psum(self, name, shape, dt=F32):
        self.uid += 1
        return self.stk[-1].enter_context(self.nc.psum_tensor("%s_%d" % (name, self.uid), list(shape), dt))

    def res(self, name=None):
        self.uid += 1
        r = Res("%s_%d" % (name or "r", self.uid))
        self.scope_res[-1].append(r)
        return r

    def semctr(self):
        while self.sem_pool:
            sc = self.sem_pool.pop()
            if sc.count < 20000:
                return sc
        return SemCtr(self.newsem("ds"))

    def newsem(self, name):
        self.uid += 1
        return self.st.enter_context(self.nc.semaphore("%s_%d" % (name, self.uid)))

    def _waits(self, eng, r, w, dma=False):
        waits = {}

        def need(ev):
            if ev is None:
                return
            s, v = ev
            if (not self.same or eng == "pe") and id(s) in self.own_ids[eng]:
                return
            if self.seen[eng].get(id(s), (None, 0))[1] < v:
                if waits.get(id(s), (None, 0))[1] < v:
                    waits[id(s)] = (s, v)

        for x in r:
            need(x.last_w)
            if dma:
                for ev in x.readers.values():
                    if id(ev[0]) not in self.esem_ids:
                        need(ev)
        for x in w:
            need(x.last_w)
            for ev in x.readers.values():
                need(ev)
        for k, sv in waits.items():
            self.seen[eng][k] = sv
        return list(waits.values())

    def op(self, eng, fn, r=(), w=()):
        waits = self._waits(eng, r, w)
        if self.cnt[eng] >= 20000:
            self.esem[eng] = self.newsem("es_" + eng)
            self.esem_ids.add(id(self.esem[eng]))
            self.own_ids[eng].add(id(self.esem[eng]))
            self.cnt[eng] = 0
        self.cnt[eng] += 1
        ev = (self.esem[eng], self.cnt[eng])
        self.q[eng].append((waits, fn, ev, 1))
        for x in r:
            x.readers[id(ev[0])] = ev
        for x in w:
            x.last_w = ev
            x.readers = {}
        self.ninst += 1
        self._autoflush()
        return ev

    def _autoflush(self):
        if sum(len(v) for v in self.q.values()) >= self.flush_every:
            self.flush()

    def _async(self, eng, fn, inc, r, w):
        dst = w[0]
        waits = self._waits(eng, r, w, dma=True)
        if dst.dsem is None:
            dst.dsem = self.semctr()
        dst.dsem.count += inc
        ev = (dst.dsem.sem, dst.dsem.count)
        self.q[eng].append((waits, fn, ev, inc))
        for x in r:
            x.readers[id(ev[0])] = ev
        dst.last_w = ev
        dst.readers = {}
        self.dma_events[id(ev[0])] = ev
        self.ninst += 1
        self._autoflush()
        return ev

    def dma(self, eng, out, in_, r=(), w=(), slow=False):
        if slow:
            return self._async(eng, lambda e: e.dma_start(out=out, in_=in_, allow_slow_non_contiguous=True), 16, r, w)
        return self._async(eng, lambda e: e.dma_start(out=out, in_=in_), 16, r, w)

    def coll(self, kind, out, in_, groups, r=(), w=()):
        return self._async("pool", lambda e: e.collective_compute(
            kind, ALU.bypass, replica_groups=groups, ins=[in_.opt()], outs=[out.opt()]), 1, r, w)

    def barrier(self):
        for eng in self.ENG:
            waits = []
            evs = [(self.esem[x], self.cnt[x]) for x in self.ENG if x != eng and self.cnt[x] > 0]
            evs += list(self.dma_events.values())
            for s, v in evs:
                if self.seen[eng].get(id(s), (None, 0))[1] < v:
                    waits.append((s, v))
                    self.seen[eng][id(s)] = (s, v)
            if waits:
                self.q[eng].append((waits, None, None, 0))
        self.dma_events = {}

    def flush(self):
        nc = self.nc
        with nc.Block() as block:
            def mk(name):
                def run(e):
                    for waits, fn, ev, inc in self.q[name]:
                        for s, v in waits:
                            e.wait_ge(s, v)
                        if fn is not None:
                            fn(e).then_inc(ev[0], inc)
                return run
            block.sync(mk("sp"))
            block.tensor(mk("pe"))
            block.vector(mk("dve"))
            block.scalar(mk("act"))
            block.gpsimd(mk("pool"))
        self.block_log.append((self.label, sum(len(v) for v in self.q.values())))
        self.q = {e: [] for e in self.ENG}

    @contextmanager
    def scope(self):
        st = ExitStack()
        self.stk.append(st)
        self.scope_res.append([])
        try:
            yield
            self.barrier()
            self.flush()
        finally:
            self.stk.pop()
            st.close()
            for r in self.scope_res.pop():
                if r.dsem is not None:
                    self.sem_pool.append(r.dsem)
                    r.dsem = None

    def finish(self):
        self.barrier()
        self.flush()
        self.st.close()
        return self.nc

    def mm(self, out, lhsT, rhs, start, stop, r, w):
        return self.op("pe", lambda e: e.matmul(out, lhsT=lhsT, rhs=rhs, start=start, stop=stop), r, w)

    def tr(self, out, in_, ident, r, w):
        return self.op("pe", lambda e: e.transpose(out, in_, ident), r, w)

    def act(self, out, in_, func, r, w, bias=0.0, scale=1.0, accum_out=None):
        if accum_out is None:
            return self.op("act", lambda e: e.activation(out=out, in_=in_, func=func, bias=bias, scale=scale), r, w)
        return self.op("act", lambda e: e.activation(out=out, in_=in_, func=func, bias=bias, scale=scale, accum_out=accum_out), r, w)

    def tt(self, eng, out, a, b, op, r, w):
        return self.op(eng, lambda e: e.tensor_tensor(out=out, in0=a, in1=b, op=op), r, w)

    def ts(self, eng, out, a, s1, op0, r, w, s2=None, op1=None):
        if op1 is None:
            return self.op(eng, lambda e: e.tensor_scalar(out=out, in0=a, scalar1=s1, scalar2=None, op0=op0), r, w)
        return self.op(eng, lambda e: e.tensor_scalar(out=out, in0=a, scalar1=s1, scalar2=s2, op0=op0, op1=op1), r, w)

    def stt(self, out, in0, scalar, in1, op0, op1, r, w):
        return self.op("dve", lambda e: e.scalar_tensor_tensor(out=out, in0=in0, scalar=scalar, in1=in1, op0=op0, op1=op1), r, w)

    def cp(self, eng, out, in_, r, w):
        if eng == "act":
            return self.op("act", lambda e: e.copy(out=out, in_=in_), r, w)
        return self.op(eng, lambda e: e.tensor_copy(out=out, in_=in_), r, w)

    def memset(self, eng, out, val, w):
        return self.op(eng, lambda e: e.memset(out, val), (), w)


def make_ident(p, dt=BF16, name="ident"):
    idf = p.sbuf(name + "f", [128, 128], F32); r = p.res(name)
    p.memset("dve", idf[:], 0.0, w=[r])
    p.op("pool", lambda e: e.affine_select(out=idf[:], in_=idf[:], pattern=[[-1, 128]], compare_op=ALU.not_equal,
                                           fill=1.0, base=0, channel_multiplier=1), r=[r], w=[r])
    if dt == F32:
        return idf, r
    idb = p.sbuf(name + "b", [128, 128], dt)
    p.cp("dve", idb[:], idf[:], r=[r], w=[r])
    return idb, r


def conv_stage(p, jobs, R, W):
    L = R * W
    PAD = W + 1
    CW = min(512, L)
    with p.scope():
        ident, idr = make_ident(p)
        stgs = [p.sbuf("stg%d" % i, [128, L], F32) for i in range(2)]; stgrs = [p.res("stg%d" % i) for i in range(2)]
        outts = [p.sbuf("outt%d" % i, [128, L], F32) for i in range(2)]; outrs = [p.res("outt%d" % i) for i in range(2)]
        nver = 3 if R > 1 else 1
        P = [p.sbuf("P%d" % i, [128, L + 2 * PAD], BF16) for i in range(nver)]
        Pr = [p.res("P%d" % i) for i in range(nver)]
        for i in range(nver):
            p.memset("pool", P[i][:, 0:PAD], 0.0, w=[Pr[i]])
            p.memset("pool", P[i][:, PAD + L:], 0.0, w=[Pr[i]])
        wsb = p.sbuf("wsb", [128, 9], F32); bsb = p.sbuf("bsb", [128, 1], F32); wr = p.res("wsb")
        D = p.sbuf("D", [128, 9, 128], BF16); Dr = p.res("D")
        ps = [p.psum("ps%d" % i, [128, 512]) for i in range(4)]; psr = [p.res("ps%d" % i) for i in range(4)]
        k = 0
        for ji, (src, srcr, w_ap, b_ap, c0, silu, dst, dstr) in enumerate(jobs):
            stg = stgs[ji % 2]; stgr = stgrs[ji % 2]; outt = outts[ji % 2]; outr = outrs[ji % 2]
            p.dma("sp" if ji % 2 else "pool", stg[:], src, r=[srcr], w=[stgr])
            p.dma("sp", wsb[:], w_ap.rearrange("a b c -> c (a b)")[c0:c0 + 128, :], w=[wr], slow=True)
            p.dma("sp", bsb[:], b_ap.rearrange("(a b) -> a b", b=1)[c0:c0 + 128, :], w=[wr])
            p.cp("act", P[0][:, PAD:PAD + L], stg[:], r=[stgr], w=[Pr[0]])
            if nver == 3:
                p.cp("act", P[1][:, PAD:PAD + L], stg[:], r=[stgr], w=[Pr[1]])
                p.cp("dve", P[2][:, PAD:PAD + L], stg[:], r=[stgr], w=[Pr[2]])
                v1 = P[1][:, PAD:PAD + L].rearrange("p (r w) -> p r w", w=W)
                v2 = P[2][:, PAD:PAD + L].rearrange("p (r w) -> p r w", w=W)
                p.memset("dve", v1[:, :, W - 1:W], 0.0, w=[Pr[1]])
                p.memset("dve", v2[:, :, 0:1], 0.0, w=[Pr[2]])
            for tap in range(9):
                p.ts("dve", D[:, tap, :], ident[:], wsb[:, tap:tap + 1], ALU.mult, r=[idr, wr], w=[Dr])
            taps = [(dy, dx) for dy in (-1, 0, 1) for dx in (-1, 0, 1) if (R > 1 or dy == 0)]
            for c in range(L // CW):
                h = k % 4; k += 1
                for ti, (dy, dx) in enumerate(taps):
                    ver = 0 if nver == 1 else (1 if dx == -1 else (2 if dx == 1 else 0))
                    o = PAD + c * CW + W * dy + dx
                    p.mm(ps[h][:, 0:CW], D[:, (dy + 1) * 3 + dx + 1, :], P[ver][:, o:o + CW], ti == 0, ti == len(taps) - 1,
                         r=[Dr, Pr[ver]], w=[psr[h]])
                p.act(outt[:, c * CW:(c + 1) * CW], ps[h][:, 0:CW], AF.Silu if silu else AF.Identity,
                      r=[psr[h], wr], w=[outr], bias=bsb[:, 0:1])
            p.dma("pool" if ji % 2 else "sp", dst, outt[:], r=[outr], w=[dstr])


HY_BANDS = 16


def hfilt_consts(L):
    t = np.arange(L, dtype=np.float64) / L
    bands = np.linspace(1e-4, HY_BANDS - 1, HY_BANDS)
    ang = 2 * np.pi * t[:, None] * bands[None, :]
    feats = np.concatenate([t[:, None], np.cos(ang), np.sin(ang)], axis=-1)
    return np.ascontiguousarray(feats.T).astype(np.float32), t.astype(np.float32)


def hfilt_stage(p, featsT, trow, w, j, L, dst, dst_res):
    CW = min(512, L)
    NCH = L // CW
    TWO_PI = 2 * np.pi
    with p.scope():
        wr = p.res("w")
        ft = p.sbuf("ft", [33, L], F32); tb = p.sbuf("tb", [128, L], F32)
        w1 = p.sbuf("w1", [33, 64], F32); w2 = p.sbuf("w2", [64, 64], F32); w3 = p.sbuf("w3", [64, 4, 128], F32)
        sc = p.sbuf("sc", [64, 4], F32)
        dc = p.sbuf("dc", [128, 4], F32)
        p.dma("sp", ft[:], featsT, w=[wr]); p.dma("sp", tb[:], trow.partition_broadcast(128), w=[wr])
        p.dma("sp", w1[:], w["f_w1"], w=[wr]); p.dma("sp", w2[:], w["f_w2"], w=[wr])
        p.dma("sp", w3[:], w["f_w3"], w=[wr])
        p.dma("sp", sc[:, 0:1], w["f_b1"].rearrange("(a b) -> a b", b=1), w=[wr])
        p.dma("sp", sc[:, 1:2], w["f_freq"].rearrange("(a b) -> a b", b=1), w=[wr])
        p.dma("sp", sc[:, 2:3], w["f_b2"].rearrange("(a b) -> a b", b=1), w=[wr])
        p.dma("sp", dc[:], w["decay"], w=[wr])
        p.act(dc[:], dc[:], AF.Abs, r=[wr], w=[wr])
        p.ts("dve", dc[:], dc[:], -1.0, ALU.mult, r=[wr], w=[wr])
        h1f = p.sbuf("h1", [128, L], F32); h1 = h1f[0:64]; h2 = p.sbuf("h2", [64, L], F32)
        h1r, h2r = p.res("h1"), p.res("h2")
        ps = [p.psum("ps%d" % i, [128, 512]) for i in range(2)]; psr = [p.res("ps%d" % i) for i in range(2)]
        tmp = p.sbuf("tmp", [64, 512], F32); tmpr = p.res("tmp")
        tm1 = p.sbuf("tm1", [64, 512], F32); tm1r = p.res("tm1"); tm2 = p.sbuf("tm2", [64, 512], F32); tm2r = p.res("tm2")
        k = 0
        for (src, srcr, lw, bcol, dsth, dstr) in ((ft[:], wr, w1, 0, h1, h1r), (h1, h1r, w2, 2, h2[:], h2r)):
            for c in range(NCH):
                cs = slice(c * CW, (c + 1) * CW)
                h = k % 2; k += 1
                p.mm(ps[h][0:64, 0:CW], lw[:], src[:, cs], True, True, r=[wr, srcr], w=[psr[h]])
                p.ts("dve", tmp[:, 0:CW], ps[h][0:64, 0:CW], sc[:, bcol:bcol + 1], ALU.add, r=[psr[h], wr], w=[tmpr],
                     s2=sc[:, 1:2], op1=ALU.mult)
                tv = tmp[:, 0:CW]
                p.ts("dve", tm1[:, 0:CW], tv, float(np.pi), ALU.is_gt, r=[tmpr], w=[tm1r], s2=-TWO_PI, op1=ALU.mult)
                p.ts("dve", tm2[:, 0:CW], tv, float(-np.pi), ALU.is_lt, r=[tmpr], w=[tm2r], s2=TWO_PI, op1=ALU.mult)
                p.tt("dve", tv, tv, tm1[:, 0:CW], ALU.add, r=[tmpr, tm1r], w=[tmpr])
                p.tt("dve", tv, tv, tm2[:, 0:CW], ALU.add, r=[tmpr, tm2r], w=[tmpr])
                p.ts("dve", tv, tv, float(np.pi), ALU.min, r=[tmpr], w=[tmpr], s2=float(-np.pi), op1=ALU.max)
                p.act(dsth[:, cs], tmp[:, 0:CW], AF.Sin, r=[tmpr], w=[dstr])
        F = [p.sbuf("F%d" % i, [128, L], F32) for i in range(2)]; Fr = [p.res("F%d" % i) for i in range(2)]
        wnd = p.sbuf("wnd", [128, 512], F32); wndr = p.res("wnd")
        ss = p.sbuf("ss", [128, 4], F32); ssr = p.res("ss")
        junk = h1f; junkr = h1r
        for o in range(2):
            for d in range(2):
                od = o * 2 + d
                for c in range(NCH):
                    cs = slice(c * CW, (c + 1) * CW)
                    h = k % 2; k += 1
                    p.mm(ps[h][:, 0:CW], w3[:, od, :], h2[:, cs], True, True, r=[wr, h2r], w=[psr[h]])
                    p.act(wnd[:, 0:CW], tb[:, cs], AF.Exp, r=[wr], w=[wndr], scale=dc[:, od:od + 1])
                    p.tt("dve", F[d][:, cs], ps[h][:, 0:CW], wnd[:, 0:CW], ALU.mult, r=[psr[h], wndr], w=[Fr[d]])
                if d == 1:
                    p.memset("dve", F[d][:, 0:1], 0.0, w=[Fr[d]])
                p.act(junk[:], F[d][:], AF.Square, r=[Fr[d]], w=[junkr], accum_out=ss[:, od:od + 1])
                ssr.last_w = junkr.last_w
            tot = ss[:, 2 * o:2 * o + 1]
            p.tt("dve", tot, tot, ss[:, 2 * o + 1:2 * o + 2], ALU.add, r=[ssr], w=[ssr])
            p.act(tot, tot, AF.Sqrt, r=[ssr], w=[ssr], bias=1e-6)
            p.op("dve", lambda e, t_=tot: e.reciprocal(out=t_, in_=t_), r=[ssr], w=[ssr])
            for d in range(2):
                p.ts("dve", F[d][:], F[d][:], tot, ALU.mult, r=[Fr[d], ssr], w=[Fr[d]])
                p.dma("sp", dst[o * 2 + d], F[d][:], r=[Fr[d]], w=[dst_res])


def hconv_tables():
    a = np.arange(128, dtype=np.float64)
    th128 = 2 * np.pi * np.outer(a, a) / 128.0
    C = np.cos(th128); S = np.sin(th128)
    bf = lambda x: np.ascontiguousarray(x.astype(np.float32)).astype(ml_dtypes.bfloat16)
    t = {}
    t["hc_sq"] = bf(np.stack([C, -C, S, -S], axis=1))
    t["hc_wab"] = bf(np.stack([np.concatenate([C, S], 1), np.concatenate([-C, -S], 1),
                                np.concatenate([-S, C], 1)], axis=1))
    for pre, N1 in (("hc", 128), ("hcc", 4)):
        NF = 128 * N1
        f1 = np.arange(N1, dtype=np.float64)
        n1 = np.arange(64, dtype=np.float64)
        th1 = 2 * np.pi * np.outer(n1, f1) / N1
        t[pre + "_w1"] = bf(np.concatenate([np.cos(th1), -np.sin(th1)], axis=1))
        thN = 2 * np.pi * np.outer(a, f1) / NF
        t[pre + "_tw"] = np.stack([np.cos(thN), -np.sin(thN)], axis=1).astype(np.float32)
        thN2 = 2 * np.pi * np.outer(a, a) / NF
        t[pre + "_tw2"] = np.stack([np.cos(thN2), -np.sin(thN2)], axis=1).astype(np.float32)
        m1 = np.arange(64, dtype=np.float64)
        thI = 2 * np.pi * np.outer(a, m1) / N1
        t[pre + "_i2"] = bf(np.stack([np.cos(thI), np.sin(thI), -np.sin(thI)], axis=1))
    return t


def hconv_stage(p, tabs, srcs, dst, dst_res, CH, L_in, first, G=32, u_out=None, u_res=None):
    NP = L_in // 128
    N1 = 128 if L_in == 8192 else 4
    NFFT = 128 * N1
    pre = "hc" if N1 == 128 else "hcc"
    with p.scope():
        w1 = p.sbuf("w1", [64, 2 * N1], BF16); sq = p.sbuf("sq", [128, 4, 128], BF16)
        wab = p.sbuf("wab", [128, 3, 256], BF16); i2 = p.sbuf("i2", [128, 3, 64], BF16)
        tw = p.sbuf("tw", [128, 2, N1], F32); tw2 = p.sbuf("tw2", [128, 2, 128], F32)
        tr = p.res("tabs")
        for sb_, nm in ((w1, pre + "_w1"), (sq, "hc_sq"), (wab, "hc_wab"), (i2, pre + "_i2"), (tw, pre + "_tw"), (tw2, pre + "_tw2")):
            p.dma("sp", sb_[:], tabs[nm], w=[tr])
        PS = p.psum("ps", [128, 4096])
        PH = [PS[:, 0:2048], PS[:, 2048:4096]]
        phr = [p.res("psA"), p.res("psB")]
        stg = [p.sbuf("stg%d" % i, [64, G, 128], F32) for i in range(3)]
        stgr = [p.res("stg%d" % i) for i in range(3)]
        skb = p.sbuf("skb", [64, G], F32); skr = p.res("skb")
        XB = [p.sbuf("X%d" % i, [64, G, 128], BF16) for i in range(2)]; xrB = [p.res("X%d" % i) for i in range(2)]
        NS4 = G // 4
        T = [p.sbuf("T%d" % i, [128, G, N1], BF16) for i in range(4)]
        TI = T if N1 == 128 else [p.sbuf("TI%d" % i, [N1, G, 128], BF16) for i in range(4)]
        TrS = [[p.res("T%d_%d" % (i, s)) for s in range(NS4)] for i in range(4)]
        Kr = p.sbuf("Kr", [128, G, N1], F32); Ki = p.sbuf("Ki", [128, G, N1], F32)
        kresS = [p.res("K%d" % s) for s in range(NS4)]
        Q = [p.sbuf("Q%d" % i, [128, G, N1], BF16) for i in range(4)]
        QrS = [[p.res("Q%d_%d" % (i, s)) for s in range(NS4)] for i in range(4)]
        OUT2 = p.sbuf("OUT2", [64, G, 128], F32); outr = p.res("OUT2")
        Trb = tw[:, 0:1, :].to_broadcast([128, 4, N1])
        Tib = tw[:, 1:2, :].to_broadcast([128, 4, N1])
        Trb2 = tw2[0:N1, 0:1, :].to_broadcast([N1, 4, 128])
        Tib2 = tw2[0:N1, 1:2, :].to_broadcast([N1, 4, 128])
        PF = [PS[:, 0:1024], PS[:, 1024:2048]]; pfr = [p.res("pfA"), p.res("pfB")]
        PX = [PS[:, 2048:3072], PS[:, 3072:4096]]; pxr = [p.res("pxA"), p.res("pxB")]
        cnt = {"f": 0, "x": 0, "i": 0}

        def t2d(ap, c0):
            return ap[c0:c0 + G, :].rearrange("g (a b) -> a g b", b=128)

        def load_u(c0, which, xi):
            X = XB[xi]; xr = xrB[xi]
            if NP < 64:
                p.memset("pool", X[:], 0.0, w=[xr])
            if which != "u" or first:
                ap, rs = srcs["v" if which == "u" else which]
                p.dma("sp" if xi else "pool", stg[xi][0:NP], t2d(ap, c0), r=[rs], w=[stgr[xi]])
                p.cp("act", X[0:NP], stg[xi][0:NP], r=[stgr[xi]], w=[xr])
            else:
                for i, nm in enumerate(("v", "c1", "x1")):
                    ap, rs = srcs[nm]
                    p.dma("sp" if i % 2 else "pool", stg[i][0:NP], t2d(ap, c0), r=[rs], w=[stgr[i]])
                ap, rs = srcs["skip"]
                p.dma("sp", skb[0:NP], ap[c0:c0 + G].partition_broadcast(NP), r=[rs], w=[skr])
                p.tt("pool", stg[0][0:NP], stg[0][0:NP], skb[0:NP, :, None].to_broadcast([NP, G, 128]), ALU.mult,
                     r=[stgr[0], skr], w=[stgr[0]])
                p.tt("pool", stg[0][0:NP], stg[0][0:NP], stg[1][0:NP], ALU.add, r=[stgr[0], stgr[1]], w=[stgr[0]])
                p.tt("pool", stg[0][0:NP], stg[0][0:NP], stg[2][0:NP], ALU.mult, r=[stgr[0], stgr[2]], w=[stgr[0]])
                p.cp("act", X[0:NP], stg[0][0:NP], r=[stgr[0]], w=[xr])
                if u_out is not None:
                    p.dma("pool", t2d(u_out, c0), stg[0][0:NP], r=[stgr[0]], w=[u_res])

        Cm, Cn, Sm, Sn = (sq[:, k, :] for k in range(4))

        def f1_tw(s4, xi):
            X = XB[xi]; xr = xrB[xi]
            h = cnt["f"] % 2; cnt["f"] += 1
            pv = PF[h][:, 0:8 * N1].rearrange("p (c k) -> p c k", k=2 * N1)
            for c in range(4):
                p.mm(pv[:, c, :], X[:, s4 * 4 + c, :], w1[:], True, True, r=[xr, tr], w=[pfr[h]])
            Ar = pv[:, :, 0:N1]; Ai = pv[:, :, N1:2 * N1]
            gs = slice(s4 * 4, s4 * 4 + 4)
            p.tt("dve", T[0][:, gs, :], Ar, Trb, ALU.mult, r=[pfr[h], tr], w=[TrS[0][s4]])
            p.tt("dve", T[1][:, gs, :], Ai, Tib, ALU.mult, r=[pfr[h], tr], w=[TrS[1][s4]])
            p.tt("dve", T[2][:, gs, :], Ar, Tib, ALU.mult, r=[pfr[h], tr], w=[TrS[2][s4]])
            p.tt("dve", T[3][:, gs, :], Ai, Trb, ALU.mult, r=[pfr[h], tr], w=[TrS[3][s4]])

        def f2_post(s4, which):
            h = cnt["x"] % 2; cnt["x"] += 1
            gs = slice(s4 * 4, s4 * 4 + 4)
            xr_ps = PX[h][:, 0:4 * N1]; xi_ps = PX[h][:, 512:512 + 4 * N1]
            rr = [tr] + [TrS[k][s4] for k in range(4)]
            p.mm(xr_ps, Cm, T[0][:, gs, :], True, False, r=rr, w=[pxr[h]])
            p.mm(xr_ps, Cn, T[1][:, gs, :], False, False, r=rr, w=[pxr[h]])
            p.mm(xr_ps, Sm, T[2][:, gs, :], False, False, r=rr, w=[pxr[h]])
            p.mm(xr_ps, Sm, T[3][:, gs, :], False, True, r=rr, w=[pxr[h]])
            p.mm(xi_ps, Cm, T[2][:, gs, :], True, False, r=rr, w=[pxr[h]])
            p.mm(xi_ps, Cm, T[3][:, gs, :], False, False, r=rr, w=[pxr[h]])
            p.mm(xi_ps, Sn, T[0][:, gs, :], False, False, r=rr, w=[pxr[h]])
            p.mm(xi_ps, Sm, T[1][:, gs, :], False, True, r=rr, w=[pxr[h]])
            xr3 = xr_ps.rearrange("p (c k) -> p c k", k=N1); xi3 = xi_ps.rearrange("p (c k) -> p c k", k=N1)
            kres = kresS[s4]
            if which == "kf":
                p.cp("act", Kr[:, gs, :], xr3, r=[pxr[h]], w=[kres])
                p.cp("act", Ki[:, gs, :], xi3, r=[pxr[h]], w=[kres])
            elif which == "kg":
                p.tt("dve", Kr[:, gs, :], xr3, Kr[:, gs, :], ALU.add, r=[pxr[h], kres], w=[kres])
                p.stt(Ki[:, gs, :], xi3, -1.0, Ki[:, gs, :], ALU.mult, ALU.add, r=[pxr[h], kres], w=[kres])
            else:
                p.tt("dve", Q[0][:, gs, :], xr3, Kr[:, gs, :], ALU.mult, r=[pxr[h], kres], w=[QrS[0][s4]])
                p.tt("dve", Q[1][:, gs, :], xi3, Ki[:, gs, :], ALU.mult, r=[pxr[h], kres], w=[QrS[1][s4]])
                p.tt("dve", Q[2][:, gs, :], xr3, Ki[:, gs, :], ALU.mult, r=[pxr[h], kres], w=[QrS[2][s4]])
                p.tt("dve", Q[3][:, gs, :], xi3, Kr[:, gs, :], ALU.mult, r=[pxr[h], kres], w=[QrS[3][s4]])

        def fwd(which, xi):
            for s4 in range(NS4 + 1):
                if s4 < NS4:
                    f1_tw(s4, xi)
                if s4 >= 1:
                    f2_post(s4 - 1, which)

        def inv(c0):
            Wa, Wan, Wb = (wab[:, k, :] for k in range(3))
            for s4 in range(NS4 + 1):
                if s4 >= 1:
                    i2_step(s4 - 1, c0)
                if s4 == NS4:
                    break
                h = cnt["f"] % 2; cnt["f"] += 1
                pv = PF[h][0:N1, :].rearrange("p (c k) -> p c k", k=256)
                rr = [tr] + [QrS[k][s4] for k in range(4)]
                for c in range(4):
                    g = s4 * 4 + c
                    p.mm(pv[:, c, :], Q[0][:, g, :], Wa, True, False, r=rr, w=[pfr[h]])
                    p.mm(pv[:, c, :], Q[1][:, g, :], Wan, False, False, r=rr, w=[pfr[h]])
                    p.mm(pv[:, c, :], Q[2][:, g, :], Wb, False, False, r=rr, w=[pfr[h]])
                    p.mm(pv[:, c, :], Q[3][:, g, :], Wb, False, True, r=rr, w=[pfr[h]])
                Br = pv[:, :, 0:128]; Bi = pv[:, :, 128:256]
                gs = slice(s4 * 4, s4 * 4 + 4)
                p.tt("dve", TI[0][0:N1, gs, :], Br, Trb2, ALU.mult, r=[pfr[h], tr], w=[TrS[0][s4]])
                p.tt("dve", TI[1][0:N1, gs, :], Bi, Tib2, ALU.mult, r=[pfr[h], tr], w=[TrS[1][s4]])
                p.tt("dve", TI[2][0:N1, gs, :], Br, Tib2, ALU.mult, r=[pfr[h], tr], w=[TrS[2][s4]])
                p.tt("dve", TI[3][0:N1, gs, :], Bi, Trb2, ALU.mult, r=[pfr[h], tr], w=[TrS[3][s4]])
            p.dma("sp", t2d(dst, c0), OUT2[0:NP], r=[outr], w=[dst_res])

        CI, SI, SIn = (i2[:, k, :] for k in range(3))

        def i2_step(s4, c0):
            h = cnt["x"] % 2; cnt["x"] += 1
            gs = slice(s4 * 4, s4 * 4 + 4)
            o_ps = PX[h][0:NP, 0:512]
            rr = [tr] + [TrS[k][s4] for k in range(4)]
            p.mm(o_ps, CI[0:N1, 0:NP], TI[0][0:N1, gs, :], True, False, r=rr, w=[pxr[h]])
            p.mm(o_ps, CI[0:N1, 0:NP], TI[1][0:N1, gs, :], False, False, r=rr, w=[pxr[h]])
            p.mm(o_ps, SI[0:N1, 0:NP], TI[2][0:N1, gs, :], False, False, r=rr, w=[pxr[h]])
            p.mm(o_ps, SIn[0:N1, 0:NP], TI[3][0:N1, gs, :], False, True, r=rr, w=[pxr[h]])
            p.act(OUT2[0:NP, gs, :], o_ps.rearrange("p (c k) -> p c k", k=128), AF.Copy, r=[pxr[h]], w=[outr], scale=1.0 / NFFT)

        items = [(c0, which) for c0 in range(0, CH, G) for which in ("kf", "kg", "u")]
        load_u(items[0][0], items[0][1], 0)
        for i, (c0, which) in enumerate(items):
            if i + 1 < len(items):
                load_u(items[i + 1][0], items[i + 1][1], (i + 1) % 2)
            fwd(which, i % 2)
            if which == "u":
                inv(c0)


LF = 8192


def fnet_tables():
    a = np.arange(128, dtype=np.float64)
    th = 2 * np.pi * np.outer(a, a) / 128.0
    C = np.cos(th); S = np.sin(th)
    b = np.arange(64, dtype=np.float64)
    th64 = 2 * np.pi * np.outer(b, b) / 64.0
    C64 = np.cos(th64); S64 = np.sin(th64)
    thL = 2 * np.pi * np.outer(b, a) / LF
    bf = lambda x: np.ascontiguousarray(x.astype(np.float32)).astype(ml_dtypes.bfloat16)
    return {"fn_w": bf(np.stack([np.concatenate([C, -S], 1), np.concatenate([S, C], 1)], axis=1)),
            "fn_64": bf(np.stack([C64, -C64, S64], axis=1)),
            "fn_tw": np.stack([np.cos(thL), -np.sin(thL)], axis=1).astype(np.float32),
            "fn_256": bf(np.stack([np.stack([np.cos(2 * np.pi * np.outer(np.arange(128 * b_, 128 * b_ + 128), np.arange(256)) / 256.0),
                                             np.sin(2 * np.pi * np.outer(np.arange(128 * b_, 128 * b_ + 128), np.arange(256)) / 256.0)], axis=1)
                                   for b_ in range(2)], axis=1))}


def fnet_stage(p, tabs, src, src_res, dst, dst_res, L_in):
    scale = 1.0 / np.sqrt(L_in * 128.0)
    f1_list = list(range(128)) if L_in == LF else [0, 32, 64, 96]
    with p.scope():
        wt = p.sbuf("wt", [128, 2, 256], BF16); t64 = p.sbuf("t64", [64, 3, 64], BF16); tw = p.sbuf("tw", [64, 2, 128], F32)
        tr = p.res("tabs")
        p.dma("sp", wt[:], tabs["fn_w"], w=[tr]); p.dma("sp", t64[:], tabs["fn_64"], w=[tr]); p.dma("sp", tw[:], tabs["fn_tw"], w=[tr])
        stg = p.sbuf("stg", [128, LF], F32); stgr = p.res("stg")
        ub = p.sbuf("ub", [128, LF], BF16); ubr = p.res("ub")
        V = p.sbuf("V", [128, 64, 256], BF16); Vr = p.res("V")
        T = [p.sbuf("T%d" % i, [64, 64, 128], BF16) for i in range(4)]; Tr_ = [p.res("T%d" % i) for i in range(4)]
        PS = p.psum("ps", [128, 4096]); PH = [PS[:, 0:2048], PS[:, 2048:4096]]; phr = [p.res("psA"), p.res("psB")]
        if L_in < LF:
            p.memset("pool", stg[:], 0.0, w=[stgr])
        p.dma("sp", stg[:, 0:L_in], src, r=[src_res], w=[stgr])
        p.cp("act", ub[:], stg[:], r=[stgr], w=[ubr])
        tog = 0
        uv = ub[:].rearrange("p (a b) -> p b a", b=64)
        for s8 in range(8):
            h = tog; tog ^= 1
            pv = PH[h].rearrange("p (c k) -> p c k", k=256)
            for c in range(8):
                n2 = s8 * 8 + c
                p.mm(pv[:, c, :], uv[:, n2, :], wt[:, 0, :], True, True, r=[ubr, tr], w=[phr[h]])
            p.cp("act" if s8 % 2 else "dve", V[:, s8 * 8:(s8 + 1) * 8, :], pv, r=[phr[h]], w=[Vr])
        Trb = tw[:, 0:1, :].to_broadcast([64, 8, 128]); Tib = tw[:, 1:2, :].to_broadcast([64, 8, 128])
        OUT2 = stg[0:64, :].rearrange("p (c k) -> p c k", k=128)
        outr = stgr
        for half in range(2):
            for s8 in range(8):
                h = tog; tog ^= 1
                pv = PH[h][0:64, :].rearrange("p (c k) -> p c k", k=256)
                for c in range(8):
                    ch = half * 64 + s8 * 8 + c
                    p.mm(pv[:, c, :], V[:, :, ch], wt[:, 0, :], True, False, r=[Vr, tr], w=[phr[h]])
                    p.mm(pv[:, c, :], V[:, :, 128 + ch], wt[:, 1, :], False, True, r=[Vr, tr], w=[phr[h]])
                Ar = pv[:, :, 0:128]; Ai = pv[:, :, 128:256]
                gs = slice(s8 * 8, s8 * 8 + 8)
                p.tt("dve", T[0][:, gs, :], Ar, Trb, ALU.mult, r=[phr[h], tr], w=[Tr_[0]])
                p.tt("dve", T[1][:, gs, :], Ai, Tib, ALU.mult, r=[phr[h], tr], w=[Tr_[1]])
                p.tt("dve", T[2][:, gs, :], Ar, Tib, ALU.mult, r=[phr[h], tr], w=[Tr_[2]])
                p.tt("dve", T[3][:, gs, :], Ai, Trb, ALU.mult, r=[phr[h], tr], w=[Tr_[3]])
            C64, C64n, S64 = (t64[:, k, :] for k in range(3))
            rr = [tr] + Tr_
            for s4 in range(16):
                h = tog; tog ^= 1
                gs = slice(s4 * 4, s4 * 4 + 4)
                o_ps = PH[h][0:64, 0:512]
                p.mm(o_ps, C64, T[0][:, gs, :], True, False, r=rr, w=[phr[h]])
                p.mm(o_ps, C64n, T[1][:, gs, :], False, False, r=rr, w=[phr[h]])
                p.mm(o_ps, S64, T[2][:, gs, :], False, False, r=rr, w=[phr[h]])
                p.mm(o_ps, S64, T[3][:, gs, :], False, True, r=rr, w=[phr[h]])
                p.act(OUT2[:, s4 * 4:s4 * 4 + 4, :], o_ps.rearrange("p (c k) -> p c k", k=128), AF.Copy,
                      r=[phr[h]], w=[outr], scale=float(scale))
            p.dma("sp" if half else "pool", dst[half * 64:half * 64 + 64, :].rearrange("c (a b) -> a c b", b=128), OUT2, r=[outr], w=[dst_res])


def fnet_ctx_stage(p, tabs, src, src_res, dst, dst_res):
    scale = 1.0 / np.sqrt(256 * 128.0)
    with p.scope():
        wt = p.sbuf("wt", [128, 2, 256], BF16); t256 = p.sbuf("t256", [128, 2, 2, 256], BF16); tr = p.res("tabs")
        p.dma("sp", wt[:], tabs["fn_w"], w=[tr]); p.dma("sp", t256[:], tabs["fn_256"], w=[tr])
        stg = p.sbuf("stg", [128, 256], F32); stgr = p.res("stg")
        ub = p.sbuf("ub", [128, 256], BF16); ubr = p.res("ub")
        V = p.sbuf("V", [128, 2, 256], BF16); Vr = p.res("V")
        ps0 = p.psum("ps0", [128, 2, 256]); ps0r = p.res("ps0")
        ps1 = p.psum("ps1", [128, 256]); ps1r = p.res("ps1")
        p.dma("sp", stg[:], src, r=[src_res], w=[stgr])
        p.cp("act", ub[:], stg[:], r=[stgr], w=[ubr])
        for blk in range(2):
            p.mm(ps0[:, blk, :], ub[:, blk * 128:(blk + 1) * 128], wt[:, 0, :], True, True, r=[ubr, tr], w=[ps0r])
        p.cp("dve", V[:], ps0[:], r=[ps0r], w=[Vr])
        k = 0
        for blk in range(2):
            for ri in range(2):
                p.mm(ps1[:], V[:, blk, ri * 128:(ri + 1) * 128], t256[:, blk, ri, :], k == 0, k == 3, r=[Vr, tr], w=[ps1r])
                k += 1
        p.act(stg[:], ps1[:], AF.Copy, r=[ps1r], w=[stgr], scale=float(scale))
        p.dma("sp", dst, stg[:], r=[stgr], w=[dst_res])


STOP = 99


NCK = 66
NEG = -30000.0


def mlstm_stage(p, src, dst_lat, dst_ctx, dst_res, normw_ap, j):
    TOK = NCK * 128
    with p.scope():
        ident, idr = make_ident(p, F32, "idf")
        identb = p.sbuf("identb", [128, 128], BF16)
        p.cp("dve", identb[:], ident[:], r=[idr], w=[idr])
        cr = p.res("consts")
        ones = p.sbuf("ones", [128, 128], F32); onesb = p.sbuf("onesb", [128, 128], BF16)
        p.memset("dve", ones[:], 1.0, w=[cr]); p.memset("dve", onesb[:], 1.0, w=[cr])
        tri = p.sbuf("tri", [128, 2, 128], F32); mneg = p.sbuf("mneg", [128, 2, 128], F32)
        p.memset("dve", tri[:], 1.0, w=[cr]); p.memset("dve", mneg[:], 0.0, w=[cr])
        p.op("pool", lambda e: e.affine_select(out=tri[:, 0, :], in_=tri[:, 0, :], pattern=[[1, 128]], compare_op=ALU.is_ge, fill=0.0, base=0, channel_multiplier=-1), r=[cr], w=[cr])
        p.op("pool", lambda e: e.affine_select(out=tri[:, 1, :], in_=tri[:, 1, :], pattern=[[-1, 128]], compare_op=ALU.is_ge, fill=0.0, base=0, channel_multiplier=1), r=[cr], w=[cr])
        p.op("pool", lambda e: e.affine_select(out=mneg[:, 0, :], in_=mneg[:, 0, :], pattern=[[1, 128]], compare_op=ALU.is_ge, fill=NEG, base=0, channel_multiplier=-1), r=[cr], w=[cr])
        p.op("pool", lambda e: e.affine_select(out=mneg[:, 1, :], in_=mneg[:, 1, :], pattern=[[-1, 128]], compare_op=ALU.is_ge, fill=NEG, base=0, channel_multiplier=1), r=[cr], w=[cr])

        stg = p.sbuf("stg", [128, TOK], F32); stgr = p.res("stg")
        qT = p.sbuf("qT", [128, TOK], BF16); kT = p.sbuf("kT", [128, TOK], BF16); vT = p.sbuf("vT", [128, TOK], BF16)
        qr, kr, vr = p.res("qT"), p.res("kT"), p.res("vT")
        for nm, t_, r_, sc in (("qT", qT, qr, 128.0 ** -0.5), ("kT", kT, kr, 1.0), ("vT", vT, vr, 1.0)):
            lat, ctx, rs = src[nm]
            p.dma("sp", stg[:, 0:256], ctx, r=[rs], w=[stgr]); p.dma("sp", stg[:, 256:], lat, r=[rs], w=[stgr])
            p.act(t_[:], stg[:], AF.Copy, r=[stgr], w=[r_], scale=sc)
        ktok = p.sbuf("ktok", [128, NCK, 128], BF16); vtok = p.sbuf("vtok", [128, NCK, 128], BF16)
        ktr, vtr = p.res("ktok"), p.res("vtok")
        ptb = [p.psum("ptb%d" % i, [128, 8, 128], BF16) for i in range(2)]; ptbr = [p.res("ptb%d" % i) for i in range(2)]
        kk = 0
        for srcT, sr, dstt, dr in ((kT, kr, ktok, ktr), (vT, vr, vtok, vtr)):
            for c8 in range(0, NCK, 8):
                n = min(8, NCK - c8)
                h = kk % 2; kk += 1
                for c in range(n):
                    p.tr(ptb[h][:, c, :], srcT[:, (c8 + c) * 128:(c8 + c + 1) * 128], identb[:], r=[sr, idr], w=[ptbr[h]])
                p.cp("act" if h else "dve", dstt[:, c8:c8 + n, :], ptb[h][:, 0:n, :], r=[ptbr[h]], w=[dr])
        if STOP == 1:
            p.dma('sp', dst_ctx, stg[:, 0:256], r=[stgr], w=[dst_res]); return
        gst = p.sbuf("gst", [NCK, 4, 128], F32); gr = p.res("gst")
        lat, ctx, rs = src["g"]
        p.dma("sp", gst[0:2], ctx.rearrange("g (c s) -> c g s", s=128), r=[rs], w=[gr])
        p.dma("sp", gst[2:NCK], lat.rearrange("g (c s) -> c g s", s=128), r=[rs], w=[gr])
        G = p.sbuf("G", [128, 4, NCK], F32); Gr = p.res("G")
        pg = p.psum("pg", [128, 4, 128]); pgr = p.res("pg")
        for g in range(4):
            p.tr(pg[:, g, 0:NCK], gst[:, g, :], ident[0:NCK, 0:NCK], r=[gr, idr], w=[pgr])
        p.cp("dve", G[:], pg[:, :, 0:NCK], r=[pgr], w=[Gr])
        LFt = p.sbuf("LF", [128, 2, NCK], F32); lfr = p.res("LF")
        for d in range(2):
            p.act(LFt[:, d, :], G[:, 2 * d + 1, :], AF.Exp, r=[Gr], w=[lfr], scale=-1.0)
        p.act(LFt[:], LFt[:], AF.Ln, r=[lfr], w=[lfr], bias=1.0)
        p.ts("dve", LFt[:], LFt[:], -1.0, ALU.mult, r=[lfr], w=[lfr])
        CUM = p.sbuf("CUM", [128, 2, NCK], F32); IB = p.sbuf("IB", [128, 2, NCK], F32)
        SC = p.sbuf("SC", [128, 2, NCK], F32); ET = p.sbuf("ET", [128, 2, NCK], F32)
        pr = p.res("pre")
        pc = p.psum("pc", [128, 4, 128]); pcr = p.res("pc")
        for d in range(2):
            p.mm(pc[:, d, 0:NCK], tri[:, d, :], LFt[:, d, :], True, True, r=[cr, lfr], w=[pcr])
            p.mm(pc[:, 2 + d, 0:NCK], ones[:], LFt[:, d, :], True, True, r=[cr, lfr], w=[pcr])
        p.cp("dve", CUM[:], pc[:, 0:2, 0:NCK], r=[pcr], w=[pr])
        for d in range(2):
            p.tt("dve", IB[:, d, :], G[:, 2 * d, :], CUM[:, d, :], ALU.subtract, r=[Gr, pr], w=[pr])
            p.tt("dve", SC[:, d, :], pc[:, 2 + d, 0:NCK], IB[:, d, :], ALU.add, r=[pcr, pr], w=[pr])
        p.act(SC[:], SC[:], AF.Exp, r=[pr], w=[pr])
        p.act(ET[:], pc[:, 2:4, 0:NCK], AF.Exp, r=[pcr], w=[pr])

        if STOP == 2:
            p.dma('sp', dst_ctx[:, 0:66], SC[:, 0, :], r=[pr], w=[dst_res]); return
        H = p.sbuf("H", [128, TOK], F32); Hr = p.res("H")
        p.memset("pool", H[:], 0.0, w=[Hr])
        PA = [p.psum("pa%d" % d, [128, 4, 128]) for d in range(2)]
        PB = [p.psum("pb%d" % d, [128, 512]) for d in range(2)]
        rP1 = [p.res() for _ in range(2)]; rP2 = [p.res() for _ in range(2)]; rKQ = [p.res() for _ in range(2)]
        rNUM = [p.res() for _ in range(2)]; rDEN = [p.res() for _ in range(2)]; rST = [p.res() for _ in range(2)]
        def T(nm, shape, dt):
            return [p.sbuf("%s%d" % (nm, d), shape, dt) for d in range(2)], [p.res("%s%d" % (nm, d)) for d in range(2)]
        DG, DGr = T("DG", [128, 128], F32); WT, WTr = T("WT", [128, 128], F32); EC, ECr = T("EC", [128, 128], F32)
        QE, QEr = T("QE", [128, 128], BF16); STt, STr = T("ST", [128, 128], BF16)
        DD, DDr = T("DD", [128, 128], F32); HT, HTr = T("HT", [128, 128], F32)
        VS, VSr = T("VS", [128, 132], BF16); CF, CFr = T("CF", [128, 132], F32)
        CB, CBr = T("CB", [128, 128], BF16); NB, NBr = T("NB", [128, 128], BF16)
        for d in range(2):
            p.memset("pool", CF[d][:], 0.0, w=[CFr[d]]); p.memset("pool", CB[d][:], 0.0, w=[CBr[d]]); p.memset("pool", NB[d][:], 0.0, w=[NBr[d]])
        LE = 'dve'
        order = [list(range(NCK)), [1, 0] + list(range(NCK - 1, 1, -1))]
        for step in range(NCK):
            for d in range(2):
                c = order[d][step]
                cs = slice(c * 128, (c + 1) * 128)
                P1 = PA[d][:, 0, :]; P2 = PA[d][:, 1, :]; KQ = PA[d][:, 2, :]
                NUM = PB[d][:, 0:128]; DEN = PB[d][:, 128:256]; STP = PB[d][:, 256:256 + 129]
                p.ts(LE, DG[d][:], ident[:], CUM[:, d, c:c + 1], ALU.mult, r=[idr, pr], w=[DGr[d]])
                p.mm(P1, ones[:], DG[d][:], True, True, r=[cr, DGr[d]], w=[rP1[d]])
                p.mm(P2, ones[:], DG[d][:], True, False, r=[cr, DGr[d]], w=[rP2[d]])
                p.mm(P2, ident[:], mneg[:, d, :], False, True, r=[cr, idr], w=[rP2[d]])
                p.mm(KQ, kT[:, cs], qT[:, cs], True, True, r=[kr, qr], w=[rKQ[d]])
                p.act(WT[d][:], P2, AF.Exp, r=[rP2[d], pr], w=[WTr[d]], bias=IB[:, d, c:c + 1])
                p.act(EC[d][:], P1, AF.Exp, r=[rP1[d]], w=[ECr[d]])
                p.tt("dve", STt[d][:], KQ, WT[d][:], ALU.mult, r=[rKQ[d], WTr[d]], w=[STr[d]])
                p.tt(LE, QE[d][:], qT[:, cs], EC[d][:], ALU.mult, r=[qr, ECr[d]], w=[QEr[d]])
                p.mm(NUM, vtok[:, c, :], STt[d][:], True, False, r=[vtr, STr[d]], w=[rNUM[d]])
                p.mm(NUM, CB[d][:], QE[d][:], False, True, r=[CBr[d], QEr[d]], w=[rNUM[d]])
                p.mm(DEN, onesb[:], STt[d][:], True, False, r=[cr, STr[d]], w=[rDEN[d]])
                p.mm(DEN, NB[d][:], QE[d][:], False, True, r=[NBr[d], QEr[d]], w=[rDEN[d]])
                p.act(DD[d][:], DEN, AF.Abs, r=[rDEN[d]], w=[DDr[d]])
                p.ts("dve", DD[d][:], DD[d][:], 1.0, ALU.max, r=[DDr[d]], w=[DDr[d]])
                p.op("dve", lambda e, t_=DD[d]: e.reciprocal(out=t_[:], in_=t_[:]), r=[DDr[d]], w=[DDr[d]])
                p.tt("dve", HT[d][:], NUM, DD[d][:], ALU.mult, r=[rNUM[d], DDr[d]], w=[HTr[d]])
                p.tt(LE, H[:, cs], H[:, cs], HT[d][:], ALU.add, r=[Hr, HTr[d]], w=[Hr])
                p.ts(LE, VS[d][:, 0:128], vtok[:, c, :], SC[:, d, c:c + 1], ALU.mult, r=[vtr, pr], w=[VSr[d]])
                p.cp(LE, VS[d][:, 128:129], SC[:, d, c:c + 1], r=[pr], w=[VSr[d]])
                p.mm(STP, ktok[:, c, :], VS[d][:, 0:129], True, True, r=[ktr, VSr[d]], w=[rST[d]])
                p.stt(CF[d][:, 0:129], CF[d][:, 0:129], ET[:, d, c:c + 1], STP, ALU.mult, ALU.add, r=[CFr[d], pr, rST[d]], w=[CFr[d]])
                p.cp("act", CB[d][:], CF[d][:, 0:128], r=[CFr[d]], w=[CBr[d]])
                p.cp(LE, NB[d][:], CF[d][:, 128:129].to_broadcast([128, 128]), r=[CFr[d]], w=[NBr[d]])

        nw = p.sbuf("nw", [128, 1], F32); nwr = p.res("nw")
        p.dma("sp", nw[:], normw_ap.rearrange("(a b) -> a b", b=1), w=[nwr])
        lat, ctx, rs = src["oT"]
        p.dma("sp", stg[:, 0:256], ctx, r=[rs], w=[stgr]); p.dma("sp", stg[:, 256:], lat, r=[rs], w=[stgr])
        sq = p.sbuf("sq", [128, 512], F32); sqr = p.res("sq")
        rs_t = p.sbuf("rs_t", [128, 512], F32); rsr = p.res("rs_t")
        pn = [PB[0], PB[1]]; pnr = [p.res(), p.res()]
        ci = 0
        for t0 in range(0, TOK, 512):
            n = min(512, TOK - t0); ts_ = slice(t0, t0 + n)
            h = ci % 2; ci += 1
            p.act(sq[:, 0:n], H[:, ts_], AF.Square, r=[Hr], w=[sqr])
            p.mm(pn[h][:, 0:n], ones[:], sq[:, 0:n], True, True, r=[cr, sqr, rNUM[h], rDEN[h], rST[h]], w=[pnr[h], rNUM[h], rDEN[h], rST[h]])
            p.act(rs_t[:, 0:n], pn[h][:, 0:n], AF.Sqrt, r=[pnr[h]], w=[rsr], scale=1.0 / 128.0, bias=1e-6)
            p.op("dve", lambda e, o=rs_t[:, 0:n]: e.reciprocal(out=o, in_=o), r=[rsr], w=[rsr])
            p.tt("dve", H[:, ts_], H[:, ts_], rs_t[:, 0:n], ALU.mult, r=[Hr, rsr], w=[Hr])
            p.act(stg[:, ts_], stg[:, ts_], AF.Sigmoid, r=[stgr], w=[stgr])
            p.stt(H[:, ts_], H[:, ts_], nw[:, 0:1], stg[:, ts_], ALU.mult, ALU.mult, r=[Hr, nwr, stgr], w=[Hr])
        p.dma("sp", dst_ctx, H[:, 0:256], r=[Hr], w=[dst_res])
        p.dma("sp", dst_lat, H[:, 256:], r=[Hr], w=[dst_res])


D = 1024
EPS = 1e-6


def ttiles(NL, NC):
    out = [(t0, min(512, NL - t0), False) for t0 in range(0, NL, 512)]
    if NC:
        out.append((NL, NC, True))
    return out


def mod_stage(p, cv_ap, ada_w_ap, ada_b_ap, mod_d, mod_res, NM=12):
    with p.scope():
        cv = p.sbuf("cv", [128, 8, 2], F32); cvr = p.res("cv")
        p.dma("sp", cv[:], cv_ap.rearrange("(k p) n -> p k n", p=128), w=[cvr])
        p.act(cv[:], cv[:], AF.Silu, r=[cvr], w=[cvr])
        ab = p.sbuf("ab", [128, NM], F32); abr = p.res("ab")
        p.dma("sp", ab[:], ada_b_ap.rearrange("(m p) -> p m", p=128), w=[abr], slow=True)
        acc = p.sbuf("acc", [128, NM, 2], F32); accr = p.res("acc")
        for n in range(2):
            p.cp("dve", acc[:, :, n], ab[:], r=[abr], w=[accr])
        wt = [p.sbuf("wt%d" % i, [128, 128 * NM], F32) for i in range(2)]; wtr = [p.res("wt%d" % i) for i in range(2)]
        ps = p.psum("ps", [128, NM, 2]); psr = p.res("ps")
        for k in range(8):
            h = k % 2
            p.dma("sp" if h else "pool", wt[h][:], ada_w_ap[128 * k:128 * k + 128, :], w=[wtr[h]])
            for m in range(NM):
                p.mm(ps[:, m, :], wt[h][:, 128 * m:128 * m + 128], cv[:, k, :], True, True, r=[wtr[h], cvr], w=[psr])
            p.tt("dve", acc[:], acc[:], ps[:], ALU.add, r=[accr, psr], w=[accr])
        p.dma("sp", mod_d, acc[:], r=[accr], w=[mod_res])


def load_mod(p, mod_d, mod_res, nw_ap, sidx, scidx):
    modt = p.sbuf("modt", [128, 48, 2], F32); mr = p.res("modt")
    p.dma("sp", modt[:], mod_d, r=[mod_res], w=[mr], slow=True)
    nw = p.sbuf("nw", [128, 8], F32)
    p.dma("sp", nw[:], nw_ap.rearrange("(k p) -> p k", p=128), w=[mr], slow=True)
    A = p.sbuf("A", [128, 8, 2], F32)
    p.ts("dve", A[:], modt[:, 8 * scidx:8 * scidx + 8, :], 1.0, ALU.add, r=[mr], w=[mr])
    p.tt("dve", A[:], A[:], nw[:, :, None].to_broadcast([128, 8, 2]), ALU.mult, r=[mr], w=[mr])
    return modt, A, mr


def norm_mod_tile(p, xt, xr, n, A, SH, col, mr, ones, onr, ps, psr, sq, sqr, rstd, rsr, hb, hbr, hf=None, hfr=None):
    for k in range(8):
        p.act(sq[:, 0:n], xt[:, k, 0:n], AF.Square, r=[xr], w=[sqr])
        p.mm(ps[:, 0:n], ones[:], sq[:, 0:n], k == 0, k == 7, r=[onr, sqr], w=[psr])
    p.act(rstd[:, 0:n], ps[:, 0:n], AF.Sqrt, r=[psr], w=[rsr], scale=1.0 / D, bias=EPS)
    p.op("dve", lambda e, o=rstd[:, 0:n]: e.reciprocal(out=o, in_=o), r=[rsr], w=[rsr])
    for k in range(8):
        p.tt("dve", sq[:, 0:n], xt[:, k, 0:n], rstd[:, 0:n], ALU.mult, r=[xr, rsr, sqr], w=[sqr])
        if hf is not None:
            p.ts("dve", hf[:, k, 0:n], sq[:, 0:n], A[:, k, col:col + 1], ALU.mult, r=[sqr, mr], w=[hfr],
                 s2=SH[:, k, col:col + 1], op1=ALU.add)
            p.cp("act", hb[:, k, 0:n], hf[:, k, 0:n], r=[hfr], w=[hbr])
        else:
            p.ts("dve", hb[:, k, 0:n], sq[:, 0:n], A[:, k, col:col + 1], ALU.mult, r=[sqr, mr], w=[hbr],
                 s2=SH[:, k, col:col + 1], op1=ALU.add)


def norm1_stage(p, xT_d, x_res, NL, NC, mod_d, mod_res, nw_ap, hT_d, h_res):
    with p.scope():
        modt, A, mr = load_mod(p, mod_d, mod_res, nw_ap, 0, 1)
        SH = modt[:, 0:8, :]
        ones = p.sbuf("ones", [128, 128], F32); onr = p.res("ones"); p.memset("dve", ones[:], 1.0, w=[onr])
        xt = [p.sbuf("xt%d" % i, [128, 8, 512], F32) for i in range(2)]; xr = [p.res() for i in range(2)]
        hb = [p.sbuf("hb%d" % i, [128, 8, 512], BF16) for i in range(2)]; hbr = [p.res() for i in range(2)]
        sq = p.sbuf("sq", [128, 512], F32); sqr = p.res(); rstd = p.sbuf("rstd", [128, 512], F32); rsr = p.res()
        ps = p.psum("ps", [128, 512]); psr = p.res()
        for i, (t0, n, isc) in enumerate(ttiles(NL, NC)):
            h = i % 2
            p.dma("sp", xt[h][:, :, 0:n], xT_d[:, t0:t0 + n].rearrange("(k p) t -> p k t", p=128), r=[x_res], w=[xr[h]])
            norm_mod_tile(p, xt[h], xr[h], n, A, SH, 1 if isc else 0, mr, ones, onr, ps, psr, sq, sqr, rstd, rsr, hb[h], hbr[h])
            p.dma("pool", hT_d[i].rearrange("(k p) t -> p k t", p=128), hb[h][:, :, 0:n], r=[hbr[h]], w=[h_res[i]])


def load_w_bf16(p, w_cols_ap, m, wst, wstr, wb, wbr, eng="act", q="sp"):
    p.dma(q, wst[:, :, 0:m], w_cols_ap.rearrange("(k p) m -> p k m", p=128), w=[wstr])
    p.cp(eng, wb[:, :, 0:m], wst[:, :, 0:m], r=[wstr], w=[wbr])


def inproj_gate_stage(p, hT_d, h_res, NL, NC, w_in_ap, b_in_ap, off, nchunk, gT_d, g_res):
    NT = NL + NC
    GW = 4
    with p.scope():
        hT = p.sbuf("hT", [128, 8, NT], BF16); hr = p.res("hT")
        for i, (t0, n, isc) in enumerate(ttiles(NL, NC)):
            p.dma("sp", hT[:, :, t0:t0 + n], hT_d[i].rearrange("(k p) t -> p k t", p=128), r=[h_res[i]], w=[hr])
        bias = p.sbuf("bias", [128, nchunk], F32); br = p.res("bias")
        p.dma("sp", bias[:], b_in_ap[off:off + 128 * nchunk].rearrange("(m p) -> p m", p=128), w=[br], slow=True)
        wst = [p.sbuf("wst%d" % i, [128, 8, 128 * GW], F32) for i in range(2)]; wstr = [p.res() for i in range(2)]
        wb = [p.sbuf("wb%d" % i, [128, 8, 128 * GW], BF16) for i in range(2)]; wbr = [p.res() for i in range(2)]
        ot = [p.sbuf("ot%d" % i, [128, NT], BF16) for i in range(2)]; otr = [p.res() for i in range(2)]
        ps = [p.psum("ps%d" % i, [128, 512]) for i in range(6)]; psr = [p.res() for i in range(6)]
        kk = 0
        ngrp = nchunk // GW
        def load(g):
            h = g % 2
            c0 = off + 128 * GW * g
            p.dma("sp" if h else "pool", wst[h][:], w_in_ap[:, c0:c0 + 128 * GW].rearrange("(k p) m -> p k m", p=128), w=[wstr[h]])
            p.cp("act", wb[h][:], wst[h][:], r=[wstr[h]], w=[wbr[h]])
        load(0)
        for g in range(ngrp):
            h = g % 2
            if g + 1 < ngrp:
                load(g + 1)
            for mi in range(GW):
                m = g * GW + mi
                o = m % 2
                for (t0, n, isc) in ttiles(NL, NC):
                    b_ = kk % 6; kk += 1
                    for k in range(8):
                        p.mm(ps[b_][:, 0:n], wb[h][:, k, 128 * mi:128 * mi + 128], hT[:, k, t0:t0 + n], k == 0, k == 7, r=[wbr[h], hr], w=[psr[b_]])
                    p.act(ot[o][:, t0:t0 + n], ps[b_][:, 0:n], AF.Sigmoid, r=[psr[b_], br], w=[otr[o]], bias=bias[:, m:m + 1])
                p.dma("sp", gT_d[128 * m:128 * m + 128, :], ot[o][:], r=[otr[o]], w=[g_res])


def inproj_mix_stage(p, hall_d, hall_res, NL, NC, w_in_ap, b_in_ap, col_list, zlat_d, zctx_d, z_res):
    NT = NL + NC
    nchunk = len(col_list)
    with p.scope():
        wst = p.sbuf("wst", [128, 8, 128], F32); wstr = p.res()
        W = p.sbuf("W", [128, nchunk, 8, 128], BF16); Wr = p.res("W")
        bias = p.sbuf("bias", [128, nchunk], F32); br = p.res("bias")
        p.memset("dve", bias[:], 0.0, w=[br])
        p.memset("dve", W[:], 0.0, w=[Wr])
        for i, (c0, m) in enumerate(col_list):
            p.dma("sp", wst[:, :, 0:m], w_in_ap[:, c0:c0 + m].rearrange("(k p) m -> p k m", p=128), w=[wstr], slow=(m < 128))
            p.cp("act" if i % 2 else "dve", W[:, i, :, 0:m], wst[:, :, 0:m], r=[wstr], w=[Wr])
            p.dma("pool", bias[0:m, i:i + 1], b_in_ap[c0:c0 + m].rearrange("(a b) -> a b", b=1), w=[br])
        hT = [p.sbuf("hT%d" % i, [128, 8, 512], BF16) for i in range(2)]; hr = [p.res() for i in range(2)]
        ot = [p.sbuf("ot%d" % i, [128, nchunk, 512], F32) for i in range(2)]; otr = [p.res() for i in range(2)]
        ps = [p.psum("ps%d" % i, [128, 512]) for i in range(4)]; psr = [p.res() for i in range(4)]
        kk = 0; ti = 0
        for r in range(4):
            for i_t, (t0, n, isc) in enumerate(ttiles(NL, NC)):
                h = ti % 2; ti += 1
                p.dma("sp", hT[h][:, :, 0:n], hall_d[i_t][1024 * r:1024 * r + 1024, :].rearrange("(k p) t -> p k t", p=128),
                      r=[hall_res[i_t]], w=[hr[h]])
                for i in range(nchunk):
                    b_ = kk % 4; kk += 1
                    for k in range(8):
                        p.mm(ps[b_][:, 0:n], W[:, i, k, :], hT[h][:, k, 0:n], k == 0, k == 7, r=[Wr, hr[h]], w=[psr[b_]])
                    p.act(ot[h][:, i, 0:n], ps[b_][:, 0:n], AF.Identity, r=[psr[b_], br], w=[otr[h]], bias=bias[:, i:i + 1])
                if isc:
                    dst = zctx_d[:, :, NC * r:NC * r + n]
                else:
                    dst = zlat_d[:, :, NL * r + t0:NL * r + t0 + n]
                p.dma("pool", dst.rearrange("i p t -> p i t"), ot[h][:, :, 0:n], r=[otr[h]], w=[z_res])


def yasm_stage(p, srcs, skip_ap, y_own_d, y_res, LL, LC):
    with p.scope():
        sk = p.sbuf("sk", [128, 1], F32); skr = p.res("sk")
        p.dma("sp", sk[:], skip_ap.rearrange("(a b) -> a b", b=1), w=[skr])
        CW = 2048
        A = [p.sbuf("A%d" % i, [128, CW], F32) for i in range(3)]; Ar = [p.res() for i in range(3)]
        O = [p.sbuf("O%d" % i, [128, CW], BF16) for i in range(2)]; Or = [p.res() for i in range(2)]
        kk = 0
        pieces = [(0, t0, min(CW, LL - t0), t0) for t0 in range(0, LL, CW)] + ([(1, 0, LC, LL)] if LC else [])
        for (which, t0, n, o0) in pieces:
            for i, nm in enumerate(("z", "c2", "x2")):
                ap = srcs[nm][which]
                p.dma("sp", A[i][:, 0:n], ap[:, t0:t0 + n], r=[srcs[nm][2]], w=[Ar[i]])
            p.stt(A[0][:, 0:n], A[0][:, 0:n], sk[:, 0:1], A[1][:, 0:n], ALU.mult, ALU.add, r=[Ar[0], Ar[1], skr], w=[Ar[0]])
            h = kk % 2; kk += 1
            p.tt("dve", O[h][:, 0:n], A[0][:, 0:n], A[2][:, 0:n], ALU.mult, r=[Ar[0], Ar[2]], w=[Or[h]])
            for (ap_, rs_, a0, an) in y_own_d(0, o0, n):
                p.dma("pool", ap_, O[h][:, a0:a0 + an], r=[Or[h]], w=[rs_])
            for bi, nm in ((1, "fn"), (2, "ml")):
                ap = srcs[nm][which]
                p.dma("sp", A[bi][:, 0:n], ap[:, t0:t0 + n], r=[srcs[nm][2]], w=[Ar[bi]])
                h = kk % 2; kk += 1
                p.cp("act", O[h][:, 0:n], A[bi][:, 0:n], r=[Ar[bi]], w=[Or[h]])
                for (ap_, rs_, a0, an) in y_own_d(bi, o0, n):
                    p.dma("pool", ap_, O[h][:, a0:a0 + an], r=[Or[h]], w=[rs_])


def merge_stage(p, y_all_d, y_res, hT_d, h_res, wgi_ap, bgi_ap, oh_ap, xT_d, x_res, NL, NC, mod_d, mod_res, wbr_ap, wout_ap, do_ctx):
    LL = 4 * NL
    with p.scope():
        modt = p.sbuf("modt", [128, 48, 2], F32); mr = p.res("modt")
        p.dma("sp", modt[:], mod_d, r=[mod_res], w=[mr])
        oh = p.sbuf("oh", [128, 4], F32); ohr = p.res("oh")
        p.dma("sp", oh[:], oh_ap, w=[ohr])
        bias = p.sbuf("bias", [128, 24], F32)
        p.dma("sp", bias[:], bgi_ap.rearrange("(m p) -> p m", p=128), w=[ohr], slow=True)
        wst = p.sbuf("wst", [128, 8, 1024], F32); wstr = p.res()
        WB = p.sbuf("WB", [128, 12, 1024], BF16); WO = p.sbuf("WO", [128, 8, 1024], BF16); Wr = p.res("W")
        WG = p.sbuf("WG", [128, 8, 3072], BF16)
        for br in range(3):
            p.dma("sp", wst[:, 0:4, :], wbr_ap[br].rearrange("(r p) d -> p r d", p=128), w=[wstr])
            p.cp("act" if br % 2 else "dve", WB[:, 4 * br:4 * br + 4, :], wst[:, 0:4, :], r=[wstr], w=[Wr])
        p.dma("sp", wst[:], wout_ap.rearrange("(k p) d -> p k d", p=128), w=[wstr])
        p.cp("act", WO[:], wst[:], r=[wstr], w=[Wr])
        for c in range(3):
            p.dma("sp" if c % 2 else "pool", wst[:], wgi_ap[:, 1024 * c:1024 * c + 1024].rearrange("(k p) m -> p k m", p=128), w=[wstr])
            p.cp("dve" if c % 2 else "act", WG[:, :, 1024 * c:1024 * c + 1024], wst[:], r=[wstr], w=[Wr])
        Yc = [p.sbuf("Yc%d" % i, [128, 12, 512], BF16) for i in range(2)]; Ycr = [p.res() for i in range(2)]
        Y = p.sbuf("Y", [128, 12, 512], BF16); Yr = p.res("Y")
        hT = p.sbuf("hT", [128, 8, 512], BF16); hr = p.res("hT")
        Gs = [p.sbuf("Gs%d" % i, [128, 512], BF16) for i in range(6)]; Gsr = [p.res() for i in range(6)]
        xt = p.sbuf("xt", [128, 8, 512], F32); xr = p.res("xt")
        mg = p.sbuf("mg", [128, 8, 512], BF16); mgr = p.res("mg")
        t1 = p.sbuf("t1", [128, 512], F32); t1r = p.res(); t2 = p.sbuf("t2", [128, 512], F32); t2r = p.res()
        ps = [p.psum("ps%d" % i, [128, 512]) for i in range(8)]; psr = [p.res() for i in range(8)]
        kk = 0; gi = 0
        tiles = ttiles(NL, NC if do_ctx else 0)
        for ti_, (t0, n, isc) in enumerate(tiles):
            col = 1 if isc else 0
            for jj in range(4):
                c0 = (LL + NC * jj) if isc else (NL * jj + t0)
                h = jj % 2
                for br in range(3):
                    ap_, rs_ = y_all_d(br, c0, n)
                    p.dma("sp" if br % 2 else "pool", Yc[h][:, 4 * br:4 * br + 4, 0:n],
                          ap_.rearrange("(q p) t -> p q t", p=128), r=[rs_], w=[Ycr[h]])
                if jj == 0:
                    p.ts("dve", Y[:, :, 0:n], Yc[h][:, :, 0:n], oh[:, 0:1], ALU.mult, r=[Ycr[h], ohr], w=[Yr])
                else:
                    p.stt(Y[:, :, 0:n], Yc[h][:, :, 0:n], oh[:, jj:jj + 1], Y[:, :, 0:n], ALU.mult, ALU.add, r=[Ycr[h], ohr, Yr], w=[Yr])
            p.dma("sp", hT[:, :, 0:n], hT_d[ti_].rearrange("(k p) t -> p k t", p=128), r=[h_res[ti_]], w=[hr])
            p.dma("pool", xt[:, :, 0:n], xT_d[:, t0:t0 + n].rearrange("(k p) t -> p k t", p=128), r=[x_res], w=[xr])
            for m in range(8):
                gsel = []
                for br in range(3):
                    b_ = kk % 8; kk += 1
                    g_ = gi % 6; gi += 1; gsel.append(g_)
                    cg = (br * 8 + m) * 128
                    for k in range(8):
                        p.mm(ps[b_][:, 0:n], WG[:, k, cg:cg + 128], hT[:, k, 0:n], k == 0, k == 7, r=[Wr, hr], w=[psr[b_]])
                    p.act(Gs[g_][:, 0:n], ps[b_][:, 0:n], AF.Sigmoid, r=[psr[b_], ohr], w=[Gsr[g_]], bias=bias[:, br * 8 + m:br * 8 + m + 1])
                pb = []
                for br in range(3):
                    b_ = kk % 8; kk += 1; pb.append(b_)
                    for r in range(4):
                        p.mm(ps[b_][:, 0:n], WB[:, 4 * br + r, 128 * m:128 * m + 128], Y[:, 4 * br + r, 0:n], r == 0, r == 3,
                             r=[Wr, Yr], w=[psr[b_]])
                p.tt("dve", t1[:, 0:n], ps[pb[0]][:, 0:n], Gs[gsel[0]][:, 0:n], ALU.mult, r=[psr[pb[0]], Gsr[gsel[0]]], w=[t1r])
                p.tt("dve", t2[:, 0:n], ps[pb[1]][:, 0:n], Gs[gsel[1]][:, 0:n], ALU.mult, r=[psr[pb[1]], Gsr[gsel[1]]], w=[t2r])
                p.tt("dve", t1[:, 0:n], t1[:, 0:n], t2[:, 0:n], ALU.add, r=[t1r, t2r], w=[t1r])
                p.tt("dve", t2[:, 0:n], ps[pb[2]][:, 0:n], Gs[gsel[2]][:, 0:n], ALU.mult, r=[psr[pb[2]], Gsr[gsel[2]]], w=[t2r])
                p.tt("dve", mg[:, m, 0:n], t1[:, 0:n], t2[:, 0:n], ALU.add, r=[t1r, t2r], w=[mgr])
            for m in range(8):
                b_ = kk % 8; kk += 1
                for k in range(8):
                    p.mm(ps[b_][:, 0:n], WO[:, k, 128 * m:128 * m + 128], mg[:, k, 0:n], k == 0, k == 7, r=[Wr, mgr], w=[psr[b_]])
                p.stt(xt[:, m, 0:n], ps[b_][:, 0:n], modt[:, 16 + m, col:col + 1], xt[:, m, 0:n], ALU.mult, ALU.add,
                      r=[psr[b_], mr, xr], w=[xr])
            p.dma("sp", xT_d[:, t0:t0 + n].rearrange("(k p) t -> p k t", p=128), xt[:, :, 0:n], r=[xr], w=[x_res])


def moe_stage(p, xT_d, x_res, NL, NC, mod_d, mod_res, nw_ap, wr_ap, br_ap, wg_ap, wu_ap, wd_ap, do_ctx,
              final_nw_ap=None, out_d=None, out_res=None):
    NCX = NC if do_ctx else 0
    NT = NL + NCX
    tiles = ttiles(NL, NCX)
    nsub = (NT + 127) // 128
    with p.scope():
        modt, A, mr = load_mod(p, mod_d, mod_res, nw_ap, 3, 4)
        SH = modt[:, 24:32, :]
        ones = p.sbuf("ones", [128, 128], F32); onr = p.res("ones"); p.memset("dve", ones[:], 1.0, w=[onr])
        identf, idr = make_ident(p, F32, "idf")
        sq = p.sbuf("sq", [128, 512], F32); sqr = p.res(); rstd = p.sbuf("rstd", [128, 512], F32); rsr = p.res()
        xt = p.sbuf("xt", [128, 8, 512], F32); xr = p.res("xt")
        hf = p.sbuf("hf", [128, 8, 512], F32); hfr = p.res("hf")
        H2 = p.sbuf("H2", [128, 8, NT], BF16); h2r = p.res("H2")
        WR = p.sbuf("WR", [128, 8, 20], F32); wrr = p.res("WR")
        p.dma("sp", WR[:], wr_ap.rearrange("(k p) n -> p k n", p=128), w=[wrr], slow=True)
        BR = p.sbuf("BR", [128, 20], F32)
        p.dma("sp", BR[:], br_ap.partition_broadcast(128), w=[wrr])
        CWt = p.sbuf("CWt", [128, nsub, 16], F32); cwr = p.res("CW")
        ps = [p.psum("ps%d" % i, [128, 512]) for i in range(6)]; psr = [p.res() for i in range(6)]
        pr_ = p.psum("pr", [128, 32]); prr = p.res("pr")
        def st(nm, w):
            return p.sbuf(nm, [128, w], F32)
        L_ = st("L", 20); gm = st("gm", 1); ge = st("ge", 4); gs = st("gs", 1); gmask = st("gmask", 4)
        tmp16 = st("tmp16", 16); eg = st("eg", 4); m1 = st("m1", 1); mk1 = st("mk1", 4); eg2 = st("eg2", 4); m2 = st("m2", 1)
        mk2 = st("mk2", 4); w1 = st("w1", 1); w2 = st("w2", 1); cwe = st("cwe", 4)
        rr = p.res("route")
        for (t0, n, isc) in tiles:
            col = 1 if isc else 0
            p.dma("sp", xt[:, :, 0:n], xT_d[:, t0:t0 + n].rearrange("(k p) t -> p k t", p=128), r=[x_res], w=[xr])
            norm_mod_tile(p, xt, xr, n, A, SH, col, mr, ones, onr, ps[0], psr[0], sq, sqr, rstd, rsr,
                          H2[:, :, t0:t0 + n], h2r, hf=hf, hfr=hfr)
            for s0 in range(0, n, 128):
                sn = min(128, n - s0); si = (t0 + s0) // 128
                for k in range(8):
                    p.mm(pr_[0:sn, 0:20], hf[:, k, s0:s0 + sn], WR[:, k, :], k == 0, k == 7, r=[hfr, wrr], w=[prr])
                R = [rr]
                p.tt("dve", L_[0:sn], pr_[0:sn, 0:20], BR[0:sn], ALU.add, r=[prr, wrr, rr], w=R)
                p.op("dve", lambda e, o=gm[0:sn], i=L_[0:sn, 0:4]: e.tensor_reduce(out=o, in_=i, axis=AX.X, op=ALU.max), r=R, w=R)
                p.ts("dve", gmask[0:sn], L_[0:sn, 0:4], gm[0:sn, 0:1], ALU.is_equal, r=R, w=R)
                p.ts("dve", gm[0:sn], gm[0:sn], -1.0, ALU.mult, r=R, w=R)
                p.act(ge[0:sn], L_[0:sn, 0:4], AF.Exp, r=R, w=R, bias=gm[0:sn, 0:1])
                p.op("dve", lambda e, o=gs[0:sn], i=ge[0:sn]: e.tensor_reduce(out=o, in_=i, axis=AX.X, op=ALU.add), r=R, w=R)
                p.op("dve", lambda e, o=gs[0:sn]: e.reciprocal(out=o, in_=o), r=R, w=R)
                p.tt("dve", tmp16[0:sn].rearrange("p (g e) -> p g e", e=4), L_[0:sn, 4:20].rearrange("p (g e) -> p g e", e=4),
                     gmask[0:sn, :, None].to_broadcast([sn, 4, 4]), ALU.mult, r=R, w=R)
                p.op("dve", lambda e, o=eg[0:sn], i=tmp16[0:sn].rearrange("p (g e) -> p e g", e=4): e.tensor_reduce(out=o, in_=i, axis=AX.X, op=ALU.add), r=R, w=R)
                p.op("dve", lambda e, o=m1[0:sn], i=eg[0:sn]: e.tensor_reduce(out=o, in_=i, axis=AX.X, op=ALU.max), r=R, w=R)
                p.ts("dve", mk1[0:sn], eg[0:sn], m1[0:sn, 0:1], ALU.is_equal, r=R, w=R)
                p.stt(eg2[0:sn], mk1[0:sn], -1e30, eg[0:sn], ALU.mult, ALU.add, r=R, w=R)
                p.op("dve", lambda e, o=m2[0:sn], i=eg2[0:sn]: e.tensor_reduce(out=o, in_=i, axis=AX.X, op=ALU.max), r=R, w=R)
                p.ts("dve", mk2[0:sn], eg2[0:sn], m2[0:sn, 0:1], ALU.is_equal, r=R, w=R)
                p.tt("dve", w1[0:sn], m2[0:sn], m1[0:sn], ALU.subtract, r=R, w=R)
                p.act(w1[0:sn], w1[0:sn], AF.Exp, r=R, w=R)
                p.ts("dve", w1[0:sn], w1[0:sn], 1.0, ALU.add, r=R, w=R)
                p.op("dve", lambda e, o=w1[0:sn]: e.reciprocal(out=o, in_=o), r=R, w=R)
                p.ts("dve", w2[0:sn], w1[0:sn], -1.0, ALU.mult, r=R, w=R, s2=1.0, op1=ALU.add)
                p.tt("dve", w1[0:sn], w1[0:sn], gs[0:sn], ALU.mult, r=R, w=R)
                p.tt("dve", w2[0:sn], w2[0:sn], gs[0:sn], ALU.mult, r=R, w=R)
                p.ts("dve", cwe[0:sn], mk1[0:sn], w1[0:sn, 0:1], ALU.mult, r=R, w=R)
                p.stt(cwe[0:sn], mk2[0:sn], w2[0:sn, 0:1], cwe[0:sn], ALU.mult, ALU.add, r=R, w=R)
                p.cp("dve", tmp16[0:sn].rearrange("p (g e) -> p g e", e=4), cwe[0:sn, None, :].to_broadcast([sn, 4, 4]), r=R, w=R)
                p.tt("dve", CWt[0:sn, si, :].rearrange("p (g e) -> p g e", e=4), tmp16[0:sn].rearrange("p (g e) -> p g e", e=4),
                     gmask[0:sn, :, None].to_broadcast([sn, 4, 4]), ALU.mult, r=R, w=[cwr, rr])
        ACC = p.sbuf("ACC", [128, nsub, 1024], F32); accr = p.res("ACC")
        p.memset("dve", ACC[:], 0.0, w=[accr])
        wst = [p.sbuf("wst%d" % i, [128, 8, 256], F32) for i in range(2)]; wstr = [p.res() for i in range(2)]
        WG = [p.sbuf("WG%d" % i, [128, 8, 256], BF16) for i in range(2)]; WU = [p.sbuf("WU%d" % i, [128, 8, 256], BF16) for i in range(2)]
        WD = [p.sbuf("WD%d" % i, [128, 2, 1024], BF16) for i in range(2)]
        wer = [p.res() for i in range(2)]
        SG = p.sbuf("SG", [128, 512], BF16); sgr = p.res()
        AA = p.sbuf("AA", [128, 2, 512], BF16); aar = p.res()
        kk = 0
        for e_ in range(16):
            h = e_ % 2
            p.dma("sp", wst[0][:], wg_ap[e_].rearrange("(k p) m -> p k m", p=128), w=[wstr[0]])
            p.cp("act", WG[h][:], wst[0][:], r=[wstr[0]], w=[wer[h]])
            p.dma("pool", wst[1][:], wu_ap[e_].rearrange("(k p) m -> p k m", p=128), w=[wstr[1]])
            p.cp("act", WU[h][:], wst[1][:], r=[wstr[1]], w=[wer[h]])
            p.dma("sp", wst[0][:].rearrange("p a b -> p (a b)").rearrange("p (c d) -> p c d", c=2), wd_ap[e_].rearrange("(c p) d -> p c d", p=128), w=[wstr[0]])
            p.cp("act", WD[h][:], wst[0][:].rearrange("p a b -> p (a b)").rearrange("p (c d) -> p c d", c=2), r=[wstr[0]], w=[wer[h]])
            for (t0, n, isc) in tiles:
                for hc in range(2):
                    bg = kk % 6; kk += 1; bu = kk % 6; kk += 1
                    for k in range(8):
                        p.mm(ps[bg][:, 0:n], WG[h][:, k, 128 * hc:128 * hc + 128], H2[:, k, t0:t0 + n], k == 0, k == 7, r=[wer[h], h2r], w=[psr[bg]])
                    for k in range(8):
                        p.mm(ps[bu][:, 0:n], WU[h][:, k, 128 * hc:128 * hc + 128], H2[:, k, t0:t0 + n], k == 0, k == 7, r=[wer[h], h2r], w=[psr[bu]])
                    p.act(SG[:, 0:n], ps[bg][:, 0:n], AF.Silu, r=[psr[bg]], w=[sgr])
                    p.tt("dve", AA[:, hc, 0:n], ps[bu][:, 0:n], SG[:, 0:n], ALU.mult, r=[psr[bu], sgr], w=[aar])
                for s0 in range(0, n, 128):
                    sn = min(128, n - s0); si = (t0 + s0) // 128
                    for dh in range(2):
                        b_ = kk % 6; kk += 1
                        for hc in range(2):
                            p.mm(ps[b_][0:sn, :], AA[:, hc, s0:s0 + sn], WD[h][:, hc, 512 * dh:512 * dh + 512], hc == 0, hc == 1,
                                 r=[aar, wer[h]], w=[psr[b_]])
                        p.stt(ACC[0:sn, si, 512 * dh:512 * dh + 512], ps[b_][0:sn, :], CWt[0:sn, si, e_:e_ + 1],
                              ACC[0:sn, si, 512 * dh:512 * dh + 512], ALU.mult, ALU.add, r=[psr[b_], cwr, accr], w=[accr])
        if final_nw_ap is not None:
            fw_ = p.sbuf("fw", [128, 8], F32); fwr = p.res()
            p.dma("sp", fw_[:], final_nw_ap.rearrange("(k p) -> p k", p=128), w=[fwr], slow=True)
        for (t0, n, isc) in tiles:
            col = 1 if isc else 0
            p.dma("sp", xt[:, :, 0:n], xT_d[:, t0:t0 + n].rearrange("(k p) t -> p k t", p=128), r=[x_res], w=[xr])
            for m in range(8):
                b_ = kk % 6; kk += 1
                for s0 in range(0, n, 128):
                    sn = min(128, n - s0); si = (t0 + s0) // 128
                    p.tr(ps[b_][:, s0:s0 + sn], ACC[0:sn, si, 128 * m:128 * m + 128], identf[0:sn, 0:sn], r=[accr, idr], w=[psr[b_]])
                p.stt(xt[:, m, 0:n], ps[b_][:, 0:n], modt[:, 40 + m, col:col + 1], xt[:, m, 0:n], ALU.mult, ALU.add,
                      r=[psr[b_], mr, xr], w=[xr])
            if final_nw_ap is None:
                p.dma("pool", xT_d[:, t0:t0 + n].rearrange("(k p) t -> p k t", p=128), xt[:, :, 0:n], r=[xr], w=[x_res])
            elif not isc:
                for k in range(8):
                    p.act(sq[:, 0:n], xt[:, k, 0:n], AF.Square, r=[xr], w=[sqr])
                    p.mm(ps[0][:, 0:n], ones[:], sq[:, 0:n], k == 0, k == 7, r=[onr, sqr], w=[psr[0]])
                p.act(rstd[:, 0:n], ps[0][:, 0:n], AF.Sqrt, r=[psr[0]], w=[rsr], scale=1.0 / D, bias=EPS)
                p.op("dve", lambda e, o=rstd[:, 0:n]: e.reciprocal(out=o, in_=o), r=[rsr], w=[rsr])
                for k in range(8):
                    p.stt(hf[:, k, 0:n], xt[:, k, 0:n], fw_[:, k:k + 1], rstd[:, 0:n], ALU.mult, ALU.mult, r=[xr, fwr, rsr], w=[hfr])
                p.dma("pool", out_d[:, t0:t0 + n].rearrange("(k p) t -> p k t", p=128), hf[:, :, 0:n], r=[hfr], w=[out_res])

NLAT, NCTX = 2048, 64
LLAT, LCTX = 8192, 256
GROUPS = [[0, 1, 2, 3], [4, 5, 6, 7]]
DEPTH = 2
OFF_FN, OFF_ML, OFF_MLG, OFF_GATE = 1536, 2048, 4096, 4112


def build_program(const_np):
    p = Prog()
    I = {}

    def inp(name, shape, dt=F32):
        I[name] = p.dram(name, shape, dt, "ExternalInput")
        return I[name]

    inp("xT0", [1024, NLAT]); inp("cT0", [1024, NCTX]); inp("cv", [1024, 2]); inp("oh", [128, 4]); inp("norm_f", [1024])
    for k, v in const_np.items():
        inp(k, v.shape, F32 if v.dtype == np.float32 else BF16)
    for l in range(DEPTH):
        L = "_%d" % l
        inp("ada_w" + L, [1024, 1536]); inp("ada_b" + L, [1536]); inp("n1w" + L, [1024]); inp("n2w" + L, [1024])
        inp("w_gate_in" + L, [1024, 3072]); inp("b_gate_in" + L, [3072]); inp("w_mix" + L, [1024, 1028]); inp("b_mix" + L, [1028])
        inp("hy_cw" + L, [3, 3, 384]); inp("hy_cb" + L, [384]); inp("ml_cw" + L, [3, 3, 256]); inp("ml_cb" + L, [256])
        inp("f_w1" + L, [33, 64]); inp("f_b1" + L, [64]); inp("f_freq" + L, [64]); inp("f_w2" + L, [64, 64]); inp("f_b2" + L, [64])
        inp("f_w3" + L, [64, 4, 128]); inp("decay" + L, [128, 4]); inp("skip" + L, [2, 128]); inp("mlnw" + L, [128])
        inp("wbr" + L, [3, 512, 1024]); inp("wout" + L, [1024, 1024])
        inp("wr" + L, [1024, 20]); inp("br" + L, [20]); inp("wg" + L, [16, 1024, 256]); inp("wu" + L, [16, 1024, 256]); inp("wd" + L, [16, 256, 1024])
    outT = p.dram("outT", [1024, NLAT], F32, "ExternalOutput"); out_res = p.res("outT")

    NT = NLAT + NCTX
    S = {}

    def scr(name, shape, dt=F32):
        S[name] = (p.dram("s_" + name, shape, dt), p.res("s_" + name))
        return S[name]

    scr("mod_own", [128, 12, 2]); scr("mod_all", [4 * 128, 24]); scr("mod_full", [128, 48, 2])
    scr("xT", [1024, NT])
    TT = ttiles(NLAT, NCTX)
    hown = [scr("hown%d" % i, [1024, n], BF16) for i, (t0, n, isc) in enumerate(TT)]
    hall = [scr("hall%d" % i, [4096, n], BF16) for i, (t0, n, isc) in enumerate(TT)]
    YCH = [(0, 3072), (3072, 3072), (6144, 2304)]
    yown = [[scr("yown%d_%d" % (br, ck), [128, w_], BF16) for ck, (c0_, w_) in enumerate(YCH)] for br in range(3)]
    yall = [[scr("yall%d_%d" % (br, ck), [512, w_], BF16) for ck, (c0_, w_) in enumerate(YCH)] for br in range(3)]

    def y_own_fn(br, col0, n):
        out = []
        for ck, (c0_, w_) in enumerate(YCH):
            lo = max(col0, c0_); hi = min(col0 + n, c0_ + w_)
            if hi > lo:
                out.append((yown[br][ck][0][:, lo - c0_:hi - c0_], yown[br][ck][1], lo - col0, hi - lo))
        return out

    def y_all_fn(br, col0, n):
        for ck, (c0_, w_) in enumerate(YCH):
            if c0_ <= col0 and col0 + n <= c0_ + w_:
                return yall[br][ck][0][:, col0 - c0_:col0 - c0_ + n], yall[br][ck][1]
        raise AssertionError("y tile straddles chunks")
    scr("zlat", [9, 128, LLAT]); scr("zctx", [9, 128, LCTX]); scr("cvl", [5, 128, LLAT]); scr("cvc", [5, 128, LCTX])
    scr("fl", [4, 128, LLAT]); scr("fc", [4, 128, LCTX])
    for nm in ("c1", "c2", "zz", "fn", "ml"):
        scr(nm + "l", [128, LLAT]); scr(nm + "c", [128, LCTX])

    tabs = {k: I[k] for k in const_np}
    xT, xres = S["xT"]
    with p.scope():
        t = p.sbuf("t", [128, 8, NT], F32); tr = p.res()
        p.dma("sp", t[:, :, 0:NLAT], I["xT0"].rearrange("(k p) t -> p k t", p=128), w=[tr])
        p.dma("pool", t[:, :, NLAT:NT], I["cT0"].rearrange("(k p) t -> p k t", p=128), w=[tr])
        p.dma("sp", xT.rearrange("(k p) t -> p k t", p=128), t[:], r=[tr], w=[xres])

    mod_view = S["mod_all"][0].rearrange("(r p) (m n) -> p r m n", p=128, n=2)

    for l in range(DEPTH):
        L = "_%d" % l
        last = (l == DEPTH - 1)
        W = lambda nm: I[nm + L]
        p.label = 'mod_stage'; mod_stage(p, I["cv"], W("ada_w"), W("ada_b"), S["mod_own"][0], S["mod_own"][1], NM=12)
        p.coll("AllGather", S["mod_all"][0], S["mod_own"][0].rearrange("p m n -> p (m n)"), GROUPS, r=[S["mod_own"][1]], w=[S["mod_all"][1]])
        with p.scope():
            mt = p.sbuf("mt", [128, 48, 2], F32); mtr = p.res()
            p.dma("sp", mt[:].rearrange("p (r m) n -> p r m n", r=4), mod_view, r=[S["mod_all"][1]], w=[mtr], slow=True)
            p.dma("sp", S["mod_full"][0], mt[:], r=[mtr], w=[S["mod_full"][1]])
        modv, modr = S["mod_full"]
        p.label = 'norm1_stage'; norm1_stage(p, xT, xres, NLAT, NCTX, modv, modr, W("n1w"), [h_[0] for h_ in hown], [h_[1] for h_ in hown])
        for i in range(len(TT)):
            p.coll("AllGather", hall[i][0], hown[i][0], GROUPS, r=[hown[i][1]], w=[hall[i][1]])
        fw = {k: W(k) for k in ("f_w1", "f_b1", "f_freq", "f_w2", "f_b2", "f_w3", "decay")}
        p.label = 'hfilt_stage'; hfilt_stage(p, I["featsT_l"], I["trow_l"], fw, 0, LLAT, S["fl"][0], S["fl"][1])
        if not last:
            p.label = 'hfilt_stage'; hfilt_stage(p, I["featsT_c"], I["trow_c"], fw, 0, LCTX, S["fc"][0], S["fc"][1])
        cols = [(128 * i, 128) for i in range(8)] + [(1024, 4)]
        p.label = 'inproj_mix_stage'; inproj_mix_stage(p, [h_[0] for h_ in hall], [h_[1] for h_ in hall], NLAT, NCTX, W("w_mix"), W("b_mix"), cols, S["zlat"][0], S["zctx"][0], S["zlat"][1])
        zl, zc, zr = S["zlat"][0], S["zctx"][0], S["zlat"][1]
        cvl, cvc = S["cvl"][0], S["cvc"][0]
        cvr = S["cvl"][1]
        jl = [(zl[0], zr, W("hy_cw"), W("hy_cb"), 0, False, cvl[0], cvr), (zl[1], zr, W("hy_cw"), W("hy_cb"), 128, False, cvl[1], cvr),
              (zl[2], zr, W("hy_cw"), W("hy_cb"), 256, False, cvl[2], cvr), (zl[4], zr, W("ml_cw"), W("ml_cb"), 0, True, cvl[3], cvr),
              (zl[5], zr, W("ml_cw"), W("ml_cb"), 128, True, cvl[4], cvr)]
        p.label = 'conv_stage'; conv_stage(p, jl, 128, 64)
        jc = [(zc[4], zr, W("ml_cw"), W("ml_cb"), 0, True, cvc[3], cvr), (zc[5], zr, W("ml_cw"), W("ml_cb"), 128, True, cvc[4], cvr)]
        if not last:
            jc += [(zc[0], zr, W("hy_cw"), W("hy_cb"), 0, False, cvc[0], cvr), (zc[1], zr, W("hy_cw"), W("hy_cb"), 128, False, cvc[1], cvr),
                   (zc[2], zr, W("hy_cw"), W("hy_cb"), 256, False, cvc[2], cvr)]
        p.label = 'conv_stage'; conv_stage(p, jc, 1, 256)
        variants = [("l", LLAT, cvl, "featsT_l", "trow_l")] + ([] if last else [("c", LCTX, cvc, "featsT_c", "trow_c")])
        for (sfx, Lx, cvx, fnm, tnm) in variants:
            filt, fr = S["f" + sfx]
            src1 = {"v": (cvx[0], cvr), "kf": (filt[0], fr), "kg": (filt[1], fr)}
            p.label = 'hconv_stage'; hconv_stage(p, tabs, src1, S["c1" + sfx][0], S["c1" + sfx][1], 128, Lx, True)
            src2 = {"v": (cvx[0], cvr), "kf": (filt[2], fr), "kg": (filt[3], fr), "x1": (cvx[1], cvr),
                    "c1": S["c1" + sfx], "skip": (W("skip")[0], p.res())}
            p.label = 'hconv_stage'; hconv_stage(p, tabs, src2, S["c2" + sfx][0], S["c2" + sfx][1], 128, Lx, False, u_out=S["zz" + sfx][0], u_res=S["zz" + sfx][1])
            zsrc = zl if sfx == "l" else zc
            p.label = 'fnet_stage'
            if sfx == "l":
                fnet_stage(p, tabs, zsrc[3], zr, S["fn" + sfx][0], S["fn" + sfx][1], Lx)
            else:
                fnet_ctx_stage(p, tabs, zsrc[3], zr, S["fn" + sfx][0], S["fn" + sfx][1])
        msrc = {"qT": (cvl[3], cvc[3], cvr), "kT": (cvl[4], cvc[4], cvr), "vT": (zl[6], zc[6], zr), "oT": (zl[7], zc[7], zr),
                "g": (zl[8][0:4], zc[8][0:4], zr)}
        p.label = 'mlstm_stage'; mlstm_stage(p, msrc, S["mll"][0], S["mlc"][0], S["mll"][1], W("mlnw"), 0)
        S["mlc"] = (S["mlc"][0], S["mll"][1])
        ys = {"x2": (cvl[2], cvc[2], cvr)}
        for nm, key in (("c2", "c2"), ("z", "zz"), ("fn", "fn"), ("ml", "ml")):
            ys[nm] = (S[key + "l"][0], S[key + "c"][0], S[key + "l"][1])
        p.label = 'yasm_stage'; yasm_stage(p, ys, W("skip")[1], y_own_fn, None, LLAT, LCTX if not last else 0)
        for br in range(3):
            for ck in range(len(YCH)):
                p.coll("AllGather", yall[br][ck][0], yown[br][ck][0], GROUPS, r=[yown[br][ck][1]], w=[yall[br][ck][1]])
        p.label = 'merge_stage'; merge_stage(p, y_all_fn, None, [h_[0] for h_ in hown], [h_[1] for h_ in hown], W("w_gate_in"), W("b_gate_in"),
                                             I["oh"], xT, xres, NLAT, NCTX, modv, modr, W("wbr"), W("wout"), not last)
        if last:
            p.label = 'moe_stage'; moe_stage(p, xT, xres, NLAT, NCTX, modv, modr, W("n2w"), W("wr"), W("br"), W("wg"), W("wu"), W("wd"), False,
                      I["norm_f"], outT, out_res)
        else:
            p.label = 'moe_stage'; moe_stage(p, xT, xres, NLAT, NCTX, modv, modr, W("n2w"), W("wr"), W("br"), W("wg"), W("wu"), W("wd"), True)
    return p.finish(), p


_CACHE = {}


def _consts():
    c = {}
    c.update(hconv_tables()); c.update(fnet_tables())
    fl, tl = hfilt_consts(LLAT); fc, tc = hfilt_consts(LCTX)
    c["featsT_l"] = fl; c["trow_l"] = tl; c["featsT_c"] = fc; c["trow_c"] = tc
    return c


def kernel(x, c, ctx, c_ctx, ada_w, ada_b, norm1_w, norm2_w, w_in, b_in,
           hy_conv_w, hy_conv_b, hy_f_w1, hy_f_b1, hy_f_w2, hy_f_b2, hy_f_w3, hy_f_freq,
           hy_decay, hy_skip, ml_conv_w, ml_conv_b, ml_norm_w, w_branch, w_out,
           moe_rg_w, moe_rg_b, moe_re_w, moe_re_b, moe_w_gate, moe_w_up, moe_w_down, norm_f_w):
    f32 = lambda a: np.ascontiguousarray(np.asarray(a, dtype=np.float32))
    x, c, ctx, c_ctx = f32(x), f32(c), f32(ctx), f32(c_ctx)
    if "nc" not in _CACHE:
        _CACHE["const"] = _consts()
        _CACHE["nc"] = build_program(_CACHE["const"])[0]
    const = _CACHE["const"]
    nc = _CACHE["nc"]
    in_maps = []
    for core in range(8):
        b, j = core // 4, core % 4
        m = dict(const)
        m["xT0"] = f32(x[b, NLAT * j:NLAT * (j + 1), :].T)
        m["cT0"] = f32(ctx[b, NCTX * j:NCTX * (j + 1), :].T)
        m["cv"] = f32(np.stack([c[b], c_ctx], axis=1))
        oh = np.zeros((128, 4), np.float32); oh[:, j] = 1.0
        m["oh"] = oh
        m["norm_f"] = f32(norm_f_w)
        sl = slice(128 * j, 128 * j + 128)
        for l in range(DEPTH):
            L = "_%d" % l
            m["ada_w" + L] = f32(ada_w[l][:, 1536 * j:1536 * (j + 1)]); m["ada_b" + L] = f32(ada_b[l][1536 * j:1536 * (j + 1)])
            m["n1w" + L] = f32(norm1_w[l]); m["n2w" + L] = f32(norm2_w[l])
            m["w_gate_in" + L] = f32(w_in[l][:, OFF_GATE:]); m["b_gate_in" + L] = f32(b_in[l][OFF_GATE:])
            mixcols = np.concatenate([np.arange(128 * j, 128 * j + 128) + o for o in
                                      (0, 512, 1024, OFF_FN, OFF_ML, OFF_ML + 512, OFF_ML + 1024, OFF_ML + 1536)]
                                     + [np.array([OFF_MLG + j, OFF_MLG + 4 + j, OFF_MLG + 8 + j, OFF_MLG + 12 + j])])
            m["w_mix" + L] = f32(w_in[l][:, mixcols]); m["b_mix" + L] = f32(b_in[l][mixcols])
            hyc = np.concatenate([np.arange(128 * j, 128 * j + 128) + o for o in (0, 512, 1024)])
            m["hy_cw" + L] = f32(hy_conv_w[l][:, :, hyc]); m["hy_cb" + L] = f32(hy_conv_b[l][hyc])
            mlc = np.concatenate([np.arange(128 * j, 128 * j + 128) + o for o in (0, 512)])
            m["ml_cw" + L] = f32(ml_conv_w[l][:, :, mlc]); m["ml_cb" + L] = f32(ml_conv_b[l][mlc])
            m["f_w1" + L] = f32(hy_f_w1[l]); m["f_b1" + L] = f32(hy_f_b1[l]); m["f_freq" + L] = f32(hy_f_freq[l])
            m["f_w2" + L] = f32(hy_f_w2[l]); m["f_b2" + L] = f32(hy_f_b2[l])
            m["f_w3" + L] = f32(np.asarray(hy_f_w3[l]).reshape(64, 4, 512)[:, :, sl])
            m["decay" + L] = f32(np.asarray(hy_decay[l]).reshape(4, 512)[:, sl].T)
            m["skip" + L] = f32(hy_skip[l][:, sl]); m["mlnw" + L] = f32(ml_norm_w[l][sl])
            m["wbr" + L] = f32(w_branch[l]); m["wout" + L] = f32(w_out[l])
            m["wr" + L] = f32(np.concatenate([moe_rg_w[l], moe_re_w[l]], axis=1)); m["br" + L] = f32(np.concatenate([moe_rg_b[l], moe_re_b[l]]))
            m["wg" + L] = f32(moe_w_gate[l]); m["wu" + L] = f32(moe_w_up[l]); m["wd" + L] = f32(moe_w_down[l])
        in_maps.append(m)
    res = run_bass_kernel_spmd(nc, in_maps, core_ids=list(range(8)))
    out = np.empty((2, LLAT, 1024), np.float32)
    for core in range(8):
        b, j = core // 4, core % 4
        out[b, NLAT * j:NLAT * (j + 1), :] = np.asarray(res.results[core]["outT"], dtype=np.float32).T
    return out
```

```python
import os
import numpy as np
import ml_dtypes

from contextlib import ExitStack, contextmanager
import concourse.bass as bass
import concourse.mybir as mybir
from concourse.bass_utils import run_bass_kernel_spmd

F32 = mybir.dt.float32
BF16 = mybir.dt.bfloat16
ALU = mybir.AluOpType
AF = mybir.ActivationFunctionType
AX = mybir.AxisListType


class SemCtr:
    __slots__ = ("sem", "count")

    def __init__(self, sem):
        self.sem = sem
        self.count = 0


class Res:
    __slots__ = ("name", "last_w", "readers", "dsem")

    def __init__(self, name):
        self.name = name
        self.last_w = None
        self.readers = {}
        self.dsem = None


class Prog:
    ENG = ("pe", "dve", "act", "pool", "sp")

    def __init__(self, same_engine_sync=True):
        self.nc = bass.Bass("TRN2", target_bir_lowering=False)
        self.st = ExitStack()
        self.stk = [self.st]
        self.q = {e: [] for e in self.ENG}
        self.cnt = {e: 0 for e in self.ENG}
        self.seen = {e: {} for e in self.ENG}
        self.esem = {e: self.st.enter_context(self.nc.semaphore("es_" + e)) for e in self.ENG}
        self.esem_ids = {id(s) for s in self.esem.values()}
        self.own_ids = {e: {id(self.esem[e])} for e in self.ENG}
        self.same = same_engine_sync
        self.dma_events = {}
        self.sem_pool = []
        self.block_log = []
        self.label = ''
        self.scope_res = [[]]
        self.uid = 0
        self.ninst = 0
        self.flush_every = int(os.environ.get("FLUSH_EVERY", "1200"))

    def dram(self, name, shape, dt, kind=None):
        if kind is None:
            t = self.nc.dram_tensor(name, list(shape), dt)
        else:
            t = self.nc.dram_tensor(name, list(shape), dt, kind=kind)
        return t.ap()

    def sbuf(self, name, shape, dt):
        self.uid += 1
        return self.stk[-1].enter_context(self.nc.sbuf_tensor("%s_%d" % (name, self.uid), list(shape), dt))

    def psum(self, name, shape, dt=F32):
        self.uid += 1
        return self.stk[-1].enter_context(self.nc.psum_tensor("%s_%d" % (name, self.uid), list(shape), dt))

    def res(self, name=None):
        self.uid += 1
        r = Res("%s_%d" % (name or "r", self.uid))
        self.scope_res[-1].append(r)
        return r

    def semctr(self):
        while self.sem_pool:
            sc = self.sem_pool.pop()
            if sc.count < 20000:
                return sc
        return SemCtr(self.newsem("ds"))

    def newsem(self, name):
        self.uid += 1
        return self.st.enter_context(self.nc.semaphore("%s_%d" % (name, self.uid)))

    def _waits(self, eng, r, w, dma=False):
        waits = {}

        def need(ev):
            if ev is None:
                return
            s, v = ev
            if (not self.same or eng == "pe") and id(s) in self.own_ids[eng]:
                return
            if self.seen[eng].get(id(s), (None, 0))[1] < v:
                if waits.get(id(s), (None, 0))[1] < v:
                    waits[id(s)] = (s, v)

        for x in r:
            need(x.last_w)
            if dma:
                for ev in x.readers.values():
                    if id(ev[0]) not in self.esem_ids:
                        need(ev)
        for x in w:
            need(x.last_w)
            for ev in x.readers.values():
                need(ev)
        for k, sv in waits.items():
            self.seen[eng][k] = sv
        return list(waits.values())

    def op(self, eng, fn, r=(), w=()):
        waits = self._waits(eng, r, w)
        if self.cnt[eng] >= 20000:
            self.esem[eng] = self.newsem("es_" + eng)
            self.esem_ids.add(id(self.esem[eng]))
            self.own_ids[eng].add(id(self.esem[eng]))
            self.cnt[eng] = 0
        self.cnt[eng] += 1
        ev = (self.esem[eng], self.cnt[eng])
        self.q[eng].append((waits, fn, ev, 1))
        for x in r:
            x.readers[id(ev[0])] = ev
        for x in w:
            x.last_w = ev
            x.readers = {}
        self.ninst += 1
        self._autoflush()
        return ev

    def _autoflush(self):
        if sum(len(v) for v in self.q.values()) >= self.flush_every:
            self.flush()

    def _async(self, eng, fn, inc, r, w):
        dst = w[0]
        waits = self._waits(eng, r, w, dma=True)
        if dst.dsem is None:
            dst.dsem = self.semctr()
        dst.dsem.count += inc
        ev = (dst.dsem.sem, dst.dsem.count)
        self.q[eng].append((waits, fn, ev, inc))
        for x in r:
            x.readers[id(ev[0])] = ev
        dst.last_w = ev
        dst.readers = {}
        self.dma_events[id(ev[0])] = ev
        self.ninst += 1
        self._autoflush()
        return ev

    def dma(self, eng, out, in_, r=(), w=(), slow=False):
        if slow:
            return self._async(eng, lambda e: e.dma_start(out=out, in_=in_, allow_slow_non_contiguous=True), 16, r, w)
        return self._async(eng, lambda e: e.dma_start(out=out, in_=in_), 16, r, w)

    def coll(self, kind, out, in_, groups, r=(), w=()):
        return self._async("pool", lambda e: e.collective_compute(
            kind, ALU.bypass, replica_groups=groups, ins=[in_.opt()], outs=[out.opt()]), 1, r, w)

    def barrier(self):
        for eng in self.ENG:
            waits = []
            evs = [(self.esem[x], self.cnt[x]) for x in self.ENG if x != eng and self.cnt[x] > 0]
            evs += list(self.dma_events.values())
            for s, v in evs:
                if self.seen[eng].get(id(s), (None, 0))[1] < v:
                    waits.append((s, v))
                    self.seen[eng][id(s)] = (s, v)
            if waits:
                self.q[eng].append((waits, None, None, 0))
        self.dma_events = {}

    def flush(self):
        nc = self.nc
        with nc.Block() as block:
            def mk(name):
                def run(e):
                    for waits, fn, ev, inc in self.q[name]:
                        for s, v in waits:
                            e.wait_ge(s, v)
                        if fn is not None:
                            fn(e).then_inc(ev[0], inc)
                return run
            block.sync(mk("sp"))
            block.tensor(mk("pe"))
            block.vector(mk("dve"))
            block.scalar(mk("act"))
            block.gpsimd(mk("pool"))
        self.block_log.append((self.label, sum(len(v) for v in self.q.values())))
        self.q = {e: [] for e in self.ENG}

    @contextmanager
    def scope(self):
        st = ExitStack()
        self.stk.append(st)
        self.scope_res.append([])
        try:
            yield
            self.barrier()
            self.flush()
        finally:
            self.stk.pop()
            st.close()
            for r in self.scope_res.pop():
                if r.dsem is not None:
                    self.sem_pool.append(r.dsem)
                    r.dsem = None

    def finish(self):
        self.barrier()
        self.flush()
        self.st.close()
        return self.nc

    def mm(self, out, lhsT, rhs, start, stop, r, w):
        return self.op("pe", lambda e: e.matmul(out, lhsT=lhsT, rhs=rhs, start=start, stop=stop), r, w)

    def tr(self, out, in_, ident, r, w):
        return self.op("pe", lambda e: e.transpose(out, in_, ident), r, w)

    def act(self, out, in_, func, r, w, bias=0.0, scale=1.0, accum_out=None):
        if accum_out is None:
            return self.op("act", lambda e: e.activation(out=out, in_=in_, func=func, bias=bias, scale=scale), r, w)
        return self.op("act", lambda e: e.activation(out=out, in_=in_, func=func, bias=bias, scale=scale, accum_out=accum_out), r, w)

    def tt(self, eng, out, a, b, op, r, w):
        return self.op(eng, lambda e: e.tensor_tensor(out=out, in0=a, in1=b, op=op), r, w)

    def ts(self, eng, out, a, s1, op0, r, w, s2=None, op1=None):
        if op1 is None:
            return self.op(eng, lambda e: e.tensor_scalar(out=out, in0=a, scalar1=s1, scalar2=None, op0=op0), r, w)
        return self.op(eng, lambda e: e.tensor_scalar(out=out, in0=a, scalar1=s1, scalar2=s2, op0=op0, op1=op1), r, w)

    def stt(self, out, in0, scalar, in1, op0, op1, r, w):
        return self.op("dve", lambda e: e.scalar_tensor_tensor(out=out, in0=in0, scalar=scalar, in1=in1, op0=op0, op1=op1), r, w)

    def cp(self, eng, out, in_, r, w):
        if eng == "act":
            return self.op("act", lambda e: e.copy(out=out, in_=in_), r, w)
        return self.op(eng, lambda e: e.tensor_copy(out=out, in_=in_), r, w)

    def memset(self, eng, out, val, w):
        return self.op(eng, lambda e: e.memset(out, val), (), w)


def make_ident(p, dt=BF16, name="ident"):
    idf = p.sbuf(name + "f", [128, 128], F32); r = p.res(name)
    p.memset("dve", idf[:], 0.0, w=[r])
    p.op("pool", lambda e: e.affine_select(out=idf[:], in_=idf[:], pattern=[[-1, 128]], compare_op=ALU.not_equal,
                                           fill=1.0, base=0, channel_multiplier=1), r=[r], w=[r])
    if dt == F32:
        return idf, r
    idb = p.sbuf(name + "b", [128, 128], dt)
    p.cp("dve", idb[:], idf[:], r=[r], w=[r])
    return idb, r


def conv_stage(p, jobs, R, W):
    L = R * W
    PAD = W + 1
    CW = min(512, L)
    with p.scope():
        ident, idr = make_ident(p)
        stgs = [p.sbuf("stg%d" % i, [128, L], F32) for i in range(2)]; stgrs = [p.res("stg%d" % i) for i in range(2)]
        outts = [p.sbuf("outt%d" % i, [128, L], F32) for i in range(2)]; outrs = [p.res("outt%d" % i) for i in range(2)]
        nver = 3 if R > 1 else 1
        P = [p.sbuf("P%d" % i, [128, L + 2 * PAD], BF16) for i in range(nver)]
        Pr = [p.res("P%d" % i) for i in range(nver)]
        for i in range(nver):
            p.memset("pool", P[i][:, 0:PAD], 0.0, w=[Pr[i]])
            p.memset("pool", P[i][:, PAD + L:], 0.0, w=[Pr[i]])
        wsb = p.sbuf("wsb", [128, 9], F32); bsb = p.sbuf("bsb", [128, 1], F32); wr = p.res("wsb")
        D = p.sbuf("D", [128, 9, 128], BF16); Dr = p.res("D")
        ps = [p.psum("ps%d" % i, [128, 512]) for i in range(4)]; psr = [p.res("ps%d" % i) for i in range(4)]
        k = 0
        for ji, (src, srcr, w_ap, b_ap, c0, silu, dst, dstr) in enumerate(jobs):
            stg = stgs[ji % 2]; stgr = stgrs[ji % 2]; outt = outts[ji % 2]; outr = outrs[ji % 2]
            p.dma("sp" if ji % 2 else "pool", stg[:], src, r=[srcr], w=[stgr])
            p.dma("sp", wsb[:], w_ap.rearrange("a b c -> c (a b)")[c0:c0 + 128, :], w=[wr], slow=True)
            p.dma("sp", bsb[:], b_ap.rearrange("(a b) -> a b", b=1)[c0:c0 + 128, :], w=[wr])
            p.cp("act", P[0][:, PAD:PAD + L], stg[:], r=[stgr], w=[Pr[0]])
            if nver == 3:
                p.cp("act", P[1][:, PAD:PAD + L], stg[:], r=[stgr], w=[Pr[1]])
                p.cp("dve", P[2][:, PAD:PAD + L], stg[:], r=[stgr], w=[Pr[2]])
                v1 = P[1][:, PAD:PAD + L].rearrange("p (r w) -> p r w", w=W)
                v2 = P[2][:, PAD:PAD + L].rearrange("p (r w) -> p r w", w=W)
                p.memset("dve", v1[:, :, W - 1:W], 0.0, w=[Pr[1]])
                p.memset("dve", v2[:, :, 0:1], 0.0, w=[Pr[2]])
            for tap in range(9):
                p.ts("dve", D[:, tap, :], ident[:], wsb[:, tap:tap + 1], ALU.mult, r=[idr, wr], w=[Dr])
            taps = [(dy, dx) for dy in (-1, 0, 1) for dx in (-1, 0, 1) if (R > 1 or dy == 0)]
            for c in range(L // CW):
                h = k % 4; k += 1
                for ti, (dy, dx) in enumerate(taps):
                    ver = 0 if nver == 1 else (1 if dx == -1 else (2 if dx == 1 else 0))
                    o = PAD + c * CW + W * dy + dx
                    p.mm(ps[h][:, 0:CW], D[:, (dy + 1) * 3 + dx + 1, :], P[ver][:, o:o + CW], ti == 0, ti == len(taps) - 1,
                         r=[Dr, Pr[ver]], w=[psr[h]])
                p.act(outt[:, c * CW:(c + 1) * CW], ps[h][:, 0:CW], AF.Silu if silu else AF.Identity,
                      r=[psr[h], wr], w=[outr], bias=bsb[:, 0:1])
            p.dma("pool" if ji % 2 else "sp", dst, outt[:], r=[outr], w=[dstr])


HY_BANDS = 16


def hfilt_consts(L):
    t = np.arange(L, dtype=np.float64) / L
    bands = np.linspace(1e-4, HY_BANDS - 1, HY_BANDS)
    ang = 2 * np.pi * t[:, None] * bands[None, :]
    feats = np.concatenate([t[:, None], np.cos(ang), np.sin(ang)], axis=-1)
    return np.ascontiguousarray(feats.T).astype(np.float32), t.astype(np.float32)


def hfilt_stage(p, featsT, trow, w, j, L, dst, dst_res):
    CW = min(512, L)
    NCH = L // CW
    TWO_PI = 2 * np.pi
    with p.scope():
        wr = p.res("w")
        ft = p.sbuf("ft", [33, L], F32); tb = p.sbuf("tb", [128, L], F32)
        w1 = p.sbuf("w1", [33, 64], F32); w2 = p.sbuf("w2", [64, 64], F32); w3 = p.sbuf("w3", [64, 4, 128], F32)
        sc = p.sbuf("sc", [64, 4], F32)
        dc = p.sbuf("dc", [128, 4], F32)
        p.dma("sp", ft[:], featsT, w=[wr]); p.dma("sp", tb[:], trow.partition_broadcast(128), w=[wr])
        p.dma("sp", w1[:], w["f_w1"], w=[wr]); p.dma("sp", w2[:], w["f_w2"], w=[wr])
        p.dma("sp", w3[:], w["f_w3"], w=[wr])
        p.dma("sp", sc[:, 0:1], w["f_b1"].rearrange("(a b) -> a b", b=1), w=[wr])
        p.dma("sp", sc[:, 1:2], w["f_freq"].rearrange("(a b) -> a b", b=1), w=[wr])
        p.dma("sp", sc[:, 2:3], w["f_b2"].rearrange("(a b) -> a b", b=1), w=[wr])
        p.dma("sp", dc[:], w["decay"], w=[wr])
        p.act(dc[:], dc[:], AF.Abs, r=[wr], w=[wr])
        p.ts("dve", dc[:], dc[:], -1.0, ALU.mult, r=[wr], w=[wr])
        h1f = p.sbuf("h1", [128, L], F32); h1 = h1f[0:64]; h2 = p.sbuf("h2", [64, L], F32)
        h1r, h2r = p.res("h1"), p.res("h2")
        ps = [p.psum("ps%d" % i, [128, 512]) for i in range(2)]; psr = [p.res("ps%d" % i) for i in range(2)]
        tmp = p.sbuf("tmp", [64, 512], F32); tmpr = p.res("tmp")
        tm1 = p.sbuf("tm1", [64, 512], F32); tm1r = p.res("tm1"); tm2 = p.sbuf("tm2", [64, 512], F32); tm2r = p.res("tm2")
        k = 0
        for (src, srcr, lw, bcol, dsth, dstr) in ((ft[:], wr, w1, 0, h1, h1r), (h1, h1r, w2, 2, h2[:], h2r)):
            for c in range(NCH):
                cs = slice(c * CW, (c + 1) * CW)
                h = k % 2; k += 1
                p.mm(ps[h][0:64, 0:CW], lw[:], src[:, cs], True, True, r=[wr, srcr], w=[psr[h]])
                p.ts("dve", tmp[:, 0:CW], ps[h][0:64, 0:CW], sc[:, bcol:bcol + 1], ALU.add, r=[psr[h], wr], w=[tmpr],
                     s2=sc[:, 1:2], op1=ALU.mult)
                tv = tmp[:, 0:CW]
                p.ts("dve", tm1[:, 0:CW], tv, float(np.pi), ALU.is_gt, r=[tmpr], w=[tm1r], s2=-TWO_PI, op1=ALU.mult)
                p.ts("dve", tm2[:, 0:CW], tv, float(-np.pi), ALU.is_lt, r=[tmpr], w=[tm2r], s2=TWO_PI, op1=ALU.mult)
                p.tt("dve", tv, tv, tm1[:, 0:CW], ALU.add, r=[tmpr, tm1r], w=[tmpr])
                p.tt("dve", tv, tv, tm2[:, 0:CW], ALU.add, r=[tmpr, tm2r], w=[tmpr])
                p.ts("dve", tv, tv, float(np.pi), ALU.min, r=[tmpr], w=[tmpr], s2=float(-np.pi), op1=ALU.max)
                p.act(dsth[:, cs], tmp[:, 0:CW], AF.Sin, r=[tmpr], w=[dstr])
        F = [p.sbuf("F%d" % i, [128, L], F32) for i in range(2)]; Fr = [p.res("F%d" % i) for i in range(2)]
        wnd = p.sbuf("wnd", [128, 512], F32); wndr = p.res("wnd")
        ss = p.sbuf("ss", [128, 4], F32); ssr = p.res("ss")
        junk = h1f; junkr = h1r
        for o in range(2):
            for d in range(2):
                od = o * 2 + d
                for c in range(NCH):
                    cs = slice(c * CW, (c + 1) * CW)
                    h = k % 2; k += 1
                    p.mm(ps[h][:, 0:CW], w3[:, od, :], h2[:, cs], True, True, r=[wr, h2r], w=[psr[h]])
                    p.act(wnd[:, 0:CW], tb[:, cs], AF.Exp, r=[wr], w=[wndr], scale=dc[:, od:od + 1])
                    p.tt("dve", F[d][:, cs], ps[h][:, 0:CW], wnd[:, 0:CW], ALU.mult, r=[psr[h], wndr], w=[Fr[d]])
                if d == 1:
                    p.memset("dve", F[d][:, 0:1], 0.0, w=[Fr[d]])
                p.act(junk[:], F[d][:], AF.Square, r=[Fr[d]], w=[junkr], accum_out=ss[:, od:od + 1])
                ssr.last_w = junkr.last_w
            tot = ss[:, 2 * o:2 * o + 1]
            p.tt("dve", tot, tot, ss[:, 2 * o + 1:2 * o + 2], ALU.add, r=[ssr], w=[ssr])
            p.act(tot, tot, AF.Sqrt, r=[ssr], w=[ssr], bias=1e-6)
            p.op("dve", lambda e, t_=tot: e.reciprocal(out=t_, in_=t_), r=[ssr], w=[ssr])
            for d in range(2):
                p.ts("dve", F[d][:], F[d][:], tot, ALU.mult, r=[Fr[d], ssr], w=[Fr[d]])
                p.dma("sp", dst[o * 2 + d], F[d][:], r=[Fr[d]], w=[dst_res])


def hconv_tables():
    a = np.arange(128, dtype=np.float64)
    th128 = 2 * np.pi * np.outer(a, a) / 128.0
    C = np.cos(th128); S = np.sin(th128)
    bf = lambda x: np.ascontiguousarray(x.astype(np.float32)).astype(ml_dtypes.bfloat16)
    t = {}
    t["hc_sq"] = bf(np.stack([C, -C, S, -S], axis=1))
    t["hc_wab"] = bf(np.stack([np.concatenate([C, S], 1), np.concatenate([-C, -S], 1),
                                np.concatenate([-S, C], 1)], axis=1))
    for pre, N1 in (("hc", 128), ("hcc", 4)):
        NF = 128 * N1
        f1 = np.arange(N1, dtype=np.float64)
        n1 = np.arange(64, dtype=np.float64)
        th1 = 2 * np.pi * np.outer(n1, f1) / N1
        t[pre + "_w1"] = bf(np.concatenate([np.cos(th1), -np.sin(th1)], axis=1))
        thN = 2 * np.pi * np.outer(a, f1) / NF
        t[pre + "_tw"] = np.stack([np.cos(thN), -np.sin(thN)], axis=1).astype(np.float32)
        thN2 = 2 * np.pi * np.outer(a, a) / NF
        t[pre + "_tw2"] = np.stack([np.cos(thN2), -np.sin(thN2)], axis=1).astype(np.float32)
        m1 = np.arange(64, dtype=np.float64)
        thI = 2 * np.pi * np.outer(a, m1) / N1
        t[pre + "_i2"] = bf(np.stack([np.cos(thI), np.sin(thI), -np.sin(thI)], axis=1))
    return t


def hconv_stage(p, tabs, srcs, dst, dst_res, CH, L_in, first, G=32, u_out=None, u_res=None):
    NP = L_in // 128
    N1 = 128 if L_in == 8192 else 4
    NFFT = 128 * N1
    pre = "hc" if N1 == 128 else "hcc"
    with p.scope():
        w1 = p.sbuf("w1", [64, 2 * N1], BF16); sq = p.sbuf("sq", [128, 4, 128], BF16)
        wab = p.sbuf("wab", [128, 3, 256], BF16); i2 = p.sbuf("i2", [128, 3, 64], BF16)
        tw = p.sbuf("tw", [128, 2, N1], F32); tw2 = p.sbuf("tw2", [128, 2, 128], F32)
        tr = p.res("tabs")
        for sb_, nm in ((w1, pre + "_w1"), (sq, "hc_sq"), (wab, "hc_wab"), (i2, pre + "_i2"), (tw, pre + "_tw"), (tw2, pre + "_tw2")):
            p.dma("sp", sb_[:], tabs[nm], w=[tr])
        PS = p.psum("ps", [128, 4096])
        PH = [PS[:, 0:2048], PS[:, 2048:4096]]
        phr = [p.res("psA"), p.res("psB")]
        stg = [p.sbuf("stg%d" % i, [64, G, 128], F32) for i in range(3)]
        stgr = [p.res("stg%d" % i) for i in range(3)]
        skb = p.sbuf("skb", [64, G], F32); skr = p.res("skb")
        XB = [p.sbuf("X%d" % i, [64, G, 128], BF16) for i in range(2)]; xrB = [p.res("X%d" % i) for i in range(2)]
        SW = 4 if N1 == 128 else G
        NS4 = G // SW
        NSI = G // 4
        T = [p.sbuf("T%d" % i, [128, G, N1], BF16) for i in range(4)]
        TI = T if N1 == 128 else [p.sbuf("TI%d" % i, [N1, G, 128], BF16) for i in range(4)]
        TrS = [[p.res("T%d_%d" % (i, s)) for s in range(NS4)] for i in range(4)]
        TIrS = TrS if N1 == 128 else [[p.res("TI%d_%d" % (i, s)) for s in range(NSI)] for i in range(4)]
        Kr = p.sbuf("Kr", [128, G, N1], F32); Ki = p.sbuf("Ki", [128, G, N1], F32)
        kresS = [p.res("K%d" % s) for s in range(NS4)]
        Q = [p.sbuf("Q%d" % i, [128, G, N1], BF16) for i in range(4)]
        QrS = [[p.res("Q%d_%d" % (i, s)) for s in range(NS4)] for i in range(4)]
        OUT2 = p.sbuf("OUT2", [64, G, 128], F32); outr = p.res("OUT2")
        Trb = tw[:, 0:1, :].to_broadcast([128, SW, N1])
        Tib = tw[:, 1:2, :].to_broadcast([128, SW, N1])
        Trb2 = tw2[0:N1, 0:1, :].to_broadcast([N1, 4, 128])
        Tib2 = tw2[0:N1, 1:2, :].to_broadcast([N1, 4, 128])
        PF = [PS[:, 0:1024], PS[:, 1024:2048]]; pfr = [p.res("pfA"), p.res("pfB")]
        PX = [PS[:, 2048:3072], PS[:, 3072:4096]]; pxr = [p.res("pxA"), p.res("pxB")]
        cnt = {"f": 0, "x": 0, "i": 0}

        def t2d(ap, c0):
            return ap[c0:c0 + G, :].rearrange("g (a b) -> a g b", b=128)

        def load_u(c0, which, xi):
            X = XB[xi]; xr = xrB[xi]
            if NP < 64:
                p.memset("pool", X[:], 0.0, w=[xr])
            if which != "u" or first:
                ap, rs = srcs["v" if which == "u" else which]
                p.dma("sp" if xi else "pool", stg[xi][0:NP], t2d(ap, c0), r=[rs], w=[stgr[xi]])
                p.cp("act", X[0:NP], stg[xi][0:NP], r=[stgr[xi]], w=[xr])
            else:
                for i, nm in enumerate(("v", "c1", "x1")):
                    ap, rs = srcs[nm]
                    p.dma("sp" if i % 2 else "pool", stg[i][0:NP], t2d(ap, c0), r=[rs], w=[stgr[i]])
                ap, rs = srcs["skip"]
                p.dma("sp", skb[0:NP], ap[c0:c0 + G].partition_broadcast(NP), r=[rs], w=[skr])
                p.tt("pool", stg[0][0:NP], stg[0][0:NP], skb[0:NP, :, None].to_broadcast([NP, G, 128]), ALU.mult,
                     r=[stgr[0], skr], w=[stgr[0]])
                p.tt("pool", stg[0][0:NP], stg[0][0:NP], stg[1][0:NP], ALU.add, r=[stgr[0], stgr[1]], w=[stgr[0]])
                p.tt("pool", stg[0][0:NP], stg[0][0:NP], stg[2][0:NP], ALU.mult, r=[stgr[0], stgr[2]], w=[stgr[0]])
                p.cp("act", X[0:NP], stg[0][0:NP], r=[stgr[0]], w=[xr])
                if u_out is not None:
                    p.dma("pool", t2d(u_out, c0), stg[0][0:NP], r=[stgr[0]], w=[u_res])

        Cm, Cn, Sm, Sn = (sq[:, k, :] for k in range(4))

        def f1_tw(s4, xi):
            X = XB[xi]; xr = xrB[xi]
            h = cnt["f"] % 2; cnt["f"] += 1
            pv = PF[h][:, 0:SW * 2 * N1].rearrange("p (c k) -> p c k", k=2 * N1)
            for c in range(SW):
                p.mm(pv[:, c, :], X[:, s4 * SW + c, :], w1[:], True, True, r=[xr, tr], w=[pfr[h]])
            Ar = pv[:, :, 0:N1]; Ai = pv[:, :, N1:2 * N1]
            gs = slice(s4 * SW, s4 * SW + SW)
            p.tt("dve", T[0][:, gs, :], Ar, Trb, ALU.mult, r=[pfr[h], tr], w=[TrS[0][s4]])
            p.tt("dve", T[1][:, gs, :], Ai, Tib, ALU.mult, r=[pfr[h], tr], w=[TrS[1][s4]])
            p.tt("dve", T[2][:, gs, :], Ar, Tib, ALU.mult, r=[pfr[h], tr], w=[TrS[2][s4]])
            p.tt("dve", T[3][:, gs, :], Ai, Trb, ALU.mult, r=[pfr[h], tr], w=[TrS[3][s4]])

        def f2_post(s4, which):
            h = cnt["x"] % 2; cnt["x"] += 1
            gs = slice(s4 * SW, s4 * SW + SW)
            xr_ps = PX[h][:, 0:SW * N1]; xi_ps = PX[h][:, 512:512 + SW * N1]
            rr = [tr] + [TrS[k][s4] for k in range(4)]
            p.mm(xr_ps, Cm, T[0][:, gs, :], True, False, r=rr, w=[pxr[h]])
            p.mm(xr_ps, Cn, T[1][:, gs, :], False, False, r=rr, w=[pxr[h]])
            p.mm(xr_ps, Sm, T[2][:, gs, :], False, False, r=rr, w=[pxr[h]])
            p.mm(xr_ps, Sm, T[3][:, gs, :], False, True, r=rr, w=[pxr[h]])
            p.mm(xi_ps, Cm, T[2][:, gs, :], True, False, r=rr, w=[pxr[h]])
            p.mm(xi_ps, Cm, T[3][:, gs, :], False, False, r=rr, w=[pxr[h]])
            p.mm(xi_ps, Sn, T[0][:, gs, :], False, False, r=rr, w=[pxr[h]])
            p.mm(xi_ps, Sm, T[1][:, gs, :], False, True, r=rr, w=[pxr[h]])
            xr3 = xr_ps.rearrange("p (c k) -> p c k", k=N1); xi3 = xi_ps.rearrange("p (c k) -> p c k", k=N1)
            kres = kresS[s4]
            if which == "kf":
                p.cp("act", Kr[:, gs, :], xr3, r=[pxr[h]], w=[kres])
                p.cp("act", Ki[:, gs, :], xi3, r=[pxr[h]], w=[kres])
            elif which == "kg":
                p.tt("dve", Kr[:, gs, :], xr3, Kr[:, gs, :], ALU.add, r=[pxr[h], kres], w=[kres])
                p.stt(Ki[:, gs, :], xi3, -1.0, Ki[:, gs, :], ALU.mult, ALU.add, r=[pxr[h], kres], w=[kres])
            else:
                p.tt("dve", Q[0][:, gs, :], xr3, Kr[:, gs, :], ALU.mult, r=[pxr[h], kres], w=[QrS[0][s4]])
                p.tt("dve", Q[1][:, gs, :], xi3, Ki[:, gs, :], ALU.mult, r=[pxr[h], kres], w=[QrS[1][s4]])
                p.tt("dve", Q[2][:, gs, :], xr3, Ki[:, gs, :], ALU.mult, r=[pxr[h], kres], w=[QrS[2][s4]])
                p.tt("dve", Q[3][:, gs, :], xi3, Kr[:, gs, :], ALU.mult, r=[pxr[h], kres], w=[QrS[3][s4]])

        def fwd(which, xi):
            for s4 in range(NS4 + 1):
                if s4 < NS4:
                    f1_tw(s4, xi)
                if s4 >= 1:
                    f2_post(s4 - 1, which)

        def inv(c0):
            Wa, Wan, Wb = (wab[:, k, :] for k in range(3))
            for s4 in range(NSI + 1):
                if s4 >= 1:
                    i2_step(s4 - 1, c0)
                if s4 == NSI:
                    break
                h = cnt["f"] % 2; cnt["f"] += 1
                pv = PF[h][0:N1, :].rearrange("p (c k) -> p c k", k=256)
                rr = [tr] + [QrS[k][(s4 * 4) // SW] for k in range(4)]
                for c in range(4):
                    g = s4 * 4 + c
                    p.mm(pv[:, c, :], Q[0][:, g, :], Wa, True, False, r=rr, w=[pfr[h]])
                    p.mm(pv[:, c, :], Q[1][:, g, :], Wan, False, False, r=rr, w=[pfr[h]])
                    p.mm(pv[:, c, :], Q[2][:, g, :], Wb, False, False, r=rr, w=[pfr[h]])
                    p.mm(pv[:, c, :], Q[3][:, g, :], Wb, False, True, r=rr, w=[pfr[h]])
                Br = pv[:, :, 0:128]; Bi = pv[:, :, 128:256]
                gs = slice(s4 * 4, s4 * 4 + 4)
                p.tt("dve", TI[0][0:N1, gs, :], Br, Trb2, ALU.mult, r=[pfr[h], tr], w=[TIrS[0][s4]])
                p.tt("dve", TI[1][0:N1, gs, :], Bi, Tib2, ALU.mult, r=[pfr[h], tr], w=[TIrS[1][s4]])
                p.tt("dve", TI[2][0:N1, gs, :], Br, Tib2, ALU.mult, r=[pfr[h], tr], w=[TIrS[2][s4]])
                p.tt("dve", TI[3][0:N1, gs, :], Bi, Trb2, ALU.mult, r=[pfr[h], tr], w=[TIrS[3][s4]])
            p.dma("sp", t2d(dst, c0), OUT2[0:NP], r=[outr], w=[dst_res])

        CI, SI, SIn = (i2[:, k, :] for k in range(3))

        def i2_step(s4, c0):
            h = cnt["x"] % 2; cnt["x"] += 1
            gs = slice(s4 * 4, s4 * 4 + 4)
            o_ps = PX[h][0:NP, 0:512]
            rr = [tr] + [TIrS[k][s4] for k in range(4)]
            p.mm(o_ps, CI[0:N1, 0:NP], TI[0][0:N1, gs, :], True, False, r=rr, w=[pxr[h]])
            p.mm(o_ps, CI[0:N1, 0:NP], TI[1][0:N1, gs, :], False, False, r=rr, w=[pxr[h]])
            p.mm(o_ps, SI[0:N1, 0:NP], TI[2][0:N1, gs, :], False, False, r=rr, w=[pxr[h]])
            p.mm(o_ps, SIn[0:N1, 0:NP], TI[3][0:N1, gs, :], False, True, r=rr, w=[pxr[h]])
            p.act(OUT2[0:NP, gs, :], o_ps.rearrange("p (c k) -> p c k", k=128), AF.Copy, r=[pxr[h]], w=[outr], scale=1.0 / NFFT)

        items = [(c0, which) for c0 in range(0, CH, G) for which in ("kf", "kg", "u")]
        load_u(items[0][0], items[0][1], 0)
        for i, (c0, which) in enumerate(items):
            if i + 1 < len(items):
                load_u(items[i + 1][0], items[i + 1][1], (i + 1) % 2)
            fwd(which, i % 2)
            if which == "u":
                inv(c0)


LF = 8192


def fnet_tables():
    a = np.arange(128, dtype=np.float64)
    th = 2 * np.pi * np.outer(a, a) / 128.0
    C = np.cos(th); S = np.sin(th)
    b = np.arange(64, dtype=np.float64)
    th64 = 2 * np.pi * np.outer(b, b) / 64.0
    C64 = np.cos(th64); S64 = np.sin(th64)
    thL = 2 * np.pi * np.outer(b, a) / LF
    bf = lambda x: np.ascontiguousarray(x.astype(np.float32)).astype(ml_dtypes.bfloat16)
    return {"fn_w": bf(np.stack([np.concatenate([C, -S], 1), np.concatenate([S, C], 1)], axis=1)),
            "fn_64": bf(np.stack([C64, -C64, S64], axis=1)),
            "fn_tw": np.stack([np.cos(thL), -np.sin(thL)], axis=1).astype(np.float32),
            "fn_256": bf(np.stack([np.stack([np.cos(2 * np.pi * np.outer(np.arange(128 * b_, 128 * b_ + 128), np.arange(256)) / 256.0),
                                             np.sin(2 * np.pi * np.outer(np.arange(128 * b_, 128 * b_ + 128), np.arange(256)) / 256.0)], axis=1)
                                   for b_ in range(2)], axis=1))}


def fnet_stage(p, tabs, src, src_res, dst, dst_res, L_in):
    scale = 1.0 / np.sqrt(L_in * 128.0)
    f1_list = list(range(128)) if L_in == LF else [0, 32, 64, 96]
    with p.scope():
        wt = p.sbuf("wt", [128, 2, 256], BF16); t64 = p.sbuf("t64", [64, 3, 64], BF16); tw = p.sbuf("tw", [64, 2, 128], F32)
        tr = p.res("tabs")
        p.dma("sp", wt[:], tabs["fn_w"], w=[tr]); p.dma("sp", t64[:], tabs["fn_64"], w=[tr]); p.dma("sp", tw[:], tabs["fn_tw"], w=[tr])
        stg = p.sbuf("stg", [128, LF], F32); stgr = p.res("stg")
        ub = p.sbuf("ub", [128, LF], BF16); ubr = p.res("ub")
        V = p.sbuf("V", [128, 64, 256], BF16); Vr = p.res("V")
        T = [p.sbuf("T%d" % i, [64, 64, 128], BF16) for i in range(4)]; Tr_ = [p.res("T%d" % i) for i in range(4)]
        PS = p.psum("ps", [128, 4096]); PH = [PS[:, 0:2048], PS[:, 2048:4096]]; phr = [p.res("psA"), p.res("psB")]
        if L_in < LF:
            p.memset("pool", stg[:], 0.0, w=[stgr])
        p.dma("sp", stg[:, 0:L_in], src, r=[src_res], w=[stgr])
        p.cp("act", ub[:], stg[:], r=[stgr], w=[ubr])
        tog = 0
        uv = ub[:].rearrange("p (a b) -> p b a", b=64)
        for s8 in range(8):
            h = tog; tog ^= 1
            pv = PH[h].rearrange("p (c k) -> p c k", k=256)
            for c in range(8):
                n2 = s8 * 8 + c
                p.mm(pv[:, c, :], uv[:, n2, :], wt[:, 0, :], True, True, r=[ubr, tr], w=[phr[h]])
            p.cp("act" if s8 % 2 else "dve", V[:, s8 * 8:(s8 + 1) * 8, :], pv, r=[phr[h]], w=[Vr])
        Trb = tw[:, 0:1, :].to_broadcast([64, 8, 128]); Tib = tw[:, 1:2, :].to_broadcast([64, 8, 128])
        OUT2 = stg[0:64, :].rearrange("p (c k) -> p c k", k=128)
        outr = stgr
        for half in range(2):
            for s8 in range(8):
                h = tog; tog ^= 1
                pv = PH[h][0:64, :].rearrange("p (c k) -> p c k", k=256)
                for c in range(8):
                    ch = half * 64 + s8 * 8 + c
                    p.mm(pv[:, c, :], V[:, :, ch], wt[:, 0, :], True, False, r=[Vr, tr], w=[phr[h]])
                    p.mm(pv[:, c, :], V[:, :, 128 + ch], wt[:, 1, :], False, True, r=[Vr, tr], w=[phr[h]])
                Ar = pv[:, :, 0:128]; Ai = pv[:, :, 128:256]
                gs = slice(s8 * 8, s8 * 8 + 8)
                p.tt("dve", T[0][:, gs, :], Ar, Trb, ALU.mult, r=[phr[h], tr], w=[Tr_[0]])
                p.tt("dve", T[1][:, gs, :], Ai, Tib, ALU.mult, r=[phr[h], tr], w=[Tr_[1]])
                p.tt("dve", T[2][:, gs, :], Ar, Tib, ALU.mult, r=[phr[h], tr], w=[Tr_[2]])
                p.tt("dve", T[3][:, gs, :], Ai, Trb, ALU.mult, r=[phr[h], tr], w=[Tr_[3]])
            C64, C64n, S64 = (t64[:, k, :] for k in range(3))
            rr = [tr] + Tr_
            for s4 in range(16):
                h = tog; tog ^= 1
                gs = slice(s4 * 4, s4 * 4 + 4)
                o_ps = PH[h][0:64, 0:512]
                p.mm(o_ps, C64, T[0][:, gs, :], True, False, r=rr, w=[phr[h]])
                p.mm(o_ps, C64n, T[1][:, gs, :], False, False, r=rr, w=[phr[h]])
                p.mm(o_ps, S64, T[2][:, gs, :], False, False, r=rr, w=[phr[h]])
                p.mm(o_ps, S64, T[3][:, gs, :], False, True, r=rr, w=[phr[h]])
                p.act(OUT2[:, s4 * 4:s4 * 4 + 4, :], o_ps.rearrange("p (c k) -> p c k", k=128), AF.Copy,
                      r=[phr[h]], w=[outr], scale=float(scale))
            p.dma("sp" if half else "pool", dst[half * 64:half * 64 + 64, :].rearrange("c (a b) -> a c b", b=128), OUT2, r=[outr], w=[dst_res])


def fnet_ctx_stage(p, tabs, src, src_res, dst, dst_res):
    scale = 1.0 / np.sqrt(256 * 128.0)
    with p.scope():
        wt = p.sbuf("wt", [128, 2, 256], BF16); t256 = p.sbuf("t256", [128, 2, 2, 256], BF16); tr = p.res("tabs")
        p.dma("sp", wt[:], tabs["fn_w"], w=[tr]); p.dma("sp", t256[:], tabs["fn_256"], w=[tr])
        stg = p.sbuf("stg", [128, 256], F32); stgr = p.res("stg")
        ub = p.sbuf("ub", [128, 256], BF16); ubr = p.res("ub")
        V = p.sbuf("V", [128, 2, 256], BF16); Vr = p.res("V")
        ps0 = p.psum("ps0", [128, 2, 256]); ps0r = p.res("ps0")
        ps1 = p.psum("ps1", [128, 256]); ps1r = p.res("ps1")
        p.dma("sp", stg[:], src, r=[src_res], w=[stgr])
        p.cp("act", ub[:], stg[:], r=[stgr], w=[ubr])
        for blk in range(2):
            p.mm(ps0[:, blk, :], ub[:, blk * 128:(blk + 1) * 128], wt[:, 0, :], True, True, r=[ubr, tr], w=[ps0r])
        p.cp("dve", V[:], ps0[:], r=[ps0r], w=[Vr])
        k = 0
        for blk in range(2):
            for ri in range(2):
                p.mm(ps1[:], V[:, blk, ri * 128:(ri + 1) * 128], t256[:, blk, ri, :], k == 0, k == 3, r=[Vr, tr], w=[ps1r])
                k += 1
        p.act(stg[:], ps1[:], AF.Copy, r=[ps1r], w=[stgr], scale=float(scale))
        p.dma("sp", dst, stg[:], r=[stgr], w=[dst_res])


STOP = 99


NCK = 66
NEG = -30000.0


def mlstm_stage(p, src, dst_lat, dst_ctx, dst_res, normw_ap, j):
    TOK = NCK * 128
    with p.scope():
        ident, idr = make_ident(p, F32, "idf")
        identb = p.sbuf("identb", [128, 128], BF16)
        p.cp("dve", identb[:], ident[:], r=[idr], w=[idr])
        cr = p.res("consts")
        ones = p.sbuf("ones", [128, 128], F32); onesb = p.sbuf("onesb", [128, 128], BF16)
        p.memset("dve", ones[:], 1.0, w=[cr]); p.memset("dve", onesb[:], 1.0, w=[cr])
        tri = p.sbuf("tri", [128, 2, 128], F32); mneg = p.sbuf("mneg", [128, 2, 128], F32)
        p.memset("dve", tri[:], 1.0, w=[cr]); p.memset("dve", mneg[:], 0.0, w=[cr])
        p.op("pool", lambda e: e.affine_select(out=tri[:, 0, :], in_=tri[:, 0, :], pattern=[[1, 128]], compare_op=ALU.is_ge, fill=0.0, base=0, channel_multiplier=-1), r=[cr], w=[cr])
        p.op("pool", lambda e: e.affine_select(out=tri[:, 1, :], in_=tri[:, 1, :], pattern=[[-1, 128]], compare_op=ALU.is_ge, fill=0.0, base=0, channel_multiplier=1), r=[cr], w=[cr])
        p.op("pool", lambda e: e.affine_select(out=mneg[:, 0, :], in_=mneg[:, 0, :], pattern=[[1, 128]], compare_op=ALU.is_ge, fill=NEG, base=0, channel_multiplier=-1), r=[cr], w=[cr])
        p.op("pool", lambda e: e.affine_select(out=mneg[:, 1, :], in_=mneg[:, 1, :], pattern=[[-1, 128]], compare_op=ALU.is_ge, fill=NEG, base=0, channel_multiplier=1), r=[cr], w=[cr])

        stg = p.sbuf("stg", [128, TOK], F32); stgr = p.res("stg")
        qT = p.sbuf("qT", [128, TOK], BF16); kT = p.sbuf("kT", [128, TOK], BF16); vT = p.sbuf("vT", [128, TOK], BF16)
        qr, kr, vr = p.res("qT"), p.res("kT"), p.res("vT")
        for nm, t_, r_, sc in (("qT", qT, qr, 128.0 ** -0.5), ("kT", kT, kr, 1.0), ("vT", vT, vr, 1.0)):
            lat, ctx, rs = src[nm]
            p.dma("sp", stg[:, 0:256], ctx, r=[rs], w=[stgr]); p.dma("sp", stg[:, 256:], lat, r=[rs], w=[stgr])
            p.act(t_[:], stg[:], AF.Copy, r=[stgr], w=[r_], scale=sc)
        ktok = p.sbuf("ktok", [128, NCK, 128], BF16); vtok = p.sbuf("vtok", [128, NCK, 128], BF16)
        ktr, vtr = p.res("ktok"), p.res("vtok")
        ptb = [p.psum("ptb%d" % i, [128, 8, 128], BF16) for i in range(2)]; ptbr = [p.res("ptb%d" % i) for i in range(2)]
        kk = 0
        for srcT, sr, dstt, dr in ((kT, kr, ktok, ktr), (vT, vr, vtok, vtr)):
            for c8 in range(0, NCK, 8):
                n = min(8, NCK - c8)
                h = kk % 2; kk += 1
                for c in range(n):
                    p.tr(ptb[h][:, c, :], srcT[:, (c8 + c) * 128:(c8 + c + 1) * 128], identb[:], r=[sr, idr], w=[ptbr[h]])
                p.cp("act" if h else "dve", dstt[:, c8:c8 + n, :], ptb[h][:, 0:n, :], r=[ptbr[h]], w=[dr])
        if STOP == 1:
            p.dma('sp', dst_ctx, stg[:, 0:256], r=[stgr], w=[dst_res]); return
        gst = p.sbuf("gst", [NCK, 4, 128], F32); gr = p.res("gst")
        lat, ctx, rs = src["g"]
        p.dma("sp", gst[0:2], ctx.rearrange("g (c s) -> c g s", s=128), r=[rs], w=[gr])
        p.dma("sp", gst[2:NCK], lat.rearrange("g (c s) -> c g s", s=128), r=[rs], w=[gr])
        G = p.sbuf("G", [128, 4, NCK], F32); Gr = p.res("G")
        pg = p.psum("pg", [128, 4, 128]); pgr = p.res("pg")
        for g in range(4):
            p.tr(pg[:, g, 0:NCK], gst[:, g, :], ident[0:NCK, 0:NCK], r=[gr, idr], w=[pgr])
        p.cp("dve", G[:], pg[:, :, 0:NCK], r=[pgr], w=[Gr])
        LFt = p.sbuf("LF", [128, 2, NCK], F32); lfr = p.res("LF")
        for d in range(2):
            p.act(LFt[:, d, :], G[:, 2 * d + 1, :], AF.Exp, r=[Gr], w=[lfr], scale=-1.0)
        p.act(LFt[:], LFt[:], AF.Ln, r=[lfr], w=[lfr], bias=1.0)
        p.ts("dve", LFt[:], LFt[:], -1.0, ALU.mult, r=[lfr], w=[lfr])
        CUM = p.sbuf("CUM", [128, 2, NCK], F32); IB = p.sbuf("IB", [128, 2, NCK], F32)
        SC = p.sbuf("SC", [128, 2, NCK], F32); ET = p.sbuf("ET", [128, 2, NCK], F32)
        pr = p.res("pre")
        pc = p.psum("pc", [128, 4, 128]); pcr = p.res("pc")
        for d in range(2):
            p.mm(pc[:, d, 0:NCK], tri[:, d, :], LFt[:, d, :], True, True, r=[cr, lfr], w=[pcr])
            p.mm(pc[:, 2 + d, 0:NCK], ones[:], LFt[:, d, :], True, True, r=[cr, lfr], w=[pcr])
        p.cp("dve", CUM[:], pc[:, 0:2, 0:NCK], r=[pcr], w=[pr])
        for d in range(2):
            p.tt("dve", IB[:, d, :], G[:, 2 * d, :], CUM[:, d, :], ALU.subtract, r=[Gr, pr], w=[pr])
            p.tt("dve", SC[:, d, :], pc[:, 2 + d, 0:NCK], IB[:, d, :], ALU.add, r=[pcr, pr], w=[pr])
        p.act(SC[:], SC[:], AF.Exp, r=[pr], w=[pr])
        p.act(ET[:], pc[:, 2:4, 0:NCK], AF.Exp, r=[pcr], w=[pr])

        if STOP == 2:
            p.dma('sp', dst_ctx[:, 0:66], SC[:, 0, :], r=[pr], w=[dst_res]); return
        H = p.sbuf("H", [128, TOK], F32); Hr = p.res("H")
        p.memset("pool", H[:], 0.0, w=[Hr])
        PA = [p.psum("pa%d" % d, [128, 4, 128]) for d in range(2)]
        PB = [p.psum("pb%d" % d, [128, 512]) for d in range(2)]
        rP1 = [p.res() for _ in range(2)]; rP2 = [p.res() for _ in range(2)]; rKQ = [p.res() for _ in range(2)]
        rNUM = [p.res() for _ in range(2)]; rDEN = [p.res() for _ in range(2)]; rST = [p.res() for _ in range(2)]
        def T(nm, shape, dt):
            return [p.sbuf("%s%d" % (nm, d), shape, dt) for d in range(2)], [p.res("%s%d" % (nm, d)) for d in range(2)]
        DG, DGr = T("DG", [128, 128], F32); WT, WTr = T("WT", [128, 128], F32); EC, ECr = T("EC", [128, 128], F32)
        QE, QEr = T("QE", [128, 128], BF16); STt, STr = T("ST", [128, 128], BF16)
        DD, DDr = T("DD", [128, 128], F32); HT, HTr = T("HT", [128, 128], F32)
        VS, VSr = T("VS", [128, 132], BF16); CF, CFr = T("CF", [128, 132], F32)
        CB, CBr = T("CB", [128, 128], BF16); NB, NBr = T("NB", [128, 128], BF16)
        for d in range(2):
            p.memset("pool", CF[d][:], 0.0, w=[CFr[d]]); p.memset("pool", CB[d][:], 0.0, w=[CBr[d]]); p.memset("pool", NB[d][:], 0.0, w=[NBr[d]])
        LE = 'dve'
        order = [list(range(NCK)), [1, 0] + list(range(NCK - 1, 1, -1))]
        for step in range(NCK):
            for d in range(2):
                c = order[d][step]
                cs = slice(c * 128, (c + 1) * 128)
                P1 = PA[d][:, 0, :]; P2 = PA[d][:, 1, :]; KQ = PA[d][:, 2, :]
                NUM = PB[d][:, 0:128]; DEN = PB[d][:, 128:256]; STP = PB[d][:, 256:256 + 129]
                p.ts(LE, DG[d][:], ident[:], CUM[:, d, c:c + 1], ALU.mult, r=[idr, pr], w=[DGr[d]])
                p.mm(P1, ones[:], DG[d][:], True, True, r=[cr, DGr[d]], w=[rP1[d]])
                p.mm(P2, ones[:], DG[d][:], True, False, r=[cr, DGr[d]], w=[rP2[d]])
                p.mm(P2, ident[:], mneg[:, d, :], False, True, r=[cr, idr], w=[rP2[d]])
                p.mm(KQ, kT[:, cs], qT[:, cs], True, True, r=[kr, qr], w=[rKQ[d]])
                p.act(WT[d][:], P2, AF.Exp, r=[rP2[d], pr], w=[WTr[d]], bias=IB[:, d, c:c + 1])
                p.act(EC[d][:], P1, AF.Exp, r=[rP1[d]], w=[ECr[d]])
                p.tt("dve", STt[d][:], KQ, WT[d][:], ALU.mult, r=[rKQ[d], WTr[d]], w=[STr[d]])
                p.tt(LE, QE[d][:], qT[:, cs], EC[d][:], ALU.mult, r=[qr, ECr[d]], w=[QEr[d]])
                p.mm(NUM, vtok[:, c, :], STt[d][:], True, False, r=[vtr, STr[d]], w=[rNUM[d]])
                p.mm(NUM, CB[d][:], QE[d][:], False, True, r=[CBr[d], QEr[d]], w=[rNUM[d]])
                p.mm(DEN, onesb[:], STt[d][:], True, False, r=[cr, STr[d]], w=[rDEN[d]])
                p.mm(DEN, NB[d][:], QE[d][:], False, True, r=[NBr[d], QEr[d]], w=[rDEN[d]])
                p.act(DD[d][:], DEN, AF.Abs, r=[rDEN[d]], w=[DDr[d]])
                p.ts("dve", DD[d][:], DD[d][:], 1.0, ALU.max, r=[DDr[d]], w=[DDr[d]])
                p.op("dve", lambda e, t_=DD[d]: e.reciprocal(out=t_[:], in_=t_[:]), r=[DDr[d]], w=[DDr[d]])
                p.tt("dve", HT[d][:], NUM, DD[d][:], ALU.mult, r=[rNUM[d], DDr[d]], w=[HTr[d]])
                p.tt(LE, H[:, cs], H[:, cs], HT[d][:], ALU.add, r=[Hr, HTr[d]], w=[Hr])
                p.ts(LE, VS[d][:, 0:128], vtok[:, c, :], SC[:, d, c:c + 1], ALU.mult, r=[vtr, pr], w=[VSr[d]])
                p.cp(LE, VS[d][:, 128:129], SC[:, d, c:c + 1], r=[pr], w=[VSr[d]])
                p.mm(STP, ktok[:, c, :], VS[d][:, 0:129], True, True, r=[ktr, VSr[d]], w=[rST[d]])
                p.stt(CF[d][:, 0:129], CF[d][:, 0:129], ET[:, d, c:c + 1], STP, ALU.mult, ALU.add, r=[CFr[d], pr, rST[d]], w=[CFr[d]])
                p.cp("act", CB[d][:], CF[d][:, 0:128], r=[CFr[d]], w=[CBr[d]])
                p.cp(LE, NB[d][:], CF[d][:, 128:129].to_broadcast([128, 128]), r=[CFr[d]], w=[NBr[d]])

        nw = p.sbuf("nw", [128, 1], F32); nwr = p.res("nw")
        p.dma("sp", nw[:], normw_ap.rearrange("(a b) -> a b", b=1), w=[nwr])
        lat, ctx, rs = src["oT"]
        p.dma("sp", stg[:, 0:256], ctx, r=[rs], w=[stgr]); p.dma("sp", stg[:, 256:], lat, r=[rs], w=[stgr])
        sq = p.sbuf("sq", [128, 512], F32); sqr = p.res("sq")
        rs_t = p.sbuf("rs_t", [128, 512], F32); rsr = p.res("rs_t")
        pn = [PB[0], PB[1]]; pnr = [p.res(), p.res()]
        ci = 0
        for t0 in range(0, TOK, 512):
            n = min(512, TOK - t0); ts_ = slice(t0, t0 + n)
            h = ci % 2; ci += 1
            p.act(sq[:, 0:n], H[:, ts_], AF.Square, r=[Hr], w=[sqr])
            p.mm(pn[h][:, 0:n], ones[:], sq[:, 0:n], True, True, r=[cr, sqr, rNUM[h], rDEN[h], rST[h]], w=[pnr[h], rNUM[h], rDEN[h], rST[h]])
            p.act(rs_t[:, 0:n], pn[h][:, 0:n], AF.Sqrt, r=[pnr[h]], w=[rsr], scale=1.0 / 128.0, bias=1e-6)
            p.op("dve", lambda e, o=rs_t[:, 0:n]: e.reciprocal(out=o, in_=o), r=[rsr], w=[rsr])
            p.tt("dve", H[:, ts_], H[:, ts_], rs_t[:, 0:n], ALU.mult, r=[Hr, rsr], w=[Hr])
            p.act(stg[:, ts_], stg[:, ts_], AF.Sigmoid, r=[stgr], w=[stgr])
            p.stt(H[:, ts_], H[:, ts_], nw[:, 0:1], stg[:, ts_], ALU.mult, ALU.mult, r=[Hr, nwr, stgr], w=[Hr])
        p.dma("sp", dst_ctx, H[:, 0:256], r=[Hr], w=[dst_res])
        p.dma("sp", dst_lat, H[:, 256:], r=[Hr], w=[dst_res])


D = 1024
EPS = 1e-6


def ttiles(NL, NC):
    out = [(t0, min(512, NL - t0), False) for t0 in range(0, NL, 512)]
    if NC:
        out.append((NL, NC, True))
    return out


def mod_stage(p, cv_ap, ada_w_ap, ada_b_ap, mod_d, mod_res, NM=12):
    with p.scope():
        cv = p.sbuf("cv", [128, 8, 2], F32); cvr = p.res("cv")
        p.dma("sp", cv[:], cv_ap.rearrange("(k p) n -> p k n", p=128), w=[cvr])
        p.act(cv[:], cv[:], AF.Silu, r=[cvr], w=[cvr])
        ab = p.sbuf("ab", [128, NM], F32); abr = p.res("ab")
        p.dma("sp", ab[:], ada_b_ap.rearrange("(m p) -> p m", p=128), w=[abr], slow=True)
        acc = p.sbuf("acc", [128, NM, 2], F32); accr = p.res("acc")
        for n in range(2):
            p.cp("dve", acc[:, :, n], ab[:], r=[abr], w=[accr])
        wt = [p.sbuf("wt%d" % i, [128, 128 * NM], F32) for i in range(2)]; wtr = [p.res("wt%d" % i) for i in range(2)]
        ps = p.psum("ps", [128, NM, 2]); psr = p.res("ps")
        for k in range(8):
            h = k % 2
            p.dma("sp" if h else "pool", wt[h][:], ada_w_ap[128 * k:128 * k + 128, :], w=[wtr[h]])
            for m in range(NM):
                p.mm(ps[:, m, :], wt[h][:, 128 * m:128 * m + 128], cv[:, k, :], True, True, r=[wtr[h], cvr], w=[psr])
            p.tt("dve", acc[:], acc[:], ps[:], ALU.add, r=[accr, psr], w=[accr])
        p.dma("sp", mod_d, acc[:], r=[accr], w=[mod_res])


def load_mod(p, mod_d, mod_res, nw_ap, sidx, scidx):
    modt = p.sbuf("modt", [128, 48, 2], F32); mr = p.res("modt")
    p.dma("sp", modt[:], mod_d, r=[mod_res], w=[mr], slow=True)
    nw = p.sbuf("nw", [128, 8], F32)
    p.dma("sp", nw[:], nw_ap.rearrange("(k p) -> p k", p=128), w=[mr], slow=True)
    A = p.sbuf("A", [128, 8, 2], F32)
    p.ts("dve", A[:], modt[:, 8 * scidx:8 * scidx + 8, :], 1.0, ALU.add, r=[mr], w=[mr])
    p.tt("dve", A[:], A[:], nw[:, :, None].to_broadcast([128, 8, 2]), ALU.mult, r=[mr], w=[mr])
    return modt, A, mr


def norm_mod_tile(p, xt, xr, n, A, SH, col, mr, ones, onr, ps, psr, sq, sqr, rstd, rsr, hb, hbr, hf=None, hfr=None):
    for k in range(8):
        p.act(sq[:, 0:n], xt[:, k, 0:n], AF.Square, r=[xr], w=[sqr])
        p.mm(ps[:, 0:n], ones[:], sq[:, 0:n], k == 0, k == 7, r=[onr, sqr], w=[psr])
    p.act(rstd[:, 0:n], ps[:, 0:n], AF.Sqrt, r=[psr], w=[rsr], scale=1.0 / D, bias=EPS)
    p.op("dve", lambda e, o=rstd[:, 0:n]: e.reciprocal(out=o, in_=o), r=[rsr], w=[rsr])
    for k in range(8):
        p.tt("dve", sq[:, 0:n], xt[:, k, 0:n], rstd[:, 0:n], ALU.mult, r=[xr, rsr, sqr], w=[sqr])
        if hf is not None:
            p.ts("dve", hf[:, k, 0:n], sq[:, 0:n], A[:, k, col:col + 1], ALU.mult, r=[sqr, mr], w=[hfr],
                 s2=SH[:, k, col:col + 1], op1=ALU.add)
            p.cp("act", hb[:, k, 0:n], hf[:, k, 0:n], r=[hfr], w=[hbr])
        else:
            p.ts("dve", hb[:, k, 0:n], sq[:, 0:n], A[:, k, col:col + 1], ALU.mult, r=[sqr, mr], w=[hbr],
                 s2=SH[:, k, col:col + 1], op1=ALU.add)


def norm1_stage(p, xT_d, x_res, NL, NC, mod_d, mod_res, nw_ap, hT_d, h_res):
    with p.scope():
        modt, A, mr = load_mod(p, mod_d, mod_res, nw_ap, 0, 1)
        SH = modt[:, 0:8, :]
        ones = p.sbuf("ones", [128, 128], F32); onr = p.res("ones"); p.memset("dve", ones[:], 1.0, w=[onr])
        xt = [p.sbuf("xt%d" % i, [128, 8, 512], F32) for i in range(2)]; xr = [p.res() for i in range(2)]
        hb = [p.sbuf("hb%d" % i, [128, 8, 512], BF16) for i in range(2)]; hbr = [p.res() for i in range(2)]
        sq = p.sbuf("sq", [128, 512], F32); sqr = p.res(); rstd = p.sbuf("rstd", [128, 512], F32); rsr = p.res()
        ps = p.psum("ps", [128, 512]); psr = p.res()
        for i, (t0, n, isc) in enumerate(ttiles(NL, NC)):
            h = i % 2
            p.dma("sp", xt[h][:, :, 0:n], xT_d[:, t0:t0 + n].rearrange("(k p) t -> p k t", p=128), r=[x_res], w=[xr[h]])
            norm_mod_tile(p, xt[h], xr[h], n, A, SH, 1 if isc else 0, mr, ones, onr, ps, psr, sq, sqr, rstd, rsr, hb[h], hbr[h])
            p.dma("pool", hT_d[i].rearrange("(k p) t -> p k t", p=128), hb[h][:, :, 0:n], r=[hbr[h]], w=[h_res[i]])


def load_w_bf16(p, w_cols_ap, m, wst, wstr, wb, wbr, eng="act", q="sp"):
    p.dma(q, wst[:, :, 0:m], w_cols_ap.rearrange("(k p) m -> p k m", p=128), w=[wstr])
    p.cp(eng, wb[:, :, 0:m], wst[:, :, 0:m], r=[wstr], w=[wbr])


def inproj_gate_stage(p, hT_d, h_res, NL, NC, w_in_ap, b_in_ap, off, nchunk, gT_d, g_res):
    NT = NL + NC
    GW = 4
    with p.scope():
        hT = p.sbuf("hT", [128, 8, NT], BF16); hr = p.res("hT")
        for i, (t0, n, isc) in enumerate(ttiles(NL, NC)):
            p.dma("sp", hT[:, :, t0:t0 + n], hT_d[i].rearrange("(k p) t -> p k t", p=128), r=[h_res[i]], w=[hr])
        bias = p.sbuf("bias", [128, nchunk], F32); br = p.res("bias")
        p.dma("sp", bias[:], b_in_ap[off:off + 128 * nchunk].rearrange("(m p) -> p m", p=128), w=[br], slow=True)
        wst = [p.sbuf("wst%d" % i, [128, 8, 128 * GW], F32) for i in range(2)]; wstr = [p.res() for i in range(2)]
        wb = [p.sbuf("wb%d" % i, [128, 8, 128 * GW], BF16) for i in range(2)]; wbr = [p.res() for i in range(2)]
        ot = [p.sbuf("ot%d" % i, [128, NT], BF16) for i in range(2)]; otr = [p.res() for i in range(2)]
        ps = [p.psum("ps%d" % i, [128, 512]) for i in range(6)]; psr = [p.res() for i in range(6)]
        kk = 0
        ngrp = nchunk // GW
        def load(g):
            h = g % 2
            c0 = off + 128 * GW * g
            p.dma("sp" if h else "pool", wst[h][:], w_in_ap[:, c0:c0 + 128 * GW].rearrange("(k p) m -> p k m", p=128), w=[wstr[h]])
            p.cp("act", wb[h][:], wst[h][:], r=[wstr[h]], w=[wbr[h]])
        load(0)
        for g in range(ngrp):
            h = g % 2
            if g + 1 < ngrp:
                load(g + 1)
            for mi in range(GW):
                m = g * GW + mi
                o = m % 2
                for (t0, n, isc) in ttiles(NL, NC):
                    b_ = kk % 6; kk += 1
                    for k in range(8):
                        p.mm(ps[b_][:, 0:n], wb[h][:, k, 128 * mi:128 * mi + 128], hT[:, k, t0:t0 + n], k == 0, k == 7, r=[wbr[h], hr], w=[psr[b_]])
                    p.act(ot[o][:, t0:t0 + n], ps[b_][:, 0:n], AF.Sigmoid, r=[psr[b_], br], w=[otr[o]], bias=bias[:, m:m + 1])
                p.dma("sp", gT_d[128 * m:128 * m + 128, :], ot[o][:], r=[otr[o]], w=[g_res])


def inproj_mix_stage(p, hall_d, hall_res, NL, NC, w_in_ap, b_in_ap, col_list, zlat_d, zctx_d, z_res):
    NT = NL + NC
    nchunk = len(col_list)
    with p.scope():
        wst = p.sbuf("wst", [128, 8, 128], F32); wstr = p.res()
        W = p.sbuf("W", [128, nchunk, 8, 128], BF16); Wr = p.res("W")
        bias = p.sbuf("bias", [128, nchunk], F32); br = p.res("bias")
        p.memset("dve", bias[:], 0.0, w=[br])
        p.memset("dve", W[:], 0.0, w=[Wr])
        for i, (c0, m) in enumerate(col_list):
            p.dma("sp", wst[:, :, 0:m], w_in_ap[:, c0:c0 + m].rearrange("(k p) m -> p k m", p=128), w=[wstr], slow=(m < 128))
            p.cp("act" if i % 2 else "dve", W[:, i, :, 0:m], wst[:, :, 0:m], r=[wstr], w=[Wr])
            p.dma("pool", bias[0:m, i:i + 1], b_in_ap[c0:c0 + m].rearrange("(a b) -> a b", b=1), w=[br])
        hT = [p.sbuf("hT%d" % i, [128, 8, 512], BF16) for i in range(2)]; hr = [p.res() for i in range(2)]
        ot = [p.sbuf("ot%d" % i, [128, nchunk, 512], F32) for i in range(2)]; otr = [p.res() for i in range(2)]
        ps = [p.psum("ps%d" % i, [128, 512]) for i in range(4)]; psr = [p.res() for i in range(4)]
        kk = 0; ti = 0
        for r in range(4):
            for i_t, (t0, n, isc) in enumerate(ttiles(NL, NC)):
                h = ti % 2; ti += 1
                p.dma("sp", hT[h][:, :, 0:n], hall_d[i_t][1024 * r:1024 * r + 1024, :].rearrange("(k p) t -> p k t", p=128),
                      r=[hall_res[i_t]], w=[hr[h]])
                for i in range(nchunk):
                    b_ = kk % 4; kk += 1
                    for k in range(8):
                        p.mm(ps[b_][:, 0:n], W[:, i, k, :], hT[h][:, k, 0:n], k == 0, k == 7, r=[Wr, hr[h]], w=[psr[b_]])
                    p.act(ot[h][:, i, 0:n], ps[b_][:, 0:n], AF.Identity, r=[psr[b_], br], w=[otr[h]], bias=bias[:, i:i + 1])
                if isc:
                    dst = zctx_d[:, :, NC * r:NC * r + n]
                else:
                    dst = zlat_d[:, :, NL * r + t0:NL * r + t0 + n]
                p.dma("pool", dst.rearrange("i p t -> p i t"), ot[h][:, :, 0:n], r=[otr[h]], w=[z_res])


def yasm_stage(p, srcs, skip_ap, y_own_d, y_res, LL, LC):
    with p.scope():
        sk = p.sbuf("sk", [128, 1], F32); skr = p.res("sk")
        p.dma("sp", sk[:], skip_ap.rearrange("(a b) -> a b", b=1), w=[skr])
        CW = 2048
        A = [p.sbuf("A%d" % i, [128, CW], F32) for i in range(3)]; Ar = [p.res() for i in range(3)]
        O = [p.sbuf("O%d" % i, [128, CW], BF16) for i in range(2)]; Or = [p.res() for i in range(2)]
        kk = 0
        pieces = [(0, t0, min(CW, LL - t0), t0) for t0 in range(0, LL, CW)] + ([(1, 0, LC, LL)] if LC else [])
        for (which, t0, n, o0) in pieces:
            for i, nm in enumerate(("z", "c2", "x2")):
                ap = srcs[nm][which]
                p.dma("sp", A[i][:, 0:n], ap[:, t0:t0 + n], r=[srcs[nm][2]], w=[Ar[i]])
            p.stt(A[0][:, 0:n], A[0][:, 0:n], sk[:, 0:1], A[1][:, 0:n], ALU.mult, ALU.add, r=[Ar[0], Ar[1], skr], w=[Ar[0]])
            h = kk % 2; kk += 1
            p.tt("dve", O[h][:, 0:n], A[0][:, 0:n], A[2][:, 0:n], ALU.mult, r=[Ar[0], Ar[2]], w=[Or[h]])
            for (ap_, rs_, a0, an) in y_own_d(0, o0, n):
                p.dma("pool", ap_, O[h][:, a0:a0 + an], r=[Or[h]], w=[rs_])
            for bi, nm in ((1, "fn"), (2, "ml")):
                ap = srcs[nm][which]
                p.dma("sp", A[bi][:, 0:n], ap[:, t0:t0 + n], r=[srcs[nm][2]], w=[Ar[bi]])
                h = kk % 2; kk += 1
                p.cp("act", O[h][:, 0:n], A[bi][:, 0:n], r=[Ar[bi]], w=[Or[h]])
                for (ap_, rs_, a0, an) in y_own_d(bi, o0, n):
                    p.dma("pool", ap_, O[h][:, a0:a0 + an], r=[Or[h]], w=[rs_])


def merge_stage(p, y_all_d, y_res, hT_d, h_res, wgi_ap, bgi_ap, oh_ap, xT_d, x_res, NL, NC, mod_d, mod_res, wbr_ap, wout_ap, do_ctx):
    LL = 4 * NL
    with p.scope():
        modt = p.sbuf("modt", [128, 48, 2], F32); mr = p.res("modt")
        p.dma("sp", modt[:], mod_d, r=[mod_res], w=[mr])
        oh = p.sbuf("oh", [128, 4], F32); ohr = p.res("oh")
        p.dma("sp", oh[:], oh_ap, w=[ohr])
        bias = p.sbuf("bias", [128, 24], F32)
        p.dma("sp", bias[:], bgi_ap.rearrange("(m p) -> p m", p=128), w=[ohr], slow=True)
        wst = p.sbuf("wst", [128, 8, 1024], F32); wstr = p.res()
        WB = p.sbuf("WB", [128, 12, 1024], BF16); WO = p.sbuf("WO", [128, 8, 1024], BF16); Wr = p.res("W")
        WG = p.sbuf("WG", [128, 8, 3072], BF16)
        for br in range(3):
            p.dma("sp", wst[:, 0:4, :], wbr_ap[br].rearrange("(r p) d -> p r d", p=128), w=[wstr])
            p.cp("act" if br % 2 else "dve", WB[:, 4 * br:4 * br + 4, :], wst[:, 0:4, :], r=[wstr], w=[Wr])
        p.dma("sp", wst[:], wout_ap.rearrange("(k p) d -> p k d", p=128), w=[wstr])
        p.cp("act", WO[:], wst[:], r=[wstr], w=[Wr])
        for c in range(3):
            p.dma("sp" if c % 2 else "pool", wst[:], wgi_ap[:, 1024 * c:1024 * c + 1024].rearrange("(k p) m -> p k m", p=128), w=[wstr])
            p.cp("dve" if c % 2 else "act", WG[:, :, 1024 * c:1024 * c + 1024], wst[:], r=[wstr], w=[Wr])
        Yc = [p.sbuf("Yc%d" % i, [128, 12, 512], BF16) for i in range(2)]; Ycr = [p.res() for i in range(2)]
        Y = p.sbuf("Y", [128, 12, 512], BF16); Yr = p.res("Y")
        hT = p.sbuf("hT", [128, 8, 512], BF16); hr = p.res("hT")
        Gs = [p.sbuf("Gs%d" % i, [128, 512], BF16) for i in range(6)]; Gsr = [p.res() for i in range(6)]
        xt = p.sbuf("xt", [128, 8, 512], F32); xr = p.res("xt")
        mg = p.sbuf("mg", [128, 8, 512], BF16); mgr = p.res("mg")
        t1 = p.sbuf("t1", [128, 512], F32); t1r = p.res(); t2 = p.sbuf("t2", [128, 512], F32); t2r = p.res()
        ps = [p.psum("ps%d" % i, [128, 512]) for i in range(8)]; psr = [p.res() for i in range(8)]
        kk = 0; gi = 0
        tiles = ttiles(NL, NC if do_ctx else 0)
        for ti_, (t0, n, isc) in enumerate(tiles):
            col = 1 if isc else 0
            for jj in range(4):
                c0 = (LL + NC * jj) if isc else (NL * jj + t0)
                h = jj % 2
                for br in range(3):
                    ap_, rs_ = y_all_d(br, c0, n)
                    p.dma("sp" if br % 2 else "pool", Yc[h][:, 4 * br:4 * br + 4, 0:n],
                          ap_.rearrange("(q p) t -> p q t", p=128), r=[rs_], w=[Ycr[h]])
                if jj == 0:
                    p.ts("dve", Y[:, :, 0:n], Yc[h][:, :, 0:n], oh[:, 0:1], ALU.mult, r=[Ycr[h], ohr], w=[Yr])
                else:
                    p.stt(Y[:, :, 0:n], Yc[h][:, :, 0:n], oh[:, jj:jj + 1], Y[:, :, 0:n], ALU.mult, ALU.add, r=[Ycr[h], ohr, Yr], w=[Yr])
            p.dma("sp", hT[:, :, 0:n], hT_d[ti_].rearrange("(k p) t -> p k t", p=128), r=[h_res[ti_]], w=[hr])
            p.dma("pool", xt[:, :, 0:n], xT_d[:, t0:t0 + n].rearrange("(k p) t -> p k t", p=128), r=[x_res], w=[xr])
            for m in range(8):
                gsel = []
                for br in range(3):
                    b_ = kk % 8; kk += 1
                    g_ = gi % 6; gi += 1; gsel.append(g_)
                    cg = (br * 8 + m) * 128
                    for k in range(8):
                        p.mm(ps[b_][:, 0:n], WG[:, k, cg:cg + 128], hT[:, k, 0:n], k == 0, k == 7, r=[Wr, hr], w=[psr[b_]])
                    p.act(Gs[g_][:, 0:n], ps[b_][:, 0:n], AF.Sigmoid, r=[psr[b_], ohr], w=[Gsr[g_]], bias=bias[:, br * 8 + m:br * 8 + m + 1])
                pb = []
                for br in range(3):
                    b_ = kk % 8; kk += 1; pb.append(b_)
                    for r in range(4):
                        p.mm(ps[b_][:, 0:n], WB[:, 4 * br + r, 128 * m:128 * m + 128], Y[:, 4 * br + r, 0:n], r == 0, r == 3,
                             r=[Wr, Yr], w=[psr[b_]])
                p.tt("dve", t1[:, 0:n], ps[pb[0]][:, 0:n], Gs[gsel[0]][:, 0:n], ALU.mult, r=[psr[pb[0]], Gsr[gsel[0]]], w=[t1r])
                p.tt("dve", t2[:, 0:n], ps[pb[1]][:, 0:n], Gs[gsel[1]][:, 0:n], ALU.mult, r=[psr[pb[1]], Gsr[gsel[1]]], w=[t2r])
                p.tt("dve", t1[:, 0:n], t1[:, 0:n], t2[:, 0:n], ALU.add, r=[t1r, t2r], w=[t1r])
                p.tt("dve", t2[:, 0:n], ps[pb[2]][:, 0:n], Gs[gsel[2]][:, 0:n], ALU.mult, r=[psr[pb[2]], Gsr[gsel[2]]], w=[t2r])
                p.tt("dve", mg[:, m, 0:n], t1[:, 0:n], t2[:, 0:n], ALU.add, r=[t1r, t2r], w=[mgr])
            for m in range(8):
                b_ = kk % 8; kk += 1
                for k in range(8):
                    p.mm(ps[b_][:, 0:n], WO[:, k, 128 * m:128 * m + 128], mg[:, k, 0:n], k == 0, k == 7, r=[Wr, mgr], w=[psr[b_]])
                p.stt(xt[:, m, 0:n], ps[b_][:, 0:n], modt[:, 16 + m, col:col + 1], xt[:, m, 0:n], ALU.mult, ALU.add,
                      r=[psr[b_], mr, xr], w=[xr])
            p.dma("sp", xT_d[:, t0:t0 + n].rearrange("(k p) t -> p k t", p=128), xt[:, :, 0:n], r=[xr], w=[x_res])


def moe_stage(p, xT_d, x_res, NL, NC, mod_d, mod_res, nw_ap, wr_ap, br_ap, wg_ap, wu_ap, wd_ap, do_ctx,
              final_nw_ap=None, out_d=None, out_res=None):
    NCX = NC if do_ctx else 0
    NT = NL + NCX
    tiles = ttiles(NL, NCX)
    nsub = (NT + 127) // 128
    with p.scope():
        modt, A, mr = load_mod(p, mod_d, mod_res, nw_ap, 3, 4)
        SH = modt[:, 24:32, :]
        ones = p.sbuf("ones", [128, 128], F32); onr = p.res("ones"); p.memset("dve", ones[:], 1.0, w=[onr])
        identf, idr = make_ident(p, F32, "idf")
        sq = p.sbuf("sq", [128, 512], F32); sqr = p.res(); rstd = p.sbuf("rstd", [128, 512], F32); rsr = p.res()
        xt = p.sbuf("xt", [128, 8, 512], F32); xr = p.res("xt")
        hf = p.sbuf("hf", [128, 8, 512], F32); hfr = p.res("hf")
        H2 = p.sbuf("H2", [128, 8, NT], BF16); h2r = p.res("H2")
        WR = p.sbuf("WR", [128, 8, 20], F32); wrr = p.res("WR")
        p.dma("sp", WR[:], wr_ap.rearrange("(k p) n -> p k n", p=128), w=[wrr], slow=True)
        BR = p.sbuf("BR", [128, 20], F32)
        p.dma("sp", BR[:], br_ap.partition_broadcast(128), w=[wrr])
        CWt = p.sbuf("CWt", [128, nsub, 16], F32); cwr = p.res("CW")
        ps = [p.psum("ps%d" % i, [128, 512]) for i in range(6)]; psr = [p.res() for i in range(6)]
        pr_ = p.psum("pr", [128, 32]); prr = p.res("pr")
        def st(nm, w):
            return p.sbuf(nm, [128, w], F32)
        L_ = st("L", 20); gm = st("gm", 1); ge = st("ge", 4); gs = st("gs", 1); gmask = st("gmask", 4)
        tmp16 = st("tmp16", 16); eg = st("eg", 4); m1 = st("m1", 1); mk1 = st("mk1", 4); eg2 = st("eg2", 4); m2 = st("m2", 1)
        mk2 = st("mk2", 4); w1 = st("w1", 1); w2 = st("w2", 1); cwe = st("cwe", 4)
        rr = p.res("route")
        for (t0, n, isc) in tiles:
            col = 1 if isc else 0
            p.dma("sp", xt[:, :, 0:n], xT_d[:, t0:t0 + n].rearrange("(k p) t -> p k t", p=128), r=[x_res], w=[xr])
            norm_mod_tile(p, xt, xr, n, A, SH, col, mr, ones, onr, ps[0], psr[0], sq, sqr, rstd, rsr,
                          H2[:, :, t0:t0 + n], h2r, hf=hf, hfr=hfr)
            for s0 in range(0, n, 128):
                sn = min(128, n - s0); si = (t0 + s0) // 128
                for k in range(8):
                    p.mm(pr_[0:sn, 0:20], hf[:, k, s0:s0 + sn], WR[:, k, :], k == 0, k == 7, r=[hfr, wrr], w=[prr])
                R = [rr]
                p.tt("dve", L_[0:sn], pr_[0:sn, 0:20], BR[0:sn], ALU.add, r=[prr, wrr, rr], w=R)
                p.op("dve", lambda e, o=gm[0:sn], i=L_[0:sn, 0:4]: e.tensor_reduce(out=o, in_=i, axis=AX.X, op=ALU.max), r=R, w=R)
                p.ts("dve", gmask[0:sn], L_[0:sn, 0:4], gm[0:sn, 0:1], ALU.is_equal, r=R, w=R)
                p.ts("dve", gm[0:sn], gm[0:sn], -1.0, ALU.mult, r=R, w=R)
                p.act(ge[0:sn], L_[0:sn, 0:4], AF.Exp, r=R, w=R, bias=gm[0:sn, 0:1])
                p.op("dve", lambda e, o=gs[0:sn], i=ge[0:sn]: e.tensor_reduce(out=o, in_=i, axis=AX.X, op=ALU.add), r=R, w=R)
                p.op("dve", lambda e, o=gs[0:sn]: e.reciprocal(out=o, in_=o), r=R, w=R)
                p.tt("dve", tmp16[0:sn].rearrange("p (g e) -> p g e", e=4), L_[0:sn, 4:20].rearrange("p (g e) -> p g e", e=4),
                     gmask[0:sn, :, None].to_broadcast([sn, 4, 4]), ALU.mult, r=R, w=R)
                p.op("dve", lambda e, o=eg[0:sn], i=tmp16[0:sn].rearrange("p (g e) -> p e g", e=4): e.tensor_reduce(out=o, in_=i, axis=AX.X, op=ALU.add), r=R, w=R)
                p.op("dve", lambda e, o=m1[0:sn], i=eg[0:sn]: e.tensor_reduce(out=o, in_=i, axis=AX.X, op=ALU.max), r=R, w=R)
                p.ts("dve", mk1[0:sn], eg[0:sn], m1[0:sn, 0:1], ALU.is_equal, r=R, w=R)
                p.stt(eg2[0:sn], mk1[0:sn], -1e30, eg[0:sn], ALU.mult, ALU.add, r=R, w=R)
                p.op("dve", lambda e, o=m2[0:sn], i=eg2[0:sn]: e.tensor_reduce(out=o, in_=i, axis=AX.X, op=ALU.max), r=R, w=R)
                p.ts("dve", mk2[0:sn], eg2[0:sn], m2[0:sn, 0:1], ALU.is_equal, r=R, w=R)
                p.tt("dve", w1[0:sn], m2[0:sn], m1[0:sn], ALU.subtract, r=R, w=R)
                p.act(w1[0:sn], w1[0:sn], AF.Exp, r=R, w=R)
                p.ts("dve", w1[0:sn], w1[0:sn], 1.0, ALU.add, r=R, w=R)
                p.op("dve", lambda e, o=w1[0:sn]: e.reciprocal(out=o, in_=o), r=R, w=R)
                p.ts("dve", w2[0:sn], w1[0:sn], -1.0, ALU.mult, r=R, w=R, s2=1.0, op1=ALU.add)
                p.tt("dve", w1[0:sn], w1[0:sn], gs[0:sn], ALU.mult, r=R, w=R)
                p.tt("dve", w2[0:sn], w2[0:sn], gs[0:sn], ALU.mult, r=R, w=R)
                p.ts("dve", cwe[0:sn], mk1[0:sn], w1[0:sn, 0:1], ALU.mult, r=R, w=R)
                p.stt(cwe[0:sn], mk2[0:sn], w2[0:sn, 0:1], cwe[0:sn], ALU.mult, ALU.add, r=R, w=R)
                p.cp("dve", tmp16[0:sn].rearrange("p (g e) -> p g e", e=4), cwe[0:sn, None, :].to_broadcast([sn, 4, 4]), r=R, w=R)
                p.tt("dve", CWt[0:sn, si, :].rearrange("p (g e) -> p g e", e=4), tmp16[0:sn].rearrange("p (g e) -> p g e", e=4),
                     gmask[0:sn, :, None].to_broadcast([sn, 4, 4]), ALU.mult, r=R, w=[cwr, rr])
        ACC = p.sbuf("ACC", [128, nsub, 1024], F32); accr = p.res("ACC")
        p.memset("dve", ACC[:], 0.0, w=[accr])
        wst = [p.sbuf("wst%d" % i, [128, 8, 256], F32) for i in range(2)]; wstr = [p.res() for i in range(2)]
        WG = [p.sbuf("WG%d" % i, [128, 8, 256], BF16) for i in range(2)]; WU = [p.sbuf("WU%d" % i, [128, 8, 256], BF16) for i in range(2)]
        WD = [p.sbuf("WD%d" % i, [128, 2, 1024], BF16) for i in range(2)]
        wer = [p.res() for i in range(2)]
        SG = p.sbuf("SG", [128, 512], BF16); sgr = p.res()
        AA = p.sbuf("AA", [128, 2, 512], BF16); aar = p.res()
        kk = 0
        for e_ in range(16):
            h = e_ % 2
            p.dma("sp", wst[0][:], wg_ap[e_].rearrange("(k p) m -> p k m", p=128), w=[wstr[0]])
            p.cp("act", WG[h][:], wst[0][:], r=[wstr[0]], w=[wer[h]])
            p.dma("pool", wst[1][:], wu_ap[e_].rearrange("(k p) m -> p k m", p=128), w=[wstr[1]])
            p.cp("act", WU[h][:], wst[1][:], r=[wstr[1]], w=[wer[h]])
            p.dma("sp", wst[0][:].rearrange("p a b -> p (a b)").rearrange("p (c d) -> p c d", c=2), wd_ap[e_].rearrange("(c p) d -> p c d", p=128), w=[wstr[0]])
            p.cp("act", WD[h][:], wst[0][:].rearrange("p a b -> p (a b)").rearrange("p (c d) -> p c d", c=2), r=[wstr[0]], w=[wer[h]])
            for (t0, n, isc) in tiles:
                for hc in range(2):
                    bg = kk % 6; kk += 1; bu = kk % 6; kk += 1
                    for k in range(8):
                        p.mm(ps[bg][:, 0:n], WG[h][:, k, 128 * hc:128 * hc + 128], H2[:, k, t0:t0 + n], k == 0, k == 7, r=[wer[h], h2r], w=[psr[bg]])
                    for k in range(8):
                        p.mm(ps[bu][:, 0:n], WU[h][:, k, 128 * hc:128 * hc + 128], H2[:, k, t0:t0 + n], k == 0, k == 7, r=[wer[h], h2r], w=[psr[bu]])
                    p.act(SG[:, 0:n], ps[bg][:, 0:n], AF.Silu, r=[psr[bg]], w=[sgr])
                    p.tt("dve", AA[:, hc, 0:n], ps[bu][:, 0:n], SG[:, 0:n], ALU.mult, r=[psr[bu], sgr], w=[aar])
                for s0 in range(0, n, 128):
                    sn = min(128, n - s0); si = (t0 + s0) // 128
                    for dh in range(2):
                        b_ = kk % 6; kk += 1
                        for hc in range(2):
                            p.mm(ps[b_][0:sn, :], AA[:, hc, s0:s0 + sn], WD[h][:, hc, 512 * dh:512 * dh + 512], hc == 0, hc == 1,
                                 r=[aar, wer[h]], w=[psr[b_]])
                        p.stt(ACC[0:sn, si, 512 * dh:512 * dh + 512], ps[b_][0:sn, :], CWt[0:sn, si, e_:e_ + 1],
                              ACC[0:sn, si, 512 * dh:512 * dh + 512], ALU.mult, ALU.add, r=[psr[b_], cwr, accr], w=[accr])
        if final_nw_ap is not None:
            fw_ = p.sbuf("fw", [128, 8], F32); fwr = p.res()
            p.dma("sp", fw_[:], final_nw_ap.rearrange("(k p) -> p k", p=128), w=[fwr], slow=True)
        for (t0, n, isc) in tiles:
            col = 1 if isc else 0
            p.dma("sp", xt[:, :, 0:n], xT_d[:, t0:t0 + n].rearrange("(k p) t -> p k t", p=128), r=[x_res], w=[xr])
            for m in range(8):
                b_ = kk % 6; kk += 1
                for s0 in range(0, n, 128):
                    sn = min(128, n - s0); si = (t0 + s0) // 128
                    p.tr(ps[b_][:, s0:s0 + sn], ACC[0:sn, si, 128 * m:128 * m + 128], identf[0:sn, 0:sn], r=[accr, idr], w=[psr[b_]])
                p.stt(xt[:, m, 0:n], ps[b_][:, 0:n], modt[:, 40 + m, col:col + 1], xt[:, m, 0:n], ALU.mult, ALU.add,
                      r=[psr[b_], mr, xr], w=[xr])
            if final_nw_ap is None:
                p.dma("pool", xT_d[:, t0:t0 + n].rearrange("(k p) t -> p k t", p=128), xt[:, :, 0:n], r=[xr], w=[x_res])
            elif not isc:
                for k in range(8):
                    p.act(sq[:, 0:n], xt[:, k, 0:n], AF.Square, r=[xr], w=[sqr])
                    p.mm(ps[0][:, 0:n], ones[:], sq[:, 0:n], k == 0, k == 7, r=[onr, sqr], w=[psr[0]])
                p.act(rstd[:, 0:n], ps[0][:, 0:n], AF.Sqrt, r=[psr[0]], w=[rsr], scale=1.0 / D, bias=EPS)
                p.op("dve", lambda e, o=rstd[:, 0:n]: e.reciprocal(out=o, in_=o), r=[rsr], w=[rsr])
                for k in range(8):
                    p.stt(hf[:, k, 0:n], xt[:, k, 0:n], fw_[:, k:k + 1], rstd[:, 0:n], ALU.mult, ALU.mult, r=[xr, fwr, rsr], w=[hfr])
                p.dma("pool", out_d[:, t0:t0 + n].rearrange("(k p) t -> p k t", p=128), hf[:, :, 0:n], r=[hfr], w=[out_res])

NLAT, NCTX = 2048, 64
LLAT, LCTX = 8192, 256
GROUPS = [[0, 1, 2, 3], [4, 5, 6, 7]]
DEPTH = 2
OFF_FN, OFF_ML, OFF_MLG, OFF_GATE = 1536, 2048, 4096, 4112


def build_program(const_np):
    p = Prog()
    I = {}

    def inp(name, shape, dt=F32):
        I[name] = p.dram(name, shape, dt, "ExternalInput")
        return I[name]

    inp("xT0", [1024, NLAT]); inp("cT0", [1024, NCTX]); inp("cv", [1024, 2]); inp("oh", [128, 4]); inp("norm_f", [1024])
    for k, v in const_np.items():
        inp(k, v.shape, F32 if v.dtype == np.float32 else BF16)
    for l in range(DEPTH):
        L = "_%d" % l
        inp("ada_w" + L, [1024, 1536]); inp("ada_b" + L, [1536]); inp("n1w" + L, [1024]); inp("n2w" + L, [1024])
        inp("w_gate_in" + L, [1024, 3072]); inp("b_gate_in" + L, [3072]); inp("w_mix" + L, [1024, 1028]); inp("b_mix" + L, [1028])
        inp("hy_cw" + L, [3, 3, 384]); inp("hy_cb" + L, [384]); inp("ml_cw" + L, [3, 3, 256]); inp("ml_cb" + L, [256])
        inp("f_w1" + L, [33, 64]); inp("f_b1" + L, [64]); inp("f_freq" + L, [64]); inp("f_w2" + L, [64, 64]); inp("f_b2" + L, [64])
        inp("f_w3" + L, [64, 4, 128]); inp("decay" + L, [128, 4]); inp("skip" + L, [2, 128]); inp("mlnw" + L, [128])
        inp("wbr" + L, [3, 512, 1024]); inp("wout" + L, [1024, 1024])
        inp("wr" + L, [1024, 20]); inp("br" + L, [20]); inp("wg" + L, [16, 1024, 256]); inp("wu" + L, [16, 1024, 256]); inp("wd" + L, [16, 256, 1024])
    outT = p.dram("outT", [1024, NLAT], F32, "ExternalOutput"); out_res = p.res("outT")

    NT = NLAT + NCTX
    S = {}

    def scr(name, shape, dt=F32):
        S[name] = (p.dram("s_" + name, shape, dt), p.res("s_" + name))
        return S[name]

    scr("mod_own", [128, 12, 2]); scr("mod_all", [4 * 128, 24]); scr("mod_full", [128, 48, 2])
    scr("xT", [1024, NT])
    TT = ttiles(NLAT, NCTX)
    hown = [scr("hown%d" % i, [1024, n], BF16) for i, (t0, n, isc) in enumerate(TT)]
    hall = [scr("hall%d" % i, [4096, n], BF16) for i, (t0, n, isc) in enumerate(TT)]
    YCH = [(0, 3072), (3072, 3072), (6144, 2304)]
    yown = [[scr("yown%d_%d" % (br, ck), [128, w_], BF16) for ck, (c0_, w_) in enumerate(YCH)] for br in range(3)]
    yall = [[scr("yall%d_%d" % (br, ck), [512, w_], BF16) for ck, (c0_, w_) in enumerate(YCH)] for br in range(3)]

    def y_own_fn(br, col0, n):
        out = []
        for ck, (c0_, w_) in enumerate(YCH):
            lo = max(col0, c0_); hi = min(col0 + n, c0_ + w_)
            if hi > lo:
                out.append((yown[br][ck][0][:, lo - c0_:hi - c0_], yown[br][ck][1], lo - col0, hi - lo))
        return out

    def y_all_fn(br, col0, n):
        for ck, (c0_, w_) in enumerate(YCH):
            if c0_ <= col0 and col0 + n <= c0_ + w_:
                return yall[br][ck][0][:, col0 - c0_:col0 - c0_ + n], yall[br][ck][1]
        raise AssertionError("y tile straddles chunks")
    scr("zlat", [9, 128, LLAT]); scr("zctx", [9, 128, LCTX]); scr("cvl", [5, 128, LLAT]); scr("cvc", [5, 128, LCTX])
    scr("fl", [4, 128, LLAT]); scr("fc", [4, 128, LCTX])
    for nm in ("c1", "c2", "zz", "fn", "ml"):
        scr(nm + "l", [128, LLAT]); scr(nm + "c", [128, LCTX])

    tabs = {k: I[k] for k in const_np}
    xT, xres = S["xT"]
    with p.scope():
        t = p.sbuf("t", [128, 8, NT], F32); tr = p.res()
        p.dma("sp", t[:, :, 0:NLAT], I["xT0"].rearrange("(k p) t -> p k t", p=128), w=[tr])
        p.dma("pool", t[:, :, NLAT:NT], I["cT0"].rearrange("(k p) t -> p k t", p=128), w=[tr])
        p.dma("sp", xT.rearrange("(k p) t -> p k t", p=128), t[:], r=[tr], w=[xres])

    mod_view = S["mod_all"][0].rearrange("(r p) (m n) -> p r m n", p=128, n=2)

    for l in range(DEPTH):
        L = "_%d" % l
        last = (l == DEPTH - 1)
        W = lambda nm: I[nm + L]
        p.label = 'mod_stage'; mod_stage(p, I["cv"], W("ada_w"), W("ada_b"), S["mod_own"][0], S["mod_own"][1], NM=12)
        p.coll("AllGather", S["mod_all"][0], S["mod_own"][0].rearrange("p m n -> p (m n)"), GROUPS, r=[S["mod_own"][1]], w=[S["mod_all"][1]])
        with p.scope():
            mt = p.sbuf("mt", [128, 48, 2], F32); mtr = p.res()
            p.dma("sp", mt[:].rearrange("p (r m) n -> p r m n", r=4), mod_view, r=[S["mod_all"][1]], w=[mtr], slow=True)
            p.dma("sp", S["mod_full"][0], mt[:], r=[mtr], w=[S["mod_full"][1]])
        modv, modr = S["mod_full"]
        p.label = 'norm1_stage'; norm1_stage(p, xT, xres, NLAT, NCTX, modv, modr, W("n1w"), [h_[0] for h_ in hown], [h_[1] for h_ in hown])
        for i in range(len(TT)):
            p.coll("AllGather", hall[i][0], hown[i][0], GROUPS, r=[hown[i][1]], w=[hall[i][1]])
        fw = {k: W(k) for k in ("f_w1", "f_b1", "f_freq", "f_w2", "f_b2", "f_w3", "decay")}
        p.label = 'hfilt_stage'; hfilt_stage(p, I["featsT_l"], I["trow_l"], fw, 0, LLAT, S["fl"][0], S["fl"][1])
        if not last:
            p.label = 'hfilt_stage'; hfilt_stage(p, I["featsT_c"], I["trow_c"], fw, 0, LCTX, S["fc"][0], S["fc"][1])
        cols = [(128 * i, 128) for i in range(8)] + [(1024, 4)]
        p.label = 'inproj_mix_stage'; inproj_mix_stage(p, [h_[0] for h_ in hall], [h_[1] for h_ in hall], NLAT, NCTX, W("w_mix"), W("b_mix"), cols, S["zlat"][0], S["zctx"][0], S["zlat"][1])
        zl, zc, zr = S["zlat"][0], S["zctx"][0], S["zlat"][1]
        cvl, cvc = S["cvl"][0], S["cvc"][0]
        cvr = S["cvl"][1]
        jl = [(zl[0], zr, W("hy_cw"), W("hy_cb"), 0, False, cvl[0], cvr), (zl[1], zr, W("hy_cw"), W("hy_cb"), 128, False, cvl[1], cvr),
              (zl[2], zr, W("hy_cw"), W("hy_cb"), 256, False, cvl[2], cvr), (zl[4], zr, W("ml_cw"), W("ml_cb"), 0, True, cvl[3], cvr),
              (zl[5], zr, W("ml_cw"), W("ml_cb"), 128, True, cvl[4], cvr)]
        p.label = 'conv_stage'; conv_stage(p, jl, 128, 64)
        jc = [(zc[4], zr, W("ml_cw"), W("ml_cb"), 0, True, cvc[3], cvr), (zc[5], zr, W("ml_cw"), W("ml_cb"), 128, True, cvc[4], cvr)]
        if not last:
            jc += [(zc[0], zr, W("hy_cw"), W("hy_cb"), 0, False, cvc[0], cvr), (zc[1], zr, W("hy_cw"), W("hy_cb"), 128, False, cvc[1], cvr),
                   (zc[2], zr, W("hy_cw"), W("hy_cb"), 256, False, cvc[2], cvr)]
        p.label = 'conv_stage'; conv_stage(p, jc, 1, 256)
        variants = [("l", LLAT, cvl, "featsT_l", "trow_l")] + ([] if last else [("c", LCTX, cvc, "featsT_c", "trow_c")])
        for (sfx, Lx, cvx, fnm, tnm) in variants:
            filt, fr = S["f" + sfx]
            src1 = {"v": (cvx[0], cvr), "kf": (filt[0], fr), "kg": (filt[1], fr)}
            p.label = 'hconv_stage'; hconv_stage(p, tabs, src1, S["c1" + sfx][0], S["c1" + sfx][1], 128, Lx, True)
            src2 = {"v": (cvx[0], cvr), "kf": (filt[2], fr), "kg": (filt[3], fr), "x1": (cvx[1], cvr),
                    "c1": S["c1" + sfx], "skip": (W("skip")[0], p.res())}
            p.label = 'hconv_stage'; hconv_stage(p, tabs, src2, S["c2" + sfx][0], S["c2" + sfx][1], 128, Lx, False, u_out=S["zz" + sfx][0], u_res=S["zz" + sfx][1])
            zsrc = zl if sfx == "l" else zc
            p.label = 'fnet_stage'
            if sfx == "l":
                fnet_stage(p, tabs, zsrc[3], zr, S["fn" + sfx][0], S["fn" + sfx][1], Lx)
            else:
                fnet_ctx_stage(p, tabs, zsrc[3], zr, S["fn" + sfx][0], S["fn" + sfx][1])
        msrc = {"qT": (cvl[3], cvc[3], cvr), "kT": (cvl[4], cvc[4], cvr), "vT": (zl[6], zc[6], zr), "oT": (zl[7], zc[7], zr),
                "g": (zl[8][0:4], zc[8][0:4], zr)}
        p.label = 'mlstm_stage'; mlstm_stage(p, msrc, S["mll"][0], S["mlc"][0], S["mll"][1], W("mlnw"), 0)
        S["mlc"] = (S["mlc"][0], S["mll"][1])
        ys = {"x2": (cvl[2], cvc[2], cvr)}
        for nm, key in (("c2", "c2"), ("z", "zz"), ("fn", "fn"), ("ml", "ml")):
            ys[nm] = (S[key + "l"][0], S[key + "c"][0], S[key + "l"][1])
        p.label = 'yasm_stage'; yasm_stage(p, ys, W("skip")[1], y_own_fn, None, LLAT, LCTX if not last else 0)
        for br in range(3):
            for ck in range(len(YCH)):
                p.coll("AllGather", yall[br][ck][0], yown[br][ck][0], GROUPS, r=[yown[br][ck][1]], w=[yall[br][ck][1]])
        p.label = 'merge_stage'; merge_stage(p, y_all_fn, None, [h_[0] for h_ in hown], [h_[1] for h_ in hown], W("w_gate_in"), W("b_gate_in"),
                                             I["oh"], xT, xres, NLAT, NCTX, modv, modr, W("wbr"), W("wout"), not last)
        if last:
            p.label = 'moe_stage'; moe_stage(p, xT, xres, NLAT, NCTX, modv, modr, W("n2w"), W("wr"), W("br"), W("wg"), W("wu"), W("wd"), False,
                      I["norm_f"], outT, out_res)
        else:
            p.label = 'moe_stage'; moe_stage(p, xT, xres, NLAT, NCTX, modv, modr, W("n2w"), W("wr"), W("br"), W("wg"), W("wu"), W("wd"), True)
    return p.finish(), p


_CACHE = {}


def _consts():
    c = {}
    c.update(hconv_tables()); c.update(fnet_tables())
    fl, tl = hfilt_consts(LLAT); fc, tc = hfilt_consts(LCTX)
    c["featsT_l"] = fl; c["trow_l"] = tl; c["featsT_c"] = fc; c["trow_c"] = tc
    return c


def kernel(x, c, ctx, c_ctx, ada_w, ada_b, norm1_w, norm2_w, w_in, b_in,
           hy_conv_w, hy_conv_b, hy_f_w1, hy_f_b1, hy_f_w2, hy_f_b2, hy_f_w3, hy_f_freq,
           hy_decay, hy_skip, ml_conv_w, ml_conv_b, ml_norm_w, w_branch, w_out,
           moe_rg_w, moe_rg_b, moe_re_w, moe_re_b, moe_w_gate, moe_w_up, moe_w_down, norm_f_w):
    f32 = lambda a: np.ascontiguousarray(np.asarray(a, dtype=np.float32))
    x, c, ctx, c_ctx = f32(x), f32(c), f32(ctx), f32(c_ctx)
    if "nc" not in _CACHE:
        _CACHE["const"] = _consts()
        _CACHE["nc"] = build_program(_CACHE["const"])[0]
    const = _CACHE["const"]
    nc = _CACHE["nc"]
    in_maps = []
    for core in range(8):
        b, j = core // 4, core % 4
        m = dict(const)
        m["xT0"] = f32(x[b, NLAT * j:NLAT * (j + 1), :].T)
        m["cT0"] = f32(ctx[b, NCTX * j:NCTX * (j + 1), :].T)
        m["cv"] = f32(np.stack([c[b], c_ctx], axis=1))
        oh = np.zeros((128, 4), np.float32); oh[:, j] = 1.0
        m["oh"] = oh
        m["norm_f"] = f32(norm_f_w)
        sl = slice(128 * j, 128 * j + 128)
        for l in range(DEPTH):
            L = "_%d" % l
            m["ada_w" + L] = f32(ada_w[l][:, 1536 * j:1536 * (j + 1)]); m["ada_b" + L] = f32(ada_b[l][1536 * j:1536 * (j + 1)])
            m["n1w" + L] = f32(norm1_w[l]); m["n2w" + L] = f32(norm2_w[l])
            m["w_gate_in" + L] = f32(w_in[l][:, OFF_GATE:]); m["b_gate_in" + L] = f32(b_in[l][OFF_GATE:])
            mixcols = np.concatenate([np.arange(128 * j, 128 * j + 128) + o for o in
                                      (0, 512, 1024, OFF_FN, OFF_ML, OFF_ML + 512, OFF_ML + 1024, OFF_ML + 1536)]
                                     + [np.array([OFF_MLG + j, OFF_MLG + 4 + j, OFF_MLG + 8 + j, OFF_MLG + 12 + j])])
            m["w_mix" + L] = f32(w_in[l][:, mixcols]); m["b_mix" + L] = f32(b_in[l][mixcols])
            hyc = np.concatenate([np.arange(128 * j, 128 * j + 128) + o for o in (0, 512, 1024)])
            m["hy_cw" + L] = f32(hy_conv_w[l][:, :, hyc]); m["hy_cb" + L] = f32(hy_conv_b[l][hyc])
            mlc = np.concatenate([np.arange(128 * j, 128 * j + 128) + o for o in (0, 512)])
            m["ml_cw" + L] = f32(ml_conv_w[l][:, :, mlc]); m["ml_cb" + L] = f32(ml_conv_b[l][mlc])
            m["f_w1" + L] = f32(hy_f_w1[l]); m["f_b1" + L] = f32(hy_f_b1[l]); m["f_freq" + L] = f32(hy_f_freq[l])
            m["f_w2" + L] = f32(hy_f_w2[l]); m["f_b2" + L] = f32(hy_f_b2[l])
            m["f_w3" + L] = f32(np.asarray(hy_f_w3[l]).reshape(64, 4, 512)[:, :, sl])
            m["decay" + L] = f32(np.asarray(hy_decay[l]).reshape(4, 512)[:, sl].T)
            m["skip" + L] = f32(hy_skip[l][:, sl]); m["mlnw" + L] = f32(ml_norm_w[l][sl])
            m["wbr" + L] = f32(w_branch[l]); m["wout" + L] = f32(w_out[l])
            m["wr" + L] = f32(np.concatenate([moe_rg_w[l], moe_re_w[l]], axis=1)); m["br" + L] = f32(np.concatenate([moe_rg_b[l], moe_re_b[l]]))
            m["wg" + L] = f32(moe_w_gate[l]); m["wu" + L] = f32(moe_w_up[l]); m["wd" + L] = f32(moe_w_down[l])
        in_maps.append(m)
    res = run_bass_kernel_spmd(nc, in_maps, core_ids=list(range(8)))
    out = np.empty((2, LLAT, 1024), np.float32)
    for core in range(8):
        b, j = core // 4, core % 4
        out[b, NLAT * j:NLAT * (j + 1), :] = np.asarray(res.results[core]["outT"], dtype=np.float32).T
    return out
```
